# Optimizing a Trainium2 kernel written in Bass

```python
import jax, jax.numpy as jnp
from jax import lax
import numpy as np

D_MODEL = 2048
BATCH = 4
SEQ = 2048
DEPTH = 2
DEC_BATCH = 128
DEC_SEQ = 4
PAST_LEN = 16384
PAGE_SIZE = 128

D_MIX = D_MODEL
D_CONV = D_MIX // 2
N_CONV_HEADS = 4
D_POOL = D_MIX - D_CONV
N_POOL_GROUPS = 4
POOL_GROUP_W = D_POOL // N_POOL_GROUPS
POOL_WINDOWS = (2, 4, 8, 16)
POOL_MAX = 16
CONV_W = 31
D_FF = 4 * D_MODEL
D_IN = 2 * D_CONV + D_POOL
EPS = 1e-6

kernel_name = "hymba_style_conformer_conv_multiscale_pool_decoder_step"


def rmsnorm(x, g):
    xf = x.astype(jnp.float32)
    y = xf * lax.rsqrt(jnp.mean(xf * xf, axis=-1, keepdims=True) + EPS)
    return (y * g.astype(jnp.float32)).astype(x.dtype)


def layernorm(x, g, b):
    xf = x.astype(jnp.float32)
    mu = jnp.mean(xf, axis=-1, keepdims=True)
    var = jnp.mean(jnp.square(xf - mu), axis=-1, keepdims=True)
    y = (xf - mu) * lax.rsqrt(var + EPS)
    return (y * g.astype(jnp.float32) + b.astype(jnp.float32)).astype(x.dtype)


def depthwise_causal_conv(u_ext, w, b):
    c = u_ext.shape[-1]
    y = lax.conv_general_dilated(
        u_ext, w[:, None, :].astype(u_ext.dtype), window_strides=(1,), padding="VALID",
        dimension_numbers=("NWC", "WIO", "NWC"), feature_group_count=c)
    return y + b


def multiscale_pool(z_ext, pos0):
    bsz, L, _ = z_ext.shape
    H = POOL_MAX - 1
    T = L - H
    zf = z_ext.astype(jnp.float32).reshape(bsz, L, N_POOL_GROUPS, POOL_GROUP_W)
    cs = jnp.concatenate([jnp.zeros_like(zf[:, :1]), jnp.cumsum(zf, axis=1)], axis=1)
    win = jnp.array(POOL_WINDOWS, dtype=jnp.int32)
    hi = H + 1 + jnp.arange(T, dtype=jnp.int32)
    lo = hi[:, None] - win[None, :]
    g_idx = jnp.arange(N_POOL_GROUPS)[None, :]
    s = cs[:, hi] - cs[:, lo, g_idx]
    pos = pos0 + jnp.arange(T, dtype=jnp.int32)
    cnt = jnp.minimum(win[None, :], pos[:, None] + 1).astype(jnp.float32)
    d = s / cnt[None, :, :, None] - zf[:, H:]
    return d.astype(z_ext.dtype)


def trunk_layer(x, hist_conv, hist_pool, pos0, norm1_g, w_in, b_in, conv_w, conv_b,
                ln_g, ln_b, pool_w, pool_scale, w_out, norm2_g, w_ff1, w_ff2):
    bsz, T, _ = x.shape
    h = rmsnorm(x, norm1_g)
    proj = jnp.einsum("btd,de->bte", h, w_in) + b_in
    a, gate, z = proj[..., :D_CONV], proj[..., D_CONV:2 * D_CONV], proj[..., 2 * D_CONV:]
    u = a * jax.nn.sigmoid(gate)
    u_ext = jnp.concatenate([hist_conv.astype(u.dtype), u], axis=1)
    c = jax.nn.silu(layernorm(depthwise_causal_conv(u_ext, conv_w, conv_b), ln_g, ln_b))
    z_ext = jnp.concatenate([hist_pool.astype(z.dtype), z], axis=1)
    d = multiscale_pool(z_ext, pos0)
    p = jnp.einsum("btgc,gce->btge", d, pool_w).reshape(bsz, T, D_POOL) * pool_scale
    mix = jnp.concatenate([c, p], axis=-1)
    x = x + jnp.einsum("btm,md->btd", mix, w_out)
    h2 = rmsnorm(x, norm2_g)
    f = jnp.square(jax.nn.relu(jnp.einsum("btd,df->btf", h2, w_ff1)))
    x = x + jnp.einsum("btf,fd->btd", f, w_ff2)
    return x, u_ext[:, -(CONV_W - 1):], z_ext[:, -(POOL_MAX - 1):]


def setup_inputs(seed: int = 0) -> dict:
    key = jax.random.key(seed)
    ks = jax.random.split(key, 20)
    n = jax.random.normal
    f32 = jnp.float32
    return {
        "x_prompt": n(ks[0], (BATCH, SEQ, D_MODEL), f32),
        "x_sample": n(ks[1], (DEC_BATCH, DEC_SEQ, D_MODEL), f32),
        "state_conv": 0.5 * n(ks[2], (DEPTH, DEC_BATCH, CONV_W - 1, D_CONV), f32),
        "state_pool": n(ks[3], (DEPTH, DEC_BATCH, POOL_MAX - 1, D_POOL), f32),
        "norm1_g": 1.0 + 0.02 * n(ks[4], (DEPTH, D_MODEL), f32),
        "w_in": n(ks[5], (DEPTH, D_MODEL, D_IN), f32) * D_MODEL ** -0.5,
        "b_in": 0.02 * n(ks[6], (DEPTH, D_IN), f32),
        "conv_w": n(ks[7], (DEPTH, CONV_W, D_CONV), f32) * CONV_W ** -0.5,
        "conv_b": 0.02 * n(ks[8], (DEPTH, D_CONV), f32),
        "ln_g": 1.0 + 0.02 * n(ks[9], (DEPTH, D_CONV), f32),
        "ln_b": 0.02 * n(ks[10], (DEPTH, D_CONV), f32),
        "pool_w": n(ks[11], (DEPTH, N_POOL_GROUPS, POOL_GROUP_W, POOL_GROUP_W), f32) * POOL_GROUP_W ** -0.5,
        "pool_scale": 1.0 + 0.1 * n(ks[12], (DEPTH, D_POOL), f32),
        "w_out": n(ks[13], (DEPTH, D_MIX, D_MODEL), f32) * D_MIX ** -0.5,
        "norm2_g": 1.0 + 0.02 * n(ks[14], (DEPTH, D_MODEL), f32),
        "w_ff1": n(ks[15], (DEPTH, D_MODEL, D_FF), f32) * D_MODEL ** -0.5,
        "w_ff2": n(ks[16], (DEPTH, D_FF, D_MODEL), f32) * D_FF ** -0.5,
        "norm_f": 1.0 + 0.02 * n(ks[17], (D_MODEL,), f32),
    }


def reference(x_prompt, x_sample, state_conv, state_pool, norm1_g, w_in, b_in, conv_w, conv_b,
              ln_g, ln_b, pool_w, pool_scale, w_out, norm2_g, w_ff1, w_ff2, norm_f):
    xp, xs = x_prompt, x_sample
    conv_p, pool_p, conv_s, pool_s = [], [], [], []
    zc = jnp.zeros((BATCH, CONV_W - 1, D_CONV), x_prompt.dtype)
    zp = jnp.zeros((BATCH, POOL_MAX - 1, D_POOL), x_prompt.dtype)
    for l in range(DEPTH):
        params = (norm1_g[l], w_in[l], b_in[l], conv_w[l], conv_b[l], ln_g[l], ln_b[l],
                  pool_w[l], pool_scale[l], w_out[l], norm2_g[l], w_ff1[l], w_ff2[l])
        xp, cp, pp = trunk_layer(xp, zc, zp, 0, *params)
        xs, cs_, ps_ = trunk_layer(xs, state_conv[l], state_pool[l], PAST_LEN, *params)
        conv_p.append(cp); pool_p.append(pp); conv_s.append(cs_); pool_s.append(ps_)
    y_prompt = rmsnorm(xp, norm_f)
    y_sample = rmsnorm(xs, norm_f)
    new_conv_prompt = jnp.stack(conv_p, axis=0)
    new_pool_prompt = jnp.stack(pool_p, axis=0)
    new_conv_sample = jnp.stack(conv_s, axis=0)
    new_pool_sample = jnp.stack(pool_s, axis=0)
    return (y_prompt, y_sample, new_conv_prompt, new_pool_prompt, new_conv_sample, new_pool_sample)
```

```python
import os
import numpy as np
from contextlib import ExitStack
import concourse.bass as bass
import concourse.mybir as mybir
from concourse.bass_utils import run_bass_kernel_spmd

F32 = mybir.dt.float32
BF16 = mybir.dt.bfloat16
AF = mybir.ActivationFunctionType
ALU = mybir.AluOpType

NCORES = 8
D = 2048
KC = 16
DIN = 3072
DFF = 8192
DEPTH = 2
T = 1152
TN = 384
HALO = 64
NPR = 1024
NSM = 64
SBQ = 16
NYR = NPR + NSM
EPS = 1e-6
NSLOT = 4
FB = 2048

PL = 336
O_G1, O_BIN, O_CW, O_CB, O_LNG, O_LNB, O_PSC, O_G2 = 0, 16, 40, 288, 296, 304, 312, 320
O_GF = DEPTH * PL
O_HM = O_GF + 16
O_IC = O_HM + 1
NPRM = O_IC + 64

SAME_ENG_SYNC = True
_DBG = set(os.environ.get("KDBG", "").split(",")) - {""}


class Buf:
    __slots__ = ("name", "w", "r", "pr", "excl")

    def __init__(self, name, excl=False):
        self.name = name
        self.w = {}
        self.r = {}
        self.pr = {}
        self.excl = excl


def _split(reads, writes):
    ex = [b for b in reads if b.excl]
    if not ex:
        return reads, writes
    return [b for b in reads if not b.excl], list(writes) + ex


def _merge(dst, src):
    for k, v in src.items():
        if dst.get(k, 0) < v:
            dst[k] = v


def handoff(src_bufs, dst_bufs):
    u = {}
    for b in src_bufs:
        _merge(u, b.w)
        _merge(u, b.r)
        _merge(u, b.pr)
    for b in dst_bufs:
        b.w = dict(u)
        b.r = {}
        b.pr = dict(u)


class Prog:
    ENG = ("pe", "act", "dve", "pool", "sp")

    def __init__(self):
        self.ops = {e: [] for e in self.ENG}
        self.cnt = {e: 0 for e in self.ENG}
        self.dcnt = {}

    def _deps(self, reads, writes, deps):
        d = {}
        for b in reads:
            _merge(d, b.w)
        for b in writes:
            _merge(d, b.r)
            _merge(d, b.pr)
            _merge(d, b.w)
        for h in deps:
            if h is not None:
                _merge(d, {h[0]: h[1]})
        return d

    def _reg(self, h, reads, writes):
        k, v = h
        for b in reads:
            if b.r.get(k, 0) < v:
                b.r[k] = v
        for b in writes:
            if b.r:
                b.pr = b.r
                b.r = {}
                b.w = {k: v}
            else:
                if b.w.get(k, 0) < v:
                    b.w[k] = v

    def op(self, eng, fn, reads=(), writes=(), deps=()):
        reads, writes = _split(reads, writes)
        d = self._deps(reads, writes, deps)
        self.cnt[eng] += 1
        h = (eng, self.cnt[eng])
        self.ops[eng].append((fn, d, (eng, 1)))
        self._reg(h, reads, writes)
        return h

    def group(self, fns, reads=(), writes=(), deps=()):
        reads, writes = _split(reads, writes)
        d = self._deps(reads, writes, deps)
        self.cnt["pe"] += 1
        h = ("pe", self.cnt["pe"])
        n = len(fns)
        for i, fn in enumerate(fns):
            self.ops["pe"].append((fn, d if i == 0 else {}, ("pe", 1) if i == n - 1 else None))
        self._reg(h, reads, writes)
        return h

    def dma(self, queue, fn, sem, reads=(), writes=(), deps=()):
        reads, writes = _split(reads, writes)
        d = self._deps(reads, writes, deps)
        self.dcnt[sem] = self.dcnt.get(sem, 0) + 16
        h = (sem, self.dcnt[sem])
        self.ops[queue].append((fn, d, (sem, 16)))
        self._reg(h, reads, writes)
        return h

    def wait(self, eng, deps):
        d = {}
        for h in deps:
            _merge(d, {h[0]: h[1]})
        self.ops[eng].append((None, d, None))

    def sem_names(self):
        return list(self.ENG) + sorted(self.dcnt.keys())

    def replay(self, eng, e, sems):
        waited = {}
        for fn, d, inc in self.ops[eng]:
            for k, v in d.items():
                if k == eng and (eng == "pe" or not SAME_ENG_SYNC):
                    continue
                if waited.get(k, 0) < v:
                    e.wait_ge(sems[k], v)
                    waited[k] = v
            if fn is None:
                continue
            inst = fn(e)
            if inc is not None:
                inst.then_inc(sems[inc[0]], inc[1])


class WStream:
    def __init__(self, P, wt, plan=None):
        self.P = P
        self.wt = wt
        self.plan = plan
        self.req = []
        self.i = 0
        self.issued = 0
        self.bufs = [Buf("w%d" % s) for s in range(NSLOT)]

    def _issue(self):
        i = self.issued
        s = i % NSLOT
        src, kc = self.plan[i]
        wt = self.wt
        self.P.dma("pool", lambda e, s=s, src=src, kc=kc: e.dma_start(out=wt[:, s, 0:kc, :], in_=src),
                   "w%d" % s, writes=[self.bufs[s]])
        self.issued += 1

    def get(self, src, kc):
        i = self.i
        self.i += 1
        if self.plan is None:
            self.req.append((src, kc))
            return i % NSLOT, self.bufs[i % NSLOT]
        while self.issued < min(i + NSLOT, len(self.plan)):
            self._issue()
        return i % NSLOT, self.bufs[i % NSLOT]


def build_program():
    nc = bass.Bass("TRN2", target_bir_lowering=False)
    dt_in = lambda n, s: nc.dram_tensor(n, s, F32, kind="ExternalInput").ap()
    dt_out = lambda n, s: nc.dram_tensor(n, s, F32, kind="ExternalOutput").ap()
    x_d = dt_in("x", [T, D])
    sconv_d = dt_in("sconv", [DEPTH, SBQ, 30, 1024])
    spool_d = dt_in("spool", [DEPTH, SBQ, 15, 1024])
    prm_d = dt_in("prm", [128, NPRM])
    ident_d = dt_in("ident", [128, 128])
    w_in_d = dt_in("w_in", [DEPTH, D, DIN])
    pool_w_d = dt_in("pool_w", [DEPTH, 4, 256, 256])
    w_out_d = dt_in("w_out", [DEPTH, D, D])
    w_ff1_d = dt_in("w_ff1", [DEPTH, D, DFF])
    w_ff2_d = dt_in("w_ff2", [DEPTH, DFF, D])
    y_d = dt_out("y", [NYR, D])
    ocs_d = dt_out("ocs", [DEPTH, SBQ, 30, 1024])
    ops_d = dt_out("ops", [DEPTH, SBQ, 15, 1024])
    ocp_d = dt_out("ocp", [DEPTH, 30, 1024])
    opp_d = dt_out("opp", [DEPTH, 15, 1024])

    w_in_v = [w_in_d[l].rearrange("(kc p) m -> p kc m", p=128) for l in range(DEPTH)]
    w_out_v = [w_out_d[l].rearrange("(kc p) m -> p kc m", p=128) for l in range(DEPTH)]
    w_ff1_v = [w_ff1_d[l].rearrange("(kc p) m -> p kc m", p=128) for l in range(DEPTH)]
    w_ff2_v = [w_ff2_d[l].rearrange("(jj kc p) m -> jj p kc m", p=128, kc=KC) for l in range(DEPTH)]
    pool_w_v = [[pool_w_d[l, g].rearrange("(kk p) m -> p kk m", p=128) for g in range(4)] for l in range(DEPTH)]

    with ExitStack() as st:
        sb = lambda n, s, d: st.enter_context(nc.sbuf_tensor(n, s, d))
        xT = sb("xT", [128, KC, T], F32)
        Bm = sb("Bm", [128, 18432], BF16)
        Cm = sb("Cm", [128, 9216], F32)
        S = sb("S", [128, 6, TN], F32)
        wt = sb("wt", [128, NSLOT, KC, 128], BF16)
        ubuf = sb("ubuf", [128, 3, 32 + TN], BF16)
        hsave = sb("hsave", [128, 8, 32], BF16)
        usamp = sb("usamp", [128, 8, SBQ, 34], BF16)
        zf = sb("zf", [128, 2, TN], F32)
        zb = sb("zb", [128, 2, 16 + TN], BF16)
        zsave = sb("zsave", [128, 8, 16], BF16)
        zsamp = sb("zsamp", [128, 8, SBQ, 19], BF16)
        dbuf = sb("dbuf", [128, 2, 2, TN], BF16)
        utail = sb("utail", [128, 8, 94], F32)
        ztail = sb("ztail", [128, 8, 94], F32)
        prm = sb("prm_t", [128, NPRM], F32)
        idf = sb("idf", [128, 128], F32)
        idb = sb("idb", [128, 128], BF16)
        ones = sb("ones", [128, 128], BF16)
        ps = st.enter_context(nc.psum_tensor("ps", [128, 8, 512], F32))

        hT = Bm[:, 0:6144].rearrange("p (k t) -> p k t", k=KC)
        mix = Bm[:, 6144:12288].rearrange("p (k t) -> p k t", k=KC)
        sqb = Bm[:, 12288:18432].rearrange("p (k t) -> p k t", k=KC)
        h2T = Bm[:, :].rearrange("p (a k t) -> p a k t", a=3, k=KC)
        tstage = Bm[:, 0:4096].bitcast(F32).rearrange("p (a c) -> p a c", a=2)
        xin = [Cm[:, s * 2048:(s + 1) * 2048] for s in range(2)]
        stgc = Cm[:, 0:4096].rearrange("p (g c) -> p g c", g=4)
        stgp = Cm[:, 4096:6144].rearrange("p (g c) -> p g c", g=2)
        vv = Cm[:, 0:3072].rearrange("p (c t) -> p c t", c=8)
        vbf = Cm[:, 3072:4608].bitcast(BF16).rearrange("p (c t) -> p c t", c=8)
        sqv = Cm[:, 4608:6144].bitcast(BF16).rearrange("p (c t) -> p c t", c=8)
        diag = Cm[:, 6144:8128].bitcast(BF16).rearrange("p (k j) -> p k j", k=31)
        sig = [Cm[:, 8128 + i * TN:8128 + (i + 1) * TN] for i in range(2)]
        t64 = Cm[:, 8896:8960]
        t16 = Cm[:, 8960:8976]
        fT = Cm[:, :].bitcast(BF16).rearrange("p (a k t) -> p a k t", a=3, k=KC)
        ybufs = [Cm[:, s * 2048:(s + 1) * 2048].rearrange("p (k t) -> p k t", k=KC) for s in range(2)]
        yout = [Cm[:, 4096 + s * 2048:4096 + (s + 1) * 2048] for s in range(2)]
        Sflat = S[:, 0:3, :].rearrange("p a t -> p (a t)")

        def pcol(c):
            return prm[:, c:c + 1]

        def emit(P, W):
            B = {}

            def bf(n):
                if n not in B:
                    B[n] = Buf(n)
                return B[n]

            pb = [bf("pb%d" % i) for i in range(8)]
            for b_ in pb:
                b_.excl = True
            xTb = [bf("xT%d" % t) for t in range(3)]
            state = {"ring": 0, "alt": 0}

            def ring3():
                b = state["ring"] % 3
                state["ring"] += 1
                return b

            def alt_eng():
                state["alt"] += 1
                return "act" if state["alt"] % 2 else "dve"

            def copy_op(eng, out, in_, reads, writes):
                if eng == "act":
                    return P.op("act", lambda e: e.activation(out=out, in_=in_, func=AF.Copy), reads=reads, writes=writes)
                return P.op("dve", lambda e: e.tensor_copy(out=out, in_=in_), reads=reads, writes=writes)

            outs = []
            P.dma("sp", lambda e: e.dma_start(out=prm[:], in_=prm_d[:, :]), "c0", writes=[bf("prm")])
            P.dma("sp", lambda e: e.dma_start(out=idf[:], in_=ident_d[:, :]), "c1", writes=[bf("idf")])
            P.op("dve", lambda e: e.tensor_copy(out=idb[:], in_=idf[:]), reads=[bf("idf")], writes=[bf("idb")])
            P.op("dve", lambda e: e.memset(ones[:], 1.0), writes=[bf("ones")])
            for l in range(DEPTH if "nopt" not in _DBG else 0):
                outs.append(P.dma("sp", lambda e, l=l: e.dma_start(out=ocs_d[l, :, 0:26, :], in_=sconv_d[l, :, 4:30, :]), "pt"))
                outs.append(P.dma("sp", lambda e, l=l: e.dma_start(out=ops_d[l, :, 0:11, :], in_=spool_d[l, :, 4:15, :]), "pt"))

            xinb = [bf("xin0"), bf("xin1")]
            bank = 0
            for i in range(T // 128):
                s = i % 2
                P.dma("sp", lambda e, i=i, s=s: e.dma_start(out=xin[s], in_=x_d[i * 128:(i + 1) * 128, :]),
                      "xi%d" % s, writes=[xinb[s]])
                for q in range(4):
                    bk = bank % 8
                    bank += 1
                    fns = [lambda e, s=s, q=q, j=j, bk=bk: e.transpose(
                        out=ps[:, bk, j * 128:(j + 1) * 128], in_=xin[s][:, (4 * q + j) * 128:(4 * q + j + 1) * 128], identity=idf[:])
                        for j in range(4)]
                    P.group(fns, reads=[xinb[s], bf("idf")], writes=[pb[bk]])
                    copy_op(alt_eng(), xT[:, 4 * q:4 * q + 4, i * 128:(i + 1) * 128],
                            ps[:, bk, :].rearrange("p (j t) -> p j t", t=128), [pb[bk]], [xTb[i // 3]])
            cbufs = list(xinb)

            def rms_stats(l_g_unused, src_cols, sq_view, sq_buf, bank_i, out_slot, out_buf, xbufs):
                for k in range(KC):
                    P.op("act", lambda e, k=k: e.activation(out=sq_view[:, k, :], in_=xT[:, k, src_cols[0]:src_cols[1]], func=AF.Square),
                         reads=xbufs, writes=[sq_buf])
                n = src_cols[1] - src_cols[0]
                fns = [lambda e, k=k: e.matmul(out=ps[:, bank_i, 0:n], lhsT=ones[:], rhs=sq_view[:, k, :], start=(k == 0), stop=(k == KC - 1))
                       for k in range(KC)]
                P.group(fns, reads=[sq_buf, bf("ones")], writes=[pb[bank_i]])
                P.op("act", lambda e: e.activation(out=out_slot, in_=ps[:, bank_i, 0:n], func=AF.Sqrt, bias=EPS, scale=1.0 / D),
                     reads=[pb[bank_i]], writes=[out_buf])
                P.op("dve", lambda e: e.reciprocal(out=out_slot, in_=out_slot), reads=[out_buf], writes=[out_buf])

            def rms1_pre(l, t):
                c0 = t * TN
                rms_stats(None, (c0, c0 + TN), sqb, bf("sqb"), 7, S[:, 4, :], bf("S4"), [xTb[t]])
                for k in range(KC):
                    P.op("dve", lambda e, k=k: e.scalar_tensor_tensor(
                        out=hT[:, k, :], in0=xT[:, k, c0:c0 + TN], scalar=pcol(l * PL + O_G1 + k), in1=S[:, 4, :],
                        op0=ALU.mult, op1=ALU.mult), reads=[xTb[t], bf("S4"), bf("prm")], writes=[bf("hT")])

            def mm_group(slot, wbuf, rhs_of_k, nk, bank_i, n, reads):
                fns = [lambda e, k=k: e.matmul(out=ps[:, bank_i, 0:n], lhsT=wt[:, slot, k, :], rhs=rhs_of_k(k),
                                               start=(k == 0), stop=(k == nk - 1)) for k in range(nk)]
                return P.group(fns, reads=[wbuf] + reads, writes=[pb[bank_i]])

            def stt(out, in0, scalar, in1, op0, op1, reads, writes):
                return P.op("dve", lambda e: e.scalar_tensor_tensor(out=out, in0=in0, scalar=scalar, in1=in1, op0=op0, op1=op1),
                            reads=reads, writes=writes)

            def ts1(out, in0, s1, op0, reads, writes):
                return P.op("dve", lambda e: e.tensor_scalar(out=out, in0=in0, scalar1=s1, scalar2=None, op0=op0),
                            reads=reads, writes=writes)

            def ts2(out, in0, s1, s2, op0, op1, reads, writes):
                return P.op("dve", lambda e: e.tensor_scalar(out=out, in0=in0, scalar1=s1, scalar2=s2, op0=op0, op1=op1),
                            reads=reads, writes=writes)

            def act(out, in_, func, reads, writes, bias=None, scale=None):
                kw = {}
                if bias is not None:
                    kw["bias"] = bias
                if scale is not None:
                    kw["scale"] = scale
                return P.op("act", lambda e: e.activation(out=out, in_=in_, func=func, **kw), reads=reads, writes=writes)

            def mixer(l, t):
                L0 = l * PL
                c0 = t * TN
                npr = TN if t < 2 else TN - NSM
                prmb = bf("prm")
                hTb, mixb = bf("hT"), bf("mix")
                vb, vbfb, sqvb = bf("v"), bf("vbf"), bf("sqv")
                sigb = [bf("sig0"), bf("sig1")]
                ubb = [bf("ub0"), bf("ub1"), bf("ub2")]

                def conv(c):
                    ui = c % 3
                    ub = ubuf[:, ui, :]
                    dgb = [bf("diagA"), bf("diagB")]
                    halves = [(0, 16), (16, 31)]
                    for hi, (k0, k1) in enumerate(halves):
                        nk = k1 - k0
                        P.op("dve", lambda e, k0=k0, k1=k1, nk=nk: e.tensor_tensor(
                            out=diag[:, k0:k1, :], in0=ps[:, 4, 0:128].unsqueeze(1).broadcast_to([128, nk, 128]),
                            in1=prm[:, L0 + O_CW + c * 31 + k0:L0 + O_CW + c * 31 + k1].unsqueeze(2).broadcast_to([128, nk, 128]),
                            op=ALU.mult), reads=[pb[4], prmb], writes=[dgb[hi]])
                    for hi, (k0, k1) in enumerate(halves):
                        fns = [lambda e, k=k: e.matmul(out=ps[:, 3, 0:npr], lhsT=diag[:, k, :], rhs=ub[:, 2 + k:2 + k + npr],
                                                       start=(k == 0), stop=(k == 30)) for k in range(k0, k1)]
                        P.group(fns, reads=[dgb[hi], ubb[ui]], writes=[pb[3]])
                    if t == 2:
                        fns = [lambda e, k=k: e.matmul(out=ps[:, 3, npr:TN], lhsT=diag[:, k, :], rhs=usamp[:, c, :, k:k + 4],
                                                       start=(k == 0), stop=(k == 30), skip_group_check=True) for k in range(31)]
                        P.group(fns, reads=[dgb[0], dgb[1], bf("usamp%d" % c)], writes=[pb[3]])
                    cb = pcol(L0 + O_CB + c)
                    act(vv[:, c, :], ps[:, 3, 0:TN], AF.Identity, [pb[3], prmb], [vb], bias=cb)
                    ts1(vbf[:, c, :], ps[:, 3, 0:TN], cb, ALU.add, [pb[3], prmb], [vbfb])
                    act(sqv[:, c, :], ps[:, 3, 0:TN], AF.Square, [pb[3], prmb], [sqvb], bias=cb)

                for c in range(8):
                    ui = c % 3
                    ub = ubuf[:, ui, :]
                    sa, wa = W.get(w_in_v[l][:, :, c * 128:(c + 1) * 128], KC)
                    ba = ring3()
                    mm_group(sa, wa, lambda k: hT[:, k, :], KC, ba, TN, [hTb])
                    sg, wg = W.get(w_in_v[l][:, :, 1024 + c * 128:1024 + (c + 1) * 128], KC)
                    bg = ring3()
                    mm_group(sg, wg, lambda k: hT[:, k, :], KC, bg, TN, [hTb])
                    si = c % 2
                    act(sig[si], ps[:, bg, 0:TN], AF.Sigmoid, [pb[bg], prmb], [sigb[si]], bias=pcol(L0 + O_BIN + 8 + c))
                    ba_col = pcol(L0 + O_BIN + c)
                    rdu = [pb[ba], sigb[si], prmb]
                    if t == 0:
                        P.op("dve", lambda e, ub=ub: e.memset(ub[:, 0:32], 0.0), writes=[ubb[ui]])
                    else:
                        P.op("dve", lambda e, ub=ub, c=c: e.tensor_copy(out=ub[:, 0:32], in_=hsave[:, c, :]),
                             reads=[bf("hsave%d" % c)], writes=[ubb[ui]])
                    if t == 0:
                        stt(t64, ps[:, ba, 0:HALO], ba_col, sig[si][:, 0:HALO], ALU.add, ALU.mult, rdu, [bf("t64")])
                        ts1(ub[:, 32:32 + HALO], t64, pcol(O_HM), ALU.mult, [bf("t64"), prmb], [ubb[ui]])
                        stt(ub[:, 32 + HALO:32 + TN], ps[:, ba, HALO:TN], ba_col, sig[si][:, HALO:TN], ALU.add, ALU.mult, rdu, [ubb[ui]])
                    else:
                        stt(ub[:, 32:32 + npr], ps[:, ba, 0:npr], ba_col, sig[si][:, 0:npr], ALU.add, ALU.mult, rdu, [ubb[ui]])
                    if t == 2:
                        stt(usamp[:, c, :, 30:34], ps[:, ba, npr:TN].rearrange("p (b j) -> p b j", j=4), ba_col,
                            sig[si][:, npr:TN].rearrange("p (b j) -> p b j", j=4), ALU.add, ALU.mult, rdu, [bf("usamp%d" % c)])
                        stt(utail[:, c, :], ps[:, ba, TN - 94:TN], ba_col, sig[si][:, TN - 94:TN], ALU.add, ALU.mult, rdu, [bf("utail")])
                    else:
                        P.op("dve", lambda e, ub=ub, c=c: e.tensor_copy(out=hsave[:, c, :], in_=ub[:, TN:TN + 32]),
                             reads=[ubb[ui]], writes=[bf("hsave%d" % c)])
                    if c > 0:
                        conv(c - 1)
                conv(7)

                def pool_group(g):
                    w = 2 << g
                    db = g % 2
                    dB = bf("d%d" % db)
                    bz = []
                    for j in range(2):
                        m = 2 * g + j
                        sz, wz = W.get(w_in_v[l][:, :, 2048 + m * 128:2048 + (m + 1) * 128], KC)
                        b_ = ring3()
                        bz.append(b_)
                        mm_group(sz, wz, lambda k: hT[:, k, :], KC, b_, TN, [hTb])
                        bzc = pcol(L0 + O_BIN + 16 + m)
                        zfB, zbB = bf("zf%d" % j), bf("zb%d" % j)
                        act(zf[:, j, :], ps[:, b_, 0:TN], AF.Identity, [pb[b_], prmb], [zfB], bias=bzc)
                        if t == 0:
                            P.op("dve", lambda e, j=j: e.memset(zb[:, j, 0:16], 0.0), writes=[zbB])
                            ts2(zb[:, j, 16:16 + HALO], ps[:, b_, 0:HALO], bzc, pcol(O_HM), ALU.add, ALU.mult, [pb[b_], prmb], [zbB])
                            ts1(zb[:, j, 16 + HALO:16 + TN], ps[:, b_, HALO:TN], bzc, ALU.add, [pb[b_], prmb], [zbB])
                        else:
                            P.op("dve", lambda e, j=j, m=m: e.tensor_copy(out=zb[:, j, 0:16], in_=zsave[:, m, :]),
                                 reads=[bf("zsave%d" % m)], writes=[zbB])
                            ts1(zb[:, j, 16:16 + npr], ps[:, b_, 0:npr], bzc, ALU.add, [pb[b_], prmb], [zbB])
                        if t == 2:
                            ts1(zsamp[:, m, :, 15:19], ps[:, b_, npr:TN].rearrange("p (b j) -> p b j", j=4), bzc, ALU.add,
                                [pb[b_], prmb], [bf("zsamp%d" % m)])
                            ts1(ztail[:, m, :], ps[:, b_, TN - 94:TN], bzc, ALU.add, [pb[b_], prmb], [bf("ztail")])
                        else:
                            P.op("dve", lambda e, j=j, m=m: e.tensor_copy(out=zsave[:, m, :], in_=zb[:, j, TN:TN + 16]),
                                 reads=[zbB], writes=[bf("zsave%d" % m)])
                    for j in range(2):
                        m = 2 * g + j
                        zfB, zbB = bf("zf%d" % j), bf("zb%d" % j)
                        bp = ring3()
                        fns = [lambda e, k=k, j=j, bp=bp, w=w: e.matmul(out=ps[:, bp, 0:npr], lhsT=idb[:], rhs=zb[:, j, 16 - k:16 - k + npr],
                                                                   start=(k == 0), stop=(k == w - 1)) for k in range(w)]
                        rd = [bf("idb"), zbB]
                        if t == 2:
                            fns += [lambda e, k=k, m=m, bp=bp, w=w: e.matmul(out=ps[:, bp, npr:TN], lhsT=idb[:], rhs=zsamp[:, m, :, 15 - k:19 - k],
                                                                        start=(k == 0), stop=(k == w - 1), skip_group_check=True) for k in range(w)]
                            rd.append(bf("zsamp%d" % m))
                        P.group(fns, reads=rd, writes=[pb[bp]])
                        rdd = [pb[bp], zfB]
                        if t == 0:
                            stt(dbuf[:, db, j, 0:HALO], ps[:, bp, 0:HALO], 1.0 / w, zf[:, j, 0:HALO], ALU.mult, ALU.subtract, rdd, [dB])
                            P.op("dve", lambda e, bp=bp, g=g: e.tensor_tensor(out=t16, in0=ps[:, bp, HALO:HALO + 16],
                                                                              in1=prm[:, O_IC + g * 16:O_IC + (g + 1) * 16], op=ALU.mult),
                                 reads=[pb[bp], prmb], writes=[bf("t16")])
                            P.op("dve", lambda e, j=j, db=db: e.tensor_tensor(out=dbuf[:, db, j, HALO:HALO + 16], in0=t16,
                                                                              in1=zf[:, j, HALO:HALO + 16], op=ALU.subtract),
                                 reads=[bf("t16"), zfB], writes=[dB])
                            stt(dbuf[:, db, j, HALO + 16:TN], ps[:, bp, HALO + 16:TN], 1.0 / w, zf[:, j, HALO + 16:TN],
                                ALU.mult, ALU.subtract, rdd, [dB])
                        else:
                            stt(dbuf[:, db, j, :], ps[:, bp, 0:TN], 1.0 / w, zf[:, j, :], ALU.mult, ALU.subtract, rdd, [dB])
                    for e_ in range(2):
                        m = 2 * g + e_
                        sp_, wp = W.get(pool_w_v[l][g][:, :, e_ * 128:(e_ + 1) * 128], 2)
                        bq = ring3()
                        mm_group(sp_, wp, lambda k, db=db: dbuf[:, db, k, :], 2, bq, TN, [dB])
                        act(mix[:, 8 + m, :], ps[:, bq, 0:TN], AF.Copy, [pb[bq], prmb], [mixb], scale=pcol(L0 + O_PSC + m))

                nxt = t + 1 if t < 2 else None
                if nxt is not None:
                    n0 = nxt * TN
                    for k in range(KC):
                        P.op("act", lambda e, k=k: e.activation(out=sqb[:, k, :], in_=xT[:, k, n0:n0 + TN], func=AF.Square),
                             reads=[xTb[nxt]], writes=[bf("sqb")])

                fns = [lambda e, c=c: e.matmul(out=ps[:, 5, 0:TN], lhsT=ones[:], rhs=vbf[:, c, :], start=(c == 0), stop=(c == 7)) for c in range(8)]
                P.group(fns, reads=[vbfb, bf("ones")], writes=[pb[5]])
                fns = [lambda e, c=c: e.matmul(out=ps[:, 6, 0:TN], lhsT=ones[:], rhs=sqv[:, c, :], start=(c == 0), stop=(c == 7)) for c in range(8)]
                P.group(fns, reads=[sqvb, bf("ones")], writes=[pb[6]])
                if nxt is not None:
                    fns = [lambda e, k=k: e.matmul(out=ps[:, 7, 0:TN], lhsT=ones[:], rhs=sqb[:, k, :], start=(k == 0), stop=(k == KC - 1))
                           for k in range(KC)]
                    P.group(fns, reads=[bf("sqb"), bf("ones")], writes=[pb[7]])
                S0b, S1b = bf("S0"), bf("S1")
                act(S[:, 0, :], ps[:, 5, 0:TN], AF.Copy, [pb[5]], [S0b], scale=1.0 / 1024)
                P.op("dve", lambda e: e.tensor_tensor(out=S[:, 1, :], in0=S[:, 0, :], in1=S[:, 0, :], op=ALU.mult), reads=[S0b], writes=[S1b])
                stt(S[:, 1, :], ps[:, 6, 0:TN], 1.0 / 1024, S[:, 1, :], ALU.mult, ALU.subtract, [pb[6], S1b], [S1b])
                act(S[:, 1, :], S[:, 1, :], AF.Sqrt, [S1b], [S1b], bias=EPS)
                P.op("dve", lambda e: e.reciprocal(out=S[:, 1, :], in_=S[:, 1, :]), reads=[S1b], writes=[S1b])
                if nxt is not None:
                    act(S[:, 4, :], ps[:, 7, 0:TN], AF.Sqrt, [pb[7]], [bf("S4")], bias=EPS, scale=1.0 / D)
                    P.op("dve", lambda e: e.reciprocal(out=S[:, 4, :], in_=S[:, 4, :]), reads=[bf("S4")], writes=[bf("S4")])
                def ln_apply(c):
                    sl = 2 + c % 2
                    Sb = bf("S%d" % sl)
                    P.op("dve", lambda e, c=c, sl=sl: e.tensor_tensor(out=S[:, sl, :], in0=vv[:, c, :], in1=S[:, 0, :], op=ALU.subtract),
                         reads=[vb, S0b], writes=[Sb])
                    P.op("dve", lambda e, sl=sl: e.tensor_tensor(out=S[:, sl, :], in0=S[:, sl, :], in1=S[:, 1, :], op=ALU.mult),
                         reads=[Sb, S1b], writes=[Sb])
                    act(mix[:, c, :], S[:, sl, :], AF.Silu, [Sb, prmb], [mixb], bias=pcol(L0 + O_LNB + c), scale=pcol(L0 + O_LNG + c))

                for g in range(4):
                    pool_group(g)
                    ln_apply(2 * g)
                    ln_apply(2 * g + 1)
                if nxt is not None:
                    n0 = nxt * TN
                    for k in range(KC):
                        stt(hT[:, k, :], xT[:, k, n0:n0 + TN], pcol(L0 + O_G1 + k), S[:, 4, :], ALU.mult, ALU.mult,
                            [xTb[nxt], bf("S4"), prmb], [hTb])
                for m in range(KC):
                    so, wo = W.get(w_out_v[l][:, :, m * 128:(m + 1) * 128], KC)
                    bo = ring3()
                    mm_group(so, wo, lambda k: mix[:, k, :], KC, bo, TN, [mixb])
                    P.op("dve", lambda e, m=m, bo=bo: e.tensor_tensor(out=xT[:, m, c0:c0 + TN], in0=xT[:, m, c0:c0 + TN], in1=ps[:, bo, 0:TN], op=ALU.add),
                         reads=[pb[bo], xTb[t]], writes=[xTb[t]])
                if t == 2 and "notails" not in _DBG:
                    for a, (tl, tlb) in enumerate([(utail, bf("utail")), (ztail, bf("ztail"))]):
                        for q in range(2):
                            bk = 5 + (2 * a + q) % 3
                            fns = [lambda e, tl=tl, q=q, i=i, bk=bk: e.transpose(out=ps[0:94, bk, i * 128:(i + 1) * 128], in_=tl[:, 4 * q + i, :], identity=idf[:])
                                   for i in range(4)]
                            P.group(fns, reads=[tlb, bf("idf")], writes=[pb[bk]])
                            copy_op(alt_eng(), tstage[0:94, a, q * 512:(q + 1) * 512], ps[0:94, bk, :], [pb[bk]], [hTb])
                    outs.append(P.dma("sp", lambda e: e.dma_start(out=ocp_d[l, :, :], in_=tstage[0:30, 0, :]), "to", reads=[hTb]))
                    outs.append(P.dma("sp", lambda e: e.dma_start(out=opp_d[l, :, :], in_=tstage[15:30, 1, :]), "to", reads=[hTb]))
                    for b in range(SBQ):
                        outs.append(P.dma("sp", lambda e, b=b: e.dma_start(out=ocs_d[l, b, 26:30, :], in_=tstage[30 + 4 * b:34 + 4 * b, 0, :]), "to", reads=[hTb]))
                        outs.append(P.dma("sp", lambda e, b=b: e.dma_start(out=ops_d[l, b, 11:15, :], in_=tstage[30 + 4 * b:34 + 4 * b, 1, :]), "to", reads=[hTb]))

            def ffn(l):
                L0 = l * PL
                prmb = bf("prm")
                h2b, sqCb, fCb = bf("h2T"), bf("sqC"), bf("fC")
                Sb = [bf("S%d" % i) for i in range(6)]
                for t in range(3):
                    c0 = t * TN
                    bk = 6 + t % 2
                    for k in range(KC):
                        P.op("act", lambda e, k=k, t=t, c0=c0: e.activation(out=fT[:, t, k, :], in_=xT[:, k, c0:c0 + TN], func=AF.Square),
                             reads=[xTb[t]], writes=[sqCb])
                    fns = [lambda e, k=k, t=t, bk=bk: e.matmul(out=ps[:, bk, 0:TN], lhsT=ones[:], rhs=fT[:, t, k, :], start=(k == 0), stop=(k == KC - 1))
                           for k in range(KC)]
                    P.group(fns, reads=[sqCb, bf("ones")], writes=[pb[bk]])
                    act(S[:, t, :], ps[:, bk, 0:TN], AF.Sqrt, [pb[bk]], [Sb[t]], bias=EPS, scale=1.0 / D)
                    P.op("dve", lambda e, t=t: e.reciprocal(out=S[:, t, :], in_=S[:, t, :]), reads=[Sb[t]], writes=[Sb[t]])
                    for k in range(KC):
                        stt(h2T[:, t, k, :], xT[:, k, c0:c0 + TN], pcol(L0 + O_G2 + k), S[:, t, :], ALU.mult, ALU.mult,
                            [xTb[t], Sb[t], prmb], [h2b])
                handoff([sqCb], [fCb])
                slot3 = 0
                sc_i = 0
                tr = [(HALO if (t == 0 and l == DEPTH - 1) else 0, TN) for t in range(3)]
                for j in range(DFF // FB):
                    for m in range(KC):
                        s1, w1 = W.get(w_ff1_v[l][:, :, j * FB + m * 128:j * FB + (m + 1) * 128], KC)
                        b0 = 3 * (slot3 % 2)
                        slot3 += 1
                        fns = [lambda e, k=k, t=t, s1=s1, b0=b0: e.matmul(out=ps[:, b0 + t, 0:tr[t][1] - tr[t][0]], lhsT=wt[:, s1, k, :],
                                                                        rhs=h2T[:, t, k, tr[t][0]:tr[t][1]],
                                                                        start=(k == 0), stop=(k == KC - 1)) for k in range(KC) for t in range(3)]
                        P.group(fns, reads=[w1, h2b], writes=[pb[b0], pb[b0 + 1], pb[b0 + 2]])
                        for t in range(3):
                            o0, o1 = tr[t]
                            si = 3 + sc_i % 3
                            sc_i += 1
                            act(S[:, si, 0:o1 - o0], ps[:, b0 + t, 0:o1 - o0], AF.Square, [pb[b0 + t]], [Sb[si]])
                            stt(fT[:, t, m, o0:o1], ps[:, b0 + t, 0:o1 - o0], 0.0, S[:, si, 0:o1 - o0], ALU.is_gt, ALU.mult, [pb[b0 + t], Sb[si]], [fCb])
                    for m in range(KC):
                        s2, w2 = W.get(w_ff2_v[l][j][:, :, m * 128:(m + 1) * 128], KC)
                        b0 = 3 * (slot3 % 2)
                        slot3 += 1
                        fns = [lambda e, k=k, t=t, s2=s2, b0=b0: e.matmul(out=ps[:, b0 + t, 0:tr[t][1] - tr[t][0]], lhsT=wt[:, s2, k, :],
                                                                        rhs=fT[:, t, k, tr[t][0]:tr[t][1]],
                                                                        start=(k == 0), stop=(k == KC - 1)) for k in range(KC) for t in range(3)]
                        P.group(fns, reads=[w2, fCb], writes=[pb[b0], pb[b0 + 1], pb[b0 + 2]])
                        for t in range(3):
                            o0, o1 = tr[t]
                            c0 = t * TN + o0
                            nn = o1 - o0
                            P.op("dve", lambda e, m=m, t=t, b0=b0, c0=c0, nn=nn: e.tensor_tensor(out=xT[:, m, c0:c0 + nn], in0=xT[:, m, c0:c0 + nn],
                                                                                             in1=ps[:, b0 + t, 0:nn], op=ALU.add),
                                 reads=[pb[b0 + t], xTb[t]], writes=[xTb[t]])
                return [h2b, fCb]

            for l in range(DEPTH if "nolayers" not in _DBG else 0):
                stb = bf("stage")
                handoff(cbufs, [stb])
                for gq in range(4):
                    P.dma("sp", lambda e, l=l, gq=gq: e.dma_start(out=stgc[0:120, gq, :],
                                                                  in_=sconv_d[l, 4 * gq:4 * gq + 4, :, :].rearrange("b j c -> (b j) c")),
                          "sg", writes=[stb])
                for gp in range(2):
                    P.dma("sp", lambda e, l=l, gp=gp: e.dma_start(out=stgp[0:120, gp, :],
                                                                  in_=spool_d[l, 8 * gp:8 * gp + 8, :, :].rearrange("b j c -> (b j) c")),
                          "sg", writes=[stb])
                bank = 0
                for c in range(8):
                    bk = bank % 8
                    bank += 1
                    fns = [lambda e, c=c, gq=gq, bk=bk: e.transpose(out=ps[:, bk, gq * 120:(gq + 1) * 120], in_=stgc[0:120, gq, c * 128:(c + 1) * 128],
                                                                   identity=idf[0:120, 0:120]) for gq in range(4)]
                    P.group(fns, reads=[stb, bf("idf")], writes=[pb[bk]])
                    copy_op(alt_eng(), usamp[:, c, :, 0:30], ps[:, bk, 0:480].rearrange("p (b j) -> p b j", j=30), [pb[bk]], [bf("usamp%d" % c)])
                    bk = bank % 8
                    bank += 1
                    fns = [lambda e, c=c, gp=gp, bk=bk: e.transpose(out=ps[:, bk, gp * 120:(gp + 1) * 120], in_=stgp[0:120, gp, c * 128:(c + 1) * 128],
                                                                   identity=idf[0:120, 0:120]) for gp in range(2)]
                    P.group(fns, reads=[stb, bf("idf")], writes=[pb[bk]])
                    copy_op(alt_eng(), zsamp[:, c, :, 0:15], ps[:, bk, 0:240].rearrange("p (b j) -> p b j", j=15), [pb[bk]], [bf("zsamp%d" % c)])
                mixbufs = [bf(n) for n in ("v", "vbf", "sqv", "diagA", "diagB", "sig0", "sig1", "t64", "t16")]
                handoff([stb], mixbufs)
                if l > 0:
                    handoff([bf("h2T")], [bf("hT"), bf("mix"), bf("sqb")])
                P.group([lambda e: e.transpose(out=ps[:, 4, 0:128], in_=idf[:], identity=idf[:])], reads=[bf("idf")], writes=[pb[4]])
                rms1_pre(l, 0)
                for t in range((2 if "t01" in _DBG else 3) if "nomixer" not in _DBG else 0):
                    mixer(l, t)
                handoff([bf("hT"), bf("mix"), bf("sqb")], [bf("h2T")])
                handoff(mixbufs, [bf("sqC")])
                cbufs = ffn(l)[1:] if "noffn" not in _DBG else [bf("sqC")]

            ybs, yob, sqFb = [bf("ybuf0"), bf("ybuf1")], [bf("yout0"), bf("yout1")], bf("sqb")
            handoff(cbufs, ybs + yob)
            if DEPTH > 0 and "nolayers" not in _DBG:
                handoff([bf("h2T")], [sqFb])
            for t in range(3):
                c0 = t * TN
                rms_stats(None, (c0, c0 + TN), sqb, sqFb, 6 + t % 2, S[:, t, :], bf("S%d" % t), [xTb[t]])
            Sall = [bf("S0"), bf("S1"), bf("S2")]
            bank = 0
            for i in range(9):
                n = 128 if i < 8 else 64
                col0 = HALO + 128 * i
                s = i % 2
                yb_v = ybufs[s]
                for k in range(KC):
                    stt(yb_v[:, k, 0:n], xT[:, k, col0:col0 + n], pcol(O_GF + k), Sflat[:, col0:col0 + n], ALU.mult, ALU.mult,
                        xTb + Sall + [bf("prm")], [ybs[s]])
                for q in range(4):
                    bk = bank % 6
                    bank += 1
                    fns = [lambda e, q=q, j=j, bk=bk, n=n, yb_v=yb_v: e.transpose(out=ps[0:n, bk, j * 128:(j + 1) * 128], in_=yb_v[:, 4 * q + j, 0:n], identity=idf[:])
                           for j in range(4)]
                    P.group(fns, reads=[ybs[s], bf("idf")], writes=[pb[bk]])
                    copy_op(alt_eng(), yout[s][0:n, q * 512:(q + 1) * 512], ps[0:n, bk, :], [pb[bk]], [yob[s]])
                outs.append(P.dma("sp", lambda e, i=i, n=n, s=s: e.dma_start(out=y_d[i * 128:i * 128 + n, :], in_=yout[s][0:n, :]),
                                  "yo%d" % s, reads=[yob[s]]))
            last = {}
            for h in outs:
                if last.get(h[0], 0) < h[1]:
                    last[h[0]] = h[1]
            P.wait("sp", list(last.items()))

        Wd = WStream(Prog(), wt, plan=None)
        emit(Wd.P, Wd)
        P = Prog()
        W = WStream(P, wt, plan=Wd.req)
        emit(P, W)
        assert W.i == len(Wd.req)

        sems = {n: st.enter_context(nc.semaphore(n)) for n in P.sem_names()}
        block = st.enter_context(nc.Block())

        @block.tensor
        def _(e):
            P.replay("pe", e, sems)

        @block.scalar
        def _(e):
            P.replay("act", e, sems)

        @block.vector
        def _(e):
            P.replay("dve", e, sems)

        @block.gpsimd
        def _(e):
            P.replay("pool", e, sems)

        @block.sync
        def _(e):
            P.replay("sp", e, sems)
    return nc


def _pack_params(norm1_g, b_in, conv_w, conv_b, ln_g, ln_b, pool_scale, norm2_g, norm_f, half):
    prm = np.zeros((128, NPRM), np.float32)

    def cols(v):
        return np.ascontiguousarray(np.asarray(v, np.float32).reshape(-1, 128).T)

    for l in range(DEPTH):
        L0 = l * PL
        prm[:, L0 + O_G1:L0 + O_G1 + 16] = cols(norm1_g[l])
        prm[:, L0 + O_BIN:L0 + O_BIN + 24] = cols(b_in[l])
        cw = np.asarray(conv_w[l], np.float32)
        prm[:, L0 + O_CW:L0 + O_CW + 248] = cw.reshape(31, 8, 128).transpose(2, 1, 0).reshape(128, 248)
        prm[:, L0 + O_CB:L0 + O_CB + 8] = cols(conv_b[l])
        prm[:, L0 + O_LNG:L0 + O_LNG + 8] = cols(ln_g[l])
        prm[:, L0 + O_LNB:L0 + O_LNB + 8] = cols(ln_b[l])
        prm[:, L0 + O_PSC:L0 + O_PSC + 8] = cols(pool_scale[l])
        prm[:, L0 + O_G2:L0 + O_G2 + 16] = cols(norm2_g[l])
    prm[:, O_GF:O_GF + 16] = cols(norm_f)
    prm[:, O_HM] = float(half)
    pos = np.arange(16)
    for g in range(4):
        w = 2 << g
        cnt = np.minimum(w, pos + 1) if half == 0 else np.full(16, w)
        prm[:, O_IC + g * 16:O_IC + (g + 1) * 16] = (1.0 / cnt.astype(np.float32))[None, :]
    return prm


_NC_CACHE = {}


def kernel(x_prompt, x_sample, state_conv, state_pool, norm1_g, w_in, b_in, conv_w, conv_b,
           ln_g, ln_b, pool_w, pool_scale, w_out, norm2_g, w_ff1, w_ff2, norm_f):
    f = lambda a: np.ascontiguousarray(np.asarray(a, dtype=np.float32))
    x_prompt, x_sample, state_conv, state_pool = f(x_prompt), f(x_sample), f(state_conv), f(state_pool)
    w_in, pool_w, w_out, w_ff1, w_ff2 = f(w_in), f(pool_w), f(w_out), f(w_ff1), f(w_ff2)
    if "nc" not in _NC_CACHE:
        _NC_CACHE["nc"] = build_program()
    nc = _NC_CACHE["nc"]
    ident = np.eye(128, dtype=np.float32)
    prms = [_pack_params(norm1_g, b_in, conv_w, conv_b, ln_g, ln_b, pool_scale, norm2_g, norm_f, h) for h in range(2)]
    in_maps = []
    for i in range(NCORES):
        b, h = i // 2, i % 2
        xc = np.zeros((T, D), np.float32)
        if h == 1:
            xc[0:HALO] = x_prompt[b, NPR - HALO:NPR]
        xc[HALO:HALO + NPR] = x_prompt[b, h * NPR:(h + 1) * NPR]
        xc[HALO + NPR:] = x_sample[SBQ * i:SBQ * (i + 1)].reshape(NSM, D)
        in_maps.append({
            "x": xc,
            "sconv": np.ascontiguousarray(state_conv[:, SBQ * i:SBQ * (i + 1)]),
            "spool": np.ascontiguousarray(state_pool[:, SBQ * i:SBQ * (i + 1)]),
            "prm": prms[h], "ident": ident,
            "w_in": w_in, "pool_w": pool_w, "w_out": w_out, "w_ff1": w_ff1, "w_ff2": w_ff2,
        })
    ncr = int(os.environ.get("KCORES", NCORES))
    res = run_bass_kernel_spmd(nc, in_maps[:ncr], core_ids=list(range(ncr)))
    R = list(res.results) + [res.results[0]] * (NCORES - ncr)
    B_, S_ = x_prompt.shape[0], x_prompt.shape[1]
    y_prompt = np.empty((B_, S_, D), np.float32)
    y_sample = np.empty(x_sample.shape, np.float32)
    ncp = np.empty((DEPTH, B_, 30, 1024), np.float32)
    npp = np.empty((DEPTH, B_, 15, 1024), np.float32)
    ncs = np.empty(state_conv.shape, np.float32)
    nps = np.empty(state_pool.shape, np.float32)
    for i in range(NCORES):
        b, h = i // 2, i % 2
        y = R[i]["y"]
        y_prompt[b, h * NPR:(h + 1) * NPR] = y[0:NPR]
        y_sample[SBQ * i:SBQ * (i + 1)] = y[NPR:].reshape(SBQ, 4, D)
        ncs[:, SBQ * i:SBQ * (i + 1)] = R[i]["ocs"]
        nps[:, SBQ * i:SBQ * (i + 1)] = R[i]["ops"]
        if h == 1:
            ncp[:, b] = R[i]["ocp"]
            npp[:, b] = R[i]["opp"]
    return (y_prompt, y_sample, ncp, npp, ncs, nps)
```

```python
import os
import numpy as np
from contextlib import ExitStack
import concourse.bass as bass
import concourse.mybir as mybir
from concourse.bass_utils import run_bass_kernel_spmd

F32 = mybir.dt.float32
BF16 = mybir.dt.bfloat16
AF = mybir.ActivationFunctionType
ALU = mybir.AluOpType

NCORES = 8
D = 2048
KC = 16
DIN = 3072
DFF = 8192
DEPTH = 2
T = 1152
TN = 384
HALO = 64
NPR = 1024
NSM = 64
SBQ = 16
NYR = NPR + NSM
EPS = 1e-6
NSLOT = 4
FB = 2048

PL = 336
O_G1, O_BIN, O_CW, O_CB, O_LNG, O_LNB, O_PSC, O_G2 = 0, 16, 40, 288, 296, 304, 312, 320
O_GF = DEPTH * PL
O_HM = O_GF + 16
O_IC = O_HM + 1
NPRM = O_IC + 64

SAME_ENG_SYNC = True
_DBG = set(os.environ.get("KDBG", "").split(",")) - {""}


class Buf:
    __slots__ = ("name", "w", "r", "pr", "excl")

    def __init__(self, name, excl=False):
        self.name = name
        self.w = {}
        self.r = {}
        self.pr = {}
        self.excl = excl


def _split(reads, writes):
    ex = [b for b in reads if b.excl]
    if not ex:
        return reads, writes
    return [b for b in reads if not b.excl], list(writes) + ex


def _merge(dst, src):
    for k, v in src.items():
        if dst.get(k, 0) < v:
            dst[k] = v


def handoff(src_bufs, dst_bufs):
    u = {}
    for b in src_bufs:
        _merge(u, b.w)
        _merge(u, b.r)
        _merge(u, b.pr)
    for b in dst_bufs:
        b.w = dict(u)
        b.r = {}
        b.pr = dict(u)


class Prog:
    ENG = ("pe", "act", "dve", "pool", "sp")

    def __init__(self):
        self.ops = {e: [] for e in self.ENG}
        self.cnt = {e: 0 for e in self.ENG}
        self.dcnt = {}

    def _deps(self, reads, writes, deps):
        d = {}
        for b in reads:
            _merge(d, b.w)
        for b in writes:
            _merge(d, b.r)
            _merge(d, b.pr)
            _merge(d, b.w)
        for h in deps:
            if h is not None:
                _merge(d, {h[0]: h[1]})
        return d

    def _reg(self, h, reads, writes):
        k, v = h
        for b in reads:
            if b.r.get(k, 0) < v:
                b.r[k] = v
        for b in writes:
            if b.r:
                b.pr = b.r
                b.r = {}
                b.w = {k: v}
            else:
                if b.w.get(k, 0) < v:
                    b.w[k] = v

    def op(self, eng, fn, reads=(), writes=(), deps=()):
        reads, writes = _split(reads, writes)
        d = self._deps(reads, writes, deps)
        self.cnt[eng] += 1
        h = (eng, self.cnt[eng])
        self.ops[eng].append((fn, d, (eng, 1)))
        self._reg(h, reads, writes)
        return h

    def group(self, fns, reads=(), writes=(), deps=()):
        reads, writes = _split(reads, writes)
        d = self._deps(reads, writes, deps)
        self.cnt["pe"] += 1
        h = ("pe", self.cnt["pe"])
        n = len(fns)
        for i, fn in enumerate(fns):
            self.ops["pe"].append((fn, d if i == 0 else {}, ("pe", 1) if i == n - 1 else None))
        self._reg(h, reads, writes)
        return h

    def dma(self, queue, fn, sem, reads=(), writes=(), deps=()):
        reads, writes = _split(reads, writes)
        d = self._deps(reads, writes, deps)
        self.dcnt[sem] = self.dcnt.get(sem, 0) + 16
        h = (sem, self.dcnt[sem])
        self.ops[queue].append((fn, d, (sem, 16)))
        self._reg(h, reads, writes)
        return h

    def wait(self, eng, deps):
        d = {}
        for h in deps:
            _merge(d, {h[0]: h[1]})
        self.ops[eng].append((None, d, None))

    def sem_names(self):
        return list(self.ENG) + sorted(self.dcnt.keys())

    def replay(self, eng, e, sems):
        waited = {}
        for fn, d, inc in self.ops[eng]:
            for k, v in d.items():
                if k == eng and (eng == "pe" or not SAME_ENG_SYNC):
                    continue
                if waited.get(k, 0) < v:
                    e.wait_ge(sems[k], v)
                    waited[k] = v
            if fn is None:
                continue
            inst = fn(e)
            if inc is not None:
                inst.then_inc(sems[inc[0]], inc[1])


class WStream:
    def __init__(self, P, wt, plan=None):
        self.P = P
        self.wt = wt
        self.plan = plan
        self.req = []
        self.i = 0
        self.issued = 0
        self.bufs = [Buf("w%d" % s) for s in range(NSLOT)]

    def _issue(self):
        i = self.issued
        s = i % NSLOT
        src, kc = self.plan[i]
        wt = self.wt
        self.P.dma("pool", lambda e, s=s, src=src, kc=kc: e.dma_start(out=wt[:, s, 0:kc, :], in_=src),
                   "w%d" % s, writes=[self.bufs[s]])
        self.issued += 1

    def get(self, src, kc):
        i = self.i
        self.i += 1
        if self.plan is None:
            self.req.append((src, kc))
            return i % NSLOT, self.bufs[i % NSLOT]
        while self.issued < min(i + NSLOT, len(self.plan)):
            self._issue()
        return i % NSLOT, self.bufs[i % NSLOT]


def build_program():
    nc = bass.Bass("TRN2", target_bir_lowering=False)
    dt_in = lambda n, s: nc.dram_tensor(n, s, F32, kind="ExternalInput").ap()
    dt_out = lambda n, s: nc.dram_tensor(n, s, F32, kind="ExternalOutput").ap()
    x_d = dt_in("x", [T, D])
    sconv_d = dt_in("sconv", [DEPTH, SBQ, 30, 1024])
    spool_d = dt_in("spool", [DEPTH, SBQ, 15, 1024])
    prm_d = dt_in("prm", [128, NPRM])
    ident_d = dt_in("ident", [128, 128])
    w_in_d = dt_in("w_in", [DEPTH, D, DIN])
    pool_w_d = dt_in("pool_w", [DEPTH, 4, 256, 256])
    w_out_d = dt_in("w_out", [DEPTH, D, D])
    w_ff1_d = dt_in("w_ff1", [DEPTH, D, DFF])
    w_ff2_d = dt_in("w_ff2", [DEPTH, DFF, D])
    y_d = dt_out("y", [NYR, D])
    ocs_d = dt_out("ocs", [DEPTH, SBQ, 30, 1024])
    ops_d = dt_out("ops", [DEPTH, SBQ, 15, 1024])
    ocp_d = dt_out("ocp", [DEPTH, 30, 1024])
    opp_d = dt_out("opp", [DEPTH, 15, 1024])

    w_in_v = [w_in_d[l].rearrange("(kc p) m -> p kc m", p=128) for l in range(DEPTH)]
    w_out_v = [w_out_d[l].rearrange("(kc p) m -> p kc m", p=128) for l in range(DEPTH)]
    w_ff1_v = [w_ff1_d[l].rearrange("(kc p) m -> p kc m", p=128) for l in range(DEPTH)]
    w_ff2_v = [w_ff2_d[l].rearrange("(jj kc p) m -> jj p kc m", p=128, kc=KC) for l in range(DEPTH)]
    pool_w_v = [[pool_w_d[l, g].rearrange("(kk p) m -> p kk m", p=128) for g in range(4)] for l in range(DEPTH)]

    with ExitStack() as st:
        sb = lambda n, s, d: st.enter_context(nc.sbuf_tensor(n, s, d))
        xT = sb("xT", [128, KC, T], F32)
        Bm = sb("Bm", [128, 18432], BF16)
        Cm = sb("Cm", [128, 9216], F32)
        S = sb("S", [128, 6, TN], F32)
        wt = sb("wt", [128, NSLOT, KC, 128], BF16)
        ubuf = sb("ubuf", [128, 3, 32 + TN], BF16)
        hsave = sb("hsave", [128, 8, 32], BF16)
        usamp = sb("usamp", [128, 8, SBQ, 34], BF16)
        zf = sb("zf", [128, 2, TN], F32)
        zb = sb("zb", [128, 2, 16 + TN], BF16)
        zsave = sb("zsave", [128, 8, 16], BF16)
        zsamp = sb("zsamp", [128, 8, SBQ, 19], BF16)
        dbuf = sb("dbuf", [128, 2, 2, TN], BF16)
        utail = sb("utail", [128, 8, 94], F32)
        ztail = sb("ztail", [128, 8, 94], F32)
        prm = sb("prm_t", [128, NPRM], F32)
        idf = sb("idf", [128, 128], F32)
        idb = sb("idb", [128, 128], BF16)
        ones = sb("ones", [128, 128], BF16)
        ps = st.enter_context(nc.psum_tensor("ps", [128, 8, 512], F32))

        hT = Bm[:, 0:6144].rearrange("p (k t) -> p k t", k=KC)
        mix = Bm[:, 6144:12288].rearrange("p (k t) -> p k t", k=KC)
        sqb = Bm[:, 12288:18432].rearrange("p (k t) -> p k t", k=KC)
        h2T = Bm[:, :].rearrange("p (a k t) -> p a k t", a=3, k=KC)
        tstage = Bm[:, 0:4096].bitcast(F32).rearrange("p (a c) -> p a c", a=2)
        xin = [Cm[:, s * 2048:(s + 1) * 2048] for s in range(2)]
        stgc = Cm[:, 0:4096].rearrange("p (g c) -> p g c", g=4)
        stgp = Cm[:, 4096:6144].rearrange("p (g c) -> p g c", g=2)
        vv = Cm[:, 0:3072].rearrange("p (c t) -> p c t", c=8)
        vbf = Cm[:, 3072:4608].bitcast(BF16).rearrange("p (c t) -> p c t", c=8)
        sqv = Cm[:, 4608:6144].bitcast(BF16).rearrange("p (c t) -> p c t", c=8)
        diag = Cm[:, 6144:8128].bitcast(BF16).rearrange("p (k j) -> p k j", k=31)
        sig = [Cm[:, 8128 + i * TN:8128 + (i + 1) * TN] for i in range(2)]
        t64 = Cm[:, 8896:8960]
        t16 = Cm[:, 8960:8976]
        fT = Cm[:, :].bitcast(BF16).rearrange("p (a k t) -> p a k t", a=3, k=KC)
        ybufs = [Cm[:, s * 2048:(s + 1) * 2048].rearrange("p (k t) -> p k t", k=KC) for s in range(2)]
        yout = [Cm[:, 4096 + s * 2048:4096 + (s + 1) * 2048] for s in range(2)]
        Sflat = S[:, 0:3, :].rearrange("p a t -> p (a t)")

        def pcol(c):
            return prm[:, c:c + 1]

        def emit(P, W):
            B = {}

            def bf(n):
                if n not in B:
                    B[n] = Buf(n)
                return B[n]

            pb = [bf("pb%d" % i) for i in range(8)]
            for b_ in pb:
                b_.excl = True
            xTb = [bf("xT%d" % t) for t in range(3)]
            state = {"ring": 0, "alt": 0}

            def ring3():
                b = state["ring"] % 3
                state["ring"] += 1
                return b

            def alt_eng():
                state["alt"] += 1
                return "act" if state["alt"] % 2 else "dve"

            def copy_op(eng, out, in_, reads, writes):
                if eng == "act":
                    return P.op("act", lambda e: e.activation(out=out, in_=in_, func=AF.Copy), reads=reads, writes=writes)
                return P.op("dve", lambda e: e.tensor_copy(out=out, in_=in_), reads=reads, writes=writes)

            outs = []
            P.dma("sp", lambda e: e.dma_start(out=prm[:], in_=prm_d[:, :]), "c0", writes=[bf("prm")])
            P.dma("sp", lambda e: e.dma_start(out=idf[:], in_=ident_d[:, :]), "c1", writes=[bf("idf")])
            P.op("dve", lambda e: e.tensor_copy(out=idb[:], in_=idf[:]), reads=[bf("idf")], writes=[bf("idb")])
            P.op("dve", lambda e: e.memset(ones[:], 1.0), writes=[bf("ones")])
            for l in range(DEPTH if "nopt" not in _DBG else 0):
                outs.append(P.dma("sp", lambda e, l=l: e.dma_start(out=ocs_d[l, :, 0:26, :], in_=sconv_d[l, :, 4:30, :]), "pt"))
                outs.append(P.dma("sp", lambda e, l=l: e.dma_start(out=ops_d[l, :, 0:11, :], in_=spool_d[l, :, 4:15, :]), "pt"))

            xinb = [bf("xin0"), bf("xin1")]
            bank = 0
            for i in range(T // 128):
                s = i % 2
                P.dma("sp", lambda e, i=i, s=s: e.dma_start(out=xin[s], in_=x_d[i * 128:(i + 1) * 128, :]),
                      "xi%d" % s, writes=[xinb[s]])
                for q in range(4):
                    bk = bank % 8
                    bank += 1
                    fns = [lambda e, s=s, q=q, j=j, bk=bk: e.transpose(
                        out=ps[:, bk, j * 128:(j + 1) * 128], in_=xin[s][:, (4 * q + j) * 128:(4 * q + j + 1) * 128], identity=idf[:])
                        for j in range(4)]
                    P.group(fns, reads=[xinb[s], bf("idf")], writes=[pb[bk]])
                    copy_op(alt_eng(), xT[:, 4 * q:4 * q + 4, i * 128:(i + 1) * 128],
                            ps[:, bk, :].rearrange("p (j t) -> p j t", t=128), [pb[bk]], [xTb[i // 3]])
            cbufs = list(xinb)

            def rms_stats(l_g_unused, src_cols, sq_view, sq_buf, bank_i, out_slot, out_buf, xbufs):
                for k in range(KC):
                    P.op("act", lambda e, k=k: e.activation(out=sq_view[:, k, :], in_=xT[:, k, src_cols[0]:src_cols[1]], func=AF.Square),
                         reads=xbufs, writes=[sq_buf])
                n = src_cols[1] - src_cols[0]
                fns = [lambda e, k=k: e.matmul(out=ps[:, bank_i, 0:n], lhsT=ones[:], rhs=sq_view[:, k, :], start=(k == 0), stop=(k == KC - 1))
                       for k in range(KC)]
                P.group(fns, reads=[sq_buf, bf("ones")], writes=[pb[bank_i]])
                rstd_finish(bank_i, n, out_slot, out_buf)

            def rstd_finish(bank_i, n, out_slot, out_buf):
                if out_slot is None:
                    out_slot, out_buf = ps[:, bank_i, 0:n], pb[bank_i]
                P.op("act", lambda e: e.activation(out=out_slot, in_=ps[:, bank_i, 0:n], func=AF.Sqrt, bias=EPS, scale=1.0 / D),
                     reads=[pb[bank_i]], writes=[out_buf])
                P.op("dve", lambda e: e.reciprocal(out=out_slot, in_=out_slot), reads=[out_buf], writes=[out_buf])

            def rms1_pre(l, t):
                c0 = t * TN
                rms_stats(None, (c0, c0 + TN), sqb, bf("sqb"), 7, None, None, [xTb[t]])
                for k in range(KC):
                    P.op("dve", lambda e, k=k: e.scalar_tensor_tensor(
                        out=hT[:, k, :], in0=xT[:, k, c0:c0 + TN], scalar=pcol(l * PL + O_G1 + k), in1=ps[:, 7, 0:TN],
                        op0=ALU.mult, op1=ALU.mult), reads=[xTb[t], pb[7], bf("prm")], writes=[bf("hT")])

            def mm_group(slot, wbuf, rhs_of_k, nk, bank_i, n, reads):
                fns = [lambda e, k=k: e.matmul(out=ps[:, bank_i, 0:n], lhsT=wt[:, slot, k, :], rhs=rhs_of_k(k),
                                               start=(k == 0), stop=(k == nk - 1)) for k in range(nk)]
                return P.group(fns, reads=[wbuf] + reads, writes=[pb[bank_i]])

            def stt(out, in0, scalar, in1, op0, op1, reads, writes):
                return P.op("dve", lambda e: e.scalar_tensor_tensor(out=out, in0=in0, scalar=scalar, in1=in1, op0=op0, op1=op1),
                            reads=reads, writes=writes)

            def ts1(out, in0, s1, op0, reads, writes):
                return P.op("dve", lambda e: e.tensor_scalar(out=out, in0=in0, scalar1=s1, scalar2=None, op0=op0),
                            reads=reads, writes=writes)

            def ts2(out, in0, s1, s2, op0, op1, reads, writes):
                return P.op("dve", lambda e: e.tensor_scalar(out=out, in0=in0, scalar1=s1, scalar2=s2, op0=op0, op1=op1),
                            reads=reads, writes=writes)

            def act(out, in_, func, reads, writes, bias=None, scale=None):
                kw = {}
                if bias is not None:
                    kw["bias"] = bias
                if scale is not None:
                    kw["scale"] = scale
                return P.op("act", lambda e: e.activation(out=out, in_=in_, func=func, **kw), reads=reads, writes=writes)

            def mixer(l, t):
                L0 = l * PL
                c0 = t * TN
                npr = TN if t < 2 else TN - NSM
                prmb = bf("prm")
                hTb, mixb = bf("hT"), bf("mix")
                vb, vbfb, sqvb = bf("v"), bf("vbf"), bf("sqv")
                sigb = [bf("sig0"), bf("sig1")]
                ubb = [bf("ub0"), bf("ub1"), bf("ub2")]

                dgb = [bf("diagA"), bf("diagB")]
                halves = [(0, 16), (16, 31)]

                def conv_build(c):
                    for hi, (k0, k1) in enumerate(halves):
                        nk = k1 - k0
                        P.op("dve", lambda e, k0=k0, k1=k1, nk=nk: e.tensor_tensor(
                            out=diag[:, k0:k1, :], in0=ps[:, 4, 0:128].unsqueeze(1).broadcast_to([128, nk, 128]),
                            in1=prm[:, L0 + O_CW + c * 31 + k0:L0 + O_CW + c * 31 + k1].unsqueeze(2).broadcast_to([128, nk, 128]),
                            op=ALU.mult), reads=[pb[4], prmb], writes=[dgb[hi]])

                def conv(c):
                    ui = c % 3
                    ub = ubuf[:, ui, :]
                    for hi, (k0, k1) in enumerate(halves):
                        fns = [lambda e, k=k: e.matmul(out=ps[:, 3, 0:npr], lhsT=diag[:, k, :], rhs=ub[:, 2 + k:2 + k + npr],
                                                       start=(k == 0), stop=(k == 30)) for k in range(k0, k1)]
                        P.group(fns, reads=[dgb[hi], ubb[ui]], writes=[pb[3]])
                    if t == 2:
                        fns = [lambda e, k=k: e.matmul(out=ps[:, 3, npr:TN], lhsT=diag[:, k, :], rhs=usamp[:, c, :, k:k + 4],
                                                       start=(k == 0), stop=(k == 30), skip_group_check=True) for k in range(31)]
                        P.group(fns, reads=[dgb[0], dgb[1], bf("usamp%d" % c)], writes=[pb[3]])
                    cb = pcol(L0 + O_CB + c)
                    act(vv[:, c, :], ps[:, 3, 0:TN], AF.Identity, [pb[3], prmb], [vb], bias=cb)
                    ts1(vbf[:, c, :], ps[:, 3, 0:TN], cb, ALU.add, [pb[3], prmb], [vbfb])
                    act(sqv[:, c, :], ps[:, 3, 0:TN], AF.Square, [pb[3], prmb], [sqvb], bias=cb)

                for c in range(8):
                    ui = c % 3
                    ub = ubuf[:, ui, :]
                    if c > 0:
                        conv_build(c - 1)
                    sa, wa = W.get(w_in_v[l][:, :, c * 128:(c + 1) * 128], KC)
                    ba = ring3()
                    mm_group(sa, wa, lambda k: hT[:, k, :], KC, ba, TN, [hTb])
                    sg, wg = W.get(w_in_v[l][:, :, 1024 + c * 128:1024 + (c + 1) * 128], KC)
                    bg = ring3()
                    mm_group(sg, wg, lambda k: hT[:, k, :], KC, bg, TN, [hTb])
                    si = c % 2
                    act(sig[si], ps[:, bg, 0:TN], AF.Sigmoid, [pb[bg], prmb], [sigb[si]], bias=pcol(L0 + O_BIN + 8 + c))
                    ba_col = pcol(L0 + O_BIN + c)
                    rdu = [pb[ba], sigb[si], prmb]
                    if t == 0:
                        P.op("dve", lambda e, ub=ub: e.memset(ub[:, 0:32], 0.0), writes=[ubb[ui]])
                    else:
                        P.op("dve", lambda e, ub=ub, c=c: e.tensor_copy(out=ub[:, 0:32], in_=hsave[:, c, :]),
                             reads=[bf("hsave%d" % c)], writes=[ubb[ui]])
                    if t == 0:
                        stt(t64, ps[:, ba, 0:HALO], ba_col, sig[si][:, 0:HALO], ALU.add, ALU.mult, rdu, [bf("t64")])
                        ts1(ub[:, 32:32 + HALO], t64, pcol(O_HM), ALU.mult, [bf("t64"), prmb], [ubb[ui]])
                        stt(ub[:, 32 + HALO:32 + TN], ps[:, ba, HALO:TN], ba_col, sig[si][:, HALO:TN], ALU.add, ALU.mult, rdu, [ubb[ui]])
                    else:
                        stt(ub[:, 32:32 + npr], ps[:, ba, 0:npr], ba_col, sig[si][:, 0:npr], ALU.add, ALU.mult, rdu, [ubb[ui]])
                    if t == 2:
                        stt(usamp[:, c, :, 30:34], ps[:, ba, npr:TN].rearrange("p (b j) -> p b j", j=4), ba_col,
                            sig[si][:, npr:TN].rearrange("p (b j) -> p b j", j=4), ALU.add, ALU.mult, rdu, [bf("usamp%d" % c)])
                        stt(utail[:, c, :], ps[:, ba, TN - 94:TN], ba_col, sig[si][:, TN - 94:TN], ALU.add, ALU.mult, rdu, [bf("utail")])
                    else:
                        P.op("dve", lambda e, ub=ub, c=c: e.tensor_copy(out=hsave[:, c, :], in_=ub[:, TN:TN + 32]),
                             reads=[ubb[ui]], writes=[bf("hsave%d" % c)])
                    if c > 0:
                        conv(c - 1)
                conv_build(7)
                conv(7)

                def pool_group(g):
                    w = 2 << g
                    db = g % 2
                    dB = bf("d%d" % db)
                    bz = []
                    for j in range(2):
                        m = 2 * g + j
                        sz, wz = W.get(w_in_v[l][:, :, 2048 + m * 128:2048 + (m + 1) * 128], KC)
                        b_ = ring3()
                        bz.append(b_)
                        mm_group(sz, wz, lambda k: hT[:, k, :], KC, b_, TN, [hTb])
                        bzc = pcol(L0 + O_BIN + 16 + m)
                        zfB, zbB = bf("zf%d" % j), bf("zb%d" % j)
                        act(zf[:, j, :], ps[:, b_, 0:TN], AF.Identity, [pb[b_], prmb], [zfB], bias=bzc)
                        if t == 0:
                            P.op("dve", lambda e, j=j: e.memset(zb[:, j, 0:16], 0.0), writes=[zbB])
                            ts2(zb[:, j, 16:16 + HALO], ps[:, b_, 0:HALO], bzc, pcol(O_HM), ALU.add, ALU.mult, [pb[b_], prmb], [zbB])
                            ts1(zb[:, j, 16 + HALO:16 + TN], ps[:, b_, HALO:TN], bzc, ALU.add, [pb[b_], prmb], [zbB])
                        else:
                            P.op("dve", lambda e, j=j, m=m: e.tensor_copy(out=zb[:, j, 0:16], in_=zsave[:, m, :]),
                                 reads=[bf("zsave%d" % m)], writes=[zbB])
                            ts1(zb[:, j, 16:16 + npr], ps[:, b_, 0:npr], bzc, ALU.add, [pb[b_], prmb], [zbB])
                        if t == 2:
                            ts1(zsamp[:, m, :, 15:19], ps[:, b_, npr:TN].rearrange("p (b j) -> p b j", j=4), bzc, ALU.add,
                                [pb[b_], prmb], [bf("zsamp%d" % m)])
                            ts1(ztail[:, m, :], ps[:, b_, TN - 94:TN], bzc, ALU.add, [pb[b_], prmb], [bf("ztail")])
                        else:
                            P.op("dve", lambda e, j=j, m=m: e.tensor_copy(out=zsave[:, m, :], in_=zb[:, j, TN:TN + 16]),
                                 reads=[zbB], writes=[bf("zsave%d" % m)])
                    for j in range(2):
                        m = 2 * g + j
                        zfB, zbB = bf("zf%d" % j), bf("zb%d" % j)
                        bp = ring3()
                        fns = [lambda e, k=k, j=j, bp=bp, w=w: e.matmul(out=ps[:, bp, 0:npr], lhsT=idb[:], rhs=zb[:, j, 16 - k:16 - k + npr],
                                                                   start=(k == 0), stop=(k == w - 1)) for k in range(w)]
                        rd = [bf("idb"), zbB]
                        if t == 2:
                            fns += [lambda e, k=k, m=m, bp=bp, w=w: e.matmul(out=ps[:, bp, npr:TN], lhsT=idb[:], rhs=zsamp[:, m, :, 15 - k:19 - k],
                                                                        start=(k == 0), stop=(k == w - 1), skip_group_check=True) for k in range(w)]
                            rd.append(bf("zsamp%d" % m))
                        P.group(fns, reads=rd, writes=[pb[bp]])
                        rdd = [pb[bp], zfB]
                        if t == 0:
                            stt(dbuf[:, db, j, 0:HALO], ps[:, bp, 0:HALO], 1.0 / w, zf[:, j, 0:HALO], ALU.mult, ALU.subtract, rdd, [dB])
                            P.op("dve", lambda e, bp=bp, g=g: e.tensor_tensor(out=t16, in0=ps[:, bp, HALO:HALO + 16],
                                                                              in1=prm[:, O_IC + g * 16:O_IC + (g + 1) * 16], op=ALU.mult),
                                 reads=[pb[bp], prmb], writes=[bf("t16")])
                            P.op("dve", lambda e, j=j, db=db: e.tensor_tensor(out=dbuf[:, db, j, HALO:HALO + 16], in0=t16,
                                                                              in1=zf[:, j, HALO:HALO + 16], op=ALU.subtract),
                                 reads=[bf("t16"), zfB], writes=[dB])
                            stt(dbuf[:, db, j, HALO + 16:TN], ps[:, bp, HALO + 16:TN], 1.0 / w, zf[:, j, HALO + 16:TN],
                                ALU.mult, ALU.subtract, rdd, [dB])
                        else:
                            stt(dbuf[:, db, j, :], ps[:, bp, 0:TN], 1.0 / w, zf[:, j, :], ALU.mult, ALU.subtract, rdd, [dB])
                    for e_ in range(2):
                        m = 2 * g + e_
                        sp_, wp = W.get(pool_w_v[l][g][:, :, e_ * 128:(e_ + 1) * 128], 2)
                        bq = ring3()
                        mm_group(sp_, wp, lambda k, db=db: dbuf[:, db, k, :], 2, bq, TN, [dB])
                        act(mix[:, 8 + m, :], ps[:, bq, 0:TN], AF.Copy, [pb[bq], prmb], [mixb], scale=pcol(L0 + O_PSC + m))

                nxt = t + 1 if t < 2 else None
                if nxt is not None:
                    n0 = nxt * TN
                    for k in range(KC):
                        P.op("act", lambda e, k=k: e.activation(out=sqb[:, k, :], in_=xT[:, k, n0:n0 + TN], func=AF.Square),
                             reads=[xTb[nxt]], writes=[bf("sqb")])

                fns = [lambda e, c=c: e.matmul(out=ps[:, 5, 0:TN], lhsT=ones[:], rhs=vbf[:, c, :], start=(c == 0), stop=(c == 7)) for c in range(8)]
                P.group(fns, reads=[vbfb, bf("ones")], writes=[pb[5]])
                fns = [lambda e, c=c: e.matmul(out=ps[:, 6, 0:TN], lhsT=ones[:], rhs=sqv[:, c, :], start=(c == 0), stop=(c == 7)) for c in range(8)]
                P.group(fns, reads=[sqvb, bf("ones")], writes=[pb[6]])
                if nxt is not None:
                    fns = [lambda e, k=k: e.matmul(out=ps[:, 7, 0:TN], lhsT=ones[:], rhs=sqb[:, k, :], start=(k == 0), stop=(k == KC - 1))
                           for k in range(KC)]
                    P.group(fns, reads=[bf("sqb"), bf("ones")], writes=[pb[7]])
                S0b, S1b = bf("S0"), bf("S1")
                act(S[:, 0, :], ps[:, 5, 0:TN], AF.Copy, [pb[5]], [S0b], scale=1.0 / 1024)
                P.op("dve", lambda e: e.tensor_tensor(out=S[:, 1, :], in0=S[:, 0, :], in1=S[:, 0, :], op=ALU.mult), reads=[S0b], writes=[S1b])
                stt(S[:, 1, :], ps[:, 6, 0:TN], 1.0 / 1024, S[:, 1, :], ALU.mult, ALU.subtract, [pb[6], S1b], [S1b])
                act(S[:, 1, :], S[:, 1, :], AF.Sqrt, [S1b], [S1b], bias=EPS)
                P.op("dve", lambda e: e.reciprocal(out=S[:, 1, :], in_=S[:, 1, :]), reads=[S1b], writes=[S1b])
                if nxt is not None:
                    rstd_finish(7, TN, None, None)
                def ln_apply(c):
                    sl = 2 + c % 2
                    Sb = bf("S%d" % sl)
                    P.op("dve", lambda e, c=c, sl=sl: e.tensor_tensor(out=S[:, sl, :], in0=vv[:, c, :], in1=S[:, 0, :], op=ALU.subtract),
                         reads=[vb, S0b], writes=[Sb])
                    P.op("dve", lambda e, sl=sl: e.tensor_tensor(out=S[:, sl, :], in0=S[:, sl, :], in1=S[:, 1, :], op=ALU.mult),
                         reads=[Sb, S1b], writes=[Sb])
                    act(mix[:, c, :], S[:, sl, :], AF.Silu, [Sb, prmb], [mixb], bias=pcol(L0 + O_LNB + c), scale=pcol(L0 + O_LNG + c))

                for g in range(4):
                    pool_group(g)
                    ln_apply(2 * g)
                    ln_apply(2 * g + 1)
                for m in range(KC):
                    so, wo = W.get(w_out_v[l][:, :, m * 128:(m + 1) * 128], KC)
                    bo = ring3()
                    mm_group(so, wo, lambda k: mix[:, k, :], KC, bo, TN, [mixb])
                    P.op("dve", lambda e, m=m, bo=bo: e.tensor_tensor(out=xT[:, m, c0:c0 + TN], in0=xT[:, m, c0:c0 + TN], in1=ps[:, bo, 0:TN], op=ALU.add),
                         reads=[pb[bo], xTb[t]], writes=[xTb[t]])
                    if nxt is not None:
                        n0 = nxt * TN
                        stt(hT[:, m, :], xT[:, m, n0:n0 + TN], pcol(L0 + O_G1 + m), ps[:, 7, 0:TN], ALU.mult, ALU.mult,
                            [xTb[nxt], pb[7], prmb], [hTb])
                if t == 2 and "notails" not in _DBG:
                    for a, (tl, tlb) in enumerate([(utail, bf("utail")), (ztail, bf("ztail"))]):
                        for q in range(2):
                            bk = 5 + (2 * a + q) % 3
                            fns = [lambda e, tl=tl, q=q, i=i, bk=bk: e.transpose(out=ps[0:94, bk, i * 128:(i + 1) * 128], in_=tl[:, 4 * q + i, :], identity=idf[:])
                                   for i in range(4)]
                            P.group(fns, reads=[tlb, bf("idf")], writes=[pb[bk]])
                            copy_op(alt_eng(), tstage[0:94, a, q * 512:(q + 1) * 512], ps[0:94, bk, :], [pb[bk]], [hTb])
                    outs.append(P.dma("sp", lambda e: e.dma_start(out=ocp_d[l, :, :], in_=tstage[0:30, 0, :]), "to", reads=[hTb]))
                    outs.append(P.dma("sp", lambda e: e.dma_start(out=opp_d[l, :, :], in_=tstage[15:30, 1, :]), "to", reads=[hTb]))
                    for b in range(SBQ):
                        outs.append(P.dma("sp", lambda e, b=b: e.dma_start(out=ocs_d[l, b, 26:30, :], in_=tstage[30 + 4 * b:34 + 4 * b, 0, :]), "to", reads=[hTb]))
                        outs.append(P.dma("sp", lambda e, b=b: e.dma_start(out=ops_d[l, b, 11:15, :], in_=tstage[30 + 4 * b:34 + 4 * b, 1, :]), "to", reads=[hTb]))

            def ffn(l):
                L0 = l * PL
                prmb = bf("prm")
                h2b, sqCb, fCb = bf("h2T"), bf("sqC"), bf("fC")
                Sb = [bf("S%d" % i) for i in range(6)]
                for t in range(3):
                    c0 = t * TN
                    bk = 6 + t % 2
                    for k in range(KC):
                        P.op("act", lambda e, k=k, t=t, c0=c0: e.activation(out=fT[:, t, k, :], in_=xT[:, k, c0:c0 + TN], func=AF.Square),
                             reads=[xTb[t]], writes=[sqCb])
                    fns = [lambda e, k=k, t=t, bk=bk: e.matmul(out=ps[:, bk, 0:TN], lhsT=ones[:], rhs=fT[:, t, k, :], start=(k == 0), stop=(k == KC - 1))
                           for k in range(KC)]
                    P.group(fns, reads=[sqCb, bf("ones")], writes=[pb[bk]])
                    rstd_finish(bk, TN, None, None)
                    for k in range(KC):
                        stt(h2T[:, t, k, :], xT[:, k, c0:c0 + TN], pcol(L0 + O_G2 + k), ps[:, bk, 0:TN], ALU.mult, ALU.mult,
                            [xTb[t], pb[bk], prmb], [h2b])
                handoff([sqCb], [fCb])
                slot3 = 0
                sc_i = 0
                tr = [(HALO if (t == 0 and l == DEPTH - 1) else 0, TN) for t in range(3)]
                for j in range(DFF // FB):
                    for m in range(KC):
                        s1, w1 = W.get(w_ff1_v[l][:, :, j * FB + m * 128:j * FB + (m + 1) * 128], KC)
                        b0 = 3 * (slot3 % 2)
                        slot3 += 1
                        fns = [lambda e, k=k, t=t, s1=s1, b0=b0: e.matmul(out=ps[:, b0 + t, 0:tr[t][1] - tr[t][0]], lhsT=wt[:, s1, k, :],
                                                                        rhs=h2T[:, t, k, tr[t][0]:tr[t][1]],
                                                                        start=(k == 0), stop=(k == KC - 1)) for k in range(KC) for t in range(3)]
                        P.group(fns, reads=[w1, h2b], writes=[pb[b0], pb[b0 + 1], pb[b0 + 2]])
                        for t in range(3):
                            o0, o1 = tr[t]
                            si = 3 + sc_i % 3
                            sc_i += 1
                            act(S[:, si, 0:o1 - o0], ps[:, b0 + t, 0:o1 - o0], AF.Square, [pb[b0 + t]], [Sb[si]])
                            stt(fT[:, t, m, o0:o1], ps[:, b0 + t, 0:o1 - o0], 0.0, S[:, si, 0:o1 - o0], ALU.is_gt, ALU.mult, [pb[b0 + t], Sb[si]], [fCb])
                    for m in range(KC):
                        s2, w2 = W.get(w_ff2_v[l][j][:, :, m * 128:(m + 1) * 128], KC)
                        b0 = 3 * (slot3 % 2)
                        slot3 += 1
                        fns = [lambda e, k=k, t=t, s2=s2, b0=b0: e.matmul(out=ps[:, b0 + t, 0:tr[t][1] - tr[t][0]], lhsT=wt[:, s2, k, :],
                                                                        rhs=fT[:, t, k, tr[t][0]:tr[t][1]],
                                                                        start=(k == 0), stop=(k == KC - 1)) for k in range(KC) for t in range(3)]
                        P.group(fns, reads=[w2, fCb], writes=[pb[b0], pb[b0 + 1], pb[b0 + 2]])
                        for t in range(3):
                            o0, o1 = tr[t]
                            c0 = t * TN + o0
                            nn = o1 - o0
                            P.op("dve", lambda e, m=m, t=t, b0=b0, c0=c0, nn=nn: e.tensor_tensor(out=xT[:, m, c0:c0 + nn], in0=xT[:, m, c0:c0 + nn],
                                                                                             in1=ps[:, b0 + t, 0:nn], op=ALU.add),
                                 reads=[pb[b0 + t], xTb[t]], writes=[xTb[t]])
                return [h2b, fCb]

            for l in range(DEPTH if "nolayers" not in _DBG else 0):
                if l > 0:
                    handoff([bf("h2T")], [bf("hT"), bf("mix"), bf("sqb")])
                P.group([lambda e: e.transpose(out=ps[:, 4, 0:128], in_=idf[:], identity=idf[:])], reads=[bf("idf")], writes=[pb[4]])
                rms1_pre(l, 0)
                stb = bf("stage")
                handoff(cbufs, [stb])
                for gq in range(4):
                    P.dma("sp", lambda e, l=l, gq=gq: e.dma_start(out=stgc[0:120, gq, :],
                                                                  in_=sconv_d[l, 4 * gq:4 * gq + 4, :, :].rearrange("b j c -> (b j) c")),
                          "sg", writes=[stb])
                for gp in range(2):
                    P.dma("sp", lambda e, l=l, gp=gp: e.dma_start(out=stgp[0:120, gp, :],
                                                                  in_=spool_d[l, 8 * gp:8 * gp + 8, :, :].rearrange("b j c -> (b j) c")),
                          "sg", writes=[stb])
                hbanks = [0, 1, 2, 3, 5, 6]
                bank = 0
                for c in range(8):
                    bk = hbanks[bank % 6]
                    bank += 1
                    fns = [lambda e, c=c, gq=gq, bk=bk: e.transpose(out=ps[:, bk, gq * 120:(gq + 1) * 120], in_=stgc[0:120, gq, c * 128:(c + 1) * 128],
                                                                   identity=idf[0:120, 0:120]) for gq in range(4)]
                    P.group(fns, reads=[stb, bf("idf")], writes=[pb[bk]])
                    copy_op(alt_eng(), usamp[:, c, :, 0:30], ps[:, bk, 0:480].rearrange("p (b j) -> p b j", j=30), [pb[bk]], [bf("usamp%d" % c)])
                    bk = hbanks[bank % 6]
                    bank += 1
                    fns = [lambda e, c=c, gp=gp, bk=bk: e.transpose(out=ps[:, bk, gp * 120:(gp + 1) * 120], in_=stgp[0:120, gp, c * 128:(c + 1) * 128],
                                                                   identity=idf[0:120, 0:120]) for gp in range(2)]
                    P.group(fns, reads=[stb, bf("idf")], writes=[pb[bk]])
                    copy_op(alt_eng(), zsamp[:, c, :, 0:15], ps[:, bk, 0:240].rearrange("p (b j) -> p b j", j=15), [pb[bk]], [bf("zsamp%d" % c)])
                mixbufs = [bf(n) for n in ("v", "vbf", "sqv", "diagA", "diagB", "sig0", "sig1", "t64", "t16")]
                handoff([stb], mixbufs)
                for t in range((2 if "t01" in _DBG else 3) if "nomixer" not in _DBG else 0):
                    mixer(l, t)
                handoff([bf("hT"), bf("mix"), bf("sqb")], [bf("h2T")])
                handoff(mixbufs, [bf("sqC")])
                cbufs = ffn(l)[1:] if "noffn" not in _DBG else [bf("sqC")]

            ybs, yob, sqFb = [bf("ybuf0"), bf("ybuf1")], [bf("yout0"), bf("yout1")], bf("sqb")
            handoff(cbufs, ybs + yob)
            if DEPTH > 0 and "nolayers" not in _DBG:
                handoff([bf("h2T")], [sqFb])
            for t in range(3):
                c0 = t * TN
                rms_stats(None, (c0, c0 + TN), sqb, sqFb, 6 + t % 2, S[:, t, :], bf("S%d" % t), [xTb[t]])
            Sall = [bf("S0"), bf("S1"), bf("S2")]
            bank = 0
            for i in range(9):
                n = 128 if i < 8 else 64
                col0 = HALO + 128 * i
                s = i % 2
                yb_v = ybufs[s]
                for k in range(KC):
                    stt(yb_v[:, k, 0:n], xT[:, k, col0:col0 + n], pcol(O_GF + k), Sflat[:, col0:col0 + n], ALU.mult, ALU.mult,
                        xTb + Sall + [bf("prm")], [ybs[s]])
                for q in range(4):
                    bk = bank % 6
                    bank += 1
                    fns = [lambda e, q=q, j=j, bk=bk, n=n, yb_v=yb_v: e.transpose(out=ps[0:n, bk, j * 128:(j + 1) * 128], in_=yb_v[:, 4 * q + j, 0:n], identity=idf[:])
                           for j in range(4)]
                    P.group(fns, reads=[ybs[s], bf("idf")], writes=[pb[bk]])
                    copy_op(alt_eng(), yout[s][0:n, q * 512:(q + 1) * 512], ps[0:n, bk, :], [pb[bk]], [yob[s]])
                outs.append(P.dma("sp", lambda e, i=i, n=n, s=s: e.dma_start(out=y_d[i * 128:i * 128 + n, :], in_=yout[s][0:n, :]),
                                  "yo%d" % s, reads=[yob[s]]))
            last = {}
            for h in outs:
                if last.get(h[0], 0) < h[1]:
                    last[h[0]] = h[1]
            P.wait("sp", list(last.items()))

        Wd = WStream(Prog(), wt, plan=None)
        emit(Wd.P, Wd)
        P = Prog()
        W = WStream(P, wt, plan=Wd.req)
        emit(P, W)
        assert W.i == len(Wd.req)

        sems = {n: st.enter_context(nc.semaphore(n)) for n in P.sem_names()}
        block = st.enter_context(nc.Block())

        @block.tensor
        def _(e):
            P.replay("pe", e, sems)

        @block.scalar
        def _(e):
            P.replay("act", e, sems)

        @block.vector
        def _(e):
            P.replay("dve", e, sems)

        @block.gpsimd
        def _(e):
            P.replay("pool", e, sems)

        @block.sync
        def _(e):
            P.replay("sp", e, sems)
    return nc


def _pack_params(norm1_g, b_in, conv_w, conv_b, ln_g, ln_b, pool_scale, norm2_g, norm_f, half):
    prm = np.zeros((128, NPRM), np.float32)

    def cols(v):
        return np.ascontiguousarray(np.asarray(v, np.float32).reshape(-1, 128).T)

    for l in range(DEPTH):
        L0 = l * PL
        prm[:, L0 + O_G1:L0 + O_G1 + 16] = cols(norm1_g[l])
        prm[:, L0 + O_BIN:L0 + O_BIN + 24] = cols(b_in[l])
        cw = np.asarray(conv_w[l], np.float32)
        prm[:, L0 + O_CW:L0 + O_CW + 248] = cw.reshape(31, 8, 128).transpose(2, 1, 0).reshape(128, 248)
        prm[:, L0 + O_CB:L0 + O_CB + 8] = cols(conv_b[l])
        prm[:, L0 + O_LNG:L0 + O_LNG + 8] = cols(ln_g[l])
        prm[:, L0 + O_LNB:L0 + O_LNB + 8] = cols(ln_b[l])
        prm[:, L0 + O_PSC:L0 + O_PSC + 8] = cols(pool_scale[l])
        prm[:, L0 + O_G2:L0 + O_G2 + 16] = cols(norm2_g[l])
    prm[:, O_GF:O_GF + 16] = cols(norm_f)
    prm[:, O_HM] = float(half)
    pos = np.arange(16)
    for g in range(4):
        w = 2 << g
        cnt = np.minimum(w, pos + 1) if half == 0 else np.full(16, w)
        prm[:, O_IC + g * 16:O_IC + (g + 1) * 16] = (1.0 / cnt.astype(np.float32))[None, :]
    return prm


_NC_CACHE = {}


def kernel(x_prompt, x_sample, state_conv, state_pool, norm1_g, w_in, b_in, conv_w, conv_b,
           ln_g, ln_b, pool_w, pool_scale, w_out, norm2_g, w_ff1, w_ff2, norm_f):
    f = lambda a: np.ascontiguousarray(np.asarray(a, dtype=np.float32))
    x_prompt, x_sample, state_conv, state_pool = f(x_prompt), f(x_sample), f(state_conv), f(state_pool)
    w_in, pool_w, w_out, w_ff1, w_ff2 = f(w_in), f(pool_w), f(w_out), f(w_ff1), f(w_ff2)
    if "nc" not in _NC_CACHE:
        _NC_CACHE["nc"] = build_program()
    nc = _NC_CACHE["nc"]
    ident = np.eye(128, dtype=np.float32)
    prms = [_pack_params(norm1_g, b_in, conv_w, conv_b, ln_g, ln_b, pool_scale, norm2_g, norm_f, h) for h in range(2)]
    in_maps = []
    for i in range(NCORES):
        b, h = i // 2, i % 2
        xc = np.zeros((T, D), np.float32)
        if h == 1:
            xc[0:HALO] = x_prompt[b, NPR - HALO:NPR]
        xc[HALO:HALO + NPR] = x_prompt[b, h * NPR:(h + 1) * NPR]
        xc[HALO + NPR:] = x_sample[SBQ * i:SBQ * (i + 1)].reshape(NSM, D)
        in_maps.append({
            "x": xc,
            "sconv": np.ascontiguousarray(state_conv[:, SBQ * i:SBQ * (i + 1)]),
            "spool": np.ascontiguousarray(state_pool[:, SBQ * i:SBQ * (i + 1)]),
            "prm": prms[h], "ident": ident,
            "w_in": w_in, "pool_w": pool_w, "w_out": w_out, "w_ff1": w_ff1, "w_ff2": w_ff2,
        })
    ncr = int(os.environ.get("KCORES", NCORES))
    res = run_bass_kernel_spmd(nc, in_maps[:ncr], core_ids=list(range(ncr)))
    R = list(res.results) + [res.results[0]] * (NCORES - ncr)
    B_, S_ = x_prompt.shape[0], x_prompt.shape[1]
    y_prompt = np.empty((B_, S_, D), np.float32)
    y_sample = np.empty(x_sample.shape, np.float32)
    ncp = np.empty((DEPTH, B_, 30, 1024), np.float32)
    npp = np.empty((DEPTH, B_, 15, 1024), np.float32)
    ncs = np.empty(state_conv.shape, np.float32)
    nps = np.empty(state_pool.shape, np.float32)
    for i in range(NCORES):
        b, h = i // 2, i % 2
        y = R[i]["y"]
        y_prompt[b, h * NPR:(h + 1) * NPR] = y[0:NPR]
        y_sample[SBQ * i:SBQ * (i + 1)] = y[NPR:].reshape(SBQ, 4, D)
        ncs[:, SBQ * i:SBQ * (i + 1)] = R[i]["ocs"]
        nps[:, SBQ * i:SBQ * (i + 1)] = R[i]["ops"]
        if h == 1:
            ncp[:, b] = R[i]["ocp"]
            npp[:, b] = R[i]["opp"]
    return (y_prompt, y_sample, ncp, npp, ncs, nps)
```

```python
import os
import numpy as np
from contextlib import ExitStack
import concourse.bass as bass
import concourse.mybir as mybir
from concourse.bass_utils import run_bass_kernel_spmd

F32 = mybir.dt.float32
BF16 = mybir.dt.bfloat16
AF = mybir.ActivationFunctionType
ALU = mybir.AluOpType

NCORES = 8
D = 2048
KC = 16
DIN = 3072
DFF = 8192
DEPTH = 2
T = 1152
TN = 384
HALO = 64
NPR = 1024
NSM = 64
SBQ = 16
NYR = NPR + NSM
EPS = 1e-6
NSLOT = 4
FB = 2048

PL = 336
O_G1, O_BIN, O_CW, O_CB, O_LNG, O_LNB, O_PSC, O_G2 = 0, 16, 40, 288, 296, 304, 312, 320
O_GF = DEPTH * PL
O_HM = O_GF + 16
O_IC = O_HM + 1
NPRM = O_IC + 64

SAME_ENG_SYNC = True
_DBG = set(os.environ.get("KDBG", "").split(",")) - {""}


class Buf:
    __slots__ = ("name", "w", "r", "pr", "excl")

    def __init__(self, name, excl=False):
        self.name = name
        self.w = {}
        self.r = {}
        self.pr = {}
        self.excl = excl


def _split(reads, writes):
    ex = [b for b in reads if b.excl]
    if not ex:
        return reads, writes
    return [b for b in reads if not b.excl], list(writes) + ex


def _merge(dst, src):
    for k, v in src.items():
        if dst.get(k, 0) < v:
            dst[k] = v


def handoff(src_bufs, dst_bufs):
    u = {}
    for b in src_bufs:
        _merge(u, b.w)
        _merge(u, b.r)
        _merge(u, b.pr)
    for b in dst_bufs:
        b.w = dict(u)
        b.r = {}
        b.pr = dict(u)


class Prog:
    ENG = ("pe", "act", "dve", "pool", "sp")

    def __init__(self):
        self.ops = {e: [] for e in self.ENG}
        self.cnt = {e: 0 for e in self.ENG}
        self.dcnt = {}

    def _deps(self, reads, writes, deps):
        d = {}
        for b in reads:
            _merge(d, b.w)
        for b in writes:
            _merge(d, b.r)
            _merge(d, b.pr)
            _merge(d, b.w)
        for h in deps:
            if h is not None:
                _merge(d, {h[0]: h[1]})
        return d

    def _reg(self, h, reads, writes):
        k, v = h
        for b in reads:
            if b.r.get(k, 0) < v:
                b.r[k] = v
        for b in writes:
            if b.r:
                b.pr = b.r
                b.r = {}
                b.w = {k: v}
            else:
                if b.w.get(k, 0) < v:
                    b.w[k] = v

    def op(self, eng, fn, reads=(), writes=(), deps=()):
        reads, writes = _split(reads, writes)
        d = self._deps(reads, writes, deps)
        self.cnt[eng] += 1
        h = (eng, self.cnt[eng])
        self.ops[eng].append((fn, d, (eng, 1)))
        self._reg(h, reads, writes)
        return h

    def group(self, fns, reads=(), writes=(), deps=()):
        reads, writes = _split(reads, writes)
        d = self._deps(reads, writes, deps)
        self.cnt["pe"] += 1
        h = ("pe", self.cnt["pe"])
        n = len(fns)
        for i, fn in enumerate(fns):
            self.ops["pe"].append((fn, d if i == 0 else {}, ("pe", 1) if i == n - 1 else None))
        self._reg(h, reads, writes)
        return h

    def dma(self, queue, fn, sem, reads=(), writes=(), deps=()):
        reads, writes = _split(reads, writes)
        d = self._deps(reads, writes, deps)
        self.dcnt[sem] = self.dcnt.get(sem, 0) + 16
        h = (sem, self.dcnt[sem])
        self.ops[queue].append((fn, d, (sem, 16)))
        self._reg(h, reads, writes)
        return h

    def wait(self, eng, deps):
        d = {}
        for h in deps:
            _merge(d, {h[0]: h[1]})
        self.ops[eng].append((None, d, None))

    def sem_names(self):
        return list(self.ENG) + sorted(self.dcnt.keys())

    def replay(self, eng, e, sems):
        waited = {}
        for fn, d, inc in self.ops[eng]:
            for k, v in d.items():
                if k == eng and (eng == "pe" or not SAME_ENG_SYNC):
                    continue
                if waited.get(k, 0) < v:
                    e.wait_ge(sems[k], v)
                    waited[k] = v
            if fn is None:
                continue
            inst = fn(e)
            if inc is not None:
                inst.then_inc(sems[inc[0]], inc[1])


class WStream:
    def __init__(self, P, wt, plan=None):
        self.P = P
        self.wt = wt
        self.plan = plan
        self.req = []
        self.i = 0
        self.issued = 0
        self.bufs = [Buf("w%d" % s) for s in range(NSLOT)]

    def _issue(self):
        i = self.issued
        s = i % NSLOT
        src, kc = self.plan[i]
        wt = self.wt
        self.P.dma("pool", lambda e, s=s, src=src, kc=kc: e.dma_start(out=wt[:, s, 0:kc, :], in_=src),
                   "w%d" % s, writes=[self.bufs[s]])
        self.issued += 1

    def get(self, src, kc):
        i = self.i
        self.i += 1
        if self.plan is None:
            self.req.append((src, kc))
            return i % NSLOT, self.bufs[i % NSLOT]
        while self.issued < min(i + NSLOT, len(self.plan)):
            self._issue()
        return i % NSLOT, self.bufs[i % NSLOT]


def build_program():
    nc = bass.Bass("TRN2", target_bir_lowering=False)
    dt_in = lambda n, s: nc.dram_tensor(n, s, F32, kind="ExternalInput").ap()
    dt_out = lambda n, s: nc.dram_tensor(n, s, F32, kind="ExternalOutput").ap()
    x_d = dt_in("x", [T, D])
    sconv_d = dt_in("sconv", [DEPTH, SBQ, 30, 1024])
    spool_d = dt_in("spool", [DEPTH, SBQ, 15, 1024])
    prm_d = dt_in("prm", [128, NPRM])
    ident_d = dt_in("ident", [128, 128])
    gfb_d = dt_in("gfb", [128, D])
    w_in_d = dt_in("w_in", [DEPTH, D, DIN])
    pool_w_d = dt_in("pool_w", [DEPTH, 4, 256, 256])
    w_out_d = dt_in("w_out", [DEPTH, D, D])
    w_ff1_d = dt_in("w_ff1", [DEPTH, D, DFF])
    w_ff2_d = dt_in("w_ff2", [DEPTH, DFF, D])
    y_d = dt_out("y", [NYR, D])
    ocs_d = dt_out("ocs", [DEPTH, SBQ, 30, 1024])
    ops_d = dt_out("ops", [DEPTH, SBQ, 15, 1024])
    ocp_d = dt_out("ocp", [DEPTH, 30, 1024])
    opp_d = dt_out("opp", [DEPTH, 15, 1024])

    w_in_v = [w_in_d[l].rearrange("(kc p) m -> p kc m", p=128) for l in range(DEPTH)]
    w_out_v = [w_out_d[l].rearrange("(kc p) m -> p kc m", p=128) for l in range(DEPTH)]
    w_ff1_v = [w_ff1_d[l].rearrange("(kc p) m -> p kc m", p=128) for l in range(DEPTH)]
    w_ff2_v = [w_ff2_d[l].rearrange("(jj kc p) m -> jj p kc m", p=128, kc=KC) for l in range(DEPTH)]
    pool_w_v = [[pool_w_d[l, g].rearrange("(kk p) m -> p kk m", p=128) for g in range(4)] for l in range(DEPTH)]

    with ExitStack() as st:
        sb = lambda n, s, d: st.enter_context(nc.sbuf_tensor(n, s, d))
        xT = sb("xT", [128, KC, T], F32)
        Bm = sb("Bm", [128, 18432], BF16)
        Cm = sb("Cm", [128, 9216], F32)
        S = sb("S", [128, 6, TN], F32)
        wt = sb("wt", [128, NSLOT, KC, 128], BF16)
        ubuf = sb("ubuf", [128, 3, 32 + TN], BF16)
        hsave = sb("hsave", [128, 8, 32], BF16)
        usamp = sb("usamp", [128, 8, SBQ, 34], BF16)
        zf = sb("zf", [128, 2, TN], F32)
        zb = sb("zb", [128, 2, 16 + TN], BF16)
        zsave = sb("zsave", [128, 8, 16], BF16)
        zsamp = sb("zsamp", [128, 8, SBQ, 19], BF16)
        dbuf = sb("dbuf", [128, 2, 2, TN], BF16)
        utail = sb("utail", [128, 8, 94], F32)
        ztail = sb("ztail", [128, 8, 94], F32)
        prm = sb("prm_t", [128, NPRM], F32)
        idf = sb("idf", [128, 128], F32)
        idb = sb("idb", [128, 128], BF16)
        ones = sb("ones", [128, 128], BF16)
        ps = st.enter_context(nc.psum_tensor("ps", [128, 8, 512], F32))

        hT = Bm[:, 0:6144].rearrange("p (k t) -> p k t", k=KC)
        mix = Bm[:, 6144:12288].rearrange("p (k t) -> p k t", k=KC)
        sqb = Bm[:, 12288:18432].rearrange("p (k t) -> p k t", k=KC)
        h2T = Bm[:, :].rearrange("p (a k t) -> p a k t", a=3, k=KC)
        tstage = Bm[:, 0:4096].bitcast(F32).rearrange("p (a c) -> p a c", a=2)
        xin = [Cm[:, s * 2048:(s + 1) * 2048] for s in range(4)]
        stgc = Cm[:, 0:4096].rearrange("p (g c) -> p g c", g=4)
        stgp = Cm[:, 4096:6144].rearrange("p (g c) -> p g c", g=2)
        vv = Cm[:, 0:3072].rearrange("p (c t) -> p c t", c=8)
        vbf = Cm[:, 3072:4608].bitcast(BF16).rearrange("p (c t) -> p c t", c=8)
        sqv = Cm[:, 4608:6144].bitcast(BF16).rearrange("p (c t) -> p c t", c=8)
        diag = Cm[:, 6144:8128].bitcast(BF16).rearrange("p (k j) -> p k j", k=31)
        sig = [Cm[:, 8128 + i * TN:8128 + (i + 1) * TN] for i in range(2)]
        t64 = Cm[:, 8896:8960]
        t16 = Cm[:, 8960:8976]
        fT = Cm[:, :].bitcast(BF16).rearrange("p (a k t) -> p a k t", a=3, k=KC)
        gbc = Cm[:, 0:2048]
        yout = [Cm[:, 2048 + s * 2048:2048 + (s + 1) * 2048] for s in range(3)]
        Sflat = S[:, 0:3, :].rearrange("p a t -> p (a t)")

        def pcol(c):
            return prm[:, c:c + 1]

        def emit(P, W):
            B = {}

            def bf(n):
                if n not in B:
                    B[n] = Buf(n)
                return B[n]

            pb = [bf("pb%d" % i) for i in range(8)]
            for b_ in pb:
                b_.excl = True
            xTb = [bf("xT%d" % t) for t in range(3)]
            state = {"ring": 0, "alt": 0}

            def ring3():
                b = state["ring"] % 3
                state["ring"] += 1
                return b

            def alt_eng():
                state["alt"] += 1
                return "act" if state["alt"] % 2 else "dve"

            def copy_op(eng, out, in_, reads, writes):
                if eng == "act":
                    return P.op("act", lambda e: e.activation(out=out, in_=in_, func=AF.Copy), reads=reads, writes=writes)
                return P.op("dve", lambda e: e.tensor_copy(out=out, in_=in_), reads=reads, writes=writes)

            outs = []
            P.dma("sp", lambda e: e.dma_start(out=prm[:], in_=prm_d[:, :]), "c0", writes=[bf("prm")])
            P.dma("sp", lambda e: e.dma_start(out=idf[:], in_=ident_d[:, :]), "c1", writes=[bf("idf")])
            P.op("dve", lambda e: e.tensor_copy(out=idb[:], in_=idf[:]), reads=[bf("idf")], writes=[bf("idb")])
            P.op("dve", lambda e: e.memset(ones[:], 1.0), writes=[bf("ones")])
            for l in range(DEPTH if "nopt" not in _DBG else 0):
                outs.append(P.dma("sp", lambda e, l=l: e.dma_start(out=ocs_d[l, :, 0:26, :], in_=sconv_d[l, :, 4:30, :]), "pt"))
                outs.append(P.dma("sp", lambda e, l=l: e.dma_start(out=ops_d[l, :, 0:11, :], in_=spool_d[l, :, 4:15, :]), "pt"))

            xinb = [bf("xin%d" % i) for i in range(4)]
            bank = 0
            for i in range(T // 128):
                s = i % 4
                P.dma("sp", lambda e, i=i, s=s: e.dma_start(out=xin[s], in_=x_d[i * 128:(i + 1) * 128, :]),
                      "xi%d" % s, writes=[xinb[s]])
                for q in range(4):
                    bk = bank % 8
                    bank += 1
                    fns = [lambda e, s=s, q=q, j=j, bk=bk: e.transpose(
                        out=ps[:, bk, j * 128:(j + 1) * 128], in_=xin[s][:, (4 * q + j) * 128:(4 * q + j + 1) * 128], identity=idf[:])
                        for j in range(4)]
                    P.group(fns, reads=[xinb[s], bf("idf")], writes=[pb[bk]])
                    copy_op(alt_eng(), xT[:, 4 * q:4 * q + 4, i * 128:(i + 1) * 128],
                            ps[:, bk, :].rearrange("p (j t) -> p j t", t=128), [pb[bk]], [xTb[i // 3]])
            cbufs = list(xinb)

            def rms_stats(l_g_unused, src_cols, sq_view, sq_buf, bank_i, out_slot, out_buf, xbufs):
                for k in range(KC):
                    P.op("act", lambda e, k=k: e.activation(out=sq_view[:, k, :], in_=xT[:, k, src_cols[0]:src_cols[1]], func=AF.Square),
                         reads=xbufs, writes=[sq_buf])
                n = src_cols[1] - src_cols[0]
                fns = [lambda e, k=k: e.matmul(out=ps[:, bank_i, 0:n], lhsT=ones[:], rhs=sq_view[:, k, :], start=(k == 0), stop=(k == KC - 1))
                       for k in range(KC)]
                P.group(fns, reads=[sq_buf, bf("ones")], writes=[pb[bank_i]])
                rstd_finish(bank_i, n, out_slot, out_buf)

            def rstd_finish(bank_i, n, out_slot, out_buf):
                if out_slot is None:
                    out_slot, out_buf = ps[:, bank_i, 0:n], pb[bank_i]
                P.op("act", lambda e: e.activation(out=out_slot, in_=ps[:, bank_i, 0:n], func=AF.Sqrt, bias=EPS, scale=1.0 / D),
                     reads=[pb[bank_i]], writes=[out_buf])
                P.op("dve", lambda e: e.reciprocal(out=out_slot, in_=out_slot), reads=[out_buf], writes=[out_buf])

            def rms1_pre(l, t):
                c0 = t * TN
                rms_stats(None, (c0, c0 + TN), sqb, bf("sqb"), 7, None, None, [xTb[t]])
                for k in range(KC):
                    P.op("dve", lambda e, k=k: e.scalar_tensor_tensor(
                        out=hT[:, k, :], in0=xT[:, k, c0:c0 + TN], scalar=pcol(l * PL + O_G1 + k), in1=ps[:, 7, 0:TN],
                        op0=ALU.mult, op1=ALU.mult), reads=[xTb[t], pb[7], bf("prm")], writes=[bf("hT")])

            def mm_group(slot, wbuf, rhs_of_k, nk, bank_i, n, reads):
                fns = [lambda e, k=k: e.matmul(out=ps[:, bank_i, 0:n], lhsT=wt[:, slot, k, :], rhs=rhs_of_k(k),
                                               start=(k == 0), stop=(k == nk - 1)) for k in range(nk)]
                return P.group(fns, reads=[wbuf] + reads, writes=[pb[bank_i]])

            def stt(out, in0, scalar, in1, op0, op1, reads, writes):
                return P.op("dve", lambda e: e.scalar_tensor_tensor(out=out, in0=in0, scalar=scalar, in1=in1, op0=op0, op1=op1),
                            reads=reads, writes=writes)

            def ts1(out, in0, s1, op0, reads, writes):
                return P.op("dve", lambda e: e.tensor_scalar(out=out, in0=in0, scalar1=s1, scalar2=None, op0=op0),
                            reads=reads, writes=writes)

            def ts2(out, in0, s1, s2, op0, op1, reads, writes):
                return P.op("dve", lambda e: e.tensor_scalar(out=out, in0=in0, scalar1=s1, scalar2=s2, op0=op0, op1=op1),
                            reads=reads, writes=writes)

            def act(out, in_, func, reads, writes, bias=None, scale=None):
                kw = {}
                if bias is not None:
                    kw["bias"] = bias
                if scale is not None:
                    kw["scale"] = scale
                return P.op("act", lambda e: e.activation(out=out, in_=in_, func=func, **kw), reads=reads, writes=writes)

            def mixer(l, t):
                L0 = l * PL
                c0 = t * TN
                npr = TN if t < 2 else TN - NSM
                prmb = bf("prm")
                hTb, mixb = bf("hT"), bf("mix")
                vb, vbfb, sqvb = bf("v"), bf("vbf"), bf("sqv")
                sigb = [bf("sig0"), bf("sig1")]
                ubb = [bf("ub0"), bf("ub1"), bf("ub2")]

                dgb = [bf("diagA"), bf("diagB")]
                halves = [(0, 16), (16, 31)]

                def conv_build(c):
                    for hi, (k0, k1) in enumerate(halves):
                        nk = k1 - k0
                        P.op("dve", lambda e, k0=k0, k1=k1, nk=nk: e.tensor_tensor(
                            out=diag[:, k0:k1, :], in0=ps[:, 4, 0:128].unsqueeze(1).broadcast_to([128, nk, 128]),
                            in1=prm[:, L0 + O_CW + c * 31 + k0:L0 + O_CW + c * 31 + k1].unsqueeze(2).broadcast_to([128, nk, 128]),
                            op=ALU.mult), reads=[pb[4], prmb], writes=[dgb[hi]])

                def conv(c):
                    ui = c % 3
                    ub = ubuf[:, ui, :]
                    for hi, (k0, k1) in enumerate(halves):
                        fns = [lambda e, k=k: e.matmul(out=ps[:, 3, 0:npr], lhsT=diag[:, k, :], rhs=ub[:, 2 + k:2 + k + npr],
                                                       start=(k == 0), stop=(k == 30)) for k in range(k0, k1)]
                        P.group(fns, reads=[dgb[hi], ubb[ui]], writes=[pb[3]])
                    if t == 2:
                        fns = [lambda e, k=k: e.matmul(out=ps[:, 3, npr:TN], lhsT=diag[:, k, :], rhs=usamp[:, c, :, k:k + 4],
                                                       start=(k == 0), stop=(k == 30), skip_group_check=True) for k in range(31)]
                        P.group(fns, reads=[dgb[0], dgb[1], bf("usamp%d" % c)], writes=[pb[3]])
                    cb = pcol(L0 + O_CB + c)
                    act(vv[:, c, :], ps[:, 3, 0:TN], AF.Identity, [pb[3], prmb], [vb], bias=cb)
                    ts1(vbf[:, c, :], ps[:, 3, 0:TN], cb, ALU.add, [pb[3], prmb], [vbfb])
                    act(sqv[:, c, :], ps[:, 3, 0:TN], AF.Square, [pb[3], prmb], [sqvb], bias=cb)

                for c in range(8):
                    ui = c % 3
                    ub = ubuf[:, ui, :]
                    if c > 0:
                        conv_build(c - 1)
                    sa, wa = W.get(w_in_v[l][:, :, c * 128:(c + 1) * 128], KC)
                    ba = ring3()
                    mm_group(sa, wa, lambda k: hT[:, k, :], KC, ba, TN, [hTb])
                    sg, wg = W.get(w_in_v[l][:, :, 1024 + c * 128:1024 + (c + 1) * 128], KC)
                    bg = ring3()
                    mm_group(sg, wg, lambda k: hT[:, k, :], KC, bg, TN, [hTb])
                    si = c % 2
                    act(sig[si], ps[:, bg, 0:TN], AF.Sigmoid, [pb[bg], prmb], [sigb[si]], bias=pcol(L0 + O_BIN + 8 + c))
                    ba_col = pcol(L0 + O_BIN + c)
                    rdu = [pb[ba], sigb[si], prmb]
                    if t == 0:
                        P.op("dve", lambda e, ub=ub: e.memset(ub[:, 0:32], 0.0), writes=[ubb[ui]])
                    else:
                        P.op("dve", lambda e, ub=ub, c=c: e.tensor_copy(out=ub[:, 0:32], in_=hsave[:, c, :]),
                             reads=[bf("hsave%d" % c)], writes=[ubb[ui]])
                    if t == 0:
                        stt(t64, ps[:, ba, 0:HALO], ba_col, sig[si][:, 0:HALO], ALU.add, ALU.mult, rdu, [bf("t64")])
                        ts1(ub[:, 32:32 + HALO], t64, pcol(O_HM), ALU.mult, [bf("t64"), prmb], [ubb[ui]])
                        stt(ub[:, 32 + HALO:32 + TN], ps[:, ba, HALO:TN], ba_col, sig[si][:, HALO:TN], ALU.add, ALU.mult, rdu, [ubb[ui]])
                    else:
                        stt(ub[:, 32:32 + npr], ps[:, ba, 0:npr], ba_col, sig[si][:, 0:npr], ALU.add, ALU.mult, rdu, [ubb[ui]])
                    if t == 2:
                        stt(usamp[:, c, :, 30:34], ps[:, ba, npr:TN].rearrange("p (b j) -> p b j", j=4), ba_col,
                            sig[si][:, npr:TN].rearrange("p (b j) -> p b j", j=4), ALU.add, ALU.mult, rdu, [bf("usamp%d" % c)])
                        stt(utail[:, c, :], ps[:, ba, TN - 94:TN], ba_col, sig[si][:, TN - 94:TN], ALU.add, ALU.mult, rdu, [bf("utail")])
                    else:
                        P.op("dve", lambda e, ub=ub, c=c: e.tensor_copy(out=hsave[:, c, :], in_=ub[:, TN:TN + 32]),
                             reads=[ubb[ui]], writes=[bf("hsave%d" % c)])
                    if c > 0:
                        conv(c - 1)
                conv_build(7)
                conv(7)

                def pool_group(g):
                    w = 2 << g
                    db = g % 2
                    dB = bf("d%d" % db)
                    bz = []
                    for j in range(2):
                        m = 2 * g + j
                        sz, wz = W.get(w_in_v[l][:, :, 2048 + m * 128:2048 + (m + 1) * 128], KC)
                        b_ = ring3()
                        bz.append(b_)
                        mm_group(sz, wz, lambda k: hT[:, k, :], KC, b_, TN, [hTb])
                        bzc = pcol(L0 + O_BIN + 16 + m)
                        zfB, zbB = bf("zf%d" % j), bf("zb%d" % j)
                        act(zf[:, j, :], ps[:, b_, 0:TN], AF.Identity, [pb[b_], prmb], [zfB], bias=bzc)
                        if t == 0:
                            P.op("dve", lambda e, j=j: e.memset(zb[:, j, 0:16], 0.0), writes=[zbB])
                            ts2(zb[:, j, 16:16 + HALO], ps[:, b_, 0:HALO], bzc, pcol(O_HM), ALU.add, ALU.mult, [pb[b_], prmb], [zbB])
                            ts1(zb[:, j, 16 + HALO:16 + TN], ps[:, b_, HALO:TN], bzc, ALU.add, [pb[b_], prmb], [zbB])
                        else:
                            P.op("dve", lambda e, j=j, m=m: e.tensor_copy(out=zb[:, j, 0:16], in_=zsave[:, m, :]),
                                 reads=[bf("zsave%d" % m)], writes=[zbB])
                            ts1(zb[:, j, 16:16 + npr], ps[:, b_, 0:npr], bzc, ALU.add, [pb[b_], prmb], [zbB])
                        if t == 2:
                            ts1(zsamp[:, m, :, 15:19], ps[:, b_, npr:TN].rearrange("p (b j) -> p b j", j=4), bzc, ALU.add,
                                [pb[b_], prmb], [bf("zsamp%d" % m)])
                            ts1(ztail[:, m, :], ps[:, b_, TN - 94:TN], bzc, ALU.add, [pb[b_], prmb], [bf("ztail")])
                        else:
                            P.op("dve", lambda e, j=j, m=m: e.tensor_copy(out=zsave[:, m, :], in_=zb[:, j, TN:TN + 16]),
                                 reads=[zbB], writes=[bf("zsave%d" % m)])
                    for j in range(2):
                        m = 2 * g + j
                        zfB, zbB = bf("zf%d" % j), bf("zb%d" % j)
                        bp = ring3()
                        fns = [lambda e, k=k, j=j, bp=bp, w=w: e.matmul(out=ps[:, bp, 0:npr], lhsT=idb[:], rhs=zb[:, j, 16 - k:16 - k + npr],
                                                                   start=(k == 0), stop=(k == w - 1)) for k in range(w)]
                        rd = [bf("idb"), zbB]
                        if t == 2:
                            fns += [lambda e, k=k, m=m, bp=bp, w=w: e.matmul(out=ps[:, bp, npr:TN], lhsT=idb[:], rhs=zsamp[:, m, :, 15 - k:19 - k],
                                                                        start=(k == 0), stop=(k == w - 1), skip_group_check=True) for k in range(w)]
                            rd.append(bf("zsamp%d" % m))
                        P.group(fns, reads=rd, writes=[pb[bp]])
                        rdd = [pb[bp], zfB]
                        if t == 0:
                            stt(dbuf[:, db, j, 0:HALO], ps[:, bp, 0:HALO], 1.0 / w, zf[:, j, 0:HALO], ALU.mult, ALU.subtract, rdd, [dB])
                            P.op("dve", lambda e, bp=bp, g=g: e.tensor_tensor(out=t16, in0=ps[:, bp, HALO:HALO + 16],
                                                                              in1=prm[:, O_IC + g * 16:O_IC + (g + 1) * 16], op=ALU.mult),
                                 reads=[pb[bp], prmb], writes=[bf("t16")])
                            P.op("dve", lambda e, j=j, db=db: e.tensor_tensor(out=dbuf[:, db, j, HALO:HALO + 16], in0=t16,
                                                                              in1=zf[:, j, HALO:HALO + 16], op=ALU.subtract),
                                 reads=[bf("t16"), zfB], writes=[dB])
                            stt(dbuf[:, db, j, HALO + 16:TN], ps[:, bp, HALO + 16:TN], 1.0 / w, zf[:, j, HALO + 16:TN],
                                ALU.mult, ALU.subtract, rdd, [dB])
                        else:
                            stt(dbuf[:, db, j, :], ps[:, bp, 0:TN], 1.0 / w, zf[:, j, :], ALU.mult, ALU.subtract, rdd, [dB])
                    for e_ in range(2):
                        m = 2 * g + e_
                        sp_, wp = W.get(pool_w_v[l][g][:, :, e_ * 128:(e_ + 1) * 128], 2)
                        bq = ring3()
                        mm_group(sp_, wp, lambda k, db=db: dbuf[:, db, k, :], 2, bq, TN, [dB])
                        act(mix[:, 8 + m, :], ps[:, bq, 0:TN], AF.Copy, [pb[bq], prmb], [mixb], scale=pcol(L0 + O_PSC + m))

                nxt = t + 1 if t < 2 else None
                if nxt is not None:
                    n0 = nxt * TN
                    for k in range(KC):
                        P.op("act", lambda e, k=k: e.activation(out=sqb[:, k, :], in_=xT[:, k, n0:n0 + TN], func=AF.Square),
                             reads=[xTb[nxt]], writes=[bf("sqb")])

                fns = [lambda e, c=c: e.matmul(out=ps[:, 5, 0:TN], lhsT=ones[:], rhs=vbf[:, c, :], start=(c == 0), stop=(c == 7)) for c in range(8)]
                P.group(fns, reads=[vbfb, bf("ones")], writes=[pb[5]])
                fns = [lambda e, c=c: e.matmul(out=ps[:, 6, 0:TN], lhsT=ones[:], rhs=sqv[:, c, :], start=(c == 0), stop=(c == 7)) for c in range(8)]
                P.group(fns, reads=[sqvb, bf("ones")], writes=[pb[6]])
                if nxt is not None:
                    fns = [lambda e, k=k: e.matmul(out=ps[:, 7, 0:TN], lhsT=ones[:], rhs=sqb[:, k, :], start=(k == 0), stop=(k == KC - 1))
                           for k in range(KC)]
                    P.group(fns, reads=[bf("sqb"), bf("ones")], writes=[pb[7]])
                S1b = bf("S1")
                act(ps[:, 5, 0:TN], ps[:, 5, 0:TN], AF.Copy, [pb[5]], [pb[5]], scale=1.0 / 1024)
                act(S[:, 1, :], ps[:, 5, 0:TN], AF.Square, [pb[5]], [S1b])
                stt(ps[:, 6, 0:TN], ps[:, 6, 0:TN], 1.0 / 1024, S[:, 1, :], ALU.mult, ALU.subtract, [pb[6], S1b], [pb[6]])
                act(ps[:, 6, 0:TN], ps[:, 6, 0:TN], AF.Sqrt, [pb[6]], [pb[6]], bias=EPS)
                P.op("dve", lambda e: e.reciprocal(out=ps[:, 6, 0:TN], in_=ps[:, 6, 0:TN]), reads=[pb[6]], writes=[pb[6]])
                if nxt is not None:
                    rstd_finish(7, TN, None, None)
                def ln_apply(c):
                    sl = 2 + c % 2
                    Sb = bf("S%d" % sl)
                    P.op("dve", lambda e, c=c, sl=sl: e.tensor_tensor(out=S[:, sl, :], in0=vv[:, c, :], in1=ps[:, 5, 0:TN], op=ALU.subtract),
                         reads=[vb, pb[5]], writes=[Sb])
                    P.op("dve", lambda e, sl=sl: e.tensor_tensor(out=S[:, sl, :], in0=S[:, sl, :], in1=ps[:, 6, 0:TN], op=ALU.mult),
                         reads=[Sb, pb[6]], writes=[Sb])
                    act(mix[:, c, :], S[:, sl, :], AF.Silu, [Sb, prmb], [mixb], bias=pcol(L0 + O_LNB + c), scale=pcol(L0 + O_LNG + c))

                for g in range(4):
                    pool_group(g)
                    ln_apply(2 * g)
                    ln_apply(2 * g + 1)
                for m in range(KC):
                    so, wo = W.get(w_out_v[l][:, :, m * 128:(m + 1) * 128], KC)
                    bo = ring3()
                    mm_group(so, wo, lambda k: mix[:, k, :], KC, bo, TN, [mixb])
                    P.op("dve", lambda e, m=m, bo=bo: e.tensor_tensor(out=xT[:, m, c0:c0 + TN], in0=xT[:, m, c0:c0 + TN], in1=ps[:, bo, 0:TN], op=ALU.add),
                         reads=[pb[bo], xTb[t]], writes=[xTb[t]])
                    if nxt is not None:
                        n0 = nxt * TN
                        stt(hT[:, m, :], xT[:, m, n0:n0 + TN], pcol(L0 + O_G1 + m), ps[:, 7, 0:TN], ALU.mult, ALU.mult,
                            [xTb[nxt], pb[7], prmb], [hTb])
                if t == 2 and "notails" not in _DBG:
                    for a, (tl, tlb) in enumerate([(utail, bf("utail")), (ztail, bf("ztail"))]):
                        for q in range(2):
                            bk = 5 + (2 * a + q) % 3
                            fns = [lambda e, tl=tl, q=q, i=i, bk=bk: e.transpose(out=ps[0:94, bk, i * 128:(i + 1) * 128], in_=tl[:, 4 * q + i, :], identity=idf[:])
                                   for i in range(4)]
                            P.group(fns, reads=[tlb, bf("idf")], writes=[pb[bk]])
                            copy_op(alt_eng(), tstage[0:94, a, q * 512:(q + 1) * 512], ps[0:94, bk, :], [pb[bk]], [hTb])
                    outs.append(P.dma("sp", lambda e: e.dma_start(out=ocp_d[l, :, :], in_=tstage[0:30, 0, :]), "to", reads=[hTb]))
                    outs.append(P.dma("sp", lambda e: e.dma_start(out=opp_d[l, :, :], in_=tstage[15:30, 1, :]), "to", reads=[hTb]))
                    for b in range(SBQ):
                        outs.append(P.dma("sp", lambda e, b=b: e.dma_start(out=ocs_d[l, b, 26:30, :], in_=tstage[30 + 4 * b:34 + 4 * b, 0, :]), "to", reads=[hTb]))
                        outs.append(P.dma("sp", lambda e, b=b: e.dma_start(out=ops_d[l, b, 11:15, :], in_=tstage[30 + 4 * b:34 + 4 * b, 1, :]), "to", reads=[hTb]))

            def ffn(l):
                L0 = l * PL
                prmb = bf("prm")
                h2b, sqCb, fCb = bf("h2T"), bf("sqC"), bf("fC")
                Sb = [bf("S%d" % i) for i in range(6)]
                for t in range(3):
                    c0 = t * TN
                    bk = 6 + t % 2
                    for k in range(KC):
                        P.op("act", lambda e, k=k, t=t, c0=c0: e.activation(out=fT[:, t, k, :], in_=xT[:, k, c0:c0 + TN], func=AF.Square),
                             reads=[xTb[t]], writes=[sqCb])
                    fns = [lambda e, k=k, t=t, bk=bk: e.matmul(out=ps[:, bk, 0:TN], lhsT=ones[:], rhs=fT[:, t, k, :], start=(k == 0), stop=(k == KC - 1))
                           for k in range(KC)]
                    P.group(fns, reads=[sqCb, bf("ones")], writes=[pb[bk]])
                    rstd_finish(bk, TN, None, None)
                    for k in range(KC):
                        stt(h2T[:, t, k, :], xT[:, k, c0:c0 + TN], pcol(L0 + O_G2 + k), ps[:, bk, 0:TN], ALU.mult, ALU.mult,
                            [xTb[t], pb[bk], prmb], [h2b])
                handoff([sqCb], [fCb])
                slot3 = 0
                sc_i = 0
                tr = [((HALO if l == DEPTH - 1 else HALO - 32) if t == 0 else 0, TN) for t in range(3)]
                for j in range(DFF // FB):
                    for m in range(KC):
                        s1, w1 = W.get(w_ff1_v[l][:, :, j * FB + m * 128:j * FB + (m + 1) * 128], KC)
                        b0 = 3 * (slot3 % 2)
                        slot3 += 1
                        fns = [lambda e, k=k, t=t, s1=s1, b0=b0: e.matmul(out=ps[:, b0 + t, 0:tr[t][1] - tr[t][0]], lhsT=wt[:, s1, k, :],
                                                                        rhs=h2T[:, t, k, tr[t][0]:tr[t][1]],
                                                                        start=(k == 0), stop=(k == KC - 1)) for k in range(KC) for t in range(3)]
                        P.group(fns, reads=[w1, h2b], writes=[pb[b0], pb[b0 + 1], pb[b0 + 2]])
                        for t in range(3):
                            o0, o1 = tr[t]
                            si = 3 + sc_i % 3
                            sc_i += 1
                            act(S[:, si, 0:o1 - o0], ps[:, b0 + t, 0:o1 - o0], AF.Square, [pb[b0 + t]], [Sb[si]])
                            stt(fT[:, t, m, o0:o1], ps[:, b0 + t, 0:o1 - o0], 0.0, S[:, si, 0:o1 - o0], ALU.is_gt, ALU.mult, [pb[b0 + t], Sb[si]], [fCb])
                    for m in range(KC):
                        s2, w2 = W.get(w_ff2_v[l][j][:, :, m * 128:(m + 1) * 128], KC)
                        b0 = 3 * (slot3 % 2)
                        slot3 += 1
                        fns = [lambda e, k=k, t=t, s2=s2, b0=b0: e.matmul(out=ps[:, b0 + t, 0:tr[t][1] - tr[t][0]], lhsT=wt[:, s2, k, :],
                                                                        rhs=fT[:, t, k, tr[t][0]:tr[t][1]],
                                                                        start=(k == 0), stop=(k == KC - 1)) for k in range(KC) for t in range(3)]
                        P.group(fns, reads=[w2, fCb], writes=[pb[b0], pb[b0 + 1], pb[b0 + 2]])
                        for t in range(3):
                            o0, o1 = tr[t]
                            c0 = t * TN + o0
                            nn = o1 - o0
                            P.op("dve", lambda e, m=m, t=t, b0=b0, c0=c0, nn=nn: e.tensor_tensor(out=xT[:, m, c0:c0 + nn], in0=xT[:, m, c0:c0 + nn],
                                                                                             in1=ps[:, b0 + t, 0:nn], op=ALU.add),
                                 reads=[pb[b0 + t], xTb[t]], writes=[xTb[t]])
                return [h2b, fCb]

            for l in range(DEPTH if "nolayers" not in _DBG else 0):
                if l > 0:
                    handoff([bf("h2T")], [bf("hT"), bf("mix"), bf("sqb")])
                P.group([lambda e: e.transpose(out=ps[:, 4, 0:128], in_=idf[:], identity=idf[:])], reads=[bf("idf")], writes=[pb[4]])
                rms1_pre(l, 0)
                stb = bf("stage")
                handoff(cbufs, [stb])
                for gq in range(4):
                    P.dma("sp", lambda e, l=l, gq=gq: e.dma_start(out=stgc[0:120, gq, :],
                                                                  in_=sconv_d[l, 4 * gq:4 * gq + 4, :, :].rearrange("b j c -> (b j) c")),
                          "sg", writes=[stb])
                for gp in range(2):
                    P.dma("sp", lambda e, l=l, gp=gp: e.dma_start(out=stgp[0:120, gp, :],
                                                                  in_=spool_d[l, 8 * gp:8 * gp + 8, :, :].rearrange("b j c -> (b j) c")),
                          "sg", writes=[stb])
                hbanks = [0, 1, 2, 3, 5, 6]
                bank = 0
                for c in range(8):
                    bk = hbanks[bank % 6]
                    bank += 1
                    fns = [lambda e, c=c, gq=gq, bk=bk: e.transpose(out=ps[:, bk, gq * 120:(gq + 1) * 120], in_=stgc[0:120, gq, c * 128:(c + 1) * 128],
                                                                   identity=idf[0:120, 0:120]) for gq in range(4)]
                    P.group(fns, reads=[stb, bf("idf")], writes=[pb[bk]])
                    copy_op(alt_eng(), usamp[:, c, :, 0:30], ps[:, bk, 0:480].rearrange("p (b j) -> p b j", j=30), [pb[bk]], [bf("usamp%d" % c)])
                    bk = hbanks[bank % 6]
                    bank += 1
                    fns = [lambda e, c=c, gp=gp, bk=bk: e.transpose(out=ps[:, bk, gp * 120:(gp + 1) * 120], in_=stgp[0:120, gp, c * 128:(c + 1) * 128],
                                                                   identity=idf[0:120, 0:120]) for gp in range(2)]
                    P.group(fns, reads=[stb, bf("idf")], writes=[pb[bk]])
                    copy_op(alt_eng(), zsamp[:, c, :, 0:15], ps[:, bk, 0:240].rearrange("p (b j) -> p b j", j=15), [pb[bk]], [bf("zsamp%d" % c)])
                mixbufs = [bf(n) for n in ("v", "vbf", "sqv", "diagA", "diagB", "sig0", "sig1", "t64", "t16")]
                handoff([stb], mixbufs)
                for t in range((2 if "t01" in _DBG else 3) if "nomixer" not in _DBG else 0):
                    mixer(l, t)
                handoff([bf("hT"), bf("mix"), bf("sqb")], [bf("h2T")])
                handoff(mixbufs, [bf("sqC")])
                cbufs = ffn(l)[1:] if "noffn" not in _DBG else [bf("sqC")]

            gB, yob = bf("gbc"), [bf("yout%d" % i) for i in range(3)]
            sq2b = [bf("sqh0"), bf("sqh1")]
            handoff(cbufs, [gB] + yob)
            if "nolayers" not in _DBG:
                handoff([bf("h2T")], sq2b)
            P.dma("sp", lambda e: e.dma_start(out=gbc, in_=gfb_d[:, :]), "c0", writes=[gB])
            rtb = bf("S5")
            bank = 0
            for i in range(9):
                n = 128 if i < 8 else 64
                col0 = HALO + 128 * i
                s2, s3 = i % 2, i % 3
                sbk = 6 + i % 2
                xb_ = sorted({col0 // TN, (col0 + n - 1) // TN})
                xbs = [xTb[j] for j in xb_]
                for k in range(KC):
                    P.op("act", lambda e, k=k, s2=s2, n=n, col0=col0: e.activation(out=sqb[:, k, s2 * 128:s2 * 128 + n], in_=xT[:, k, col0:col0 + n], func=AF.Square),
                         reads=xbs, writes=[sq2b[s2]])
                fns = [lambda e, k=k, s2=s2, n=n, sbk=sbk, i=i: e.matmul(out=ps[0:n, sbk, i:i + 1], lhsT=sqb[:, k, s2 * 128:s2 * 128 + n], rhs=ones[:, 0:1],
                                                                       start=(k == 0), stop=(k == KC - 1), skip_group_check=True) for k in range(KC)]
                P.group(fns, reads=[sq2b[s2], bf("ones")], writes=[pb[sbk]])
                P.op("act", lambda e, n=n, sbk=sbk, i=i: e.activation(out=S[0:n, 5, i:i + 1], in_=ps[0:n, sbk, i:i + 1], func=AF.Sqrt, bias=EPS, scale=1.0 / D),
                     reads=[pb[sbk]], writes=[rtb])
                P.op("dve", lambda e, n=n, i=i: e.reciprocal(out=S[0:n, 5, i:i + 1], in_=S[0:n, 5, i:i + 1]), reads=[rtb], writes=[rtb])
                for q in range(4):
                    bk = bank % 6
                    bank += 1
                    fns = [lambda e, q=q, j=j, bk=bk, n=n, col0=col0: e.transpose(out=ps[0:n, bk, j * 128:(j + 1) * 128], in_=xT[:, 4 * q + j, col0:col0 + n], identity=idf[:])
                           for j in range(4)]
                    P.group(fns, reads=xbs + [bf("idf")], writes=[pb[bk]])
                    stt(yout[s3][0:n, q * 512:(q + 1) * 512], ps[0:n, bk, :], S[0:n, 5, i:i + 1], gbc[0:n, q * 512:(q + 1) * 512], ALU.mult, ALU.mult,
                        [pb[bk], rtb, gB], [yob[s3]])
                outs.append(P.dma("sp", lambda e, i=i, n=n, s3=s3: e.dma_start(out=y_d[i * 128:i * 128 + n, :], in_=yout[s3][0:n, :]),
                                  "yo%d" % s3, reads=[yob[s3]]))
            last = {}
            for h in outs:
                if last.get(h[0], 0) < h[1]:
                    last[h[0]] = h[1]
            P.wait("sp", list(last.items()))

        Wd = WStream(Prog(), wt, plan=None)
        emit(Wd.P, Wd)
        P = Prog()
        W = WStream(P, wt, plan=Wd.req)
        emit(P, W)
        assert W.i == len(Wd.req)

        sems = {n: st.enter_context(nc.semaphore(n)) for n in P.sem_names()}
        block = st.enter_context(nc.Block())

        @block.tensor
        def _(e):
            P.replay("pe", e, sems)

        @block.scalar
        def _(e):
            P.replay("act", e, sems)

        @block.vector
        def _(e):
            P.replay("dve", e, sems)

        @block.gpsimd
        def _(e):
            P.replay("pool", e, sems)

        @block.sync
        def _(e):
            P.replay("sp", e, sems)
    return nc


def _pack_params(norm1_g, b_in, conv_w, conv_b, ln_g, ln_b, pool_scale, norm2_g, norm_f, half):
    prm = np.zeros((128, NPRM), np.float32)

    def cols(v):
        return np.ascontiguousarray(np.asarray(v, np.float32).reshape(-1, 128).T)

    for l in range(DEPTH):
        L0 = l * PL
        prm[:, L0 + O_G1:L0 + O_G1 + 16] = cols(norm1_g[l])
        prm[:, L0 + O_BIN:L0 + O_BIN + 24] = cols(b_in[l])
        cw = np.asarray(conv_w[l], np.float32)
        prm[:, L0 + O_CW:L0 + O_CW + 248] = cw.reshape(31, 8, 128).transpose(2, 1, 0).reshape(128, 248)
        prm[:, L0 + O_CB:L0 + O_CB + 8] = cols(conv_b[l])
        prm[:, L0 + O_LNG:L0 + O_LNG + 8] = cols(ln_g[l])
        prm[:, L0 + O_LNB:L0 + O_LNB + 8] = cols(ln_b[l])
        prm[:, L0 + O_PSC:L0 + O_PSC + 8] = cols(pool_scale[l])
        prm[:, L0 + O_G2:L0 + O_G2 + 16] = cols(norm2_g[l])
    prm[:, O_GF:O_GF + 16] = cols(norm_f)
    prm[:, O_HM] = float(half)
    pos = np.arange(16)
    for g in range(4):
        w = 2 << g
        cnt = np.minimum(w, pos + 1) if half == 0 else np.full(16, w)
        prm[:, O_IC + g * 16:O_IC + (g + 1) * 16] = (1.0 / cnt.astype(np.float32))[None, :]
    return prm


_NC_CACHE = {}


def kernel(x_prompt, x_sample, state_conv, state_pool, norm1_g, w_in, b_in, conv_w, conv_b,
           ln_g, ln_b, pool_w, pool_scale, w_out, norm2_g, w_ff1, w_ff2, norm_f):
    f = lambda a: np.ascontiguousarray(np.asarray(a, dtype=np.float32))
    x_prompt, x_sample, state_conv, state_pool = f(x_prompt), f(x_sample), f(state_conv), f(state_pool)
    w_in, pool_w, w_out, w_ff1, w_ff2 = f(w_in), f(pool_w), f(w_out), f(w_ff1), f(w_ff2)
    if "nc" not in _NC_CACHE:
        _NC_CACHE["nc"] = build_program()
    nc = _NC_CACHE["nc"]
    ident = np.eye(128, dtype=np.float32)
    gfb = np.ascontiguousarray(np.broadcast_to(np.asarray(norm_f, np.float32)[None, :], (128, D)))
    prms = [_pack_params(norm1_g, b_in, conv_w, conv_b, ln_g, ln_b, pool_scale, norm2_g, norm_f, h) for h in range(2)]
    in_maps = []
    for i in range(NCORES):
        b, h = i // 2, i % 2
        xc = np.zeros((T, D), np.float32)
        if h == 1:
            xc[0:HALO] = x_prompt[b, NPR - HALO:NPR]
        xc[HALO:HALO + NPR] = x_prompt[b, h * NPR:(h + 1) * NPR]
        xc[HALO + NPR:] = x_sample[SBQ * i:SBQ * (i + 1)].reshape(NSM, D)
        in_maps.append({
            "x": xc,
            "sconv": np.ascontiguousarray(state_conv[:, SBQ * i:SBQ * (i + 1)]),
            "spool": np.ascontiguousarray(state_pool[:, SBQ * i:SBQ * (i + 1)]),
            "prm": prms[h], "ident": ident, "gfb": gfb,
            "w_in": w_in, "pool_w": pool_w, "w_out": w_out, "w_ff1": w_ff1, "w_ff2": w_ff2,
        })
    ncr = int(os.environ.get("KCORES", NCORES))
    res = run_bass_kernel_spmd(nc, in_maps[:ncr], core_ids=list(range(ncr)))
    R = list(res.results) + [res.results[0]] * (NCORES - ncr)
    B_, S_ = x_prompt.shape[0], x_prompt.shape[1]
    y_prompt = np.empty((B_, S_, D), np.float32)
    y_sample = np.empty(x_sample.shape, np.float32)
    ncp = np.empty((DEPTH, B_, 30, 1024), np.float32)
    npp = np.empty((DEPTH, B_, 15, 1024), np.float32)
    ncs = np.empty(state_conv.shape, np.float32)
    nps = np.empty(state_pool.shape, np.float32)
    for i in range(NCORES):
        b, h = i // 2, i % 2
        y = R[i]["y"]
        y_prompt[b, h * NPR:(h + 1) * NPR] = y[0:NPR]
        y_sample[SBQ * i:SBQ * (i + 1)] = y[NPR:].reshape(SBQ, 4, D)
        ncs[:, SBQ * i:SBQ * (i + 1)] = R[i]["ocs"]
        nps[:, SBQ * i:SBQ * (i + 1)] = R[i]["ops"]
        if h == 1:
            ncp[:, b] = R[i]["ocp"]
            npp[:, b] = R[i]["opp"]
    return (y_prompt, y_sample, ncp, npp, ncs, nps)
```

```python
import os
import numpy as np
from contextlib import ExitStack
import concourse.bass as bass
import concourse.mybir as mybir
from concourse.bass_utils import run_bass_kernel_spmd

F32 = mybir.dt.float32
BF16 = mybir.dt.bfloat16
AF = mybir.ActivationFunctionType
ALU = mybir.AluOpType

NCORES = 8
D = 2048
KC = 16
DIN = 3072
DFF = 8192
DEPTH = 2
T = 1152
TN = 384
HALO = 64
NPR = 1024
NSM = 64
SBQ = 16
NYR = NPR + NSM
EPS = 1e-6
NSLOT = 5
FB = 2048

PL = 336
O_G1, O_BIN, O_CW, O_CB, O_LNG, O_LNB, O_PSC, O_G2 = 0, 16, 40, 288, 296, 304, 312, 320
O_GF = DEPTH * PL
O_HM = O_GF + 16
O_IC = O_HM + 1
NPRM = O_IC + 64

SAME_ENG_SYNC = True
_DBG = set(os.environ.get("KDBG", "").split(",")) - {""}


class Buf:
    __slots__ = ("name", "w", "r", "pr", "excl")

    def __init__(self, name, excl=False):
        self.name = name
        self.w = {}
        self.r = {}
        self.pr = {}
        self.excl = excl


def _split(reads, writes):
    ex = [b for b in reads if b.excl]
    if not ex:
        return reads, writes
    return [b for b in reads if not b.excl], list(writes) + ex


def _merge(dst, src):
    for k, v in src.items():
        if dst.get(k, 0) < v:
            dst[k] = v


def handoff(src_bufs, dst_bufs):
    u = {}
    for b in src_bufs:
        _merge(u, b.w)
        _merge(u, b.r)
        _merge(u, b.pr)
    for b in dst_bufs:
        b.w = dict(u)
        b.r = {}
        b.pr = dict(u)


class Prog:
    ENG = ("pe", "act", "dve", "pool", "sp")

    def __init__(self):
        self.ops = {e: [] for e in self.ENG}
        self.cnt = {e: 0 for e in self.ENG}
        self.dcnt = {}

    def _deps(self, reads, writes, deps):
        d = {}
        for b in reads:
            _merge(d, b.w)
        for b in writes:
            _merge(d, b.r)
            _merge(d, b.pr)
            _merge(d, b.w)
        for h in deps:
            if h is not None:
                _merge(d, {h[0]: h[1]})
        return d

    def _reg(self, h, reads, writes):
        k, v = h
        for b in reads:
            if b.r.get(k, 0) < v:
                b.r[k] = v
        for b in writes:
            if b.r:
                b.pr = b.r
                b.r = {}
                b.w = {k: v}
            else:
                if b.w.get(k, 0) < v:
                    b.w[k] = v

    def op(self, eng, fn, reads=(), writes=(), deps=()):
        reads, writes = _split(reads, writes)
        d = self._deps(reads, writes, deps)
        self.cnt[eng] += 1
        h = (eng, self.cnt[eng])
        self.ops[eng].append((fn, d, (eng, 1)))
        self._reg(h, reads, writes)
        return h

    def group(self, fns, reads=(), writes=(), deps=()):
        reads, writes = _split(reads, writes)
        d = self._deps(reads, writes, deps)
        self.cnt["pe"] += 1
        h = ("pe", self.cnt["pe"])
        n = len(fns)
        for i, fn in enumerate(fns):
            self.ops["pe"].append((fn, d if i == 0 else {}, ("pe", 1) if i == n - 1 else None))
        self._reg(h, reads, writes)
        return h

    def dma(self, queue, fn, sem, reads=(), writes=(), deps=()):
        reads, writes = _split(reads, writes)
        d = self._deps(reads, writes, deps)
        self.dcnt[sem] = self.dcnt.get(sem, 0) + 16
        h = (sem, self.dcnt[sem])
        self.ops[queue].append((fn, d, (sem, 16)))
        self._reg(h, reads, writes)
        return h

    def wait(self, eng, deps):
        d = {}
        for h in deps:
            _merge(d, {h[0]: h[1]})
        self.ops[eng].append((None, d, None))

    def sem_names(self):
        return list(self.ENG) + sorted(self.dcnt.keys())

    def replay(self, eng, e, sems):
        waited = {}
        for fn, d, inc in self.ops[eng]:
            for k, v in d.items():
                if k == eng and (eng == "pe" or not SAME_ENG_SYNC):
                    continue
                if waited.get(k, 0) < v:
                    e.wait_ge(sems[k], v)
                    waited[k] = v
            if fn is None:
                continue
            inst = fn(e)
            if inc is not None:
                inst.then_inc(sems[inc[0]], inc[1])


class WStream:
    def __init__(self, P, wt, plan=None):
        self.P = P
        self.wt = wt
        self.plan = plan
        self.req = []
        self.i = 0
        self.issued = 0
        self.bufs = [Buf("w%d" % s) for s in range(NSLOT)]

    def _issue(self):
        i = self.issued
        s = i % NSLOT
        src, kc = self.plan[i]
        wt = self.wt
        self.P.dma("pool", lambda e, s=s, src=src, kc=kc: e.dma_start(out=wt[:, s, 0:kc, :], in_=src),
                   "w%d" % s, writes=[self.bufs[s]])
        self.issued += 1

    def get(self, src, kc):
        i = self.i
        self.i += 1
        if self.plan is None:
            self.req.append((src, kc))
            return i % NSLOT, self.bufs[i % NSLOT]
        while self.issued < min(i + NSLOT, len(self.plan)):
            self._issue()
        return i % NSLOT, self.bufs[i % NSLOT]


def build_program():
    nc = bass.Bass("TRN2", target_bir_lowering=False)
    dt_in = lambda n, s: nc.dram_tensor(n, s, F32, kind="ExternalInput").ap()
    dt_out = lambda n, s: nc.dram_tensor(n, s, F32, kind="ExternalOutput").ap()
    x_d = dt_in("x", [T, D])
    sconv_d = dt_in("sconv", [DEPTH, SBQ, 30, 1024])
    spool_d = dt_in("spool", [DEPTH, SBQ, 15, 1024])
    prm_d = dt_in("prm", [128, NPRM])
    ident_d = dt_in("ident", [128, 128])
    gfb_d = dt_in("gfb", [128, D])
    w_in_d = dt_in("w_in", [DEPTH, D, DIN])
    pool_w_d = dt_in("pool_w", [DEPTH, 4, 256, 256])
    w_out_d = dt_in("w_out", [DEPTH, D, D])
    w_ff1_d = dt_in("w_ff1", [DEPTH, D, DFF])
    w_ff2_d = dt_in("w_ff2", [DEPTH, DFF, D])
    y_d = dt_out("y", [NYR, D])
    ocs_d = dt_out("ocs", [DEPTH, SBQ, 30, 1024])
    ops_d = dt_out("ops", [DEPTH, SBQ, 15, 1024])
    ocp_d = dt_out("ocp", [DEPTH, 30, 1024])
    opp_d = dt_out("opp", [DEPTH, 15, 1024])

    w_in_v = [w_in_d[l].rearrange("(kc p) m -> p kc m", p=128) for l in range(DEPTH)]
    w_out_v = [w_out_d[l].rearrange("(kc p) m -> p kc m", p=128) for l in range(DEPTH)]
    w_ff1_v = [w_ff1_d[l].rearrange("(kc p) m -> p kc m", p=128) for l in range(DEPTH)]
    w_ff2_v = [w_ff2_d[l].rearrange("(jj kc p) m -> jj p kc m", p=128, kc=KC) for l in range(DEPTH)]
    pool_w_v = [[pool_w_d[l, g].rearrange("(kk p) m -> p kk m", p=128) for g in range(4)] for l in range(DEPTH)]

    with ExitStack() as st:
        sb = lambda n, s, d: st.enter_context(nc.sbuf_tensor(n, s, d))
        xT = sb("xT", [128, KC, T], F32)
        Bm = sb("Bm", [128, 18432], BF16)
        Cm = sb("Cm", [128, 9216], F32)
        S = sb("S", [128, 6, TN], F32)
        wt = sb("wt", [128, NSLOT, KC, 128], BF16)
        ubuf = sb("ubuf", [128, 3, 32 + TN], BF16)
        hsave = sb("hsave", [128, 8, 32], BF16)
        usamp = sb("usamp", [128, 8, SBQ, 34], BF16)
        zf = sb("zf", [128, 2, TN], F32)
        zb = sb("zb", [128, 2, 16 + TN], BF16)
        zsave = sb("zsave", [128, 8, 16], BF16)
        zsamp = sb("zsamp", [128, 8, SBQ, 19], BF16)
        dbuf = sb("dbuf", [128, 2, 2, TN], BF16)
        utail = sb("utail", [128, 8, 94], F32)
        ztail = sb("ztail", [128, 8, 94], F32)
        prm = sb("prm_t", [128, NPRM], F32)
        idf = sb("idf", [128, 128], F32)
        idb = sb("idb", [128, 128], BF16)
        ones = sb("ones", [128, 128], BF16)
        ps = st.enter_context(nc.psum_tensor("ps", [128, 8, 512], F32))

        hT = Bm[:, 0:6144].rearrange("p (k t) -> p k t", k=KC)
        mix = Bm[:, 6144:12288].rearrange("p (k t) -> p k t", k=KC)
        sqb = Bm[:, 12288:18432].rearrange("p (k t) -> p k t", k=KC)
        h2T = Bm[:, :].rearrange("p (a k t) -> p a k t", a=3, k=KC)
        tstage = S[:, :, :].rearrange("p a t -> p (a t)")[:, 0:2048].rearrange("p (a c) -> p a c", a=2)
        xin = [Cm[:, s * 2048:(s + 1) * 2048] for s in range(4)]
        stgc = Cm[:, 0:4096].rearrange("p (g c) -> p g c", g=4)
        stgp = Cm[:, 4096:6144].rearrange("p (g c) -> p g c", g=2)
        vv = Cm[:, 0:3072].rearrange("p (c t) -> p c t", c=8)
        vbf = Cm[:, 3072:4608].bitcast(BF16).rearrange("p (c t) -> p c t", c=8)
        sqv = Cm[:, 4608:6144].bitcast(BF16).rearrange("p (c t) -> p c t", c=8)
        diag = Cm[:, 6144:8128].bitcast(BF16).rearrange("p (k j) -> p k j", k=31)
        sig = [Cm[:, 8128 + i * TN:8128 + (i + 1) * TN] for i in range(2)]
        t64 = Cm[:, 8896:8960]
        t16 = Cm[:, 8960:8976]
        fT = Cm[:, :].bitcast(BF16).rearrange("p (a k t) -> p a k t", a=3, k=KC)
        gbc = Cm[:, 0:2048]
        yout = [Cm[:, 2048 + s * 2048:2048 + (s + 1) * 2048] for s in range(3)]
        Sflat = S[:, 0:3, :].rearrange("p a t -> p (a t)")

        def pcol(c):
            return prm[:, c:c + 1]

        def emit(P, W):
            B = {}

            def bf(n):
                if n not in B:
                    B[n] = Buf(n)
                return B[n]

            mixbufs_ref = [None]
            pb = [bf("pb%d" % i) for i in range(8)]
            for b_ in pb:
                b_.excl = True
            xTb = [bf("xT%d" % t) for t in range(3)]
            state = {"ring": 0, "alt": 0}

            def ring3():
                b = state["ring"] % 3
                state["ring"] += 1
                return b

            def alt_eng():
                state["alt"] += 1
                return "act" if state["alt"] % 2 else "dve"

            def copy_op(eng, out, in_, reads, writes):
                if eng == "act":
                    return P.op("act", lambda e: e.activation(out=out, in_=in_, func=AF.Copy), reads=reads, writes=writes)
                return P.op("dve", lambda e: e.tensor_copy(out=out, in_=in_), reads=reads, writes=writes)

            outs = []
            P.dma("sp", lambda e: e.dma_start(out=prm[:], in_=prm_d[:, :]), "c0", writes=[bf("prm")])
            P.dma("sp", lambda e: e.dma_start(out=idf[:], in_=ident_d[:, :]), "c1", writes=[bf("idf")])
            P.op("dve", lambda e: e.tensor_copy(out=idb[:], in_=idf[:]), reads=[bf("idf")], writes=[bf("idb")])
            P.op("dve", lambda e: e.memset(ones[:], 1.0), writes=[bf("ones")])
            for l in range(DEPTH if "nopt" not in _DBG else 0):
                outs.append(P.dma("sp", lambda e, l=l: e.dma_start(out=ocs_d[l, :, 0:26, :], in_=sconv_d[l, :, 4:30, :]), "pt"))
                outs.append(P.dma("sp", lambda e, l=l: e.dma_start(out=ops_d[l, :, 0:11, :], in_=spool_d[l, :, 4:15, :]), "pt"))

            xinb = [bf("xin%d" % i) for i in range(4)]
            bank = 0
            for i in range(T // 128):
                s = i % 4
                P.dma("sp", lambda e, i=i, s=s: e.dma_start(out=xin[s], in_=x_d[i * 128:(i + 1) * 128, :]),
                      "xi%d" % s, writes=[xinb[s]])
                for q in range(4):
                    bk = bank % 8
                    bank += 1
                    fns = [lambda e, s=s, q=q, j=j, bk=bk: e.transpose(
                        out=ps[:, bk, j * 128:(j + 1) * 128], in_=xin[s][:, (4 * q + j) * 128:(4 * q + j + 1) * 128], identity=idf[:])
                        for j in range(4)]
                    P.group(fns, reads=[xinb[s], bf("idf")], writes=[pb[bk]])
                    copy_op(alt_eng(), xT[:, 4 * q:4 * q + 4, i * 128:(i + 1) * 128],
                            ps[:, bk, :].rearrange("p (j t) -> p j t", t=128), [pb[bk]], [xTb[i // 3]])
            cbufs = list(xinb)

            def rms_stats(l_g_unused, src_cols, sq_view, sq_buf, bank_i, out_slot, out_buf, xbufs):
                for k in range(KC):
                    P.op("act", lambda e, k=k: e.activation(out=sq_view[:, k, :], in_=xT[:, k, src_cols[0]:src_cols[1]], func=AF.Square),
                         reads=xbufs, writes=[sq_buf])
                n = src_cols[1] - src_cols[0]
                fns = [lambda e, k=k: e.matmul(out=ps[:, bank_i, 0:n], lhsT=ones[:], rhs=sq_view[:, k, :], start=(k == 0), stop=(k == KC - 1))
                       for k in range(KC)]
                P.group(fns, reads=[sq_buf, bf("ones")], writes=[pb[bank_i]])
                rstd_finish(bank_i, n, out_slot, out_buf)

            def rstd_finish(bank_i, n, out_slot, out_buf):
                if out_slot is None:
                    out_slot, out_buf = ps[:, bank_i, 0:n], pb[bank_i]
                P.op("act", lambda e: e.activation(out=out_slot, in_=ps[:, bank_i, 0:n], func=AF.Sqrt, bias=EPS, scale=1.0 / D),
                     reads=[pb[bank_i]], writes=[out_buf])
                P.op("dve", lambda e: e.reciprocal(out=out_slot, in_=out_slot), reads=[out_buf], writes=[out_buf])

            def rms1_pre(l, t):
                c0 = t * TN
                rms_stats(None, (c0, c0 + TN), sqb, bf("sqb"), 7, None, None, [xTb[t]])
                for k in range(KC):
                    P.op("dve", lambda e, k=k: e.scalar_tensor_tensor(
                        out=hT[:, k, :], in0=xT[:, k, c0:c0 + TN], scalar=pcol(l * PL + O_G1 + k), in1=ps[:, 7, 0:TN],
                        op0=ALU.mult, op1=ALU.mult), reads=[xTb[t], pb[7], bf("prm")], writes=[bf("hT")])

            def mm_group(slot, wbuf, rhs_of_k, nk, bank_i, n, reads):
                fns = [lambda e, k=k: e.matmul(out=ps[:, bank_i, 0:n], lhsT=wt[:, slot, k, :], rhs=rhs_of_k(k),
                                               start=(k == 0), stop=(k == nk - 1)) for k in range(nk)]
                return P.group(fns, reads=[wbuf] + reads, writes=[pb[bank_i]])

            def stt(out, in0, scalar, in1, op0, op1, reads, writes):
                return P.op("dve", lambda e: e.scalar_tensor_tensor(out=out, in0=in0, scalar=scalar, in1=in1, op0=op0, op1=op1),
                            reads=reads, writes=writes)

            def ts1(out, in0, s1, op0, reads, writes):
                return P.op("dve", lambda e: e.tensor_scalar(out=out, in0=in0, scalar1=s1, scalar2=None, op0=op0),
                            reads=reads, writes=writes)

            def ts2(out, in0, s1, s2, op0, op1, reads, writes):
                return P.op("dve", lambda e: e.tensor_scalar(out=out, in0=in0, scalar1=s1, scalar2=s2, op0=op0, op1=op1),
                            reads=reads, writes=writes)

            def act(out, in_, func, reads, writes, bias=None, scale=None):
                kw = {}
                if bias is not None:
                    kw["bias"] = bias
                if scale is not None:
                    kw["scale"] = scale
                return P.op("act", lambda e: e.activation(out=out, in_=in_, func=func, **kw), reads=reads, writes=writes)

            def mixer(l, t):
                L0 = l * PL
                c0 = t * TN
                npr = TN if t < 2 else TN - NSM
                prmb = bf("prm")
                hTb, mixb = bf("hT"), bf("mix")
                vb, vbfb, sqvb = bf("v"), bf("vbf"), bf("sqv")
                sigb = [bf("sig0"), bf("sig1")]
                ubb = [bf("ub0"), bf("ub1"), bf("ub2")]

                nxt = t + 1 if t < 2 else None
                dgb = [bf("diagA"), bf("diagB")]
                halves = [(0, 16), (16, 31)]

                def conv_build(c):
                    for hi, (k0, k1) in enumerate(halves):
                        nk = k1 - k0
                        P.op("dve", lambda e, k0=k0, k1=k1, nk=nk: e.tensor_tensor(
                            out=diag[:, k0:k1, :], in0=ps[:, 4, 0:128].unsqueeze(1).broadcast_to([128, nk, 128]),
                            in1=prm[:, L0 + O_CW + c * 31 + k0:L0 + O_CW + c * 31 + k1].unsqueeze(2).broadcast_to([128, nk, 128]),
                            op=ALU.mult), reads=[pb[4], prmb], writes=[dgb[hi]])

                def conv(c):
                    ui = c % 3
                    ub = ubuf[:, ui, :]
                    for hi, (k0, k1) in enumerate(halves):
                        fns = [lambda e, k=k: e.matmul(out=ps[:, 3, 0:npr], lhsT=diag[:, k, :], rhs=ub[:, 2 + k:2 + k + npr],
                                                       start=(k == 0), stop=(k == 30)) for k in range(k0, k1)]
                        P.group(fns, reads=[dgb[hi], ubb[ui]], writes=[pb[3]])
                    if t == 2:
                        fns = [lambda e, k=k: e.matmul(out=ps[:, 3, npr:TN], lhsT=diag[:, k, :], rhs=usamp[:, c, :, k:k + 4],
                                                       start=(k == 0), stop=(k == 30), skip_group_check=True) for k in range(31)]
                        P.group(fns, reads=[dgb[0], dgb[1], bf("usamp%d" % c)], writes=[pb[3]])
                    cb = pcol(L0 + O_CB + c)
                    act(vv[:, c, :], ps[:, 3, 0:TN], AF.Identity, [pb[3], prmb], [vb], bias=cb)
                    ts1(vbf[:, c, :], ps[:, 3, 0:TN], cb, ALU.add, [pb[3], prmb], [vbfb])
                    act(sqv[:, c, :], ps[:, 3, 0:TN], AF.Square, [pb[3], prmb], [sqvb], bias=cb)

                for c in range(8):
                    ui = c % 3
                    ub = ubuf[:, ui, :]
                    if c > 0:
                        conv_build(c - 1)
                    sa, wa = W.get(w_in_v[l][:, :, c * 128:(c + 1) * 128], KC)
                    ba = ring3()
                    mm_group(sa, wa, lambda k: hT[:, k, :], KC, ba, TN, [hTb])
                    sg, wg = W.get(w_in_v[l][:, :, 1024 + c * 128:1024 + (c + 1) * 128], KC)
                    bg = ring3()
                    mm_group(sg, wg, lambda k: hT[:, k, :], KC, bg, TN, [hTb])
                    si = c % 2
                    act(sig[si], ps[:, bg, 0:TN], AF.Sigmoid, [pb[bg], prmb], [sigb[si]], bias=pcol(L0 + O_BIN + 8 + c))
                    ba_col = pcol(L0 + O_BIN + c)
                    rdu = [pb[ba], sigb[si], prmb]
                    if t == 0:
                        P.op("dve", lambda e, ub=ub: e.memset(ub[:, 0:32], 0.0), writes=[ubb[ui]])
                    else:
                        P.op("dve", lambda e, ub=ub, c=c: e.tensor_copy(out=ub[:, 0:32], in_=hsave[:, c, :]),
                             reads=[bf("hsave%d" % c)], writes=[ubb[ui]])
                    if t == 0:
                        stt(t64, ps[:, ba, 0:HALO], ba_col, sig[si][:, 0:HALO], ALU.add, ALU.mult, rdu, [bf("t64")])
                        ts1(ub[:, 32:32 + HALO], t64, pcol(O_HM), ALU.mult, [bf("t64"), prmb], [ubb[ui]])
                        stt(ub[:, 32 + HALO:32 + TN], ps[:, ba, HALO:TN], ba_col, sig[si][:, HALO:TN], ALU.add, ALU.mult, rdu, [ubb[ui]])
                    else:
                        stt(ub[:, 32:32 + npr], ps[:, ba, 0:npr], ba_col, sig[si][:, 0:npr], ALU.add, ALU.mult, rdu, [ubb[ui]])
                    if t == 2:
                        stt(usamp[:, c, :, 30:34], ps[:, ba, npr:TN].rearrange("p (b j) -> p b j", j=4), ba_col,
                            sig[si][:, npr:TN].rearrange("p (b j) -> p b j", j=4), ALU.add, ALU.mult, rdu, [bf("usamp%d" % c)])
                        stt(utail[:, c, :], ps[:, ba, TN - 94:TN], ba_col, sig[si][:, TN - 94:TN], ALU.add, ALU.mult, rdu, [bf("utail")])
                    else:
                        P.op("dve", lambda e, ub=ub, c=c: e.tensor_copy(out=hsave[:, c, :], in_=ub[:, TN:TN + 32]),
                             reads=[ubb[ui]], writes=[bf("hsave%d" % c)])
                    if nxt is not None:
                        n0 = nxt * TN
                        for k in (2 * c, 2 * c + 1):
                            P.op("act", lambda e, k=k, n0=n0: e.activation(out=sqb[:, k, :], in_=xT[:, k, n0:n0 + TN], func=AF.Square),
                                 reads=[xTb[nxt]], writes=[bf("sqb")])
                    if c > 0:
                        conv(c - 1)
                conv_build(7)
                conv(7)

                def pool_group(g):
                    w = 2 << g
                    db = g % 2
                    dB = bf("d%d" % db)
                    bz = []
                    for j in range(2):
                        m = 2 * g + j
                        sz, wz = W.get(w_in_v[l][:, :, 2048 + m * 128:2048 + (m + 1) * 128], KC)
                        b_ = ring3()
                        bz.append(b_)
                        mm_group(sz, wz, lambda k: hT[:, k, :], KC, b_, TN, [hTb])
                        bzc = pcol(L0 + O_BIN + 16 + m)
                        zfB, zbB = bf("zf%d" % j), bf("zb%d" % j)
                        act(zf[:, j, :], ps[:, b_, 0:TN], AF.Identity, [pb[b_], prmb], [zfB], bias=bzc)
                        if t == 0:
                            P.op("dve", lambda e, j=j: e.memset(zb[:, j, 0:16], 0.0), writes=[zbB])
                            ts2(zb[:, j, 16:16 + HALO], ps[:, b_, 0:HALO], bzc, pcol(O_HM), ALU.add, ALU.mult, [pb[b_], prmb], [zbB])
                            ts1(zb[:, j, 16 + HALO:16 + TN], ps[:, b_, HALO:TN], bzc, ALU.add, [pb[b_], prmb], [zbB])
                        else:
                            P.op("dve", lambda e, j=j, m=m: e.tensor_copy(out=zb[:, j, 0:16], in_=zsave[:, m, :]),
                                 reads=[bf("zsave%d" % m)], writes=[zbB])
                            ts1(zb[:, j, 16:16 + npr], ps[:, b_, 0:npr], bzc, ALU.add, [pb[b_], prmb], [zbB])
                        if t == 2:
                            ts1(zsamp[:, m, :, 15:19], ps[:, b_, npr:TN].rearrange("p (b j) -> p b j", j=4), bzc, ALU.add,
                                [pb[b_], prmb], [bf("zsamp%d" % m)])
                            ts1(ztail[:, m, :], ps[:, b_, TN - 94:TN], bzc, ALU.add, [pb[b_], prmb], [bf("ztail")])
                        else:
                            P.op("dve", lambda e, j=j, m=m: e.tensor_copy(out=zsave[:, m, :], in_=zb[:, j, TN:TN + 16]),
                                 reads=[zbB], writes=[bf("zsave%d" % m)])
                    for j in range(2):
                        m = 2 * g + j
                        zfB, zbB = bf("zf%d" % j), bf("zb%d" % j)
                        bp = ring3()
                        fns = [lambda e, k=k, j=j, bp=bp, w=w: e.matmul(out=ps[:, bp, 0:npr], lhsT=idb[:], rhs=zb[:, j, 16 - k:16 - k + npr],
                                                                   start=(k == 0), stop=(k == w - 1)) for k in range(w)]
                        rd = [bf("idb"), zbB]
                        if t == 2:
                            fns += [lambda e, k=k, m=m, bp=bp, w=w: e.matmul(out=ps[:, bp, npr:TN], lhsT=idb[:], rhs=zsamp[:, m, :, 15 - k:19 - k],
                                                                        start=(k == 0), stop=(k == w - 1), skip_group_check=True) for k in range(w)]
                            rd.append(bf("zsamp%d" % m))
                        P.group(fns, reads=rd, writes=[pb[bp]])
                        rdd = [pb[bp], zfB]
                        if t == 0:
                            stt(dbuf[:, db, j, 0:HALO], ps[:, bp, 0:HALO], 1.0 / w, zf[:, j, 0:HALO], ALU.mult, ALU.subtract, rdd, [dB])
                            P.op("dve", lambda e, bp=bp, g=g: e.tensor_tensor(out=t16, in0=ps[:, bp, HALO:HALO + 16],
                                                                              in1=prm[:, O_IC + g * 16:O_IC + (g + 1) * 16], op=ALU.mult),
                                 reads=[pb[bp], prmb], writes=[bf("t16")])
                            P.op("dve", lambda e, j=j, db=db: e.tensor_tensor(out=dbuf[:, db, j, HALO:HALO + 16], in0=t16,
                                                                              in1=zf[:, j, HALO:HALO + 16], op=ALU.subtract),
                                 reads=[bf("t16"), zfB], writes=[dB])
                            stt(dbuf[:, db, j, HALO + 16:TN], ps[:, bp, HALO + 16:TN], 1.0 / w, zf[:, j, HALO + 16:TN],
                                ALU.mult, ALU.subtract, rdd, [dB])
                        else:
                            stt(dbuf[:, db, j, :], ps[:, bp, 0:TN], 1.0 / w, zf[:, j, :], ALU.mult, ALU.subtract, rdd, [dB])
                    for e_ in range(2):
                        m = 2 * g + e_
                        sp_, wp = W.get(pool_w_v[l][g][:, :, e_ * 128:(e_ + 1) * 128], 2)
                        bq = ring3()
                        mm_group(sp_, wp, lambda k, db=db: dbuf[:, db, k, :], 2, bq, TN, [dB])
                        act(mix[:, 8 + m, :], ps[:, bq, 0:TN], AF.Copy, [pb[bq], prmb], [mixb], scale=pcol(L0 + O_PSC + m))

                fns = [lambda e, c=c: e.matmul(out=ps[:, 5, 0:TN], lhsT=ones[:], rhs=vbf[:, c, :], start=(c == 0), stop=(c == 7)) for c in range(8)]
                P.group(fns, reads=[vbfb, bf("ones")], writes=[pb[5]])
                fns = [lambda e, c=c: e.matmul(out=ps[:, 6, 0:TN], lhsT=ones[:], rhs=sqv[:, c, :], start=(c == 0), stop=(c == 7)) for c in range(8)]
                P.group(fns, reads=[sqvb, bf("ones")], writes=[pb[6]])
                if nxt is not None:
                    fns = [lambda e, k=k: e.matmul(out=ps[:, 7, 0:TN], lhsT=ones[:], rhs=sqb[:, k, :], start=(k == 0), stop=(k == KC - 1))
                           for k in range(KC)]
                    P.group(fns, reads=[bf("sqb"), bf("ones")], writes=[pb[7]])
                S1b = bf("S1")
                act(ps[:, 5, 0:TN], ps[:, 5, 0:TN], AF.Copy, [pb[5]], [pb[5]], scale=1.0 / 1024)
                act(S[:, 1, :], ps[:, 5, 0:TN], AF.Square, [pb[5]], [S1b])
                stt(ps[:, 6, 0:TN], ps[:, 6, 0:TN], 1.0 / 1024, S[:, 1, :], ALU.mult, ALU.subtract, [pb[6], S1b], [pb[6]])
                act(ps[:, 6, 0:TN], ps[:, 6, 0:TN], AF.Sqrt, [pb[6]], [pb[6]], bias=EPS)
                P.op("dve", lambda e: e.reciprocal(out=ps[:, 6, 0:TN], in_=ps[:, 6, 0:TN]), reads=[pb[6]], writes=[pb[6]])
                if nxt is not None:
                    rstd_finish(7, TN, None, None)
                def ln_apply(c):
                    sl = 2 + c % 2
                    Sb = bf("S%d" % sl)
                    P.op("dve", lambda e, c=c, sl=sl: e.tensor_tensor(out=S[:, sl, :], in0=vv[:, c, :], in1=ps[:, 5, 0:TN], op=ALU.subtract),
                         reads=[vb, pb[5]], writes=[Sb])
                    P.op("dve", lambda e, sl=sl: e.tensor_tensor(out=S[:, sl, :], in0=S[:, sl, :], in1=ps[:, 6, 0:TN], op=ALU.mult),
                         reads=[Sb, pb[6]], writes=[Sb])
                    act(mix[:, c, :], S[:, sl, :], AF.Silu, [Sb, prmb], [mixb], bias=pcol(L0 + O_LNB + c), scale=pcol(L0 + O_LNG + c))

                for g in range(4):
                    pool_group(g)
                    ln_apply(2 * g)
                    ln_apply(2 * g + 1)
                early2 = (t == 2 and "noffn" not in _DBG)
                if early2:
                    handoff(mixbufs_ref[0], [bf("sqC")])
                    for tt in range(2):
                        for k in range(KC):
                            P.op("act", lambda e, k=k, tt=tt: e.activation(out=fT[:, tt, k, :], in_=xT[:, k, tt * TN:(tt + 1) * TN], func=AF.Square),
                                 reads=[xTb[tt]], writes=[bf("sqC")])
                        fns = [lambda e, k=k, tt=tt: e.matmul(out=ps[:, 5 + tt, 0:TN], lhsT=ones[:], rhs=fT[:, tt, k, :], start=(k == 0), stop=(k == KC - 1))
                               for k in range(KC)]
                        P.group(fns, reads=[bf("sqC"), bf("ones")], writes=[pb[5 + tt]])
                        rstd_finish(5 + tt, TN, None, None)
                for m in range(KC):
                    so, wo = W.get(w_out_v[l][:, :, m * 128:(m + 1) * 128], KC)
                    bo = ring3()
                    mm_group(so, wo, lambda k: mix[:, k, :], KC, bo, TN, [mixb])
                    P.op("dve", lambda e, m=m, bo=bo: e.tensor_tensor(out=xT[:, m, c0:c0 + TN], in0=xT[:, m, c0:c0 + TN], in1=ps[:, bo, 0:TN], op=ALU.add),
                         reads=[pb[bo], xTb[t]], writes=[xTb[t]])
                    if nxt is not None:
                        n0 = nxt * TN
                        stt(hT[:, m, :], xT[:, m, n0:n0 + TN], pcol(L0 + O_G1 + m), ps[:, 7, 0:TN], ALU.mult, ALU.mult,
                            [xTb[nxt], pb[7], prmb], [hTb])
                    elif early2:
                        P.op("act", lambda e, m=m: e.activation(out=fT[:, 2, m, :], in_=xT[:, m, c0:c0 + TN], func=AF.Square),
                             reads=[xTb[2]], writes=[bf("sqC")])
                        stt(h2T[:, 0, m, :], xT[:, m, 0:TN], pcol(L0 + O_G2 + m), ps[:, 5, 0:TN], ALU.mult, ALU.mult,
                            [xTb[0], pb[5], prmb], [hTb])
                if t == 2 and "notails" not in _DBG:
                    Sall6 = [bf("S%d" % i) for i in range(6)]
                    for a, (tl, tlb) in enumerate([(utail, bf("utail")), (ztail, bf("ztail"))]):
                        for q in range(2):
                            bk = (2 * a + q) % 4
                            fns = [lambda e, tl=tl, q=q, i=i, bk=bk: e.transpose(out=ps[0:94, bk, i * 128:(i + 1) * 128], in_=tl[:, 4 * q + i, :], identity=idf[:])
                                   for i in range(4)]
                            P.group(fns, reads=[tlb, bf("idf")], writes=[pb[bk]])
                            copy_op(alt_eng(), tstage[0:94, a, q * 512:(q + 1) * 512], ps[0:94, bk, :], [pb[bk]], Sall6)
                    outs.append(P.dma("sp", lambda e: e.dma_start(out=ocp_d[l, :, :], in_=tstage[0:30, 0, :]), "to", reads=Sall6))
                    outs.append(P.dma("sp", lambda e: e.dma_start(out=opp_d[l, :, :], in_=tstage[15:30, 1, :]), "to", reads=Sall6))
                    for b in range(SBQ):
                        outs.append(P.dma("sp", lambda e, b=b: e.dma_start(out=ocs_d[l, b, 26:30, :], in_=tstage[30 + 4 * b:34 + 4 * b, 0, :]), "to", reads=Sall6))
                        outs.append(P.dma("sp", lambda e, b=b: e.dma_start(out=ops_d[l, b, 11:15, :], in_=tstage[30 + 4 * b:34 + 4 * b, 1, :]), "to", reads=Sall6))

            def ffn(l):
                L0 = l * PL
                prmb = bf("prm")
                h2b, sqCb, fCb = bf("h2T"), bf("sqC"), bf("fC")
                Sb = [bf("S%d" % i) for i in range(6)]
                early = "nomixer" not in _DBG
                for t in range(3):
                    c0 = t * TN
                    bk = 5 + t
                    if not early:
                        for k in range(KC):
                            P.op("act", lambda e, k=k, t=t, c0=c0: e.activation(out=fT[:, t, k, :], in_=xT[:, k, c0:c0 + TN], func=AF.Square),
                                 reads=[xTb[t]], writes=[sqCb])
                    if not early or t == 2:
                        fns = [lambda e, k=k, t=t, bk=bk: e.matmul(out=ps[:, bk, 0:TN], lhsT=ones[:], rhs=fT[:, t, k, :], start=(k == 0), stop=(k == KC - 1))
                               for k in range(KC)]
                        P.group(fns, reads=[sqCb, bf("ones")], writes=[pb[bk]])
                        rstd_finish(bk, TN, None, None)
                    if not early or t > 0:
                        for k in range(KC):
                            stt(h2T[:, t, k, :], xT[:, k, c0:c0 + TN], pcol(L0 + O_G2 + k), ps[:, bk, 0:TN], ALU.mult, ALU.mult,
                                [xTb[t], pb[bk], prmb], [h2b])
                handoff([sqCb], [fCb])
                slot3 = 0
                sc_i = 0
                tr = [((HALO if l == DEPTH - 1 else HALO - 32) if t == 0 else 0, TN) for t in range(3)]
                for j in range(DFF // FB):
                    for m in range(KC):
                        s1, w1 = W.get(w_ff1_v[l][:, :, j * FB + m * 128:j * FB + (m + 1) * 128], KC)
                        b0 = 3 * (slot3 % 2)
                        slot3 += 1
                        fns = [lambda e, k=k, t=t, s1=s1, b0=b0: e.matmul(out=ps[:, b0 + t, 0:tr[t][1] - tr[t][0]], lhsT=wt[:, s1, k, :],
                                                                        rhs=h2T[:, t, k, tr[t][0]:tr[t][1]],
                                                                        start=(k == 0), stop=(k == KC - 1)) for k in range(KC) for t in range(3)]
                        P.group(fns, reads=[w1, h2b], writes=[pb[b0], pb[b0 + 1], pb[b0 + 2]])
                        for t in range(3):
                            o0, o1 = tr[t]
                            si = 3 + sc_i % 3
                            sc_i += 1
                            act(S[:, si, 0:o1 - o0], ps[:, b0 + t, 0:o1 - o0], AF.Square, [pb[b0 + t]], [Sb[si]])
                            stt(fT[:, t, m, o0:o1], ps[:, b0 + t, 0:o1 - o0], 0.0, S[:, si, 0:o1 - o0], ALU.is_gt, ALU.mult, [pb[b0 + t], Sb[si]], [fCb])
                    for m in range(KC):
                        s2, w2 = W.get(w_ff2_v[l][j][:, :, m * 128:(m + 1) * 128], KC)
                        b0 = 3 * (slot3 % 2)
                        slot3 += 1
                        fns = [lambda e, k=k, t=t, s2=s2, b0=b0: e.matmul(out=ps[:, b0 + t, 0:tr[t][1] - tr[t][0]], lhsT=wt[:, s2, k, :],
                                                                        rhs=fT[:, t, k, tr[t][0]:tr[t][1]],
                                                                        start=(k == 0), stop=(k == KC - 1)) for k in range(KC) for t in range(3)]
                        P.group(fns, reads=[w2, fCb], writes=[pb[b0], pb[b0 + 1], pb[b0 + 2]])
                        for t in range(3):
                            o0, o1 = tr[t]
                            c0 = t * TN + o0
                            nn = o1 - o0
                            P.op("dve", lambda e, m=m, t=t, b0=b0, c0=c0, nn=nn: e.tensor_tensor(out=xT[:, m, c0:c0 + nn], in0=xT[:, m, c0:c0 + nn],
                                                                                             in1=ps[:, b0 + t, 0:nn], op=ALU.add),
                                 reads=[pb[b0 + t], xTb[t]], writes=[xTb[t]])
                return [h2b, fCb]

            for l in range(DEPTH if "nolayers" not in _DBG else 0):
                if l > 0:
                    handoff([bf("h2T")], [bf("hT"), bf("mix"), bf("sqb")])
                P.group([lambda e: e.transpose(out=ps[:, 4, 0:128], in_=idf[:], identity=idf[:])], reads=[bf("idf")], writes=[pb[4]])
                rms1_pre(l, 0)
                stb = bf("stage")
                handoff(cbufs, [stb])
                for gq in range(4):
                    P.dma("sp", lambda e, l=l, gq=gq: e.dma_start(out=stgc[0:120, gq, :],
                                                                  in_=sconv_d[l, 4 * gq:4 * gq + 4, :, :].rearrange("b j c -> (b j) c")),
                          "sg", writes=[stb])
                for gp in range(2):
                    P.dma("sp", lambda e, l=l, gp=gp: e.dma_start(out=stgp[0:120, gp, :],
                                                                  in_=spool_d[l, 8 * gp:8 * gp + 8, :, :].rearrange("b j c -> (b j) c")),
                          "sg", writes=[stb])
                hbanks = [0, 1, 2, 3, 5, 6]
                bank = 0
                for c in range(8):
                    bk = hbanks[bank % 6]
                    bank += 1
                    fns = [lambda e, c=c, gq=gq, bk=bk: e.transpose(out=ps[:, bk, gq * 120:(gq + 1) * 120], in_=stgc[0:120, gq, c * 128:(c + 1) * 128],
                                                                   identity=idf[0:120, 0:120]) for gq in range(4)]
                    P.group(fns, reads=[stb, bf("idf")], writes=[pb[bk]])
                    copy_op(alt_eng(), usamp[:, c, :, 0:30], ps[:, bk, 0:480].rearrange("p (b j) -> p b j", j=30), [pb[bk]], [bf("usamp%d" % c)])
                    bk = hbanks[bank % 6]
                    bank += 1
                    fns = [lambda e, c=c, gp=gp, bk=bk: e.transpose(out=ps[:, bk, gp * 120:(gp + 1) * 120], in_=stgp[0:120, gp, c * 128:(c + 1) * 128],
                                                                   identity=idf[0:120, 0:120]) for gp in range(2)]
                    P.group(fns, reads=[stb, bf("idf")], writes=[pb[bk]])
                    copy_op(alt_eng(), zsamp[:, c, :, 0:15], ps[:, bk, 0:240].rearrange("p (b j) -> p b j", j=15), [pb[bk]], [bf("zsamp%d" % c)])
                mixbufs = [bf(n) for n in ("v", "vbf", "sqv", "diagA", "diagB", "sig0", "sig1", "t64", "t16")]
                handoff([stb], mixbufs)
                mixbufs_ref[0] = mixbufs
                for t in range((2 if "t01" in _DBG else 3) if "nomixer" not in _DBG else 0):
                    mixer(l, t)
                handoff([bf("hT"), bf("mix"), bf("sqb")], [bf("h2T")])
                if "nomixer" in _DBG or "noffn" in _DBG or "t01" in _DBG:
                    handoff(mixbufs, [bf("sqC")])
                cbufs = ffn(l)[1:] if "noffn" not in _DBG else [bf("sqC")]

            gB, yob = bf("gbc"), [bf("yout%d" % i) for i in range(3)]
            sq2b = [bf("sqh0"), bf("sqh1")]
            handoff(cbufs, [gB] + yob)
            if "nolayers" not in _DBG:
                handoff([bf("h2T")], sq2b)
            P.dma("sp", lambda e: e.dma_start(out=gbc, in_=gfb_d[:, :]), "c0", writes=[gB])
            rtb = bf("S5")
            bank = 0
            for i in range(9):
                n = 128 if i < 8 else 64
                col0 = HALO + 128 * i
                s2, s3 = i % 2, i % 3
                sbk = 6 + i % 2
                xb_ = sorted({col0 // TN, (col0 + n - 1) // TN})
                xbs = [xTb[j] for j in xb_]
                for k in range(KC):
                    P.op("act", lambda e, k=k, s2=s2, n=n, col0=col0: e.activation(out=sqb[:, k, s2 * 128:s2 * 128 + n], in_=xT[:, k, col0:col0 + n], func=AF.Square),
                         reads=xbs, writes=[sq2b[s2]])
                fns = [lambda e, k=k, s2=s2, n=n, sbk=sbk, i=i: e.matmul(out=ps[0:n, sbk, i:i + 1], lhsT=sqb[:, k, s2 * 128:s2 * 128 + n], rhs=ones[:, 0:1],
                                                                       start=(k == 0), stop=(k == KC - 1), skip_group_check=True) for k in range(KC)]
                P.group(fns, reads=[sq2b[s2], bf("ones")], writes=[pb[sbk]])
                P.op("act", lambda e, n=n, sbk=sbk, i=i: e.activation(out=S[0:n, 5, i:i + 1], in_=ps[0:n, sbk, i:i + 1], func=AF.Sqrt, bias=EPS, scale=1.0 / D),
                     reads=[pb[sbk]], writes=[rtb])
                P.op("dve", lambda e, n=n, i=i: e.reciprocal(out=S[0:n, 5, i:i + 1], in_=S[0:n, 5, i:i + 1]), reads=[rtb], writes=[rtb])
                for q in range(4):
                    bk = bank % 6
                    bank += 1
                    fns = [lambda e, q=q, j=j, bk=bk, n=n, col0=col0: e.transpose(out=ps[0:n, bk, j * 128:(j + 1) * 128], in_=xT[:, 4 * q + j, col0:col0 + n], identity=idf[:])
                           for j in range(4)]
                    P.group(fns, reads=xbs + [bf("idf")], writes=[pb[bk]])
                    stt(yout[s3][0:n, q * 512:(q + 1) * 512], ps[0:n, bk, :], S[0:n, 5, i:i + 1], gbc[0:n, q * 512:(q + 1) * 512], ALU.mult, ALU.mult,
                        [pb[bk], rtb, gB], [yob[s3]])
                outs.append(P.dma("sp", lambda e, i=i, n=n, s3=s3: e.dma_start(out=y_d[i * 128:i * 128 + n, :], in_=yout[s3][0:n, :]),
                                  "yo%d" % s3, reads=[yob[s3]]))
            last = {}
            for h in outs:
                if last.get(h[0], 0) < h[1]:
                    last[h[0]] = h[1]
            P.wait("sp", list(last.items()))

        Wd = WStream(Prog(), wt, plan=None)
        emit(Wd.P, Wd)
        P = Prog()
        W = WStream(P, wt, plan=Wd.req)
        emit(P, W)
        assert W.i == len(Wd.req)

        sems = {n: st.enter_context(nc.semaphore(n)) for n in P.sem_names()}
        block = st.enter_context(nc.Block())

        @block.tensor
        def _(e):
            P.replay("pe", e, sems)

        @block.scalar
        def _(e):
            P.replay("act", e, sems)

        @block.vector
        def _(e):
            P.replay("dve", e, sems)

        @block.gpsimd
        def _(e):
            P.replay("pool", e, sems)

        @block.sync
        def _(e):
            P.replay("sp", e, sems)
    return nc


def _pack_params(norm1_g, b_in, conv_w, conv_b, ln_g, ln_b, pool_scale, norm2_g, norm_f, half):
    prm = np.zeros((128, NPRM), np.float32)

    def cols(v):
        return np.ascontiguousarray(np.asarray(v, np.float32).reshape(-1, 128).T)

    for l in range(DEPTH):
        L0 = l * PL
        prm[:, L0 + O_G1:L0 + O_G1 + 16] = cols(norm1_g[l])
        prm[:, L0 + O_BIN:L0 + O_BIN + 24] = cols(b_in[l])
        cw = np.asarray(conv_w[l], np.float32)
        prm[:, L0 + O_CW:L0 + O_CW + 248] = cw.reshape(31, 8, 128).transpose(2, 1, 0).reshape(128, 248)
        prm[:, L0 + O_CB:L0 + O_CB + 8] = cols(conv_b[l])
        prm[:, L0 + O_LNG:L0 + O_LNG + 8] = cols(ln_g[l])
        prm[:, L0 + O_LNB:L0 + O_LNB + 8] = cols(ln_b[l])
        prm[:, L0 + O_PSC:L0 + O_PSC + 8] = cols(pool_scale[l])
        prm[:, L0 + O_G2:L0 + O_G2 + 16] = cols(norm2_g[l])
    prm[:, O_GF:O_GF + 16] = cols(norm_f)
    prm[:, O_HM] = float(half)
    pos = np.arange(16)
    for g in range(4):
        w = 2 << g
        cnt = np.minimum(w, pos + 1) if half == 0 else np.full(16, w)
        prm[:, O_IC + g * 16:O_IC + (g + 1) * 16] = (1.0 / cnt.astype(np.float32))[None, :]
    return prm


_NC_CACHE = {}


def kernel(x_prompt, x_sample, state_conv, state_pool, norm1_g, w_in, b_in, conv_w, conv_b,
           ln_g, ln_b, pool_w, pool_scale, w_out, norm2_g, w_ff1, w_ff2, norm_f):
    f = lambda a: np.ascontiguousarray(np.asarray(a, dtype=np.float32))
    x_prompt, x_sample, state_conv, state_pool = f(x_prompt), f(x_sample), f(state_conv), f(state_pool)
    w_in, pool_w, w_out, w_ff1, w_ff2 = f(w_in), f(pool_w), f(w_out), f(w_ff1), f(w_ff2)
    if "nc" not in _NC_CACHE:
        _NC_CACHE["nc"] = build_program()
    nc = _NC_CACHE["nc"]
    ident = np.eye(128, dtype=np.float32)
    gfb = np.ascontiguousarray(np.broadcast_to(np.asarray(norm_f, np.float32)[None, :], (128, D)))
    prms = [_pack_params(norm1_g, b_in, conv_w, conv_b, ln_g, ln_b, pool_scale, norm2_g, norm_f, h) for h in range(2)]
    in_maps = []
    for i in range(NCORES):
        b, h = i // 2, i % 2
        xc = np.zeros((T, D), np.float32)
        if h == 1:
            xc[0:HALO] = x_prompt[b, NPR - HALO:NPR]
        xc[HALO:HALO + NPR] = x_prompt[b, h * NPR:(h + 1) * NPR]
        xc[HALO + NPR:] = x_sample[SBQ * i:SBQ * (i + 1)].reshape(NSM, D)
        in_maps.append({
            "x": xc,
            "sconv": np.ascontiguousarray(state_conv[:, SBQ * i:SBQ * (i + 1)]),
            "spool": np.ascontiguousarray(state_pool[:, SBQ * i:SBQ * (i + 1)]),
            "prm": prms[h], "ident": ident, "gfb": gfb,
            "w_in": w_in, "pool_w": pool_w, "w_out": w_out, "w_ff1": w_ff1, "w_ff2": w_ff2,
        })
    ncr = int(os.environ.get("KCORES", NCORES))
    res = run_bass_kernel_spmd(nc, in_maps[:ncr], core_ids=list(range(ncr)))
    R = list(res.results) + [res.results[0]] * (NCORES - ncr)
    B_, S_ = x_prompt.shape[0], x_prompt.shape[1]
    y_prompt = np.empty((B_, S_, D), np.float32)
    y_sample = np.empty(x_sample.shape, np.float32)
    ncp = np.empty((DEPTH, B_, 30, 1024), np.float32)
    npp = np.empty((DEPTH, B_, 15, 1024), np.float32)
    ncs = np.empty(state_conv.shape, np.float32)
    nps = np.empty(state_pool.shape, np.float32)
    for i in range(NCORES):
        b, h = i // 2, i % 2
        y = R[i]["y"]
        y_prompt[b, h * NPR:(h + 1) * NPR] = y[0:NPR]
        y_sample[SBQ * i:SBQ * (i + 1)] = y[NPR:].reshape(SBQ, 4, D)
        ncs[:, SBQ * i:SBQ * (i + 1)] = R[i]["ocs"]
        nps[:, SBQ * i:SBQ * (i + 1)] = R[i]["ops"]
        if h == 1:
            ncp[:, b] = R[i]["ocp"]
            npp[:, b] = R[i]["opp"]
    return (y_prompt, y_sample, ncp, npp, ncs, nps)
```

```python
import os
import numpy as np
from contextlib import ExitStack
import concourse.bass as bass
import concourse.mybir as mybir
from concourse.bass_utils import run_bass_kernel_spmd

F32 = mybir.dt.float32
BF16 = mybir.dt.bfloat16
AF = mybir.ActivationFunctionType
ALU = mybir.AluOpType

NCORES = 8
D = 2048
KC = 16
DIN = 3072
DFF = 8192
DEPTH = 2
T = 1152
TN = 384
HALO = 64
NPR = 1024
NSM = 64
SBQ = 16
NYR = NPR + NSM
EPS = 1e-6
NSLOT = 5
FB = 2048
TMAP = [0, 2, 1]

PL = 336
O_G1, O_BIN, O_CW, O_CB, O_LNG, O_LNB, O_PSC, O_G2 = 0, 16, 40, 288, 296, 304, 312, 320
O_GF = DEPTH * PL
O_HM = O_GF + 16
O_IC = O_HM + 1
NPRM = O_IC + 64

SAME_ENG_SYNC = True
_DBG = set(os.environ.get("KDBG", "").split(",")) - {""}


class Buf:
    __slots__ = ("name", "w", "r", "pr", "excl")

    def __init__(self, name, excl=False):
        self.name = name
        self.w = {}
        self.r = {}
        self.pr = {}
        self.excl = excl


def _split(reads, writes):
    ex = [b for b in reads if b.excl]
    if not ex:
        return reads, writes
    return [b for b in reads if not b.excl], list(writes) + ex


def _merge(dst, src):
    for k, v in src.items():
        if dst.get(k, 0) < v:
            dst[k] = v


def handoff(src_bufs, dst_bufs):
    u = {}
    for b in src_bufs:
        _merge(u, b.w)
        _merge(u, b.r)
        _merge(u, b.pr)
    for b in dst_bufs:
        b.w = dict(u)
        b.r = {}
        b.pr = dict(u)


class Prog:
    ENG = ("pe", "act", "dve", "pool", "sp")

    def __init__(self):
        self.ops = {e: [] for e in self.ENG}
        self.cnt = {e: 0 for e in self.ENG}
        self.dcnt = {}

    def _deps(self, reads, writes, deps):
        d = {}
        for b in reads:
            _merge(d, b.w)
        for b in writes:
            _merge(d, b.r)
            _merge(d, b.pr)
            _merge(d, b.w)
        for h in deps:
            if h is not None:
                _merge(d, {h[0]: h[1]})
        return d

    def _reg(self, h, reads, writes):
        k, v = h
        for b in reads:
            if b.r.get(k, 0) < v:
                b.r[k] = v
        for b in writes:
            if b.r:
                b.pr = b.r
                b.r = {}
                b.w = {k: v}
            else:
                if b.w.get(k, 0) < v:
                    b.w[k] = v

    def op(self, eng, fn, reads=(), writes=(), deps=()):
        reads, writes = _split(reads, writes)
        d = self._deps(reads, writes, deps)
        self.cnt[eng] += 1
        h = (eng, self.cnt[eng])
        self.ops[eng].append((fn, d, (eng, 1)))
        self._reg(h, reads, writes)
        return h

    def group(self, fns, reads=(), writes=(), deps=()):
        reads, writes = _split(reads, writes)
        d = self._deps(reads, writes, deps)
        self.cnt["pe"] += 1
        h = ("pe", self.cnt["pe"])
        n = len(fns)
        for i, fn in enumerate(fns):
            self.ops["pe"].append((fn, d if i == 0 else {}, ("pe", 1) if i == n - 1 else None))
        self._reg(h, reads, writes)
        return h

    def dma(self, queue, fn, sem, reads=(), writes=(), deps=()):
        reads, writes = _split(reads, writes)
        d = self._deps(reads, writes, deps)
        self.dcnt[sem] = self.dcnt.get(sem, 0) + 16
        h = (sem, self.dcnt[sem])
        self.ops[queue].append((fn, d, (sem, 16)))
        self._reg(h, reads, writes)
        return h

    def wait(self, eng, deps):
        d = {}
        for h in deps:
            _merge(d, {h[0]: h[1]})
        self.ops[eng].append((None, d, None))

    def sem_names(self):
        return list(self.ENG) + sorted(self.dcnt.keys())

    def replay(self, eng, e, sems):
        waited = {}
        for fn, d, inc in self.ops[eng]:
            for k, v in d.items():
                if k == eng and (eng == "pe" or not SAME_ENG_SYNC):
                    continue
                if waited.get(k, 0) < v:
                    e.wait_ge(sems[k], v)
                    waited[k] = v
            if fn is None:
                continue
            inst = fn(e)
            if inc is not None:
                inst.then_inc(sems[inc[0]], inc[1])


class WStream:
    def __init__(self, P, wt, plan=None):
        self.P = P
        self.wt = wt
        self.plan = plan
        self.req = []
        self.i = 0
        self.issued = 0
        self.bufs = [Buf("w%d" % s) for s in range(NSLOT)]

    def _issue(self):
        i = self.issued
        s = i % NSLOT
        src, kc = self.plan[i]
        wt = self.wt
        self.P.dma("pool", lambda e, s=s, src=src, kc=kc: e.dma_start(out=wt[:, s, 0:kc, :], in_=src),
                   "w%d" % s, writes=[self.bufs[s]])
        self.issued += 1

    def get(self, src, kc):
        i = self.i
        self.i += 1
        if self.plan is None:
            self.req.append((src, kc))
            return i % NSLOT, self.bufs[i % NSLOT]
        while self.issued < min(i + NSLOT, len(self.plan)):
            self._issue()
        return i % NSLOT, self.bufs[i % NSLOT]


def build_program():
    nc = bass.Bass("TRN2", target_bir_lowering=False)
    dt_in = lambda n, s: nc.dram_tensor(n, s, F32, kind="ExternalInput").ap()
    dt_out = lambda n, s: nc.dram_tensor(n, s, F32, kind="ExternalOutput").ap()
    x_d = dt_in("x", [T, D])
    sconv_d = dt_in("sconv", [DEPTH, SBQ, 30, 1024])
    spool_d = dt_in("spool", [DEPTH, SBQ, 15, 1024])
    prm_d = dt_in("prm", [128, NPRM])
    ident_d = dt_in("ident", [128, 128])
    gfb_d = dt_in("gfb", [128, D])
    w_in_d = dt_in("w_in", [DEPTH, D, DIN])
    pool_w_d = dt_in("pool_w", [DEPTH, 4, 256, 256])
    w_out_d = dt_in("w_out", [DEPTH, D, D])
    w_ff1_d = dt_in("w_ff1", [DEPTH, D, DFF])
    w_ff2_d = dt_in("w_ff2", [DEPTH, DFF, D])
    y_d = dt_out("y", [NYR, D])
    ocs_d = dt_out("ocs", [DEPTH, SBQ, 30, 1024])
    ops_d = dt_out("ops", [DEPTH, SBQ, 15, 1024])
    ocp_d = dt_out("ocp", [DEPTH, 30, 1024])
    opp_d = dt_out("opp", [DEPTH, 15, 1024])

    w_in_v = [w_in_d[l].rearrange("(kc p) m -> p kc m", p=128) for l in range(DEPTH)]
    w_out_v = [w_out_d[l].rearrange("(kc p) m -> p kc m", p=128) for l in range(DEPTH)]
    w_ff1_v = [w_ff1_d[l].rearrange("(kc p) m -> p kc m", p=128) for l in range(DEPTH)]
    w_ff2_v = [w_ff2_d[l].rearrange("(jj kc p) m -> jj p kc m", p=128, kc=KC) for l in range(DEPTH)]
    pool_w_v = [[pool_w_d[l, g].rearrange("(kk p) m -> p kk m", p=128) for g in range(4)] for l in range(DEPTH)]

    with ExitStack() as st:
        sb = lambda n, s, d: st.enter_context(nc.sbuf_tensor(n, s, d))
        xT = sb("xT", [128, KC, T], F32)
        Bm = sb("Bm", [128, 18432], BF16)
        Cm = sb("Cm", [128, 9216], F32)
        S = sb("S", [128, 6, TN], F32)
        wt = sb("wt", [128, NSLOT, KC, 128], BF16)
        ubuf = sb("ubuf", [128, 3, 32 + TN], BF16)
        hsave = sb("hsave", [128, 8, 32], BF16)
        usamp = sb("usamp", [128, 8, SBQ, 34], BF16)
        zf = sb("zf", [128, 2, TN], F32)
        zb = sb("zb", [128, 2, 16 + TN], BF16)
        zsave = sb("zsave", [128, 8, 16], BF16)
        zsamp = sb("zsamp", [128, 8, SBQ, 19], BF16)
        dbuf = sb("dbuf", [128, 2, 2, TN], BF16)
        utail = sb("utail", [128, 8, 94], F32)
        ztail = sb("ztail", [128, 8, 94], F32)
        prm = sb("prm_t", [128, NPRM], F32)
        idf = sb("idf", [128, 128], F32)
        idb = sb("idb", [128, 128], BF16)
        ones = sb("ones", [128, 128], BF16)
        ps = st.enter_context(nc.psum_tensor("ps", [128, 8, 512], F32))

        hT = Bm[:, 0:6144].rearrange("p (k t) -> p k t", k=KC)
        mix = Bm[:, 6144:12288].rearrange("p (k t) -> p k t", k=KC)
        sqb = Bm[:, 12288:18432].rearrange("p (k t) -> p k t", k=KC)
        h2T = Bm[:, :].rearrange("p (a k t) -> p a k t", a=3, k=KC)
        tstage = S[:, :, :].rearrange("p a t -> p (a t)")[:, 0:2048].rearrange("p (a c) -> p a c", a=2)
        xin = [Cm[:, s * 2048:(s + 1) * 2048] for s in range(4)]
        stgc = Cm[:, 0:4096].rearrange("p (g c) -> p g c", g=4)
        stgp = Cm[:, 4096:6144].rearrange("p (g c) -> p g c", g=2)
        vv = Cm[:, 0:3072].rearrange("p (c t) -> p c t", c=8)
        vbf = Cm[:, 3072:4608].bitcast(BF16).rearrange("p (c t) -> p c t", c=8)
        sqv = Cm[:, 4608:6144].bitcast(BF16).rearrange("p (c t) -> p c t", c=8)
        diag = Cm[:, 6144:8128].bitcast(BF16).rearrange("p (k j) -> p k j", k=31)
        sig = [Cm[:, 8128 + i * TN:8128 + (i + 1) * TN] for i in range(2)]
        t64 = Cm[:, 8896:8960]
        t16 = Cm[:, 8960:8976]
        fT = Cm[:, :].bitcast(BF16).rearrange("p (a k t) -> p a k t", a=3, k=KC)
        gbc = Cm[:, 0:2048]
        yout = [Cm[:, 2048 + s * 2048:2048 + (s + 1) * 2048] for s in range(3)]
        Sflat = S[:, 0:3, :].rearrange("p a t -> p (a t)")

        def pcol(c):
            return prm[:, c:c + 1]

        def emit(P, W):
            B = {}

            def bf(n):
                if n not in B:
                    B[n] = Buf(n)
                return B[n]

            mixbufs_ref = [None]
            pb = [bf("pb%d" % i) for i in range(8)]
            for b_ in pb:
                b_.excl = True
            xTb = [bf("xT%d" % t) for t in range(3)]
            state = {"ring": 0, "alt": 0}

            def ring3():
                b = state["ring"] % 3
                state["ring"] += 1
                return b

            def alt_eng():
                state["alt"] += 1
                return "act" if state["alt"] % 2 else "dve"

            def copy_op(eng, out, in_, reads, writes):
                if eng == "act":
                    return P.op("act", lambda e: e.activation(out=out, in_=in_, func=AF.Copy), reads=reads, writes=writes)
                return P.op("dve", lambda e: e.tensor_copy(out=out, in_=in_), reads=reads, writes=writes)

            outs = []
            P.dma("sp", lambda e: e.dma_start(out=prm[:], in_=prm_d[:, :]), "c0", writes=[bf("prm")])
            P.dma("sp", lambda e: e.dma_start(out=idf[:], in_=ident_d[:, :]), "c1", writes=[bf("idf")])
            P.op("dve", lambda e: e.tensor_copy(out=idb[:], in_=idf[:]), reads=[bf("idf")], writes=[bf("idb")])
            P.op("dve", lambda e: e.memset(ones[:], 1.0), writes=[bf("ones")])
            for l in range(DEPTH if "nopt" not in _DBG else 0):
                outs.append(P.dma("sp", lambda e, l=l: e.dma_start(out=ocs_d[l, :, 0:26, :], in_=sconv_d[l, :, 4:30, :]), "pt"))
                outs.append(P.dma("sp", lambda e, l=l: e.dma_start(out=ops_d[l, :, 0:11, :], in_=spool_d[l, :, 4:15, :]), "pt"))

            xinb = [bf("xin%d" % i) for i in range(4)]
            bank = 0
            for i in range(T // 128):
                s = i % 4
                P.dma("sp", lambda e, i=i, s=s: e.dma_start(out=xin[s], in_=x_d[i * 128:(i + 1) * 128, :]),
                      "xi%d" % s, writes=[xinb[s]])
                for q in range(4):
                    bk = bank % 8
                    bank += 1
                    fns = [lambda e, s=s, q=q, j=j, bk=bk: e.transpose(
                        out=ps[:, bk, j * 128:(j + 1) * 128], in_=xin[s][:, (4 * q + j) * 128:(4 * q + j + 1) * 128], identity=idf[:])
                        for j in range(4)]
                    P.group(fns, reads=[xinb[s], bf("idf")], writes=[pb[bk]])
                    copy_op(alt_eng(), xT[:, 4 * q:4 * q + 4, i * 128:(i + 1) * 128],
                            ps[:, bk, :].rearrange("p (j t) -> p j t", t=128), [pb[bk]], [xTb[i // 3]])
            cbufs = list(xinb)

            def rms_stats(l_g_unused, src_cols, sq_view, sq_buf, bank_i, out_slot, out_buf, xbufs):
                for k in range(KC):
                    P.op("act", lambda e, k=k: e.activation(out=sq_view[:, k, :], in_=xT[:, k, src_cols[0]:src_cols[1]], func=AF.Square),
                         reads=xbufs, writes=[sq_buf])
                n = src_cols[1] - src_cols[0]
                fns = [lambda e, k=k: e.matmul(out=ps[:, bank_i, 0:n], lhsT=ones[:], rhs=sq_view[:, k, :], start=(k == 0), stop=(k == KC - 1))
                       for k in range(KC)]
                P.group(fns, reads=[sq_buf, bf("ones")], writes=[pb[bank_i]])
                rstd_finish(bank_i, n, out_slot, out_buf)

            def rstd_finish(bank_i, n, out_slot, out_buf):
                if out_slot is None:
                    out_slot, out_buf = ps[:, bank_i, 0:n], pb[bank_i]
                P.op("act", lambda e: e.activation(out=out_slot, in_=ps[:, bank_i, 0:n], func=AF.Sqrt, bias=EPS, scale=1.0 / D),
                     reads=[pb[bank_i]], writes=[out_buf])
                P.op("dve", lambda e: e.reciprocal(out=out_slot, in_=out_slot), reads=[out_buf], writes=[out_buf])

            def rms1_pre(l, t):
                c0 = t * TN
                rms_stats(None, (c0, c0 + TN), sqb, bf("sqb"), 7, None, None, [xTb[t]])
                for k in range(KC):
                    P.op("dve", lambda e, k=k: e.scalar_tensor_tensor(
                        out=hT[:, k, :], in0=xT[:, k, c0:c0 + TN], scalar=pcol(l * PL + O_G1 + k), in1=ps[:, 7, 0:TN],
                        op0=ALU.mult, op1=ALU.mult), reads=[xTb[t], pb[7], bf("prm")], writes=[bf("hT")])

            def mm_group(slot, wbuf, rhs_of_k, nk, bank_i, n, reads):
                fns = [lambda e, k=k: e.matmul(out=ps[:, bank_i, 0:n], lhsT=wt[:, slot, k, :], rhs=rhs_of_k(k),
                                               start=(k == 0), stop=(k == nk - 1)) for k in range(nk)]
                return P.group(fns, reads=[wbuf] + reads, writes=[pb[bank_i]])

            def stt(out, in0, scalar, in1, op0, op1, reads, writes):
                return P.op("dve", lambda e: e.scalar_tensor_tensor(out=out, in0=in0, scalar=scalar, in1=in1, op0=op0, op1=op1),
                            reads=reads, writes=writes)

            def ts1(out, in0, s1, op0, reads, writes):
                return P.op("dve", lambda e: e.tensor_scalar(out=out, in0=in0, scalar1=s1, scalar2=None, op0=op0),
                            reads=reads, writes=writes)

            def ts2(out, in0, s1, s2, op0, op1, reads, writes):
                return P.op("dve", lambda e: e.tensor_scalar(out=out, in0=in0, scalar1=s1, scalar2=s2, op0=op0, op1=op1),
                            reads=reads, writes=writes)

            def act(out, in_, func, reads, writes, bias=None, scale=None):
                kw = {}
                if bias is not None:
                    kw["bias"] = bias
                if scale is not None:
                    kw["scale"] = scale
                return P.op("act", lambda e: e.activation(out=out, in_=in_, func=func, **kw), reads=reads, writes=writes)

            def mixer(l, t):
                L0 = l * PL
                c0 = t * TN
                npr = TN if t < 2 else TN - NSM
                prmb = bf("prm")
                hTb, mixb = bf("hT"), bf("mix")
                vb, vbfb, sqvb = bf("v"), bf("vbf"), bf("sqv")
                sigb = [bf("sig0"), bf("sig1")]
                ubb = [bf("ub0"), bf("ub1"), bf("ub2")]

                nxt = t + 1 if t < 2 else None
                dgb = [bf("diagA"), bf("diagB")]
                halves = [(0, 16), (16, 31)]

                def conv_build(c):
                    for hi, (k0, k1) in enumerate(halves):
                        nk = k1 - k0
                        P.op("dve", lambda e, k0=k0, k1=k1, nk=nk: e.tensor_tensor(
                            out=diag[:, k0:k1, :], in0=ps[:, 4, 0:128].unsqueeze(1).broadcast_to([128, nk, 128]),
                            in1=prm[:, L0 + O_CW + c * 31 + k0:L0 + O_CW + c * 31 + k1].unsqueeze(2).broadcast_to([128, nk, 128]),
                            op=ALU.mult), reads=[pb[4], prmb], writes=[dgb[hi]])

                def conv(c):
                    ui = c % 3
                    ub = ubuf[:, ui, :]
                    for hi, (k0, k1) in enumerate(halves):
                        fns = [lambda e, k=k: e.matmul(out=ps[:, 3, 0:npr], lhsT=diag[:, k, :], rhs=ub[:, 2 + k:2 + k + npr],
                                                       start=(k == 0), stop=(k == 30)) for k in range(k0, k1)]
                        P.group(fns, reads=[dgb[hi], ubb[ui]], writes=[pb[3]])
                    if t == 2:
                        fns = [lambda e, k=k: e.matmul(out=ps[:, 3, npr:TN], lhsT=diag[:, k, :], rhs=usamp[:, c, :, k:k + 4],
                                                       start=(k == 0), stop=(k == 30), skip_group_check=True) for k in range(31)]
                        P.group(fns, reads=[dgb[0], dgb[1], bf("usamp%d" % c)], writes=[pb[3]])
                    cb = pcol(L0 + O_CB + c)
                    act(vv[:, c, :], ps[:, 3, 0:TN], AF.Identity, [pb[3], prmb], [vb], bias=cb)
                    ts1(vbf[:, c, :], ps[:, 3, 0:TN], cb, ALU.add, [pb[3], prmb], [vbfb])
                    act(sqv[:, c, :], ps[:, 3, 0:TN], AF.Square, [pb[3], prmb], [sqvb], bias=cb)

                for c in range(8):
                    ui = c % 3
                    ub = ubuf[:, ui, :]
                    if c > 0:
                        conv_build(c - 1)
                    sa, wa = W.get(w_in_v[l][:, :, c * 128:(c + 1) * 128], KC)
                    ba = ring3()
                    mm_group(sa, wa, lambda k: hT[:, k, :], KC, ba, TN, [hTb])
                    sg, wg = W.get(w_in_v[l][:, :, 1024 + c * 128:1024 + (c + 1) * 128], KC)
                    bg = ring3()
                    mm_group(sg, wg, lambda k: hT[:, k, :], KC, bg, TN, [hTb])
                    si = c % 2
                    act(sig[si], ps[:, bg, 0:TN], AF.Sigmoid, [pb[bg], prmb], [sigb[si]], bias=pcol(L0 + O_BIN + 8 + c))
                    ba_col = pcol(L0 + O_BIN + c)
                    rdu = [pb[ba], sigb[si], prmb]
                    if t == 0:
                        P.op("dve", lambda e, ub=ub: e.memset(ub[:, 0:32], 0.0), writes=[ubb[ui]])
                    else:
                        P.op("dve", lambda e, ub=ub, c=c: e.tensor_copy(out=ub[:, 0:32], in_=hsave[:, c, :]),
                             reads=[bf("hsave%d" % c)], writes=[ubb[ui]])
                    if t == 0:
                        stt(t64, ps[:, ba, 0:HALO], ba_col, sig[si][:, 0:HALO], ALU.add, ALU.mult, rdu, [bf("t64")])
                        ts1(ub[:, 32:32 + HALO], t64, pcol(O_HM), ALU.mult, [bf("t64"), prmb], [ubb[ui]])
                        stt(ub[:, 32 + HALO:32 + TN], ps[:, ba, HALO:TN], ba_col, sig[si][:, HALO:TN], ALU.add, ALU.mult, rdu, [ubb[ui]])
                    else:
                        stt(ub[:, 32:32 + npr], ps[:, ba, 0:npr], ba_col, sig[si][:, 0:npr], ALU.add, ALU.mult, rdu, [ubb[ui]])
                    if t == 2:
                        stt(usamp[:, c, :, 30:34], ps[:, ba, npr:TN].rearrange("p (b j) -> p b j", j=4), ba_col,
                            sig[si][:, npr:TN].rearrange("p (b j) -> p b j", j=4), ALU.add, ALU.mult, rdu, [bf("usamp%d" % c)])
                        stt(utail[:, c, :], ps[:, ba, TN - 94:TN], ba_col, sig[si][:, TN - 94:TN], ALU.add, ALU.mult, rdu, [bf("utail")])
                    else:
                        P.op("dve", lambda e, ub=ub, c=c: e.tensor_copy(out=hsave[:, c, :], in_=ub[:, TN:TN + 32]),
                             reads=[ubb[ui]], writes=[bf("hsave%d" % c)])
                    if nxt is not None:
                        n0 = nxt * TN
                        for k in (2 * c, 2 * c + 1):
                            P.op("act", lambda e, k=k, n0=n0: e.activation(out=sqb[:, k, :], in_=xT[:, k, n0:n0 + TN], func=AF.Square),
                                 reads=[xTb[nxt]], writes=[bf("sqb")])
                    if c > 0:
                        conv(c - 1)
                conv_build(7)
                conv(7)

                def pool_A(g):
                    bz = []
                    for j in range(2):
                        m = 2 * g + j
                        sz, wz = W.get(w_in_v[l][:, :, 2048 + m * 128:2048 + (m + 1) * 128], KC)
                        b_ = ring3()
                        bz.append(b_)
                        mm_group(sz, wz, lambda k: hT[:, k, :], KC, b_, TN, [hTb])
                        bzc = pcol(L0 + O_BIN + 16 + m)
                        zfB, zbB = bf("zf%d" % j), bf("zb%d" % j)
                        act(zf[:, j, :], ps[:, b_, 0:TN], AF.Identity, [pb[b_], prmb], [zfB], bias=bzc)
                        if t == 0:
                            P.op("dve", lambda e, j=j: e.memset(zb[:, j, 0:16], 0.0), writes=[zbB])
                            ts2(zb[:, j, 16:16 + HALO], ps[:, b_, 0:HALO], bzc, pcol(O_HM), ALU.add, ALU.mult, [pb[b_], prmb], [zbB])
                            ts1(zb[:, j, 16 + HALO:16 + TN], ps[:, b_, HALO:TN], bzc, ALU.add, [pb[b_], prmb], [zbB])
                        else:
                            P.op("dve", lambda e, j=j, m=m: e.tensor_copy(out=zb[:, j, 0:16], in_=zsave[:, m, :]),
                                 reads=[bf("zsave%d" % m)], writes=[zbB])
                            ts1(zb[:, j, 16:16 + npr], ps[:, b_, 0:npr], bzc, ALU.add, [pb[b_], prmb], [zbB])
                        if t == 2:
                            ts1(zsamp[:, m, :, 15:19], ps[:, b_, npr:TN].rearrange("p (b j) -> p b j", j=4), bzc, ALU.add,
                                [pb[b_], prmb], [bf("zsamp%d" % m)])
                            ts1(ztail[:, m, :], ps[:, b_, TN - 94:TN], bzc, ALU.add, [pb[b_], prmb], [bf("ztail")])
                        else:
                            P.op("dve", lambda e, j=j, m=m: e.tensor_copy(out=zsave[:, m, :], in_=zb[:, j, TN:TN + 16]),
                                 reads=[zbB], writes=[bf("zsave%d" % m)])

                def pool_B(g):
                    w = 2 << g
                    db = g % 2
                    dB = bf("d%d" % db)
                    for j in range(2):
                        m = 2 * g + j
                        zfB, zbB = bf("zf%d" % j), bf("zb%d" % j)
                        bp = ring3()
                        fns = [lambda e, k=k, j=j, bp=bp, w=w: e.matmul(out=ps[:, bp, 0:npr], lhsT=idb[:], rhs=zb[:, j, 16 - k:16 - k + npr],
                                                                   start=(k == 0), stop=(k == w - 1)) for k in range(w)]
                        rd = [bf("idb"), zbB]
                        if t == 2:
                            fns += [lambda e, k=k, m=m, bp=bp, w=w: e.matmul(out=ps[:, bp, npr:TN], lhsT=idb[:], rhs=zsamp[:, m, :, 15 - k:19 - k],
                                                                        start=(k == 0), stop=(k == w - 1), skip_group_check=True) for k in range(w)]
                            rd.append(bf("zsamp%d" % m))
                        P.group(fns, reads=rd, writes=[pb[bp]])
                        rdd = [pb[bp], zfB]
                        if t == 0:
                            stt(dbuf[:, db, j, 0:HALO], ps[:, bp, 0:HALO], 1.0 / w, zf[:, j, 0:HALO], ALU.mult, ALU.subtract, rdd, [dB])
                            P.op("dve", lambda e, bp=bp, g=g: e.tensor_tensor(out=t16, in0=ps[:, bp, HALO:HALO + 16],
                                                                              in1=prm[:, O_IC + g * 16:O_IC + (g + 1) * 16], op=ALU.mult),
                                 reads=[pb[bp], prmb], writes=[bf("t16")])
                            P.op("dve", lambda e, j=j, db=db: e.tensor_tensor(out=dbuf[:, db, j, HALO:HALO + 16], in0=t16,
                                                                              in1=zf[:, j, HALO:HALO + 16], op=ALU.subtract),
                                 reads=[bf("t16"), zfB], writes=[dB])
                            stt(dbuf[:, db, j, HALO + 16:TN], ps[:, bp, HALO + 16:TN], 1.0 / w, zf[:, j, HALO + 16:TN],
                                ALU.mult, ALU.subtract, rdd, [dB])
                        else:
                            stt(dbuf[:, db, j, :], ps[:, bp, 0:TN], 1.0 / w, zf[:, j, :], ALU.mult, ALU.subtract, rdd, [dB])

                def pool_C(g):
                    db = g % 2
                    dB = bf("d%d" % db)
                    for e_ in range(2):
                        m = 2 * g + e_
                        sp_, wp = W.get(pool_w_v[l][g][:, :, e_ * 128:(e_ + 1) * 128], 2)
                        bq = ring3()
                        mm_group(sp_, wp, lambda k, db=db: dbuf[:, db, k, :], 2, bq, TN, [dB])
                        act(mix[:, 8 + m, :], ps[:, bq, 0:TN], AF.Copy, [pb[bq], prmb], [mixb], scale=pcol(L0 + O_PSC + m))

                fns = [lambda e, c=c: e.matmul(out=ps[:, 5, 0:TN], lhsT=ones[:], rhs=vbf[:, c, :], start=(c == 0), stop=(c == 7)) for c in range(8)]
                P.group(fns, reads=[vbfb, bf("ones")], writes=[pb[5]])
                fns = [lambda e, c=c: e.matmul(out=ps[:, 6, 0:TN], lhsT=ones[:], rhs=sqv[:, c, :], start=(c == 0), stop=(c == 7)) for c in range(8)]
                P.group(fns, reads=[sqvb, bf("ones")], writes=[pb[6]])
                if nxt is not None:
                    fns = [lambda e, k=k: e.matmul(out=ps[:, 7, 0:TN], lhsT=ones[:], rhs=sqb[:, k, :], start=(k == 0), stop=(k == KC - 1))
                           for k in range(KC)]
                    P.group(fns, reads=[bf("sqb"), bf("ones")], writes=[pb[7]])
                S1b = bf("S1")
                act(ps[:, 5, 0:TN], ps[:, 5, 0:TN], AF.Copy, [pb[5]], [pb[5]], scale=1.0 / 1024)
                act(S[:, 1, :], ps[:, 5, 0:TN], AF.Square, [pb[5]], [S1b])
                stt(ps[:, 6, 0:TN], ps[:, 6, 0:TN], 1.0 / 1024, S[:, 1, :], ALU.mult, ALU.subtract, [pb[6], S1b], [pb[6]])
                act(ps[:, 6, 0:TN], ps[:, 6, 0:TN], AF.Sqrt, [pb[6]], [pb[6]], bias=EPS)
                P.op("dve", lambda e: e.reciprocal(out=ps[:, 6, 0:TN], in_=ps[:, 6, 0:TN]), reads=[pb[6]], writes=[pb[6]])
                if nxt is not None:
                    rstd_finish(7, TN, None, None)
                def ln_apply(c):
                    sl = 2 + c % 2
                    Sb = bf("S%d" % sl)
                    P.op("dve", lambda e, c=c, sl=sl: e.tensor_tensor(out=S[:, sl, :], in0=vv[:, c, :], in1=ps[:, 5, 0:TN], op=ALU.subtract),
                         reads=[vb, pb[5]], writes=[Sb])
                    P.op("dve", lambda e, sl=sl: e.tensor_tensor(out=S[:, sl, :], in0=S[:, sl, :], in1=ps[:, 6, 0:TN], op=ALU.mult),
                         reads=[Sb, pb[6]], writes=[Sb])
                    act(mix[:, c, :], S[:, sl, :], AF.Silu, [Sb, prmb], [mixb], bias=pcol(L0 + O_LNB + c), scale=pcol(L0 + O_LNG + c))

                early2 = (t == 2 and "noffn" not in _DBG)
                sqCb = [bf("sqC%d" % i) for i in range(3)]
                if early2:
                    handoff([vbfb, sqvb], [sqCb[1]])
                    for k in range(KC):
                        P.op("act", lambda e, k=k: e.activation(out=fT[:, 1, k, :], in_=xT[:, k, TN:2 * TN], func=AF.Square),
                             reads=[xTb[1]], writes=[sqCb[1]])
                seq = ["A0", "B0", "A1", "C0", "B1", "A2", "C1", "B2", "A3", "C2", "B3", "C3"]
                for st_ in seq:
                    g = int(st_[1])
                    if st_[0] == "A":
                        pool_A(g)
                        if early2 and g == 1:
                            fns = [lambda e, k=k: e.matmul(out=ps[:, 7, 0:TN], lhsT=ones[:], rhs=fT[:, 1, k, :], start=(k == 0), stop=(k == KC - 1))
                                   for k in range(KC)]
                            P.group(fns, reads=[sqCb[1], bf("ones")], writes=[pb[7]])
                            rstd_finish(7, TN, None, None)
                    elif st_[0] == "B":
                        pool_B(g)
                        ln_apply(2 * g)
                        ln_apply(2 * g + 1)
                    else:
                        pool_C(g)
                if early2:
                    handoff([vb], [sqCb[0]])
                    handoff([bf(n) for n in ("diagA", "diagB", "sig0", "sig1", "t64", "t16")], [sqCb[2]])
                    for k in range(KC):
                        P.op("act", lambda e, k=k: e.activation(out=fT[:, 0, k, :], in_=xT[:, k, 0:TN], func=AF.Square),
                             reads=[xTb[0]], writes=[sqCb[0]])
                for m in range(KC):
                    so, wo = W.get(w_out_v[l][:, :, m * 128:(m + 1) * 128], KC)
                    bo = ring3()
                    mm_group(so, wo, lambda k: mix[:, k, :], KC, bo, TN, [mixb])
                    P.op("dve", lambda e, m=m, bo=bo: e.tensor_tensor(out=xT[:, m, c0:c0 + TN], in0=xT[:, m, c0:c0 + TN], in1=ps[:, bo, 0:TN], op=ALU.add),
                         reads=[pb[bo], xTb[t]], writes=[xTb[t]])
                    if nxt is not None:
                        n0 = nxt * TN
                        stt(hT[:, m, :], xT[:, m, n0:n0 + TN], pcol(L0 + O_G1 + m), ps[:, 7, 0:TN], ALU.mult, ALU.mult,
                            [xTb[nxt], pb[7], prmb], [hTb])
                    elif early2:
                        P.op("act", lambda e, m=m: e.activation(out=fT[:, 2, m, :], in_=xT[:, m, c0:c0 + TN], func=AF.Square),
                             reads=[xTb[2]], writes=[sqCb[2]])
                        stt(h2T[:, TMAP[1], m, :], xT[:, m, TN:2 * TN], pcol(L0 + O_G2 + m), ps[:, 7, 0:TN], ALU.mult, ALU.mult,
                            [xTb[1], pb[7], prmb], [bf("sqb")])
                        if m == 4:
                            fns = [lambda e, k=k: e.matmul(out=ps[:, 5, 0:TN], lhsT=ones[:], rhs=fT[:, 0, k, :], start=(k == 0), stop=(k == KC - 1))
                                   for k in range(KC)]
                            P.group(fns, reads=[sqCb[0], bf("ones")], writes=[pb[5]])
                            rstd_finish(5, TN, None, None)
                        if 5 <= m <= 12:
                            for k in (2 * (m - 5), 2 * (m - 5) + 1):
                                stt(h2T[:, TMAP[0], k, :], xT[:, k, 0:TN], pcol(L0 + O_G2 + k), ps[:, 5, 0:TN], ALU.mult, ALU.mult,
                                    [xTb[0], pb[5], prmb], [hTb])
                if t == 2 and "notails" not in _DBG:
                    Sall6 = [bf("S%d" % i) for i in range(6)]
                    for a, (tl, tlb) in enumerate([(utail, bf("utail")), (ztail, bf("ztail"))]):
                        for q in range(2):
                            bk = (2 * a + q) % 4
                            fns = [lambda e, tl=tl, q=q, i=i, bk=bk: e.transpose(out=ps[0:94, bk, i * 128:(i + 1) * 128], in_=tl[:, 4 * q + i, :], identity=idf[:])
                                   for i in range(4)]
                            P.group(fns, reads=[tlb, bf("idf")], writes=[pb[bk]])
                            copy_op(alt_eng(), tstage[0:94, a, q * 512:(q + 1) * 512], ps[0:94, bk, :], [pb[bk]], Sall6)
                    outs.append(P.dma("sp", lambda e: e.dma_start(out=ocp_d[l, :, :], in_=tstage[0:30, 0, :]), "to", reads=Sall6))
                    outs.append(P.dma("sp", lambda e: e.dma_start(out=opp_d[l, :, :], in_=tstage[15:30, 1, :]), "to", reads=Sall6))
                    for b in range(SBQ):
                        outs.append(P.dma("sp", lambda e, b=b: e.dma_start(out=ocs_d[l, b, 26:30, :], in_=tstage[30 + 4 * b:34 + 4 * b, 0, :]), "to", reads=Sall6))
                        outs.append(P.dma("sp", lambda e, b=b: e.dma_start(out=ops_d[l, b, 11:15, :], in_=tstage[30 + 4 * b:34 + 4 * b, 1, :]), "to", reads=Sall6))

            def ffn(l):
                L0 = l * PL
                prmb = bf("prm")
                h2b, fCb = bf("h2T"), bf("fC")
                Sb = [bf("S%d" % i) for i in range(6)]
                early = "nomixer" not in _DBG and "t01" not in _DBG
                sqCb = [bf("sqC%d" % i) for i in range(3)]
                bks = [5, 7, 6]
                for t in range(3):
                    c0 = t * TN
                    bk = bks[t]
                    if not early:
                        for k in range(KC):
                            P.op("act", lambda e, k=k, t=t, c0=c0: e.activation(out=fT[:, t, k, :], in_=xT[:, k, c0:c0 + TN], func=AF.Square),
                                 reads=[xTb[t]], writes=[sqCb[t]])
                    if not early or t == 2:
                        fns = [lambda e, k=k, t=t, bk=bk: e.matmul(out=ps[:, bk, 0:TN], lhsT=ones[:], rhs=fT[:, t, k, :], start=(k == 0), stop=(k == KC - 1))
                               for k in range(KC)]
                        P.group(fns, reads=[sqCb[t], bf("ones")], writes=[pb[bk]])
                        rstd_finish(bk, TN, None, None)
                        for k in range(KC):
                            stt(h2T[:, TMAP[t], k, :], xT[:, k, c0:c0 + TN], pcol(L0 + O_G2 + k), ps[:, bk, 0:TN], ALU.mult, ALU.mult,
                                [xTb[t], pb[bk], prmb], [h2b])
                handoff(sqCb, [fCb])
                slot3 = 0
                sc_i = 0
                tr = [((HALO if l == DEPTH - 1 else HALO - 32) if t == 0 else 0, TN) for t in range(3)]
                for j in range(DFF // FB):
                    for m in range(KC):
                        s1, w1 = W.get(w_ff1_v[l][:, :, j * FB + m * 128:j * FB + (m + 1) * 128], KC)
                        b0 = 3 * (slot3 % 2)
                        slot3 += 1
                        fns = [lambda e, k=k, t=t, s1=s1, b0=b0: e.matmul(out=ps[:, b0 + t, 0:tr[t][1] - tr[t][0]], lhsT=wt[:, s1, k, :],
                                                                        rhs=h2T[:, TMAP[t], k, tr[t][0]:tr[t][1]],
                                                                        start=(k == 0), stop=(k == KC - 1)) for k in range(KC) for t in range(3)]
                        P.group(fns, reads=[w1, h2b], writes=[pb[b0], pb[b0 + 1], pb[b0 + 2]])
                        for t in range(3):
                            o0, o1 = tr[t]
                            si = 3 + sc_i % 3
                            sc_i += 1
                            act(S[:, si, 0:o1 - o0], ps[:, b0 + t, 0:o1 - o0], AF.Square, [pb[b0 + t]], [Sb[si]])
                            stt(fT[:, t, m, o0:o1], ps[:, b0 + t, 0:o1 - o0], 0.0, S[:, si, 0:o1 - o0], ALU.is_gt, ALU.mult, [pb[b0 + t], Sb[si]], [fCb])
                    for m in range(KC):
                        s2, w2 = W.get(w_ff2_v[l][j][:, :, m * 128:(m + 1) * 128], KC)
                        b0 = 3 * (slot3 % 2)
                        slot3 += 1
                        fns = [lambda e, k=k, t=t, s2=s2, b0=b0: e.matmul(out=ps[:, b0 + t, 0:tr[t][1] - tr[t][0]], lhsT=wt[:, s2, k, :],
                                                                        rhs=fT[:, t, k, tr[t][0]:tr[t][1]],
                                                                        start=(k == 0), stop=(k == KC - 1)) for k in range(KC) for t in range(3)]
                        P.group(fns, reads=[w2, fCb], writes=[pb[b0], pb[b0 + 1], pb[b0 + 2]])
                        for t in range(3):
                            o0, o1 = tr[t]
                            c0 = t * TN + o0
                            nn = o1 - o0
                            P.op("dve", lambda e, m=m, t=t, b0=b0, c0=c0, nn=nn: e.tensor_tensor(out=xT[:, m, c0:c0 + nn], in0=xT[:, m, c0:c0 + nn],
                                                                                             in1=ps[:, b0 + t, 0:nn], op=ALU.add),
                                 reads=[pb[b0 + t], xTb[t]], writes=[xTb[t]])
                return [h2b, fCb]

            for l in range(DEPTH if "nolayers" not in _DBG else 0):
                if l > 0:
                    handoff([bf("h2T")], [bf("hT"), bf("mix"), bf("sqb")])
                P.group([lambda e: e.transpose(out=ps[:, 4, 0:128], in_=idf[:], identity=idf[:])], reads=[bf("idf")], writes=[pb[4]])
                rms1_pre(l, 0)
                stb = bf("stage")
                handoff(cbufs, [stb])
                for gq in range(4):
                    P.dma("sp", lambda e, l=l, gq=gq: e.dma_start(out=stgc[0:120, gq, :],
                                                                  in_=sconv_d[l, 4 * gq:4 * gq + 4, :, :].rearrange("b j c -> (b j) c")),
                          "sg", writes=[stb])
                for gp in range(2):
                    P.dma("sp", lambda e, l=l, gp=gp: e.dma_start(out=stgp[0:120, gp, :],
                                                                  in_=spool_d[l, 8 * gp:8 * gp + 8, :, :].rearrange("b j c -> (b j) c")),
                          "sg", writes=[stb])
                hbanks = [0, 1, 2, 3, 5, 6]
                bank = 0
                for c in range(8):
                    bk = hbanks[bank % 6]
                    bank += 1
                    fns = [lambda e, c=c, gq=gq, bk=bk: e.transpose(out=ps[:, bk, gq * 120:(gq + 1) * 120], in_=stgc[0:120, gq, c * 128:(c + 1) * 128],
                                                                   identity=idf[0:120, 0:120]) for gq in range(4)]
                    P.group(fns, reads=[stb, bf("idf")], writes=[pb[bk]])
                    copy_op(alt_eng(), usamp[:, c, :, 0:30], ps[:, bk, 0:480].rearrange("p (b j) -> p b j", j=30), [pb[bk]], [bf("usamp%d" % c)])
                    bk = hbanks[bank % 6]
                    bank += 1
                    fns = [lambda e, c=c, gp=gp, bk=bk: e.transpose(out=ps[:, bk, gp * 120:(gp + 1) * 120], in_=stgp[0:120, gp, c * 128:(c + 1) * 128],
                                                                   identity=idf[0:120, 0:120]) for gp in range(2)]
                    P.group(fns, reads=[stb, bf("idf")], writes=[pb[bk]])
                    copy_op(alt_eng(), zsamp[:, c, :, 0:15], ps[:, bk, 0:240].rearrange("p (b j) -> p b j", j=15), [pb[bk]], [bf("zsamp%d" % c)])
                mixbufs = [bf(n) for n in ("v", "vbf", "sqv", "diagA", "diagB", "sig0", "sig1", "t64", "t16")]
                handoff([stb], mixbufs)
                mixbufs_ref[0] = mixbufs
                for t in range((2 if "t01" in _DBG else 3) if "nomixer" not in _DBG else 0):
                    mixer(l, t)
                handoff([bf("hT"), bf("mix"), bf("sqb")], [bf("h2T")])
                if "nomixer" in _DBG or "noffn" in _DBG or "t01" in _DBG:
                    handoff(mixbufs, [bf("sqC%d" % i) for i in range(3)])
                cbufs = ffn(l)[1:] if "noffn" not in _DBG else [bf("sqC%d" % i) for i in range(3)]

            gB, yob = bf("gbc"), [bf("yout%d" % i) for i in range(3)]
            sq2b = [bf("sqh0"), bf("sqh1")]
            handoff(cbufs, [gB] + yob)
            if "nolayers" not in _DBG:
                handoff([bf("h2T")], sq2b)
            P.dma("sp", lambda e: e.dma_start(out=gbc, in_=gfb_d[:, :]), "c0", writes=[gB])
            rtb = bf("S5")
            bank = 0
            for i in range(9):
                n = 128 if i < 8 else 64
                col0 = HALO + 128 * i
                s2, s3 = i % 2, i % 3
                sbk = 6 + i % 2
                xb_ = sorted({col0 // TN, (col0 + n - 1) // TN})
                xbs = [xTb[j] for j in xb_]
                for k in range(KC):
                    P.op("act", lambda e, k=k, s2=s2, n=n, col0=col0: e.activation(out=sqb[:, k, s2 * 128:s2 * 128 + n], in_=xT[:, k, col0:col0 + n], func=AF.Square),
                         reads=xbs, writes=[sq2b[s2]])
                fns = [lambda e, k=k, s2=s2, n=n, sbk=sbk, i=i: e.matmul(out=ps[0:n, sbk, i:i + 1], lhsT=sqb[:, k, s2 * 128:s2 * 128 + n], rhs=ones[:, 0:1],
                                                                       start=(k == 0), stop=(k == KC - 1), skip_group_check=True) for k in range(KC)]
                P.group(fns, reads=[sq2b[s2], bf("ones")], writes=[pb[sbk]])
                P.op("act", lambda e, n=n, sbk=sbk, i=i: e.activation(out=S[0:n, 5, i:i + 1], in_=ps[0:n, sbk, i:i + 1], func=AF.Sqrt, bias=EPS, scale=1.0 / D),
                     reads=[pb[sbk]], writes=[rtb])
                P.op("dve", lambda e, n=n, i=i: e.reciprocal(out=S[0:n, 5, i:i + 1], in_=S[0:n, 5, i:i + 1]), reads=[rtb], writes=[rtb])
                for q in range(4):
                    bk = bank % 6
                    bank += 1
                    fns = [lambda e, q=q, j=j, bk=bk, n=n, col0=col0: e.transpose(out=ps[0:n, bk, j * 128:(j + 1) * 128], in_=xT[:, 4 * q + j, col0:col0 + n], identity=idf[:])
                           for j in range(4)]
                    P.group(fns, reads=xbs + [bf("idf")], writes=[pb[bk]])
                    stt(yout[s3][0:n, q * 512:(q + 1) * 512], ps[0:n, bk, :], S[0:n, 5, i:i + 1], gbc[0:n, q * 512:(q + 1) * 512], ALU.mult, ALU.mult,
                        [pb[bk], rtb, gB], [yob[s3]])
                outs.append(P.dma("sp", lambda e, i=i, n=n, s3=s3: e.dma_start(out=y_d[i * 128:i * 128 + n, :], in_=yout[s3][0:n, :]),
                                  "yo%d" % s3, reads=[yob[s3]]))
            last = {}
            for h in outs:
                if last.get(h[0], 0) < h[1]:
                    last[h[0]] = h[1]
            P.wait("sp", list(last.items()))

        Wd = WStream(Prog(), wt, plan=None)
        emit(Wd.P, Wd)
        P = Prog()
        W = WStream(P, wt, plan=Wd.req)
        emit(P, W)
        assert W.i == len(Wd.req)

        sems = {n: st.enter_context(nc.semaphore(n)) for n in P.sem_names()}
        block = st.enter_context(nc.Block())

        @block.tensor
        def _(e):
            P.replay("pe", e, sems)

        @block.scalar
        def _(e):
            P.replay("act", e, sems)

        @block.vector
        def _(e):
            P.replay("dve", e, sems)

        @block.gpsimd
        def _(e):
            P.replay("pool", e, sems)

        @block.sync
        def _(e):
            P.replay("sp", e, sems)
    return nc


def _pack_params(norm1_g, b_in, conv_w, conv_b, ln_g, ln_b, pool_scale, norm2_g, norm_f, half):
    prm = np.zeros((128, NPRM), np.float32)

    def cols(v):
        return np.ascontiguousarray(np.asarray(v, np.float32).reshape(-1, 128).T)

    for l in range(DEPTH):
        L0 = l * PL
        prm[:, L0 + O_G1:L0 + O_G1 + 16] = cols(norm1_g[l])
        prm[:, L0 + O_BIN:L0 + O_BIN + 24] = cols(b_in[l])
        cw = np.asarray(conv_w[l], np.float32)
        prm[:, L0 + O_CW:L0 + O_CW + 248] = cw.reshape(31, 8, 128).transpose(2, 1, 0).reshape(128, 248)
        prm[:, L0 + O_CB:L0 + O_CB + 8] = cols(conv_b[l])
        prm[:, L0 + O_LNG:L0 + O_LNG + 8] = cols(ln_g[l])
        prm[:, L0 + O_LNB:L0 + O_LNB + 8] = cols(ln_b[l])
        prm[:, L0 + O_PSC:L0 + O_PSC + 8] = cols(pool_scale[l])
        prm[:, L0 + O_G2:L0 + O_G2 + 16] = cols(norm2_g[l])
    prm[:, O_GF:O_GF + 16] = cols(norm_f)
    prm[:, O_HM] = float(half)
    pos = np.arange(16)
    for g in range(4):
        w = 2 << g
        cnt = np.minimum(w, pos + 1) if half == 0 else np.full(16, w)
        prm[:, O_IC + g * 16:O_IC + (g + 1) * 16] = (1.0 / cnt.astype(np.float32))[None, :]
    return prm


_NC_CACHE = {}


def kernel(x_prompt, x_sample, state_conv, state_pool, norm1_g, w_in, b_in, conv_w, conv_b,
           ln_g, ln_b, pool_w, pool_scale, w_out, norm2_g, w_ff1, w_ff2, norm_f):
    f = lambda a: np.ascontiguousarray(np.asarray(a, dtype=np.float32))
    x_prompt, x_sample, state_conv, state_pool = f(x_prompt), f(x_sample), f(state_conv), f(state_pool)
    w_in, pool_w, w_out, w_ff1, w_ff2 = f(w_in), f(pool_w), f(w_out), f(w_ff1), f(w_ff2)
    if "nc" not in _NC_CACHE:
        _NC_CACHE["nc"] = build_program()
    nc = _NC_CACHE["nc"]
    ident = np.eye(128, dtype=np.float32)
    gfb = np.ascontiguousarray(np.broadcast_to(np.asarray(norm_f, np.float32)[None, :], (128, D)))
    prms = [_pack_params(norm1_g, b_in, conv_w, conv_b, ln_g, ln_b, pool_scale, norm2_g, norm_f, h) for h in range(2)]
    in_maps = []
    for i in range(NCORES):
        b, h = i // 2, i % 2
        xc = np.zeros((T, D), np.float32)
        if h == 1:
            xc[0:HALO] = x_prompt[b, NPR - HALO:NPR]
        xc[HALO:HALO + NPR] = x_prompt[b, h * NPR:(h + 1) * NPR]
        xc[HALO + NPR:] = x_sample[SBQ * i:SBQ * (i + 1)].reshape(NSM, D)
        in_maps.append({
            "x": xc,
            "sconv": np.ascontiguousarray(state_conv[:, SBQ * i:SBQ * (i + 1)]),
            "spool": np.ascontiguousarray(state_pool[:, SBQ * i:SBQ * (i + 1)]),
            "prm": prms[h], "ident": ident, "gfb": gfb,
            "w_in": w_in, "pool_w": pool_w, "w_out": w_out, "w_ff1": w_ff1, "w_ff2": w_ff2,
        })
    ncr = int(os.environ.get("KCORES", NCORES))
    res = run_bass_kernel_spmd(nc, in_maps[:ncr], core_ids=list(range(ncr)))
    R = list(res.results) + [res.results[0]] * (NCORES - ncr)
    B_, S_ = x_prompt.shape[0], x_prompt.shape[1]
    y_prompt = np.empty((B_, S_, D), np.float32)
    y_sample = np.empty(x_sample.shape, np.float32)
    ncp = np.empty((DEPTH, B_, 30, 1024), np.float32)
    npp = np.empty((DEPTH, B_, 15, 1024), np.float32)
    ncs = np.empty(state_conv.shape, np.float32)
    nps = np.empty(state_pool.shape, np.float32)
    for i in range(NCORES):
        b, h = i // 2, i % 2
        y = R[i]["y"]
        y_prompt[b, h * NPR:(h + 1) * NPR] = y[0:NPR]
        y_sample[SBQ * i:SBQ * (i + 1)] = y[NPR:].reshape(SBQ, 4, D)
        ncs[:, SBQ * i:SBQ * (i + 1)] = R[i]["ocs"]
        nps[:, SBQ * i:SBQ * (i + 1)] = R[i]["ops"]
        if h == 1:
            ncp[:, b] = R[i]["ocp"]
            npp[:, b] = R[i]["opp"]
    return (y_prompt, y_sample, ncp, npp, ncs, nps)
```

```python
import os
import numpy as np
from contextlib import ExitStack
import concourse.bass as bass
import concourse.mybir as mybir
from concourse.bass_utils import run_bass_kernel_spmd

F32 = mybir.dt.float32
BF16 = mybir.dt.bfloat16
AF = mybir.ActivationFunctionType
ALU = mybir.AluOpType

NCORES = 8
D = 2048
KC = 16
DIN = 3072
DFF = 8192
DEPTH = 2
T = 1152
TN = 384
HALO = 64
NPR = 1024
NSM = 64
SBQ = 16
NYR = NPR + NSM
EPS = 1e-6
NSLOT = 5
FB = 2048
TMAP = [0, 2, 1]

PL = 336
O_G1, O_BIN, O_CW, O_CB, O_LNG, O_LNB, O_PSC, O_G2 = 0, 16, 40, 288, 296, 304, 312, 320
O_GF = DEPTH * PL
O_HM = O_GF + 16
O_IC = O_HM + 1
NPRM = O_IC + 64

SAME_ENG_SYNC = True
_DBG = set(os.environ.get("KDBG", "").split(",")) - {""}


class Buf:
    __slots__ = ("name", "w", "r", "pr", "excl")

    def __init__(self, name, excl=False):
        self.name = name
        self.w = {}
        self.r = {}
        self.pr = {}
        self.excl = excl


def _split(reads, writes):
    ex = [b for b in reads if b.excl]
    if not ex:
        return reads, writes
    return [b for b in reads if not b.excl], list(writes) + ex


def _merge(dst, src):
    for k, v in src.items():
        if dst.get(k, 0) < v:
            dst[k] = v


def handoff(src_bufs, dst_bufs):
    u = {}
    for b in src_bufs:
        _merge(u, b.w)
        _merge(u, b.r)
        _merge(u, b.pr)
    for b in dst_bufs:
        b.w = dict(u)
        b.r = {}
        b.pr = dict(u)


class Prog:
    ENG = ("pe", "act", "dve", "pool", "sp")

    def __init__(self):
        self.ops = {e: [] for e in self.ENG}
        self.cnt = {e: 0 for e in self.ENG}
        self.dcnt = {}

    def _deps(self, reads, writes, deps):
        d = {}
        for b in reads:
            _merge(d, b.w)
        for b in writes:
            _merge(d, b.r)
            _merge(d, b.pr)
            _merge(d, b.w)
        for h in deps:
            if h is not None:
                _merge(d, {h[0]: h[1]})
        return d

    def _reg(self, h, reads, writes):
        k, v = h
        for b in reads:
            if b.r.get(k, 0) < v:
                b.r[k] = v
        for b in writes:
            if b.r:
                b.pr = b.r
                b.r = {}
                b.w = {k: v}
            else:
                if b.w.get(k, 0) < v:
                    b.w[k] = v

    def op(self, eng, fn, reads=(), writes=(), deps=()):
        reads, writes = _split(reads, writes)
        d = self._deps(reads, writes, deps)
        self.cnt[eng] += 1
        h = (eng, self.cnt[eng])
        self.ops[eng].append((fn, d, (eng, 1)))
        self._reg(h, reads, writes)
        return h

    def group(self, fns, reads=(), writes=(), deps=()):
        reads, writes = _split(reads, writes)
        d = self._deps(reads, writes, deps)
        self.cnt["pe"] += 1
        h = ("pe", self.cnt["pe"])
        n = len(fns)
        for i, fn in enumerate(fns):
            self.ops["pe"].append((fn, d if i == 0 else {}, ("pe", 1) if i == n - 1 else None))
        self._reg(h, reads, writes)
        return h

    def dma(self, queue, fn, sem, reads=(), writes=(), deps=()):
        reads, writes = _split(reads, writes)
        d = self._deps(reads, writes, deps)
        self.dcnt[sem] = self.dcnt.get(sem, 0) + 16
        h = (sem, self.dcnt[sem])
        self.ops[queue].append((fn, d, (sem, 16)))
        self._reg(h, reads, writes)
        return h

    def wait(self, eng, deps):
        d = {}
        for h in deps:
            _merge(d, {h[0]: h[1]})
        self.ops[eng].append((None, d, None))

    def sem_names(self):
        return list(self.ENG) + sorted(self.dcnt.keys())

    def replay(self, eng, e, sems):
        waited = {}
        for fn, d, inc in self.ops[eng]:
            for k, v in d.items():
                if k == eng and (eng == "pe" or not SAME_ENG_SYNC):
                    continue
                if waited.get(k, 0) < v:
                    e.wait_ge(sems[k], v)
                    waited[k] = v
            if fn is None:
                continue
            inst = fn(e)
            if inc is not None:
                inst.then_inc(sems[inc[0]], inc[1])


class WStream:
    def __init__(self, P, wt, plan=None):
        self.P = P
        self.wt = wt
        self.plan = plan
        self.req = []
        self.i = 0
        self.issued = 0
        self.bufs = [Buf("w%d" % s) for s in range(NSLOT)]

    def _issue(self):
        i = self.issued
        s = i % NSLOT
        src, kc = self.plan[i]
        wt = self.wt
        self.P.dma("pool", lambda e, s=s, src=src, kc=kc: e.dma_start(out=wt[:, s, 0:kc, :], in_=src),
                   "w%d" % s, writes=[self.bufs[s]])
        self.issued += 1

    def get(self, src, kc):
        i = self.i
        self.i += 1
        if self.plan is None:
            self.req.append((src, kc))
            return i % NSLOT, self.bufs[i % NSLOT]
        while self.issued < min(i + NSLOT, len(self.plan)):
            self._issue()
        return i % NSLOT, self.bufs[i % NSLOT]


def build_program():
    nc = bass.Bass("TRN2", target_bir_lowering=False)
    dt_in = lambda n, s: nc.dram_tensor(n, s, F32, kind="ExternalInput").ap()
    dt_out = lambda n, s: nc.dram_tensor(n, s, F32, kind="ExternalOutput").ap()
    x_d = dt_in("x", [T, D])
    sconv_d = dt_in("sconv", [DEPTH, SBQ, 30, 1024])
    spool_d = dt_in("spool", [DEPTH, SBQ, 15, 1024])
    prm_d = dt_in("prm", [128, NPRM])
    ident_d = dt_in("ident", [128, 128])
    gfb_d = dt_in("gfb", [128, D])
    w_in_d = dt_in("w_in", [DEPTH, D, DIN])
    pool_w_d = dt_in("pool_w", [DEPTH, 4, 256, 256])
    w_out_d = dt_in("w_out", [DEPTH, D, D])
    w_ff1_d = dt_in("w_ff1", [DEPTH, D, DFF])
    w_ff2_d = dt_in("w_ff2", [DEPTH, DFF, D])
    y_d = dt_out("y", [NYR, D])
    ocs_d = dt_out("ocs", [DEPTH, SBQ, 30, 1024])
    ops_d = dt_out("ops", [DEPTH, SBQ, 15, 1024])
    ocp_d = dt_out("ocp", [DEPTH, 30, 1024])
    opp_d = dt_out("opp", [DEPTH, 15, 1024])

    w_in_v = [w_in_d[l].rearrange("(kc p) m -> p kc m", p=128) for l in range(DEPTH)]
    w_out_v = [w_out_d[l].rearrange("(kc p) m -> p kc m", p=128) for l in range(DEPTH)]
    w_ff1_v = [w_ff1_d[l].rearrange("(kc p) m -> p kc m", p=128) for l in range(DEPTH)]
    w_ff2_v = [w_ff2_d[l].rearrange("(jj kc p) m -> jj p kc m", p=128, kc=KC) for l in range(DEPTH)]
    pool_w_v = [[pool_w_d[l, g].rearrange("(kk p) m -> p kk m", p=128) for g in range(4)] for l in range(DEPTH)]

    with ExitStack() as st:
        sb = lambda n, s, d: st.enter_context(nc.sbuf_tensor(n, s, d))
        xT = sb("xT", [128, KC, T], F32)
        Bm = sb("Bm", [128, 18432], BF16)
        Cm = sb("Cm", [128, 9216], F32)
        S = sb("S", [128, 6, TN], F32)
        wt = sb("wt", [128, NSLOT, KC, 128], BF16)
        ubuf = sb("ubuf", [128, 3, 32 + TN], BF16)
        hsave = sb("hsave", [128, 8, 32], BF16)
        usamp = sb("usamp", [128, 8, SBQ, 34], BF16)
        zf = sb("zf", [128, 2, TN], F32)
        zb = sb("zb", [128, 2, 16 + TN], BF16)
        zsave = sb("zsave", [128, 8, 16], BF16)
        zsamp = sb("zsamp", [128, 8, SBQ, 19], BF16)
        dbuf = sb("dbuf", [128, 2, 2, TN], BF16)
        utail = sb("utail", [128, 8, 94], F32)
        ztail = sb("ztail", [128, 8, 94], F32)
        prm = sb("prm_t", [128, NPRM], F32)
        idf = sb("idf", [128, 128], F32)
        idb = sb("idb", [128, 128], BF16)
        ones = sb("ones", [128, 128], BF16)
        ps = st.enter_context(nc.psum_tensor("ps", [128, 8, 512], F32))

        hT = Bm[:, 0:6144].rearrange("p (k t) -> p k t", k=KC)
        mix = Bm[:, 6144:12288].rearrange("p (k t) -> p k t", k=KC)
        sqb = Bm[:, 12288:18432].rearrange("p (k t) -> p k t", k=KC)
        h2T = Bm[:, :].rearrange("p (a k t) -> p a k t", a=3, k=KC)
        tstage = S[:, :, :].rearrange("p a t -> p (a t)")[:, 0:2048].rearrange("p (a c) -> p a c", a=2)
        xin = [Cm[:, s * 2048:(s + 1) * 2048] for s in range(4)]
        stgc = Cm[:, 0:4096].rearrange("p (g c) -> p g c", g=4)
        stgp = Cm[:, 4096:6144].rearrange("p (g c) -> p g c", g=2)
        vv = Cm[:, 0:3072].rearrange("p (c t) -> p c t", c=8)
        vbf = Cm[:, 3072:4608].bitcast(BF16).rearrange("p (c t) -> p c t", c=8)
        sqv = Cm[:, 4608:6144].bitcast(BF16).rearrange("p (c t) -> p c t", c=8)
        diag = Cm[:, 6144:8128].bitcast(BF16).rearrange("p (k j) -> p k j", k=31)
        sig = [Cm[:, 8128 + i * TN:8128 + (i + 1) * TN] for i in range(2)]
        t64 = Cm[:, 8896:8960]
        t16 = Cm[:, 8960:8976]
        fT = Cm[:, :].bitcast(BF16).rearrange("p (a k t) -> p a k t", a=3, k=KC)
        sqFin = Bm[:, :].rearrange("p (k t) -> p k t", k=KC)
        gbc = Cm[:, 0:2048]
        yout = [Cm[:, 2048 + s * 2048:2048 + (s + 1) * 2048] for s in range(3)]
        Sflat = S[:, 0:3, :].rearrange("p a t -> p (a t)")

        def pcol(c):
            return prm[:, c:c + 1]

        def emit(P, W):
            B = {}

            def bf(n):
                if n not in B:
                    B[n] = Buf(n)
                return B[n]

            mixbufs_ref = [None]
            pb = [bf("pb%d" % i) for i in range(8)]
            for b_ in pb:
                b_.excl = True
            xTb = [bf("xT%d" % t) for t in range(3)]
            state = {"ring": 0, "alt": 0}

            def ring3():
                b = state["ring"] % 3
                state["ring"] += 1
                return b

            def alt_eng():
                state["alt"] += 1
                return "act" if state["alt"] % 2 else "dve"

            def copy_op(eng, out, in_, reads, writes):
                if eng == "act":
                    return P.op("act", lambda e: e.activation(out=out, in_=in_, func=AF.Copy), reads=reads, writes=writes)
                return P.op("dve", lambda e: e.tensor_copy(out=out, in_=in_), reads=reads, writes=writes)

            outs = []
            P.dma("sp", lambda e: e.dma_start(out=prm[:], in_=prm_d[:, :]), "c0", writes=[bf("prm")])
            P.dma("sp", lambda e: e.dma_start(out=idf[:], in_=ident_d[:, :]), "c1", writes=[bf("idf")])
            P.op("dve", lambda e: e.tensor_copy(out=idb[:], in_=idf[:]), reads=[bf("idf")], writes=[bf("idb")])
            P.op("dve", lambda e: e.memset(ones[:], 1.0), writes=[bf("ones")])
            for l in range(DEPTH if "nopt" not in _DBG else 0):
                outs.append(P.dma("sp", lambda e, l=l: e.dma_start(out=ocs_d[l, :, 0:26, :], in_=sconv_d[l, :, 4:30, :]), "pt"))
                outs.append(P.dma("sp", lambda e, l=l: e.dma_start(out=ops_d[l, :, 0:11, :], in_=spool_d[l, :, 4:15, :]), "pt"))

            xinb = [bf("xin%d" % i) for i in range(4)]
            bank = 0
            for i in range(T // 128):
                s = i % 4
                P.dma("sp", lambda e, i=i, s=s: e.dma_start(out=xin[s], in_=x_d[i * 128:(i + 1) * 128, :]),
                      "xi%d" % s, writes=[xinb[s]])
                for q in range(4):
                    bk = bank % 8
                    bank += 1
                    fns = [lambda e, s=s, q=q, j=j, bk=bk: e.transpose(
                        out=ps[:, bk, j * 128:(j + 1) * 128], in_=xin[s][:, (4 * q + j) * 128:(4 * q + j + 1) * 128], identity=idf[:])
                        for j in range(4)]
                    P.group(fns, reads=[xinb[s], bf("idf")], writes=[pb[bk]])
                    copy_op(alt_eng(), xT[:, 4 * q:4 * q + 4, i * 128:(i + 1) * 128],
                            ps[:, bk, :].rearrange("p (j t) -> p j t", t=128), [pb[bk]], [xTb[i // 3]])
            cbufs = list(xinb)

            def rms_stats(l_g_unused, src_cols, sq_view, sq_buf, bank_i, out_slot, out_buf, xbufs):
                for k in range(KC):
                    P.op("act", lambda e, k=k: e.activation(out=sq_view[:, k, :], in_=xT[:, k, src_cols[0]:src_cols[1]], func=AF.Square),
                         reads=xbufs, writes=[sq_buf])
                n = src_cols[1] - src_cols[0]
                fns = [lambda e, k=k: e.matmul(out=ps[:, bank_i, 0:n], lhsT=ones[:], rhs=sq_view[:, k, :], start=(k == 0), stop=(k == KC - 1))
                       for k in range(KC)]
                P.group(fns, reads=[sq_buf, bf("ones")], writes=[pb[bank_i]])
                rstd_finish(bank_i, n, out_slot, out_buf)

            def rstd_finish(bank_i, n, out_slot, out_buf):
                if out_slot is None:
                    out_slot, out_buf = ps[:, bank_i, 0:n], pb[bank_i]
                P.op("act", lambda e: e.activation(out=out_slot, in_=ps[:, bank_i, 0:n], func=AF.Sqrt, bias=EPS, scale=1.0 / D),
                     reads=[pb[bank_i]], writes=[out_buf])
                P.op("dve", lambda e: e.reciprocal(out=out_slot, in_=out_slot), reads=[out_buf], writes=[out_buf])

            def rms1_pre(l, t):
                c0 = t * TN
                rms_stats(None, (c0, c0 + TN), sqb, bf("sqb"), 7, None, None, [xTb[t]])
                for k in range(KC):
                    P.op("dve", lambda e, k=k: e.scalar_tensor_tensor(
                        out=hT[:, k, :], in0=xT[:, k, c0:c0 + TN], scalar=pcol(l * PL + O_G1 + k), in1=ps[:, 7, 0:TN],
                        op0=ALU.mult, op1=ALU.mult), reads=[xTb[t], pb[7], bf("prm")], writes=[bf("hT")])

            def mm_group(slot, wbuf, rhs_of_k, nk, bank_i, n, reads):
                fns = [lambda e, k=k: e.matmul(out=ps[:, bank_i, 0:n], lhsT=wt[:, slot, k, :], rhs=rhs_of_k(k),
                                               start=(k == 0), stop=(k == nk - 1)) for k in range(nk)]
                return P.group(fns, reads=[wbuf] + reads, writes=[pb[bank_i]])

            def stt(out, in0, scalar, in1, op0, op1, reads, writes):
                return P.op("dve", lambda e: e.scalar_tensor_tensor(out=out, in0=in0, scalar=scalar, in1=in1, op0=op0, op1=op1),
                            reads=reads, writes=writes)

            def ts1(out, in0, s1, op0, reads, writes):
                return P.op("dve", lambda e: e.tensor_scalar(out=out, in0=in0, scalar1=s1, scalar2=None, op0=op0),
                            reads=reads, writes=writes)

            def ts2(out, in0, s1, s2, op0, op1, reads, writes):
                return P.op("dve", lambda e: e.tensor_scalar(out=out, in0=in0, scalar1=s1, scalar2=s2, op0=op0, op1=op1),
                            reads=reads, writes=writes)

            def act(out, in_, func, reads, writes, bias=None, scale=None):
                kw = {}
                if bias is not None:
                    kw["bias"] = bias
                if scale is not None:
                    kw["scale"] = scale
                return P.op("act", lambda e: e.activation(out=out, in_=in_, func=func, **kw), reads=reads, writes=writes)

            def mixer(l, t):
                L0 = l * PL
                c0 = t * TN
                npr = TN if t < 2 else TN - NSM
                prmb = bf("prm")
                hTb, mixb = bf("hT"), bf("mix")
                vb, vbfb, sqvb = bf("v"), bf("vbf"), bf("sqv")
                sigb = [bf("sig0"), bf("sig1")]
                ubb = [bf("ub0"), bf("ub1"), bf("ub2")]

                nxt = t + 1 if t < 2 else None
                dgb = [bf("diagA"), bf("diagB")]
                halves = [(0, 16), (16, 31)]

                def conv_build(c):
                    for hi, (k0, k1) in enumerate(halves):
                        nk = k1 - k0
                        P.op("dve", lambda e, k0=k0, k1=k1, nk=nk: e.tensor_tensor(
                            out=diag[:, k0:k1, :], in0=ps[:, 4, 0:128].unsqueeze(1).broadcast_to([128, nk, 128]),
                            in1=prm[:, L0 + O_CW + c * 31 + k0:L0 + O_CW + c * 31 + k1].unsqueeze(2).broadcast_to([128, nk, 128]),
                            op=ALU.mult), reads=[pb[4], prmb], writes=[dgb[hi]])

                def conv(c):
                    ui = c % 3
                    ub = ubuf[:, ui, :]
                    for hi, (k0, k1) in enumerate(halves):
                        fns = [lambda e, k=k: e.matmul(out=ps[:, 3, 0:npr], lhsT=diag[:, k, :], rhs=ub[:, 2 + k:2 + k + npr],
                                                       start=(k == 0), stop=(k == 30)) for k in range(k0, k1)]
                        P.group(fns, reads=[dgb[hi], ubb[ui]], writes=[pb[3]])
                    if t == 2:
                        fns = [lambda e, k=k: e.matmul(out=ps[:, 3, npr:TN], lhsT=diag[:, k, :], rhs=usamp[:, c, :, k:k + 4],
                                                       start=(k == 0), stop=(k == 30), skip_group_check=True) for k in range(31)]
                        P.group(fns, reads=[dgb[0], dgb[1], bf("usamp%d" % c)], writes=[pb[3]])
                    cb = pcol(L0 + O_CB + c)
                    act(vv[:, c, :], ps[:, 3, 0:TN], AF.Identity, [pb[3], prmb], [vb], bias=cb)
                    ts1(vbf[:, c, :], ps[:, 3, 0:TN], cb, ALU.add, [pb[3], prmb], [vbfb])
                    act(sqv[:, c, :], ps[:, 3, 0:TN], AF.Square, [pb[3], prmb], [sqvb], bias=cb)

                for c in range(8):
                    ui = c % 3
                    ub = ubuf[:, ui, :]
                    if c > 0:
                        conv_build(c - 1)
                    sa, wa = W.get(w_in_v[l][:, :, c * 128:(c + 1) * 128], KC)
                    ba = ring3()
                    mm_group(sa, wa, lambda k: hT[:, k, :], KC, ba, TN, [hTb])
                    sg, wg = W.get(w_in_v[l][:, :, 1024 + c * 128:1024 + (c + 1) * 128], KC)
                    bg = ring3()
                    mm_group(sg, wg, lambda k: hT[:, k, :], KC, bg, TN, [hTb])
                    si = c % 2
                    act(sig[si], ps[:, bg, 0:TN], AF.Sigmoid, [pb[bg], prmb], [sigb[si]], bias=pcol(L0 + O_BIN + 8 + c))
                    ba_col = pcol(L0 + O_BIN + c)
                    rdu = [pb[ba], sigb[si], prmb]
                    if t == 0:
                        P.op("dve", lambda e, ub=ub: e.memset(ub[:, 0:32], 0.0), writes=[ubb[ui]])
                    else:
                        P.op("dve", lambda e, ub=ub, c=c: e.tensor_copy(out=ub[:, 0:32], in_=hsave[:, c, :]),
                             reads=[bf("hsave%d" % c)], writes=[ubb[ui]])
                    if t == 0:
                        stt(t64, ps[:, ba, 0:HALO], ba_col, sig[si][:, 0:HALO], ALU.add, ALU.mult, rdu, [bf("t64")])
                        ts1(ub[:, 32:32 + HALO], t64, pcol(O_HM), ALU.mult, [bf("t64"), prmb], [ubb[ui]])
                        stt(ub[:, 32 + HALO:32 + TN], ps[:, ba, HALO:TN], ba_col, sig[si][:, HALO:TN], ALU.add, ALU.mult, rdu, [ubb[ui]])
                    else:
                        stt(ub[:, 32:32 + npr], ps[:, ba, 0:npr], ba_col, sig[si][:, 0:npr], ALU.add, ALU.mult, rdu, [ubb[ui]])
                    if t == 2:
                        stt(usamp[:, c, :, 30:34], ps[:, ba, npr:TN].rearrange("p (b j) -> p b j", j=4), ba_col,
                            sig[si][:, npr:TN].rearrange("p (b j) -> p b j", j=4), ALU.add, ALU.mult, rdu, [bf("usamp%d" % c)])
                        stt(utail[:, c, :], ps[:, ba, TN - 94:TN], ba_col, sig[si][:, TN - 94:TN], ALU.add, ALU.mult, rdu, [bf("utail")])
                    else:
                        P.op("dve", lambda e, ub=ub, c=c: e.tensor_copy(out=hsave[:, c, :], in_=ub[:, TN:TN + 32]),
                             reads=[ubb[ui]], writes=[bf("hsave%d" % c)])
                    if nxt is not None:
                        n0 = nxt * TN
                        for k in (2 * c, 2 * c + 1):
                            P.op("act", lambda e, k=k, n0=n0: e.activation(out=sqb[:, k, :], in_=xT[:, k, n0:n0 + TN], func=AF.Square),
                                 reads=[xTb[nxt]], writes=[bf("sqb")])
                    if c > 0:
                        conv(c - 1)
                conv_build(7)
                conv(7)

                def pool_A(g):
                    bz = []
                    for j in range(2):
                        m = 2 * g + j
                        sz, wz = W.get(w_in_v[l][:, :, 2048 + m * 128:2048 + (m + 1) * 128], KC)
                        b_ = ring3()
                        bz.append(b_)
                        mm_group(sz, wz, lambda k: hT[:, k, :], KC, b_, TN, [hTb])
                        bzc = pcol(L0 + O_BIN + 16 + m)
                        zfB, zbB = bf("zf%d" % j), bf("zb%d" % j)
                        act(zf[:, j, :], ps[:, b_, 0:TN], AF.Identity, [pb[b_], prmb], [zfB], bias=bzc)
                        if t == 0:
                            P.op("dve", lambda e, j=j: e.memset(zb[:, j, 0:16], 0.0), writes=[zbB])
                            ts2(zb[:, j, 16:16 + HALO], ps[:, b_, 0:HALO], bzc, pcol(O_HM), ALU.add, ALU.mult, [pb[b_], prmb], [zbB])
                            ts1(zb[:, j, 16 + HALO:16 + TN], ps[:, b_, HALO:TN], bzc, ALU.add, [pb[b_], prmb], [zbB])
                        else:
                            P.op("dve", lambda e, j=j, m=m: e.tensor_copy(out=zb[:, j, 0:16], in_=zsave[:, m, :]),
                                 reads=[bf("zsave%d" % m)], writes=[zbB])
                            ts1(zb[:, j, 16:16 + npr], ps[:, b_, 0:npr], bzc, ALU.add, [pb[b_], prmb], [zbB])
                        if t == 2:
                            ts1(zsamp[:, m, :, 15:19], ps[:, b_, npr:TN].rearrange("p (b j) -> p b j", j=4), bzc, ALU.add,
                                [pb[b_], prmb], [bf("zsamp%d" % m)])
                            ts1(ztail[:, m, :], ps[:, b_, TN - 94:TN], bzc, ALU.add, [pb[b_], prmb], [bf("ztail")])
                        else:
                            P.op("dve", lambda e, j=j, m=m: e.tensor_copy(out=zsave[:, m, :], in_=zb[:, j, TN:TN + 16]),
                                 reads=[zbB], writes=[bf("zsave%d" % m)])

                def pool_B(g):
                    w = 2 << g
                    db = g % 2
                    dB = bf("d%d" % db)
                    for j in range(2):
                        m = 2 * g + j
                        zfB, zbB = bf("zf%d" % j), bf("zb%d" % j)
                        bp = ring3()
                        fns = [lambda e, k=k, j=j, bp=bp, w=w: e.matmul(out=ps[:, bp, 0:npr], lhsT=idb[:], rhs=zb[:, j, 16 - k:16 - k + npr],
                                                                   start=(k == 0), stop=(k == w - 1)) for k in range(w)]
                        rd = [bf("idb"), zbB]
                        if t == 2:
                            fns += [lambda e, k=k, m=m, bp=bp, w=w: e.matmul(out=ps[:, bp, npr:TN], lhsT=idb[:], rhs=zsamp[:, m, :, 15 - k:19 - k],
                                                                        start=(k == 0), stop=(k == w - 1), skip_group_check=True) for k in range(w)]
                            rd.append(bf("zsamp%d" % m))
                        P.group(fns, reads=rd, writes=[pb[bp]])
                        rdd = [pb[bp], zfB]
                        if t == 0:
                            stt(dbuf[:, db, j, 0:HALO], ps[:, bp, 0:HALO], 1.0 / w, zf[:, j, 0:HALO], ALU.mult, ALU.subtract, rdd, [dB])
                            P.op("dve", lambda e, bp=bp, g=g: e.tensor_tensor(out=t16, in0=ps[:, bp, HALO:HALO + 16],
                                                                              in1=prm[:, O_IC + g * 16:O_IC + (g + 1) * 16], op=ALU.mult),
                                 reads=[pb[bp], prmb], writes=[bf("t16")])
                            P.op("dve", lambda e, j=j, db=db: e.tensor_tensor(out=dbuf[:, db, j, HALO:HALO + 16], in0=t16,
                                                                              in1=zf[:, j, HALO:HALO + 16], op=ALU.subtract),
                                 reads=[bf("t16"), zfB], writes=[dB])
                            stt(dbuf[:, db, j, HALO + 16:TN], ps[:, bp, HALO + 16:TN], 1.0 / w, zf[:, j, HALO + 16:TN],
                                ALU.mult, ALU.subtract, rdd, [dB])
                        else:
                            stt(dbuf[:, db, j, :], ps[:, bp, 0:TN], 1.0 / w, zf[:, j, :], ALU.mult, ALU.subtract, rdd, [dB])

                def pool_C(g):
                    db = g % 2
                    dB = bf("d%d" % db)
                    for e_ in range(2):
                        m = 2 * g + e_
                        sp_, wp = W.get(pool_w_v[l][g][:, :, e_ * 128:(e_ + 1) * 128], 2)
                        bq = ring3()
                        mm_group(sp_, wp, lambda k, db=db: dbuf[:, db, k, :], 2, bq, TN, [dB])
                        act(mix[:, 8 + m, :], ps[:, bq, 0:TN], AF.Copy, [pb[bq], prmb], [mixb], scale=pcol(L0 + O_PSC + m))

                fns = [lambda e, c=c: e.matmul(out=ps[:, 5, 0:TN], lhsT=ones[:], rhs=vbf[:, c, :], start=(c == 0), stop=(c == 7)) for c in range(8)]
                P.group(fns, reads=[vbfb, bf("ones")], writes=[pb[5]])
                fns = [lambda e, c=c: e.matmul(out=ps[:, 6, 0:TN], lhsT=ones[:], rhs=sqv[:, c, :], start=(c == 0), stop=(c == 7)) for c in range(8)]
                P.group(fns, reads=[sqvb, bf("ones")], writes=[pb[6]])
                if nxt is not None:
                    fns = [lambda e, k=k: e.matmul(out=ps[:, 7, 0:TN], lhsT=ones[:], rhs=sqb[:, k, :], start=(k == 0), stop=(k == KC - 1))
                           for k in range(KC)]
                    P.group(fns, reads=[bf("sqb"), bf("ones")], writes=[pb[7]])
                S1b = bf("S1")
                act(ps[:, 5, 0:TN], ps[:, 5, 0:TN], AF.Copy, [pb[5]], [pb[5]], scale=1.0 / 1024)
                act(S[:, 1, :], ps[:, 5, 0:TN], AF.Square, [pb[5]], [S1b])
                stt(ps[:, 6, 0:TN], ps[:, 6, 0:TN], 1.0 / 1024, S[:, 1, :], ALU.mult, ALU.subtract, [pb[6], S1b], [pb[6]])
                act(ps[:, 6, 0:TN], ps[:, 6, 0:TN], AF.Sqrt, [pb[6]], [pb[6]], bias=EPS)
                P.op("dve", lambda e: e.reciprocal(out=ps[:, 6, 0:TN], in_=ps[:, 6, 0:TN]), reads=[pb[6]], writes=[pb[6]])
                if nxt is not None:
                    rstd_finish(7, TN, None, None)
                def ln_apply(c):
                    sl = 2 + c % 2
                    Sb = bf("S%d" % sl)
                    P.op("dve", lambda e, c=c, sl=sl: e.tensor_tensor(out=S[:, sl, :], in0=vv[:, c, :], in1=ps[:, 5, 0:TN], op=ALU.subtract),
                         reads=[vb, pb[5]], writes=[Sb])
                    P.op("dve", lambda e, sl=sl: e.tensor_tensor(out=S[:, sl, :], in0=S[:, sl, :], in1=ps[:, 6, 0:TN], op=ALU.mult),
                         reads=[Sb, pb[6]], writes=[Sb])
                    act(mix[:, c, :], S[:, sl, :], AF.Silu, [Sb, prmb], [mixb], bias=pcol(L0 + O_LNB + c), scale=pcol(L0 + O_LNG + c))

                early2 = (t == 2 and "noffn" not in _DBG)
                sqCb = [bf("sqC%d" % i) for i in range(3)]
                if early2:
                    handoff([vbfb, sqvb], [sqCb[1]])
                    for k in range(KC):
                        P.op("act", lambda e, k=k: e.activation(out=fT[:, 1, k, :], in_=xT[:, k, TN:2 * TN], func=AF.Square),
                             reads=[xTb[1]], writes=[sqCb[1]])
                seq = ["A0", "B0", "A1", "C0", "B1", "A2", "C1", "B2", "A3", "C2", "B3", "C3"]
                for st_ in seq:
                    g = int(st_[1])
                    if st_[0] == "A":
                        pool_A(g)
                        if early2 and g == 1:
                            fns = [lambda e, k=k: e.matmul(out=ps[:, 7, 0:TN], lhsT=ones[:], rhs=fT[:, 1, k, :], start=(k == 0), stop=(k == KC - 1))
                                   for k in range(KC)]
                            P.group(fns, reads=[sqCb[1], bf("ones")], writes=[pb[7]])
                            rstd_finish(7, TN, None, None)
                    elif st_[0] == "B":
                        pool_B(g)
                        ln_apply(2 * g)
                        ln_apply(2 * g + 1)
                    else:
                        pool_C(g)
                if early2:
                    handoff([vb], [sqCb[0]])
                    handoff([bf(n) for n in ("diagA", "diagB", "sig0", "sig1", "t64", "t16")], [sqCb[2]])
                    for k in range(KC):
                        P.op("act", lambda e, k=k: e.activation(out=fT[:, 0, k, :], in_=xT[:, k, 0:TN], func=AF.Square),
                             reads=[xTb[0]], writes=[sqCb[0]])
                for m in range(KC):
                    so, wo = W.get(w_out_v[l][:, :, m * 128:(m + 1) * 128], KC)
                    bo = ring3()
                    mm_group(so, wo, lambda k: mix[:, k, :], KC, bo, TN, [mixb])
                    P.op("dve", lambda e, m=m, bo=bo: e.tensor_tensor(out=xT[:, m, c0:c0 + TN], in0=xT[:, m, c0:c0 + TN], in1=ps[:, bo, 0:TN], op=ALU.add),
                         reads=[pb[bo], xTb[t]], writes=[xTb[t]])
                    if nxt is not None:
                        n0 = nxt * TN
                        stt(hT[:, m, :], xT[:, m, n0:n0 + TN], pcol(L0 + O_G1 + m), ps[:, 7, 0:TN], ALU.mult, ALU.mult,
                            [xTb[nxt], pb[7], prmb], [hTb])
                    elif early2:
                        P.op("act", lambda e, m=m: e.activation(out=fT[:, 2, m, :], in_=xT[:, m, c0:c0 + TN], func=AF.Square),
                             reads=[xTb[2]], writes=[sqCb[2]])
                        stt(h2T[:, TMAP[1], m, :], xT[:, m, TN:2 * TN], pcol(L0 + O_G2 + m), ps[:, 7, 0:TN], ALU.mult, ALU.mult,
                            [xTb[1], pb[7], prmb], [bf("sqb")])
                        if m == 4:
                            fns = [lambda e, k=k: e.matmul(out=ps[:, 5, 0:TN], lhsT=ones[:], rhs=fT[:, 0, k, :], start=(k == 0), stop=(k == KC - 1))
                                   for k in range(KC)]
                            P.group(fns, reads=[sqCb[0], bf("ones")], writes=[pb[5]])
                            rstd_finish(5, TN, None, None)
                        if 5 <= m <= 12:
                            for k in (2 * (m - 5), 2 * (m - 5) + 1):
                                stt(h2T[:, TMAP[0], k, :], xT[:, k, 0:TN], pcol(L0 + O_G2 + k), ps[:, 5, 0:TN], ALU.mult, ALU.mult,
                                    [xTb[0], pb[5], prmb], [hTb])
                if t == 2 and "notails" not in _DBG:
                    Sall6 = [bf("S%d" % i) for i in range(6)]
                    for a, (tl, tlb) in enumerate([(utail, bf("utail")), (ztail, bf("ztail"))]):
                        for q in range(2):
                            bk = (2 * a + q) % 4
                            fns = [lambda e, tl=tl, q=q, i=i, bk=bk: e.transpose(out=ps[0:94, bk, i * 128:(i + 1) * 128], in_=tl[:, 4 * q + i, :], identity=idf[:])
                                   for i in range(4)]
                            P.group(fns, reads=[tlb, bf("idf")], writes=[pb[bk]])
                            copy_op(alt_eng(), tstage[0:94, a, q * 512:(q + 1) * 512], ps[0:94, bk, :], [pb[bk]], Sall6)
                    outs.append(P.dma("sp", lambda e: e.dma_start(out=ocp_d[l, :, :], in_=tstage[0:30, 0, :]), "to", reads=Sall6))
                    outs.append(P.dma("sp", lambda e: e.dma_start(out=opp_d[l, :, :], in_=tstage[15:30, 1, :]), "to", reads=Sall6))
                    for b in range(SBQ):
                        outs.append(P.dma("sp", lambda e, b=b: e.dma_start(out=ocs_d[l, b, 26:30, :], in_=tstage[30 + 4 * b:34 + 4 * b, 0, :]), "to", reads=Sall6))
                        outs.append(P.dma("sp", lambda e, b=b: e.dma_start(out=ops_d[l, b, 11:15, :], in_=tstage[30 + 4 * b:34 + 4 * b, 1, :]), "to", reads=Sall6))

            def ffn(l):
                L0 = l * PL
                prmb = bf("prm")
                h2b, fCb = bf("h2T"), bf("fC")
                Sb = [bf("S%d" % i) for i in range(6)]
                early = "nomixer" not in _DBG and "t01" not in _DBG
                sqCb = [bf("sqC%d" % i) for i in range(3)]
                bks = [5, 7, 6]
                for t in range(3):
                    c0 = t * TN
                    bk = bks[t]
                    if not early:
                        for k in range(KC):
                            P.op("act", lambda e, k=k, t=t, c0=c0: e.activation(out=fT[:, t, k, :], in_=xT[:, k, c0:c0 + TN], func=AF.Square),
                                 reads=[xTb[t]], writes=[sqCb[t]])
                    if not early or t == 2:
                        fns = [lambda e, k=k, t=t, bk=bk: e.matmul(out=ps[:, bk, 0:TN], lhsT=ones[:], rhs=fT[:, t, k, :], start=(k == 0), stop=(k == KC - 1))
                               for k in range(KC)]
                        P.group(fns, reads=[sqCb[t], bf("ones")], writes=[pb[bk]])
                        rstd_finish(bk, TN, None, None)
                        for k in range(KC):
                            stt(h2T[:, TMAP[t], k, :], xT[:, k, c0:c0 + TN], pcol(L0 + O_G2 + k), ps[:, bk, 0:TN], ALU.mult, ALU.mult,
                                [xTb[t], pb[bk], prmb], [h2b])
                handoff(sqCb, [fCb])
                slot3 = 0
                sc_i = 0
                tr = [((HALO if l == DEPTH - 1 else HALO - 32) if t == 0 else 0, TN) for t in range(3)]
                for j in range(DFF // FB):
                    for m in range(KC):
                        s1, w1 = W.get(w_ff1_v[l][:, :, j * FB + m * 128:j * FB + (m + 1) * 128], KC)
                        b0 = 3 * (slot3 % 2)
                        slot3 += 1
                        fns = [lambda e, k=k, t=t, s1=s1, b0=b0: e.matmul(out=ps[:, b0 + t, 0:tr[t][1] - tr[t][0]], lhsT=wt[:, s1, k, :],
                                                                        rhs=h2T[:, TMAP[t], k, tr[t][0]:tr[t][1]],
                                                                        start=(k == 0), stop=(k == KC - 1)) for k in range(KC) for t in range(3)]
                        P.group(fns, reads=[w1, h2b], writes=[pb[b0], pb[b0 + 1], pb[b0 + 2]])
                        for t in range(3):
                            o0, o1 = tr[t]
                            si = 3 + sc_i % 3
                            sc_i += 1
                            act(S[:, si, 0:o1 - o0], ps[:, b0 + t, 0:o1 - o0], AF.Square, [pb[b0 + t]], [Sb[si]])
                            stt(fT[:, t, m, o0:o1], ps[:, b0 + t, 0:o1 - o0], 0.0, S[:, si, 0:o1 - o0], ALU.is_gt, ALU.mult, [pb[b0 + t], Sb[si]], [fCb])
                    for m in range(KC):
                        s2, w2 = W.get(w_ff2_v[l][j][:, :, m * 128:(m + 1) * 128], KC)
                        b0 = 3 * (slot3 % 2)
                        slot3 += 1
                        fns = [lambda e, k=k, t=t, s2=s2, b0=b0: e.matmul(out=ps[:, b0 + t, 0:tr[t][1] - tr[t][0]], lhsT=wt[:, s2, k, :],
                                                                        rhs=fT[:, t, k, tr[t][0]:tr[t][1]],
                                                                        start=(k == 0), stop=(k == KC - 1)) for k in range(KC) for t in range(3)]
                        P.group(fns, reads=[w2, fCb], writes=[pb[b0], pb[b0 + 1], pb[b0 + 2]])
                        for t in range(3):
                            o0, o1 = tr[t]
                            c0 = t * TN + o0
                            nn = o1 - o0
                            P.op("dve", lambda e, m=m, t=t, b0=b0, c0=c0, nn=nn: e.tensor_tensor(out=xT[:, m, c0:c0 + nn], in0=xT[:, m, c0:c0 + nn],
                                                                                             in1=ps[:, b0 + t, 0:nn], op=ALU.add),
                                 reads=[pb[b0 + t], xTb[t]], writes=[xTb[t]])
                        if l == DEPTH - 1 and j == DFF // FB - 1:
                            if m == 0:
                                handoff([h2b], [bf("sqFin")])
                            P.op("act", lambda e, m=m: e.activation(out=sqFin[:, m, :], in_=xT[:, m, :], func=AF.Square),
                                 reads=xTb, writes=[bf("sqFin")])
                return [h2b, fCb]

            for l in range(DEPTH if "nolayers" not in _DBG else 0):
                if l > 0:
                    handoff([bf("h2T")], [bf("hT"), bf("mix"), bf("sqb")])
                P.group([lambda e: e.transpose(out=ps[:, 4, 0:128], in_=idf[:], identity=idf[:])], reads=[bf("idf")], writes=[pb[4]])
                rms1_pre(l, 0)
                stb = bf("stage")
                handoff(cbufs, [stb])
                for gq in range(4):
                    P.dma("sp", lambda e, l=l, gq=gq: e.dma_start(out=stgc[0:120, gq, :],
                                                                  in_=sconv_d[l, 4 * gq:4 * gq + 4, :, :].rearrange("b j c -> (b j) c")),
                          "sg", writes=[stb])
                for gp in range(2):
                    P.dma("sp", lambda e, l=l, gp=gp: e.dma_start(out=stgp[0:120, gp, :],
                                                                  in_=spool_d[l, 8 * gp:8 * gp + 8, :, :].rearrange("b j c -> (b j) c")),
                          "sg", writes=[stb])
                hbanks = [0, 1, 2, 3, 5, 6]
                bank = 0
                for c in range(8):
                    bk = hbanks[bank % 6]
                    bank += 1
                    fns = [lambda e, c=c, gq=gq, bk=bk: e.transpose(out=ps[:, bk, gq * 120:(gq + 1) * 120], in_=stgc[0:120, gq, c * 128:(c + 1) * 128],
                                                                   identity=idf[0:120, 0:120]) for gq in range(4)]
                    P.group(fns, reads=[stb, bf("idf")], writes=[pb[bk]])
                    copy_op(alt_eng(), usamp[:, c, :, 0:30], ps[:, bk, 0:480].rearrange("p (b j) -> p b j", j=30), [pb[bk]], [bf("usamp%d" % c)])
                    bk = hbanks[bank % 6]
                    bank += 1
                    fns = [lambda e, c=c, gp=gp, bk=bk: e.transpose(out=ps[:, bk, gp * 120:(gp + 1) * 120], in_=stgp[0:120, gp, c * 128:(c + 1) * 128],
                                                                   identity=idf[0:120, 0:120]) for gp in range(2)]
                    P.group(fns, reads=[stb, bf("idf")], writes=[pb[bk]])
                    copy_op(alt_eng(), zsamp[:, c, :, 0:15], ps[:, bk, 0:240].rearrange("p (b j) -> p b j", j=15), [pb[bk]], [bf("zsamp%d" % c)])
                mixbufs = [bf(n) for n in ("v", "vbf", "sqv", "diagA", "diagB", "sig0", "sig1", "t64", "t16")]
                handoff([stb], mixbufs)
                mixbufs_ref[0] = mixbufs
                for t in range((2 if "t01" in _DBG else 3) if "nomixer" not in _DBG else 0):
                    mixer(l, t)
                handoff([bf("hT"), bf("mix"), bf("sqb")], [bf("h2T")])
                if "nomixer" in _DBG or "noffn" in _DBG or "t01" in _DBG:
                    handoff(mixbufs, [bf("sqC%d" % i) for i in range(3)])
                cbufs = ffn(l)[1:] if "noffn" not in _DBG else [bf("sqC%d" % i) for i in range(3)]

            gB, yob = bf("gbc"), [bf("yout%d" % i) for i in range(3)]
            sq2b = [bf("sqh0"), bf("sqh1")]
            handoff(cbufs, [gB] + yob)
            pre_sq = not (_DBG & {"nolayers", "noffn"})
            if "nolayers" not in _DBG and not pre_sq:
                handoff([bf("h2T")], sq2b)
            P.dma("sp", lambda e: e.dma_start(out=gbc, in_=gfb_d[:, :]), "c0", writes=[gB])
            rtb = bf("S5")
            bank = 0
            for i in range(9):
                n = 128 if i < 8 else 64
                col0 = HALO + 128 * i
                s2, s3 = i % 2, i % 3
                sbk = 6 + i % 2
                xb_ = sorted({col0 // TN, (col0 + n - 1) // TN})
                xbs = [xTb[j] for j in xb_]
                if pre_sq:
                    fns = [lambda e, k=k, n=n, sbk=sbk, i=i, col0=col0: e.matmul(out=ps[0:n, sbk, i:i + 1], lhsT=sqFin[:, k, col0:col0 + n], rhs=ones[:, 0:1],
                                                                                 start=(k == 0), stop=(k == KC - 1), skip_group_check=True) for k in range(KC)]
                    P.group(fns, reads=[bf("sqFin"), bf("ones")], writes=[pb[sbk]])
                else:
                    for k in range(KC):
                        P.op("act", lambda e, k=k, s2=s2, n=n, col0=col0: e.activation(out=sqb[:, k, s2 * 128:s2 * 128 + n], in_=xT[:, k, col0:col0 + n], func=AF.Square),
                             reads=xbs, writes=[sq2b[s2]])
                    fns = [lambda e, k=k, s2=s2, n=n, sbk=sbk, i=i: e.matmul(out=ps[0:n, sbk, i:i + 1], lhsT=sqb[:, k, s2 * 128:s2 * 128 + n], rhs=ones[:, 0:1],
                                                                           start=(k == 0), stop=(k == KC - 1), skip_group_check=True) for k in range(KC)]
                    P.group(fns, reads=[sq2b[s2], bf("ones")], writes=[pb[sbk]])
                P.op("act", lambda e, n=n, sbk=sbk, i=i: e.activation(out=S[0:n, 5, i:i + 1], in_=ps[0:n, sbk, i:i + 1], func=AF.Sqrt, bias=EPS, scale=1.0 / D),
                     reads=[pb[sbk]], writes=[rtb])
                P.op("dve", lambda e, n=n, i=i: e.reciprocal(out=S[0:n, 5, i:i + 1], in_=S[0:n, 5, i:i + 1]), reads=[rtb], writes=[rtb])
                for q in range(4):
                    bk = bank % 6
                    bank += 1
                    fns = [lambda e, q=q, j=j, bk=bk, n=n, col0=col0: e.transpose(out=ps[0:n, bk, j * 128:(j + 1) * 128], in_=xT[:, 4 * q + j, col0:col0 + n], identity=idf[:])
                           for j in range(4)]
                    P.group(fns, reads=xbs + [bf("idf")], writes=[pb[bk]])
                    stt(yout[s3][0:n, q * 512:(q + 1) * 512], ps[0:n, bk, :], S[0:n, 5, i:i + 1], gbc[0:n, q * 512:(q + 1) * 512], ALU.mult, ALU.mult,
                        [pb[bk], rtb, gB], [yob[s3]])
                outs.append(P.dma("sp", lambda e, i=i, n=n, s3=s3: e.dma_start(out=y_d[i * 128:i * 128 + n, :], in_=yout[s3][0:n, :]),
                                  "yo%d" % s3, reads=[yob[s3]]))
            last = {}
            for h in outs:
                if last.get(h[0], 0) < h[1]:
                    last[h[0]] = h[1]
            P.wait("sp", list(last.items()))

        Wd = WStream(Prog(), wt, plan=None)
        emit(Wd.P, Wd)
        P = Prog()
        W = WStream(P, wt, plan=Wd.req)
        emit(P, W)
        assert W.i == len(Wd.req)

        sems = {n: st.enter_context(nc.semaphore(n)) for n in P.sem_names()}
        block = st.enter_context(nc.Block())

        @block.tensor
        def _(e):
            P.replay("pe", e, sems)

        @block.scalar
        def _(e):
            P.replay("act", e, sems)

        @block.vector
        def _(e):
            P.replay("dve", e, sems)

        @block.gpsimd
        def _(e):
            P.replay("pool", e, sems)

        @block.sync
        def _(e):
            P.replay("sp", e, sems)
    return nc


def _pack_params(norm1_g, b_in, conv_w, conv_b, ln_g, ln_b, pool_scale, norm2_g, norm_f, half):
    prm = np.zeros((128, NPRM), np.float32)

    def cols(v):
        return np.ascontiguousarray(np.asarray(v, np.float32).reshape(-1, 128).T)

    for l in range(DEPTH):
        L0 = l * PL
        prm[:, L0 + O_G1:L0 + O_G1 + 16] = cols(norm1_g[l])
        prm[:, L0 + O_BIN:L0 + O_BIN + 24] = cols(b_in[l])
        cw = np.asarray(conv_w[l], np.float32)
        prm[:, L0 + O_CW:L0 + O_CW + 248] = cw.reshape(31, 8, 128).transpose(2, 1, 0).reshape(128, 248)
        prm[:, L0 + O_CB:L0 + O_CB + 8] = cols(conv_b[l])
        prm[:, L0 + O_LNG:L0 + O_LNG + 8] = cols(ln_g[l])
        prm[:, L0 + O_LNB:L0 + O_LNB + 8] = cols(ln_b[l])
        prm[:, L0 + O_PSC:L0 + O_PSC + 8] = cols(pool_scale[l])
        prm[:, L0 + O_G2:L0 + O_G2 + 16] = cols(norm2_g[l])
    prm[:, O_GF:O_GF + 16] = cols(norm_f)
    prm[:, O_HM] = float(half)
    pos = np.arange(16)
    for g in range(4):
        w = 2 << g
        cnt = np.minimum(w, pos + 1) if half == 0 else np.full(16, w)
        prm[:, O_IC + g * 16:O_IC + (g + 1) * 16] = (1.0 / cnt.astype(np.float32))[None, :]
    return prm


_NC_CACHE = {}


def kernel(x_prompt, x_sample, state_conv, state_pool, norm1_g, w_in, b_in, conv_w, conv_b,
           ln_g, ln_b, pool_w, pool_scale, w_out, norm2_g, w_ff1, w_ff2, norm_f):
    f = lambda a: np.ascontiguousarray(np.asarray(a, dtype=np.float32))
    x_prompt, x_sample, state_conv, state_pool = f(x_prompt), f(x_sample), f(state_conv), f(state_pool)
    w_in, pool_w, w_out, w_ff1, w_ff2 = f(w_in), f(pool_w), f(w_out), f(w_ff1), f(w_ff2)
    if "nc" not in _NC_CACHE:
        _NC_CACHE["nc"] = build_program()
    nc = _NC_CACHE["nc"]
    ident = np.eye(128, dtype=np.float32)
    gfb = np.ascontiguousarray(np.broadcast_to(np.asarray(norm_f, np.float32)[None, :], (128, D)))
    prms = [_pack_params(norm1_g, b_in, conv_w, conv_b, ln_g, ln_b, pool_scale, norm2_g, norm_f, h) for h in range(2)]
    in_maps = []
    for i in range(NCORES):
        b, h = i // 2, i % 2
        xc = np.zeros((T, D), np.float32)
        if h == 1:
            xc[0:HALO] = x_prompt[b, NPR - HALO:NPR]
        xc[HALO:HALO + NPR] = x_prompt[b, h * NPR:(h + 1) * NPR]
        xc[HALO + NPR:] = x_sample[SBQ * i:SBQ * (i + 1)].reshape(NSM, D)
        in_maps.append({
            "x": xc,
            "sconv": np.ascontiguousarray(state_conv[:, SBQ * i:SBQ * (i + 1)]),
            "spool": np.ascontiguousarray(state_pool[:, SBQ * i:SBQ * (i + 1)]),
            "prm": prms[h], "ident": ident, "gfb": gfb,
            "w_in": w_in, "pool_w": pool_w, "w_out": w_out, "w_ff1": w_ff1, "w_ff2": w_ff2,
        })
    ncr = int(os.environ.get("KCORES", NCORES))
    res = run_bass_kernel_spmd(nc, in_maps[:ncr], core_ids=list(range(ncr)))
    R = list(res.results) + [res.results[0]] * (NCORES - ncr)
    B_, S_ = x_prompt.shape[0], x_prompt.shape[1]
    y_prompt = np.empty((B_, S_, D), np.float32)
    y_sample = np.empty(x_sample.shape, np.float32)
    ncp = np.empty((DEPTH, B_, 30, 1024), np.float32)
    npp = np.empty((DEPTH, B_, 15, 1024), np.float32)
    ncs = np.empty(state_conv.shape, np.float32)
    nps = np.empty(state_pool.shape, np.float32)
    for i in range(NCORES):
        b, h = i // 2, i % 2
        y = R[i]["y"]
        y_prompt[b, h * NPR:(h + 1) * NPR] = y[0:NPR]
        y_sample[SBQ * i:SBQ * (i + 1)] = y[NPR:].reshape(SBQ, 4, D)
        ncs[:, SBQ * i:SBQ * (i + 1)] = R[i]["ocs"]
        nps[:, SBQ * i:SBQ * (i + 1)] = R[i]["ops"]
        if h == 1:
            ncp[:, b] = R[i]["ocp"]
            npp[:, b] = R[i]["opp"]
    return (y_prompt, y_sample, ncp, npp, ncs, nps)
```

```python
import os
import numpy as np
from contextlib import ExitStack
import concourse.bass as bass
import concourse.mybir as mybir
from concourse.bass_utils import run_bass_kernel_spmd

F32 = mybir.dt.float32
BF16 = mybir.dt.bfloat16
AF = mybir.ActivationFunctionType
ALU = mybir.AluOpType

NCORES = 8
D = 2048
KC = 16
DIN = 3072
DFF = 8192
DEPTH = 2
T = 1152
TN = 384
HALO = 64
NPR = 1024
NSM = 64
SBQ = 16
NYR = NPR + NSM
EPS = 1e-6
NSLOT = 3
FB = 2048
TMAP = [0, 2, 1]

PL = 336
O_G1, O_BIN, O_CW, O_CB, O_LNG, O_LNB, O_PSC, O_G2 = 0, 16, 40, 288, 296, 304, 312, 320
O_GF = DEPTH * PL
O_HM = O_GF + 16
O_IC = O_HM + 1
NPRM = O_IC + 64

SAME_ENG_SYNC = True
_DBG = set(os.environ.get("KDBG", "").split(",")) - {""}


class Buf:
    __slots__ = ("name", "w", "r", "pr", "excl")

    def __init__(self, name, excl=False):
        self.name = name
        self.w = {}
        self.r = {}
        self.pr = {}
        self.excl = excl


def _split(reads, writes):
    ex = [b for b in reads if b.excl]
    if not ex:
        return reads, writes
    return [b for b in reads if not b.excl], list(writes) + ex


def _merge(dst, src):
    for k, v in src.items():
        if dst.get(k, 0) < v:
            dst[k] = v


def handoff(src_bufs, dst_bufs):
    u = {}
    for b in src_bufs:
        _merge(u, b.w)
        _merge(u, b.r)
        _merge(u, b.pr)
    for b in dst_bufs:
        b.w = dict(u)
        b.r = {}
        b.pr = dict(u)


class Prog:
    ENG = ("pe", "act", "dve", "pool", "sp")

    def __init__(self):
        self.ops = {e: [] for e in self.ENG}
        self.cnt = {e: 0 for e in self.ENG}
        self.dcnt = {}

    def _deps(self, reads, writes, deps):
        d = {}
        for b in reads:
            _merge(d, b.w)
        for b in writes:
            _merge(d, b.r)
            _merge(d, b.pr)
            _merge(d, b.w)
        for h in deps:
            if h is not None:
                _merge(d, {h[0]: h[1]})
        return d

    def _reg(self, h, reads, writes):
        k, v = h
        for b in reads:
            if b.r.get(k, 0) < v:
                b.r[k] = v
        for b in writes:
            if b.r:
                b.pr = b.r
                b.r = {}
                b.w = {k: v}
            else:
                if b.w.get(k, 0) < v:
                    b.w[k] = v

    def op(self, eng, fn, reads=(), writes=(), deps=()):
        reads, writes = _split(reads, writes)
        d = self._deps(reads, writes, deps)
        self.cnt[eng] += 1
        h = (eng, self.cnt[eng])
        self.ops[eng].append((fn, d, (eng, 1)))
        self._reg(h, reads, writes)
        return h

    def group(self, fns, reads=(), writes=(), deps=()):
        reads, writes = _split(reads, writes)
        d = self._deps(reads, writes, deps)
        self.cnt["pe"] += 1
        h = ("pe", self.cnt["pe"])
        n = len(fns)
        for i, fn in enumerate(fns):
            self.ops["pe"].append((fn, d if i == 0 else {}, ("pe", 1) if i == n - 1 else None))
        self._reg(h, reads, writes)
        return h

    def dma(self, queue, fn, sem, reads=(), writes=(), deps=()):
        reads, writes = _split(reads, writes)
        d = self._deps(reads, writes, deps)
        self.dcnt[sem] = self.dcnt.get(sem, 0) + 16
        h = (sem, self.dcnt[sem])
        self.ops[queue].append((fn, d, (sem, 16)))
        self._reg(h, reads, writes)
        return h

    def wait(self, eng, deps):
        d = {}
        for h in deps:
            _merge(d, {h[0]: h[1]})
        self.ops[eng].append((None, d, None))

    def sem_names(self):
        return list(self.ENG) + sorted(self.dcnt.keys())

    def replay(self, eng, e, sems):
        waited = {}
        for fn, d, inc in self.ops[eng]:
            for k, v in d.items():
                if k == eng and (eng == "pe" or not SAME_ENG_SYNC):
                    continue
                if waited.get(k, 0) < v:
                    e.wait_ge(sems[k], v)
                    waited[k] = v
            if fn is None:
                continue
            inst = fn(e)
            if inc is not None:
                inst.then_inc(sems[inc[0]], inc[1])


class WStream:
    def __init__(self, P, wt, plan=None):
        self.P = P
        self.wt = wt
        self.plan = plan
        self.req = []
        self.i = 0
        self.issued = 0
        self.bufs = [Buf("w%d" % s) for s in range(NSLOT)]

    def _issue(self):
        i = self.issued
        s = i % NSLOT
        src, kc = self.plan[i]
        wt = self.wt
        self.P.dma("pool", lambda e, s=s, src=src, kc=kc: e.dma_start(out=wt[:, s, 0:kc, :], in_=src),
                   "w%d" % s, writes=[self.bufs[s]])
        self.issued += 1

    def get(self, src, kc):
        i = self.i
        self.i += 1
        if self.plan is None:
            self.req.append((src, kc))
            return i % NSLOT, self.bufs[i % NSLOT]
        while self.issued < min(i + NSLOT, len(self.plan)):
            self._issue()
        return i % NSLOT, self.bufs[i % NSLOT]


def build_program():
    nc = bass.Bass("TRN2", target_bir_lowering=False)
    dt_in = lambda n, s: nc.dram_tensor(n, s, F32, kind="ExternalInput").ap()
    dt_out = lambda n, s: nc.dram_tensor(n, s, F32, kind="ExternalOutput").ap()
    x_d = dt_in("x", [T, D])
    sconv_d = dt_in("sconv", [DEPTH, SBQ, 30, 1024])
    spool_d = dt_in("spool", [DEPTH, SBQ, 15, 1024])
    prm_d = dt_in("prm", [128, NPRM])
    ident_d = dt_in("ident", [128, 128])
    gfb_d = dt_in("gfb", [128, D])
    w_in_d = dt_in("w_in", [DEPTH, D, DIN])
    pool_w_d = dt_in("pool_w", [DEPTH, 4, 256, 256])
    w_out_d = dt_in("w_out", [DEPTH, D, D])
    w_ff1_d = dt_in("w_ff1", [DEPTH, D, DFF])
    w_ff2_d = dt_in("w_ff2", [DEPTH, DFF, D])
    y_d = dt_out("y", [NYR, D])
    ocs_d = dt_out("ocs", [DEPTH, SBQ, 30, 1024])
    ops_d = dt_out("ops", [DEPTH, SBQ, 15, 1024])
    ocp_d = dt_out("ocp", [DEPTH, 30, 1024])
    opp_d = dt_out("opp", [DEPTH, 15, 1024])

    w_in_v = [w_in_d[l].rearrange("(kc p) m -> p kc m", p=128) for l in range(DEPTH)]
    w_out_v = [w_out_d[l].rearrange("(kc p) m -> p kc m", p=128) for l in range(DEPTH)]
    w_ff1_v = [w_ff1_d[l].rearrange("(kc p) m -> p kc m", p=128) for l in range(DEPTH)]
    w_ff2_v = [w_ff2_d[l].rearrange("(jj kc p) m -> jj p kc m", p=128, kc=KC) for l in range(DEPTH)]
    pool_w_v = [[pool_w_d[l, g].rearrange("(kk p) m -> p kk m", p=128) for g in range(4)] for l in range(DEPTH)]

    with ExitStack() as st:
        sb = lambda n, s, d: st.enter_context(nc.sbuf_tensor(n, s, d))
        xT = sb("xT", [128, KC, T], F32)
        Bm = sb("Bm", [128, 18432], BF16)
        Cm = sb("Cm", [128, 9216], F32)
        S = sb("S", [128, 6, TN], F32)
        wt = sb("wt", [128, NSLOT, KC, 256], BF16)
        ubuf = sb("ubuf", [128, 3, 32 + TN], BF16)
        hsave = sb("hsave", [128, 8, 32], BF16)
        usamp = sb("usamp", [128, 8, SBQ, 34], BF16)
        zf = sb("zf", [128, 2, TN], F32)
        zb = sb("zb", [128, 2, 16 + TN], BF16)
        zsave = sb("zsave", [128, 8, 16], BF16)
        zsamp = sb("zsamp", [128, 8, SBQ, 19], BF16)
        dbuf = sb("dbuf", [128, 2, 2, TN], BF16)
        prm = sb("prm_t", [128, NPRM], F32)
        idf = sb("idf", [128, 128], F32)
        idb = sb("idb", [128, 128], BF16)
        ones = sb("ones", [128, 128], BF16)
        ps = st.enter_context(nc.psum_tensor("ps", [128, 8, 512], F32))

        hT = Bm[:, 0:6144].rearrange("p (k t) -> p k t", k=KC)
        mix = Bm[:, 6144:12288].rearrange("p (k t) -> p k t", k=KC)
        sqb = Bm[:, 12288:18432].rearrange("p (k t) -> p k t", k=KC)
        utail = Bm[:, 12288:12288 + 1504].bitcast(F32).rearrange("p (c t) -> p c t", c=8)
        ztail = Bm[:, 12288 + 1504:12288 + 3008].bitcast(F32).rearrange("p (c t) -> p c t", c=8)
        h2T = Bm[:, :].rearrange("p (a k t) -> p a k t", a=3, k=KC)
        tstage = S[:, :, :].rearrange("p a t -> p (a t)")[:, 0:2048].rearrange("p (a c) -> p a c", a=2)
        xin = [Cm[:, s * 2048:(s + 1) * 2048] for s in range(4)]
        stgc = Cm[:, 0:4096].rearrange("p (g c) -> p g c", g=4)
        stgp = Cm[:, 4096:6144].rearrange("p (g c) -> p g c", g=2)
        vv = Cm[:, 0:3072].rearrange("p (c t) -> p c t", c=8)
        vbf = Cm[:, 3072:4608].bitcast(BF16).rearrange("p (c t) -> p c t", c=8)
        sqv = Cm[:, 4608:6144].bitcast(BF16).rearrange("p (c t) -> p c t", c=8)
        diag = Cm[:, 6144:8128].bitcast(BF16).rearrange("p (k j) -> p k j", k=31)
        sig = [Cm[:, 8128 + i * TN:8128 + (i + 1) * TN] for i in range(2)]
        t64 = Cm[:, 8896:8960]
        t16 = Cm[:, 8960:8976]
        fT = Cm[:, :].bitcast(BF16).rearrange("p (a k t) -> p a k t", a=3, k=KC)
        sqFin = Bm[:, :].rearrange("p (k t) -> p k t", k=KC)
        gbc = Cm[:, 0:2048]
        yout = [Cm[:, 2048 + s * 2048:2048 + (s + 1) * 2048] for s in range(3)]
        Sflat = S[:, 0:3, :].rearrange("p a t -> p (a t)")

        def pcol(c):
            return prm[:, c:c + 1]

        def emit(P, W):
            B = {}

            def bf(n):
                if n not in B:
                    B[n] = Buf(n)
                return B[n]

            mixbufs_ref = [None]
            pb = [bf("pb%d" % i) for i in range(8)]
            for b_ in pb:
                b_.excl = True
            xTb = [bf("xT%d" % t) for t in range(3)]
            state = {"ring": 0, "alt": 0}

            def ring3():
                b = state["ring"] % 3
                state["ring"] += 1
                return b

            def alt_eng():
                state["alt"] += 1
                return "act" if state["alt"] % 2 else "dve"

            def copy_op(eng, out, in_, reads, writes):
                if eng == "act":
                    return P.op("act", lambda e: e.activation(out=out, in_=in_, func=AF.Copy), reads=reads, writes=writes)
                return P.op("dve", lambda e: e.tensor_copy(out=out, in_=in_), reads=reads, writes=writes)

            outs = []
            P.dma("sp", lambda e: e.dma_start(out=prm[:], in_=prm_d[:, :]), "c0", writes=[bf("prm")])
            P.dma("sp", lambda e: e.dma_start(out=idf[:], in_=ident_d[:, :]), "c1", writes=[bf("idf")])
            P.op("dve", lambda e: e.tensor_copy(out=idb[:], in_=idf[:]), reads=[bf("idf")], writes=[bf("idb")])
            P.op("dve", lambda e: e.memset(ones[:], 1.0), writes=[bf("ones")])
            for l in range(DEPTH if "nopt" not in _DBG else 0):
                outs.append(P.dma("sp", lambda e, l=l: e.dma_start(out=ocs_d[l, :, 0:26, :], in_=sconv_d[l, :, 4:30, :]), "pt"))
                outs.append(P.dma("sp", lambda e, l=l: e.dma_start(out=ops_d[l, :, 0:11, :], in_=spool_d[l, :, 4:15, :]), "pt"))

            xinb = [bf("xin%d" % i) for i in range(4)]
            bank = 0
            for i in range(T // 128):
                s = i % 4
                P.dma("sp", lambda e, i=i, s=s: e.dma_start(out=xin[s], in_=x_d[i * 128:(i + 1) * 128, :]),
                      "xi%d" % s, writes=[xinb[s]])
                for q in range(4):
                    bk = bank % 8
                    bank += 1
                    fns = [lambda e, s=s, q=q, j=j, bk=bk: e.transpose(
                        out=ps[:, bk, j * 128:(j + 1) * 128], in_=xin[s][:, (4 * q + j) * 128:(4 * q + j + 1) * 128], identity=idf[:])
                        for j in range(4)]
                    P.group(fns, reads=[xinb[s], bf("idf")], writes=[pb[bk]])
                    copy_op(alt_eng(), xT[:, 4 * q:4 * q + 4, i * 128:(i + 1) * 128],
                            ps[:, bk, :].rearrange("p (j t) -> p j t", t=128), [pb[bk]], [xTb[i // 3]])
            cbufs = list(xinb)

            def rms_stats(l_g_unused, src_cols, sq_view, sq_buf, bank_i, out_slot, out_buf, xbufs):
                for k in range(KC):
                    P.op("act", lambda e, k=k: e.activation(out=sq_view[:, k, :], in_=xT[:, k, src_cols[0]:src_cols[1]], func=AF.Square),
                         reads=xbufs, writes=[sq_buf])
                n = src_cols[1] - src_cols[0]
                fns = [lambda e, k=k: e.matmul(out=ps[:, bank_i, 0:n], lhsT=ones[:], rhs=sq_view[:, k, :], start=(k == 0), stop=(k == KC - 1))
                       for k in range(KC)]
                P.group(fns, reads=[sq_buf, bf("ones")], writes=[pb[bank_i]])
                rstd_finish(bank_i, n, out_slot, out_buf)

            def rstd_finish(bank_i, n, out_slot, out_buf):
                if out_slot is None:
                    out_slot, out_buf = ps[:, bank_i, 0:n], pb[bank_i]
                P.op("act", lambda e: e.activation(out=out_slot, in_=ps[:, bank_i, 0:n], func=AF.Sqrt, bias=EPS, scale=1.0 / D),
                     reads=[pb[bank_i]], writes=[out_buf])
                P.op("dve", lambda e: e.reciprocal(out=out_slot, in_=out_slot), reads=[out_buf], writes=[out_buf])

            def rms1_pre(l, t):
                c0 = t * TN
                rms_stats(None, (c0, c0 + TN), sqb, bf("sqb"), 7, None, None, [xTb[t]])
                for k in range(KC):
                    P.op("dve", lambda e, k=k: e.scalar_tensor_tensor(
                        out=hT[:, k, :], in0=xT[:, k, c0:c0 + TN], scalar=pcol(l * PL + O_G1 + k), in1=ps[:, 7, 0:TN],
                        op0=ALU.mult, op1=ALU.mult), reads=[xTb[t], pb[7], bf("prm")], writes=[bf("hT")])

            def mm_group(slot, wbuf, rhs_of_k, nk, bank_i, n, reads, half=0):
                fns = [lambda e, k=k: e.matmul(out=ps[:, bank_i, 0:n], lhsT=wt[:, slot, k, half * 128:(half + 1) * 128], rhs=rhs_of_k(k),
                                               start=(k == 0), stop=(k == nk - 1)) for k in range(nk)]
                return P.group(fns, reads=[wbuf] + reads, writes=[pb[bank_i]])

            def stt(out, in0, scalar, in1, op0, op1, reads, writes):
                return P.op("dve", lambda e: e.scalar_tensor_tensor(out=out, in0=in0, scalar=scalar, in1=in1, op0=op0, op1=op1),
                            reads=reads, writes=writes)

            def ts1(out, in0, s1, op0, reads, writes):
                return P.op("dve", lambda e: e.tensor_scalar(out=out, in0=in0, scalar1=s1, scalar2=None, op0=op0),
                            reads=reads, writes=writes)

            def ts2(out, in0, s1, s2, op0, op1, reads, writes):
                return P.op("dve", lambda e: e.tensor_scalar(out=out, in0=in0, scalar1=s1, scalar2=s2, op0=op0, op1=op1),
                            reads=reads, writes=writes)

            def act(out, in_, func, reads, writes, bias=None, scale=None):
                kw = {}
                if bias is not None:
                    kw["bias"] = bias
                if scale is not None:
                    kw["scale"] = scale
                return P.op("act", lambda e: e.activation(out=out, in_=in_, func=func, **kw), reads=reads, writes=writes)

            def mixer(l, t):
                L0 = l * PL
                c0 = t * TN
                npr = TN if t < 2 else TN - NSM
                prmb = bf("prm")
                hTb, mixb = bf("hT"), bf("mix")
                vb, vbfb, sqvb = bf("v"), bf("vbf"), bf("sqv")
                sigb = [bf("sig0"), bf("sig1")]
                ubb = [bf("ub0"), bf("ub1"), bf("ub2")]

                nxt = t + 1 if t < 2 else None
                dgb = [bf("diagA"), bf("diagB")]
                halves = [(0, 16), (16, 31)]

                def conv_build(c):
                    for hi, (k0, k1) in enumerate(halves):
                        nk = k1 - k0
                        P.op("dve", lambda e, k0=k0, k1=k1, nk=nk: e.tensor_tensor(
                            out=diag[:, k0:k1, :], in0=ps[:, 4, 0:128].unsqueeze(1).broadcast_to([128, nk, 128]),
                            in1=prm[:, L0 + O_CW + c * 31 + k0:L0 + O_CW + c * 31 + k1].unsqueeze(2).broadcast_to([128, nk, 128]),
                            op=ALU.mult), reads=[pb[4], prmb], writes=[dgb[hi]])

                def conv(c):
                    ui = c % 3
                    ub = ubuf[:, ui, :]
                    for hi, (k0, k1) in enumerate(halves):
                        fns = [lambda e, k=k: e.matmul(out=ps[:, 3, 0:npr], lhsT=diag[:, k, :], rhs=ub[:, 2 + k:2 + k + npr],
                                                       start=(k == 0), stop=(k == 30)) for k in range(k0, k1)]
                        P.group(fns, reads=[dgb[hi], ubb[ui]], writes=[pb[3]])
                    if t == 2:
                        fns = [lambda e, k=k: e.matmul(out=ps[:, 3, npr:TN], lhsT=diag[:, k, :], rhs=usamp[:, c, :, k:k + 4],
                                                       start=(k == 0), stop=(k == 30), skip_group_check=True) for k in range(31)]
                        P.group(fns, reads=[dgb[0], dgb[1], bf("usamp%d" % c)], writes=[pb[3]])
                    cb = pcol(L0 + O_CB + c)
                    act(vv[:, c, :], ps[:, 3, 0:TN], AF.Identity, [pb[3], prmb], [vb], bias=cb)
                    ts1(vbf[:, c, :], ps[:, 3, 0:TN], cb, ALU.add, [pb[3], prmb], [vbfb])
                    act(sqv[:, c, :], ps[:, 3, 0:TN], AF.Square, [pb[3], prmb], [sqvb], bias=cb)

                for c in range(8):
                    ui = c % 3
                    ub = ubuf[:, ui, :]
                    if c > 0:
                        conv_build(c - 1)
                    sa, wa = W.get(w_in_v[l][:, :, c * 256:(c + 1) * 256], KC)
                    ba = ring3()
                    mm_group(sa, wa, lambda k: hT[:, k, :], KC, ba, TN, [hTb], half=0)
                    bg = ring3()
                    mm_group(sa, wa, lambda k: hT[:, k, :], KC, bg, TN, [hTb], half=1)
                    si = c % 2
                    act(sig[si], ps[:, bg, 0:TN], AF.Sigmoid, [pb[bg], prmb], [sigb[si]], bias=pcol(L0 + O_BIN + 8 + c))
                    ba_col = pcol(L0 + O_BIN + c)
                    rdu = [pb[ba], sigb[si], prmb]
                    if t == 0:
                        P.op("dve", lambda e, ub=ub: e.memset(ub[:, 0:32], 0.0), writes=[ubb[ui]])
                    else:
                        P.op("dve", lambda e, ub=ub, c=c: e.tensor_copy(out=ub[:, 0:32], in_=hsave[:, c, :]),
                             reads=[bf("hsave%d" % c)], writes=[ubb[ui]])
                    if t == 0:
                        stt(t64, ps[:, ba, 0:HALO], ba_col, sig[si][:, 0:HALO], ALU.add, ALU.mult, rdu, [bf("t64")])
                        ts1(ub[:, 32:32 + HALO], t64, pcol(O_HM), ALU.mult, [bf("t64"), prmb], [ubb[ui]])
                        stt(ub[:, 32 + HALO:32 + TN], ps[:, ba, HALO:TN], ba_col, sig[si][:, HALO:TN], ALU.add, ALU.mult, rdu, [ubb[ui]])
                    else:
                        stt(ub[:, 32:32 + npr], ps[:, ba, 0:npr], ba_col, sig[si][:, 0:npr], ALU.add, ALU.mult, rdu, [ubb[ui]])
                    if t == 2:
                        stt(usamp[:, c, :, 30:34], ps[:, ba, npr:TN].rearrange("p (b j) -> p b j", j=4), ba_col,
                            sig[si][:, npr:TN].rearrange("p (b j) -> p b j", j=4), ALU.add, ALU.mult, rdu, [bf("usamp%d" % c)])
                        stt(utail[:, c, :], ps[:, ba, TN - 94:TN], ba_col, sig[si][:, TN - 94:TN], ALU.add, ALU.mult, rdu, [bf("sqb")])
                    else:
                        P.op("dve", lambda e, ub=ub, c=c: e.tensor_copy(out=hsave[:, c, :], in_=ub[:, TN:TN + 32]),
                             reads=[ubb[ui]], writes=[bf("hsave%d" % c)])
                    if nxt is not None:
                        n0 = nxt * TN
                        for k in (2 * c, 2 * c + 1):
                            P.op("act", lambda e, k=k, n0=n0: e.activation(out=sqb[:, k, :], in_=xT[:, k, n0:n0 + TN], func=AF.Square),
                                 reads=[xTb[nxt]], writes=[bf("sqb")])
                    if c > 0:
                        conv(c - 1)
                conv_build(7)
                conv(7)

                def pool_A(g):
                    bz = []
                    sz, wz = W.get(w_in_v[l][:, :, 2048 + g * 256:2048 + (g + 1) * 256], KC)
                    for j in range(2):
                        m = 2 * g + j
                        b_ = ring3()
                        bz.append(b_)
                        mm_group(sz, wz, lambda k: hT[:, k, :], KC, b_, TN, [hTb], half=j)
                        bzc = pcol(L0 + O_BIN + 16 + m)
                        zfB, zbB = bf("zf%d" % j), bf("zb%d" % j)
                        if t == 0:
                            P.op("dve", lambda e, j=j: e.memset(zb[:, j, 0:16], 0.0), writes=[zbB])
                            ts2(zb[:, j, 16:16 + HALO], ps[:, b_, 0:HALO], bzc, pcol(O_HM), ALU.add, ALU.mult, [pb[b_], prmb], [zbB])
                            ts1(zb[:, j, 16 + HALO:16 + TN], ps[:, b_, HALO:TN], bzc, ALU.add, [pb[b_], prmb], [zbB])
                        else:
                            P.op("dve", lambda e, j=j, m=m: e.tensor_copy(out=zb[:, j, 0:16], in_=zsave[:, m, :]),
                                 reads=[bf("zsave%d" % m)], writes=[zbB])
                            ts1(zb[:, j, 16:16 + npr], ps[:, b_, 0:npr], bzc, ALU.add, [pb[b_], prmb], [zbB])
                        if t == 2:
                            ts1(zsamp[:, m, :, 15:19], ps[:, b_, npr:TN].rearrange("p (b j) -> p b j", j=4), bzc, ALU.add,
                                [pb[b_], prmb], [bf("zsamp%d" % m)])
                        act(zf[:, j, :], ps[:, b_, 0:TN], AF.Identity, [pb[b_], prmb], [zfB], bias=bzc)
                        if t == 2:
                            ts1(ztail[:, m, :], ps[:, b_, TN - 94:TN], bzc, ALU.add, [pb[b_], prmb], [bf("sqb")])
                        else:
                            P.op("dve", lambda e, j=j, m=m: e.tensor_copy(out=zsave[:, m, :], in_=zb[:, j, TN:TN + 16]),
                                 reads=[zbB], writes=[bf("zsave%d" % m)])

                def pool_B(g):
                    w = 2 << g
                    db = g % 2
                    dB = bf("d%d" % db)
                    for j in range(2):
                        m = 2 * g + j
                        zfB, zbB = bf("zf%d" % j), bf("zb%d" % j)
                        bp = ring3()
                        fns = [lambda e, k=k, j=j, bp=bp, w=w: e.matmul(out=ps[:, bp, 0:npr], lhsT=idb[:], rhs=zb[:, j, 16 - k:16 - k + npr],
                                                                   start=(k == 0), stop=(k == w - 1)) for k in range(w)]
                        rd = [bf("idb"), zbB]
                        if t == 2:
                            fns += [lambda e, k=k, m=m, bp=bp, w=w: e.matmul(out=ps[:, bp, npr:TN], lhsT=idb[:], rhs=zsamp[:, m, :, 15 - k:19 - k],
                                                                        start=(k == 0), stop=(k == w - 1), skip_group_check=True) for k in range(w)]
                            rd.append(bf("zsamp%d" % m))
                        P.group(fns, reads=rd, writes=[pb[bp]])
                        rdd = [pb[bp], zfB]
                        if t == 0:
                            stt(dbuf[:, db, j, 0:HALO], ps[:, bp, 0:HALO], 1.0 / w, zf[:, j, 0:HALO], ALU.mult, ALU.subtract, rdd, [dB])
                            P.op("dve", lambda e, bp=bp, g=g: e.tensor_tensor(out=t16, in0=ps[:, bp, HALO:HALO + 16],
                                                                              in1=prm[:, O_IC + g * 16:O_IC + (g + 1) * 16], op=ALU.mult),
                                 reads=[pb[bp], prmb], writes=[bf("t16")])
                            P.op("dve", lambda e, j=j, db=db: e.tensor_tensor(out=dbuf[:, db, j, HALO:HALO + 16], in0=t16,
                                                                              in1=zf[:, j, HALO:HALO + 16], op=ALU.subtract),
                                 reads=[bf("t16"), zfB], writes=[dB])
                            stt(dbuf[:, db, j, HALO + 16:TN], ps[:, bp, HALO + 16:TN], 1.0 / w, zf[:, j, HALO + 16:TN],
                                ALU.mult, ALU.subtract, rdd, [dB])
                        else:
                            stt(dbuf[:, db, j, :], ps[:, bp, 0:TN], 1.0 / w, zf[:, j, :], ALU.mult, ALU.subtract, rdd, [dB])

                def pool_C(g):
                    db = g % 2
                    dB = bf("d%d" % db)
                    sp_, wp = W.get(pool_w_v[l][g][:, :, :], 2)
                    for e_ in range(2):
                        m = 2 * g + e_
                        bq = ring3()
                        mm_group(sp_, wp, lambda k, db=db: dbuf[:, db, k, :], 2, bq, TN, [dB], half=e_)
                        act(mix[:, 8 + m, :], ps[:, bq, 0:TN], AF.Copy, [pb[bq], prmb], [mixb], scale=pcol(L0 + O_PSC + m))

                fns = [lambda e, c=c: e.matmul(out=ps[:, 5, 0:TN], lhsT=ones[:], rhs=vbf[:, c, :], start=(c == 0), stop=(c == 7)) for c in range(8)]
                P.group(fns, reads=[vbfb, bf("ones")], writes=[pb[5]])
                fns = [lambda e, c=c: e.matmul(out=ps[:, 6, 0:TN], lhsT=ones[:], rhs=sqv[:, c, :], start=(c == 0), stop=(c == 7)) for c in range(8)]
                P.group(fns, reads=[sqvb, bf("ones")], writes=[pb[6]])
                if nxt is not None:
                    fns = [lambda e, k=k: e.matmul(out=ps[:, 7, 0:TN], lhsT=ones[:], rhs=sqb[:, k, :], start=(k == 0), stop=(k == KC - 1))
                           for k in range(KC)]
                    P.group(fns, reads=[bf("sqb"), bf("ones")], writes=[pb[7]])
                S1b = bf("S1")
                act(ps[:, 5, 0:TN], ps[:, 5, 0:TN], AF.Copy, [pb[5]], [pb[5]], scale=1.0 / 1024)
                act(S[:, 1, :], ps[:, 5, 0:TN], AF.Square, [pb[5]], [S1b])
                stt(ps[:, 6, 0:TN], ps[:, 6, 0:TN], 1.0 / 1024, S[:, 1, :], ALU.mult, ALU.subtract, [pb[6], S1b], [pb[6]])
                act(ps[:, 6, 0:TN], ps[:, 6, 0:TN], AF.Sqrt, [pb[6]], [pb[6]], bias=EPS)
                P.op("dve", lambda e: e.reciprocal(out=ps[:, 6, 0:TN], in_=ps[:, 6, 0:TN]), reads=[pb[6]], writes=[pb[6]])
                if nxt is not None:
                    rstd_finish(7, TN, None, None)
                def ln_apply(c):
                    sl = 2 + c % 2
                    Sb = bf("S%d" % sl)
                    P.op("dve", lambda e, c=c, sl=sl: e.tensor_tensor(out=S[:, sl, :], in0=vv[:, c, :], in1=ps[:, 5, 0:TN], op=ALU.subtract),
                         reads=[vb, pb[5]], writes=[Sb])
                    P.op("dve", lambda e, sl=sl: e.tensor_tensor(out=S[:, sl, :], in0=S[:, sl, :], in1=ps[:, 6, 0:TN], op=ALU.mult),
                         reads=[Sb, pb[6]], writes=[Sb])
                    act(mix[:, c, :], S[:, sl, :], AF.Silu, [Sb, prmb], [mixb], bias=pcol(L0 + O_LNB + c), scale=pcol(L0 + O_LNG + c))

                early2 = (t == 2 and "noffn" not in _DBG)
                sqCb = [bf("sqC%d" % i) for i in range(3)]
                if early2:
                    handoff([vbfb, sqvb], [sqCb[1]])
                    for k in range(KC):
                        P.op("act", lambda e, k=k: e.activation(out=fT[:, 1, k, :], in_=xT[:, k, TN:2 * TN], func=AF.Square),
                             reads=[xTb[1]], writes=[sqCb[1]])
                seq = ["A0", "B0", "A1", "C0", "B1", "A2", "C1", "B2", "A3", "C2", "B3", "C3"]
                for st_ in seq:
                    g = int(st_[1])
                    if st_[0] == "A":
                        pool_A(g)
                        if early2 and g == 1:
                            fns = [lambda e, k=k: e.matmul(out=ps[:, 7, 0:TN], lhsT=ones[:], rhs=fT[:, 1, k, :], start=(k == 0), stop=(k == KC - 1))
                                   for k in range(KC)]
                            P.group(fns, reads=[sqCb[1], bf("ones")], writes=[pb[7]])
                            rstd_finish(7, TN, None, None)
                    elif st_[0] == "B":
                        pool_B(g)
                        ln_apply(2 * g)
                        ln_apply(2 * g + 1)
                    else:
                        pool_C(g)
                if t == 2 and "notails" not in _DBG:
                    Sall6 = [bf("S%d" % i) for i in range(6)]
                    for a, (tl, tlb) in enumerate([(utail, bf("sqb")), (ztail, bf("sqb"))]):
                        for q in range(2):
                            bk = (2 * a + q) % 4
                            fns = [lambda e, tl=tl, q=q, i=i, bk=bk: e.transpose(out=ps[0:94, bk, i * 128:(i + 1) * 128], in_=tl[:, 4 * q + i, :], identity=idf[:])
                                   for i in range(4)]
                            P.group(fns, reads=[tlb, bf("idf")], writes=[pb[bk]])
                            copy_op(alt_eng(), tstage[0:94, a, q * 512:(q + 1) * 512], ps[0:94, bk, :], [pb[bk]], Sall6)
                    outs.append(P.dma("sp", lambda e: e.dma_start(out=ocp_d[l, :, :], in_=tstage[0:30, 0, :]), "to", reads=Sall6))
                    outs.append(P.dma("sp", lambda e: e.dma_start(out=opp_d[l, :, :], in_=tstage[15:30, 1, :]), "to", reads=Sall6))
                    for b in range(SBQ):
                        outs.append(P.dma("sp", lambda e, b=b: e.dma_start(out=ocs_d[l, b, 26:30, :], in_=tstage[30 + 4 * b:34 + 4 * b, 0, :]), "to", reads=Sall6))
                        outs.append(P.dma("sp", lambda e, b=b: e.dma_start(out=ops_d[l, b, 11:15, :], in_=tstage[30 + 4 * b:34 + 4 * b, 1, :]), "to", reads=Sall6))
                if early2:
                    handoff([vb], [sqCb[0]])
                    handoff([bf(n) for n in ("diagA", "diagB", "sig0", "sig1", "t64", "t16")], [sqCb[2]])
                    for k in range(KC):
                        P.op("act", lambda e, k=k: e.activation(out=fT[:, 0, k, :], in_=xT[:, k, 0:TN], func=AF.Square),
                             reads=[xTb[0]], writes=[sqCb[0]])
                for m in range(KC):
                    if m % 2 == 0:
                        so, wo = W.get(w_out_v[l][:, :, m * 128:(m + 2) * 128], KC)
                    bo = ring3()
                    mm_group(so, wo, lambda k: mix[:, k, :], KC, bo, TN, [mixb], half=m % 2)
                    P.op("dve", lambda e, m=m, bo=bo: e.tensor_tensor(out=xT[:, m, c0:c0 + TN], in0=xT[:, m, c0:c0 + TN], in1=ps[:, bo, 0:TN], op=ALU.add),
                         reads=[pb[bo], xTb[t]], writes=[xTb[t]])
                    if nxt is not None:
                        n0 = nxt * TN
                        stt(hT[:, m, :], xT[:, m, n0:n0 + TN], pcol(L0 + O_G1 + m), ps[:, 7, 0:TN], ALU.mult, ALU.mult,
                            [xTb[nxt], pb[7], prmb], [hTb])
                    elif early2:
                        P.op("act", lambda e, m=m: e.activation(out=fT[:, 2, m, :], in_=xT[:, m, c0:c0 + TN], func=AF.Square),
                             reads=[xTb[2]], writes=[sqCb[2]])
                        stt(h2T[:, TMAP[1], m, :], xT[:, m, TN:2 * TN], pcol(L0 + O_G2 + m), ps[:, 7, 0:TN], ALU.mult, ALU.mult,
                            [xTb[1], pb[7], prmb], [bf("sqb")])
                        if m == 4:
                            fns = [lambda e, k=k: e.matmul(out=ps[:, 5, 0:TN], lhsT=ones[:], rhs=fT[:, 0, k, :], start=(k == 0), stop=(k == KC - 1))
                                   for k in range(KC)]
                            P.group(fns, reads=[sqCb[0], bf("ones")], writes=[pb[5]])
                            rstd_finish(5, TN, None, None)
                        if 5 <= m <= 12:
                            for k in (2 * (m - 5), 2 * (m - 5) + 1):
                                stt(h2T[:, TMAP[0], k, :], xT[:, k, 0:TN], pcol(L0 + O_G2 + k), ps[:, 5, 0:TN], ALU.mult, ALU.mult,
                                    [xTb[0], pb[5], prmb], [hTb])
            def ffn(l):
                L0 = l * PL
                prmb = bf("prm")
                h2b, fCb = bf("h2T"), bf("fC")
                Sb = [bf("S%d" % i) for i in range(6)]
                early = "nomixer" not in _DBG and "t01" not in _DBG
                h2tb = [bf("hT"), bf("sqb"), bf("mix")] if early else None
                sqCb = [bf("sqC%d" % i) for i in range(3)]
                bks = [5, 7, 6]
                for t in range(3):
                    c0 = t * TN
                    bk = bks[t]
                    if not early:
                        for k in range(KC):
                            P.op("act", lambda e, k=k, t=t, c0=c0: e.activation(out=fT[:, t, k, :], in_=xT[:, k, c0:c0 + TN], func=AF.Square),
                                 reads=[xTb[t]], writes=[sqCb[t]])
                    if not early or t == 2:
                        fns = [lambda e, k=k, t=t, bk=bk: e.matmul(out=ps[:, bk, 0:TN], lhsT=ones[:], rhs=fT[:, t, k, :], start=(k == 0), stop=(k == KC - 1))
                               for k in range(KC)]
                        P.group(fns, reads=[sqCb[t], bf("ones")], writes=[pb[bk]])
                        rstd_finish(bk, TN, None, None)
                        for k in range(KC):
                            stt(h2T[:, TMAP[t], k, :], xT[:, k, c0:c0 + TN], pcol(L0 + O_G2 + k), ps[:, bk, 0:TN], ALU.mult, ALU.mult,
                                [xTb[t], pb[bk], prmb], [h2tb[t] if early else h2b])
                handoff(sqCb, [fCb])
                slot3 = 0
                sc_i = 0
                tr = [((HALO if l == DEPTH - 1 else HALO - 32) if t == 0 else 0, TN) for t in range(3)]
                for j in range(DFF // FB):
                    for m in range(KC):
                        if m % 2 == 0:
                            s1, w1 = W.get(w_ff1_v[l][:, :, j * FB + m * 128:j * FB + (m + 2) * 128], KC)
                        hf = m % 2
                        b0 = 3 * (slot3 % 2)
                        slot3 += 1
                        if j == 0 and m < 3 and h2tb is not None:
                            for t in range(3):
                                fns = [lambda e, k=k, t=t, s1=s1, b0=b0, hf=hf: e.matmul(out=ps[:, b0 + t, 0:tr[t][1] - tr[t][0]], lhsT=wt[:, s1, k, hf * 128:(hf + 1) * 128],
                                                                                rhs=h2T[:, TMAP[t], k, tr[t][0]:tr[t][1]],
                                                                                start=(k == 0), stop=(k == KC - 1)) for k in range(KC)]
                                P.group(fns, reads=[w1, h2tb[t]], writes=[pb[b0 + t]])
                        else:
                            fns = [lambda e, k=k, t=t, s1=s1, b0=b0, hf=hf: e.matmul(out=ps[:, b0 + t, 0:tr[t][1] - tr[t][0]], lhsT=wt[:, s1, k, hf * 128:(hf + 1) * 128],
                                                                            rhs=h2T[:, TMAP[t], k, tr[t][0]:tr[t][1]],
                                                                            start=(k == 0), stop=(k == KC - 1)) for k in range(KC) for t in range(3)]
                            P.group(fns, reads=[w1, h2b] + (h2tb or []), writes=[pb[b0], pb[b0 + 1], pb[b0 + 2]])
                        for t in range(3):
                            o0, o1 = tr[t]
                            si = 3 + sc_i % 3
                            sc_i += 1
                            act(S[:, si, 0:o1 - o0], ps[:, b0 + t, 0:o1 - o0], AF.Square, [pb[b0 + t]], [Sb[si]])
                            stt(fT[:, t, m, o0:o1], ps[:, b0 + t, 0:o1 - o0], 0.0, S[:, si, 0:o1 - o0], ALU.is_gt, ALU.mult, [pb[b0 + t], Sb[si]], [fCb])
                    for m in range(KC):
                        if m % 2 == 0:
                            s2, w2 = W.get(w_ff2_v[l][j][:, :, m * 128:(m + 2) * 128], KC)
                        hf = m % 2
                        b0 = 3 * (slot3 % 2)
                        slot3 += 1
                        fns = [lambda e, k=k, t=t, s2=s2, b0=b0, hf=hf: e.matmul(out=ps[:, b0 + t, 0:tr[t][1] - tr[t][0]], lhsT=wt[:, s2, k, hf * 128:(hf + 1) * 128],
                                                                        rhs=fT[:, t, k, tr[t][0]:tr[t][1]],
                                                                        start=(k == 0), stop=(k == KC - 1)) for k in range(KC) for t in range(3)]
                        P.group(fns, reads=[w2, fCb], writes=[pb[b0], pb[b0 + 1], pb[b0 + 2]])
                        for t in range(3):
                            o0, o1 = tr[t]
                            c0 = t * TN + o0
                            nn = o1 - o0
                            P.op("dve", lambda e, m=m, t=t, b0=b0, c0=c0, nn=nn: e.tensor_tensor(out=xT[:, m, c0:c0 + nn], in0=xT[:, m, c0:c0 + nn],
                                                                                             in1=ps[:, b0 + t, 0:nn], op=ALU.add),
                                 reads=[pb[b0 + t], xTb[t]], writes=[xTb[t]])
                        if l == DEPTH - 1 and j == DFF // FB - 1:
                            if m == 0:
                                handoff([h2b, bf("hT"), bf("mix"), bf("sqb")], [bf("sqFin")])
                            P.op("act", lambda e, m=m: e.activation(out=sqFin[:, m, :], in_=xT[:, m, :], func=AF.Square),
                                 reads=xTb, writes=[bf("sqFin")])
                return [h2b, fCb]

            for l in range(DEPTH if "nolayers" not in _DBG else 0):
                if l > 0:
                    handoff([bf("h2T"), bf("hT"), bf("mix"), bf("sqb")], [bf("hT"), bf("mix"), bf("sqb")])
                P.group([lambda e: e.transpose(out=ps[:, 4, 0:128], in_=idf[:], identity=idf[:])], reads=[bf("idf")], writes=[pb[4]])
                rms1_pre(l, 0)
                stb = bf("stage")
                handoff(cbufs, [stb])
                for gq in range(4):
                    P.dma("sp", lambda e, l=l, gq=gq: e.dma_start(out=stgc[0:120, gq, :],
                                                                  in_=sconv_d[l, 4 * gq:4 * gq + 4, :, :].rearrange("b j c -> (b j) c")),
                          "sg", writes=[stb])
                for gp in range(2):
                    P.dma("sp", lambda e, l=l, gp=gp: e.dma_start(out=stgp[0:120, gp, :],
                                                                  in_=spool_d[l, 8 * gp:8 * gp + 8, :, :].rearrange("b j c -> (b j) c")),
                          "sg", writes=[stb])
                hbanks = [0, 1, 2, 3, 5, 6]
                bank = 0
                for c in range(8):
                    bk = hbanks[bank % 6]
                    bank += 1
                    fns = [lambda e, c=c, gq=gq, bk=bk: e.transpose(out=ps[:, bk, gq * 120:(gq + 1) * 120], in_=stgc[0:120, gq, c * 128:(c + 1) * 128],
                                                                   identity=idf[0:120, 0:120]) for gq in range(4)]
                    P.group(fns, reads=[stb, bf("idf")], writes=[pb[bk]])
                    copy_op(alt_eng(), usamp[:, c, :, 0:30], ps[:, bk, 0:480].rearrange("p (b j) -> p b j", j=30), [pb[bk]], [bf("usamp%d" % c)])
                    bk = hbanks[bank % 6]
                    bank += 1
                    fns = [lambda e, c=c, gp=gp, bk=bk: e.transpose(out=ps[:, bk, gp * 120:(gp + 1) * 120], in_=stgp[0:120, gp, c * 128:(c + 1) * 128],
                                                                   identity=idf[0:120, 0:120]) for gp in range(2)]
                    P.group(fns, reads=[stb, bf("idf")], writes=[pb[bk]])
                    copy_op(alt_eng(), zsamp[:, c, :, 0:15], ps[:, bk, 0:240].rearrange("p (b j) -> p b j", j=15), [pb[bk]], [bf("zsamp%d" % c)])
                mixbufs = [bf(n) for n in ("v", "vbf", "sqv", "diagA", "diagB", "sig0", "sig1", "t64", "t16")]
                handoff([stb], mixbufs)
                mixbufs_ref[0] = mixbufs
                for t in range((2 if "t01" in _DBG else 3) if "nomixer" not in _DBG else 0):
                    mixer(l, t)
                if _DBG & {"nomixer", "t01"}:
                    handoff([bf("hT"), bf("mix"), bf("sqb")], [bf("h2T")])
                if "nomixer" in _DBG or "noffn" in _DBG or "t01" in _DBG:
                    handoff(mixbufs, [bf("sqC%d" % i) for i in range(3)])
                cbufs = ffn(l)[1:] if "noffn" not in _DBG else [bf("sqC%d" % i) for i in range(3)]

            gB, yob = bf("gbc"), [bf("yout%d" % i) for i in range(3)]
            sq2b = [bf("sqh0"), bf("sqh1")]
            handoff(cbufs, [gB] + yob)
            pre_sq = not (_DBG & {"nolayers", "noffn"})
            if "nolayers" not in _DBG and not pre_sq:
                handoff([bf("h2T"), bf("hT"), bf("mix"), bf("sqb")], sq2b)
            P.dma("sp", lambda e: e.dma_start(out=gbc, in_=gfb_d[:, :]), "c0", writes=[gB])
            rtb = bf("S5")
            bank = 0
            for i in range(9):
                n = 128 if i < 8 else 64
                col0 = HALO + 128 * i
                s2, s3 = i % 2, i % 3
                sbk = 6 + i % 2
                xb_ = sorted({col0 // TN, (col0 + n - 1) // TN})
                xbs = [xTb[j] for j in xb_]
                if pre_sq:
                    fns = [lambda e, k=k, n=n, sbk=sbk, i=i, col0=col0: e.matmul(out=ps[0:n, sbk, i:i + 1], lhsT=sqFin[:, k, col0:col0 + n], rhs=ones[:, 0:1],
                                                                                 start=(k == 0), stop=(k == KC - 1), skip_group_check=True) for k in range(KC)]
                    P.group(fns, reads=[bf("sqFin"), bf("ones")], writes=[pb[sbk]])
                else:
                    for k in range(KC):
                        P.op("act", lambda e, k=k, s2=s2, n=n, col0=col0: e.activation(out=sqb[:, k, s2 * 128:s2 * 128 + n], in_=xT[:, k, col0:col0 + n], func=AF.Square),
                             reads=xbs, writes=[sq2b[s2]])
                    fns = [lambda e, k=k, s2=s2, n=n, sbk=sbk, i=i: e.matmul(out=ps[0:n, sbk, i:i + 1], lhsT=sqb[:, k, s2 * 128:s2 * 128 + n], rhs=ones[:, 0:1],
                                                                           start=(k == 0), stop=(k == KC - 1), skip_group_check=True) for k in range(KC)]
                    P.group(fns, reads=[sq2b[s2], bf("ones")], writes=[pb[sbk]])
                P.op("act", lambda e, n=n, sbk=sbk, i=i: e.activation(out=S[0:n, 5, i:i + 1], in_=ps[0:n, sbk, i:i + 1], func=AF.Sqrt, bias=EPS, scale=1.0 / D),
                     reads=[pb[sbk]], writes=[rtb])
                P.op("dve", lambda e, n=n, i=i: e.reciprocal(out=S[0:n, 5, i:i + 1], in_=S[0:n, 5, i:i + 1]), reads=[rtb], writes=[rtb])
                for q in range(4):
                    bk = bank % 6
                    bank += 1
                    fns = [lambda e, q=q, j=j, bk=bk, n=n, col0=col0: e.transpose(out=ps[0:n, bk, j * 128:(j + 1) * 128], in_=xT[:, 4 * q + j, col0:col0 + n], identity=idf[:])
                           for j in range(4)]
                    P.group(fns, reads=xbs + [bf("idf")], writes=[pb[bk]])
                    stt(yout[s3][0:n, q * 512:(q + 1) * 512], ps[0:n, bk, :], S[0:n, 5, i:i + 1], gbc[0:n, q * 512:(q + 1) * 512], ALU.mult, ALU.mult,
                        [pb[bk], rtb, gB], [yob[s3]])
                outs.append(P.dma("sp", lambda e, i=i, n=n, s3=s3: e.dma_start(out=y_d[i * 128:i * 128 + n, :], in_=yout[s3][0:n, :]),
                                  "yo%d" % s3, reads=[yob[s3]]))
            last = {}
            for h in outs:
                if last.get(h[0], 0) < h[1]:
                    last[h[0]] = h[1]
            P.wait("sp", list(last.items()))

        Wd = WStream(Prog(), wt, plan=None)
        emit(Wd.P, Wd)
        P = Prog()
        W = WStream(P, wt, plan=Wd.req)
        emit(P, W)
        assert W.i == len(Wd.req)

        sems = {n: st.enter_context(nc.semaphore(n)) for n in P.sem_names()}
        block = st.enter_context(nc.Block())

        @block.tensor
        def _(e):
            P.replay("pe", e, sems)

        @block.scalar
        def _(e):
            P.replay("act", e, sems)

        @block.vector
        def _(e):
            P.replay("dve", e, sems)

        @block.gpsimd
        def _(e):
            P.replay("pool", e, sems)

        @block.sync
        def _(e):
            P.replay("sp", e, sems)
    return nc


def _pack_params(norm1_g, b_in, conv_w, conv_b, ln_g, ln_b, pool_scale, norm2_g, norm_f, half):
    prm = np.zeros((128, NPRM), np.float32)

    def cols(v):
        return np.ascontiguousarray(np.asarray(v, np.float32).reshape(-1, 128).T)

    for l in range(DEPTH):
        L0 = l * PL
        prm[:, L0 + O_G1:L0 + O_G1 + 16] = cols(norm1_g[l])
        prm[:, L0 + O_BIN:L0 + O_BIN + 24] = cols(b_in[l])
        cw = np.asarray(conv_w[l], np.float32)
        prm[:, L0 + O_CW:L0 + O_CW + 248] = cw.reshape(31, 8, 128).transpose(2, 1, 0).reshape(128, 248)
        prm[:, L0 + O_CB:L0 + O_CB + 8] = cols(conv_b[l])
        prm[:, L0 + O_LNG:L0 + O_LNG + 8] = cols(ln_g[l])
        prm[:, L0 + O_LNB:L0 + O_LNB + 8] = cols(ln_b[l])
        prm[:, L0 + O_PSC:L0 + O_PSC + 8] = cols(pool_scale[l])
        prm[:, L0 + O_G2:L0 + O_G2 + 16] = cols(norm2_g[l])
    prm[:, O_GF:O_GF + 16] = cols(norm_f)
    prm[:, O_HM] = float(half)
    pos = np.arange(16)
    for g in range(4):
        w = 2 << g
        cnt = np.minimum(w, pos + 1) if half == 0 else np.full(16, w)
        prm[:, O_IC + g * 16:O_IC + (g + 1) * 16] = (1.0 / cnt.astype(np.float32))[None, :]
    return prm


_NC_CACHE = {}


def kernel(x_prompt, x_sample, state_conv, state_pool, norm1_g, w_in, b_in, conv_w, conv_b,
           ln_g, ln_b, pool_w, pool_scale, w_out, norm2_g, w_ff1, w_ff2, norm_f):
    f = lambda a: np.ascontiguousarray(np.asarray(a, dtype=np.float32))
    x_prompt, x_sample, state_conv, state_pool = f(x_prompt), f(x_sample), f(state_conv), f(state_pool)
    w_in, pool_w, w_out, w_ff1, w_ff2 = f(w_in), f(pool_w), f(w_out), f(w_ff1), f(w_ff2)
    cidx = np.concatenate([np.concatenate([np.arange(c * 128, (c + 1) * 128), np.arange(1024 + c * 128, 1024 + (c + 1) * 128)])
                           for c in range(8)] + [np.arange(2048, DIN)])
    w_in = np.ascontiguousarray(w_in[:, :, cidx])
    if "nc" not in _NC_CACHE:
        _NC_CACHE["nc"] = build_program()
    nc = _NC_CACHE["nc"]
    ident = np.eye(128, dtype=np.float32)
    gfb = np.ascontiguousarray(np.broadcast_to(np.asarray(norm_f, np.float32)[None, :], (128, D)))
    prms = [_pack_params(norm1_g, b_in, conv_w, conv_b, ln_g, ln_b, pool_scale, norm2_g, norm_f, h) for h in range(2)]
    in_maps = []
    for i in range(NCORES):
        b, h = i // 2, i % 2
        xc = np.zeros((T, D), np.float32)
        if h == 1:
            xc[0:HALO] = x_prompt[b, NPR - HALO:NPR]
        xc[HALO:HALO + NPR] = x_prompt[b, h * NPR:(h + 1) * NPR]
        xc[HALO + NPR:] = x_sample[SBQ * i:SBQ * (i + 1)].reshape(NSM, D)
        in_maps.append({
            "x": xc,
            "sconv": np.ascontiguousarray(state_conv[:, SBQ * i:SBQ * (i + 1)]),
            "spool": np.ascontiguousarray(state_pool[:, SBQ * i:SBQ * (i + 1)]),
            "prm": prms[h], "ident": ident, "gfb": gfb,
            "w_in": w_in, "pool_w": pool_w, "w_out": w_out, "w_ff1": w_ff1, "w_ff2": w_ff2,
        })
    ncr = int(os.environ.get("KCORES", NCORES))
    res = run_bass_kernel_spmd(nc, in_maps[:ncr], core_ids=list(range(ncr)))
    R = list(res.results) + [res.results[0]] * (NCORES - ncr)
    B_, S_ = x_prompt.shape[0], x_prompt.shape[1]
    y_prompt = np.empty((B_, S_, D), np.float32)
    y_sample = np.empty(x_sample.shape, np.float32)
    ncp = np.empty((DEPTH, B_, 30, 1024), np.float32)
    npp = np.empty((DEPTH, B_, 15, 1024), np.float32)
    ncs = np.empty(state_conv.shape, np.float32)
    nps = np.empty(state_pool.shape, np.float32)
    for i in range(NCORES):
        b, h = i // 2, i % 2
        y = R[i]["y"]
        y_prompt[b, h * NPR:(h + 1) * NPR] = y[0:NPR]
        y_sample[SBQ * i:SBQ * (i + 1)] = y[NPR:].reshape(SBQ, 4, D)
        ncs[:, SBQ * i:SBQ * (i + 1)] = R[i]["ocs"]
        nps[:, SBQ * i:SBQ * (i + 1)] = R[i]["ops"]
        if h == 1:
            ncp[:, b] = R[i]["ocp"]
            npp[:, b] = R[i]["opp"]
    return (y_prompt, y_sample, ncp, npp, ncs, nps)
```

```python
import os
import numpy as np
from contextlib import ExitStack
import concourse.bass as bass
import concourse.mybir as mybir
from concourse.bass_utils import run_bass_kernel_spmd

F32 = mybir.dt.float32
BF16 = mybir.dt.bfloat16
AF = mybir.ActivationFunctionType
ALU = mybir.AluOpType

NCORES = 8
D = 2048
KC = 16
DIN = 3072
DFF = 8192
DEPTH = 2
T = 1152
TN = 384
HALO = 64
NPR = 1024
NSM = 64
SBQ = 16
NYR = NPR + NSM
EPS = 1e-6
NSLOT = 3
FB = 2048
TMAP = [0, 2, 1]

PL = 336
O_G1, O_BIN, O_CW, O_CB, O_LNG, O_LNB, O_PSC, O_G2 = 0, 16, 40, 288, 296, 304, 312, 320
O_GF = DEPTH * PL
O_HM = O_GF + 16
O_IC = O_HM + 1
NPRM = O_IC + 64

SAME_ENG_SYNC = True
_DBG = set(os.environ.get("KDBG", "").split(",")) - {""}


class Buf:
    __slots__ = ("name", "w", "r", "pr", "excl")

    def __init__(self, name, excl=False):
        self.name = name
        self.w = {}
        self.r = {}
        self.pr = {}
        self.excl = excl


def _split(reads, writes):
    ex = [b for b in reads if b.excl]
    if not ex:
        return reads, writes
    return [b for b in reads if not b.excl], list(writes) + ex


def _merge(dst, src):
    for k, v in src.items():
        if dst.get(k, 0) < v:
            dst[k] = v


def handoff(src_bufs, dst_bufs):
    u = {}
    for b in src_bufs:
        _merge(u, b.w)
        _merge(u, b.r)
        _merge(u, b.pr)
    for b in dst_bufs:
        b.w = dict(u)
        b.r = {}
        b.pr = dict(u)


class Prog:
    ENG = ("pe", "act", "dve", "pool", "sp")

    def __init__(self):
        self.ops = {e: [] for e in self.ENG}
        self.cnt = {e: 0 for e in self.ENG}
        self.dcnt = {}

    def _deps(self, reads, writes, deps):
        d = {}
        for b in reads:
            _merge(d, b.w)
        for b in writes:
            _merge(d, b.r)
            _merge(d, b.pr)
            _merge(d, b.w)
        for h in deps:
            if h is not None:
                _merge(d, {h[0]: h[1]})
        return d

    def _reg(self, h, reads, writes):
        k, v = h
        for b in reads:
            if b.r.get(k, 0) < v:
                b.r[k] = v
        for b in writes:
            if b.r:
                b.pr = b.r
                b.r = {}
                b.w = {k: v}
            else:
                if b.w.get(k, 0) < v:
                    b.w[k] = v

    def op(self, eng, fn, reads=(), writes=(), deps=()):
        reads, writes = _split(reads, writes)
        d = self._deps(reads, writes, deps)
        self.cnt[eng] += 1
        h = (eng, self.cnt[eng])
        self.ops[eng].append((fn, d, (eng, 1)))
        self._reg(h, reads, writes)
        return h

    def group(self, fns, reads=(), writes=(), deps=()):
        reads, writes = _split(reads, writes)
        d = self._deps(reads, writes, deps)
        self.cnt["pe"] += 1
        h = ("pe", self.cnt["pe"])
        n = len(fns)
        for i, fn in enumerate(fns):
            self.ops["pe"].append((fn, d if i == 0 else {}, ("pe", 1) if i == n - 1 else None))
        self._reg(h, reads, writes)
        return h

    def dma(self, queue, fn, sem, reads=(), writes=(), deps=()):
        reads, writes = _split(reads, writes)
        d = self._deps(reads, writes, deps)
        self.dcnt[sem] = self.dcnt.get(sem, 0) + 16
        h = (sem, self.dcnt[sem])
        self.ops[queue].append((fn, d, (sem, 16)))
        self._reg(h, reads, writes)
        return h

    def wait(self, eng, deps):
        d = {}
        for h in deps:
            _merge(d, {h[0]: h[1]})
        self.ops[eng].append((None, d, None))

    def sem_names(self):
        return list(self.ENG) + sorted(self.dcnt.keys())

    def replay(self, eng, e, sems):
        waited = {}
        for fn, d, inc in self.ops[eng]:
            for k, v in d.items():
                if k == eng and (eng == "pe" or not SAME_ENG_SYNC):
                    continue
                if waited.get(k, 0) < v:
                    e.wait_ge(sems[k], v)
                    waited[k] = v
            if fn is None:
                continue
            inst = fn(e)
            if inc is not None:
                inst.then_inc(sems[inc[0]], inc[1])


class WStream:
    def __init__(self, P, wt, plan=None):
        self.P = P
        self.wt = wt
        self.plan = plan
        self.req = []
        self.i = 0
        self.issued = 0
        self.bufs = [Buf("w%d" % s) for s in range(NSLOT)]

    def _issue(self):
        i = self.issued
        s = i % NSLOT
        src, kc = self.plan[i]
        wt = self.wt
        self.P.dma("pool", lambda e, s=s, src=src, kc=kc: e.dma_start(out=wt[:, s, 0:kc, :], in_=src),
                   "w%d" % s, writes=[self.bufs[s]])
        self.issued += 1

    def get(self, src, kc):
        i = self.i
        self.i += 1
        if self.plan is None:
            self.req.append((src, kc))
            return i % NSLOT, self.bufs[i % NSLOT]
        while self.issued < min(i + NSLOT, len(self.plan)):
            self._issue()
        return i % NSLOT, self.bufs[i % NSLOT]


def build_program():
    nc = bass.Bass("TRN2", target_bir_lowering=False)
    dt_in = lambda n, s: nc.dram_tensor(n, s, F32, kind="ExternalInput").ap()
    dt_out = lambda n, s: nc.dram_tensor(n, s, F32, kind="ExternalOutput").ap()
    x_d = dt_in("x", [T, D])
    sconv_d = dt_in("sconv", [DEPTH, SBQ, 30, 1024])
    spool_d = dt_in("spool", [DEPTH, SBQ, 15, 1024])
    prm_d = dt_in("prm", [128, NPRM])
    ident_d = dt_in("ident", [128, 128])
    gfb_d = dt_in("gfb", [128, D])
    w_in_d = dt_in("w_in", [DEPTH, D, DIN])
    pool_w_d = dt_in("pool_w", [DEPTH, 4, 256, 256])
    w_out_d = dt_in("w_out", [DEPTH, D, D])
    w_ff1_d = dt_in("w_ff1", [DEPTH, D, DFF])
    w_ff2_d = dt_in("w_ff2", [DEPTH, DFF, D])
    y_d = dt_out("y", [NYR, D])
    ocs_d = dt_out("ocs", [DEPTH, SBQ, 30, 1024])
    ops_d = dt_out("ops", [DEPTH, SBQ, 15, 1024])
    ocp_d = dt_out("ocp", [DEPTH, 30, 1024])
    opp_d = dt_out("opp", [DEPTH, 15, 1024])

    w_in_v = [w_in_d[l].rearrange("(kc p) m -> p kc m", p=128) for l in range(DEPTH)]
    w_out_v = [w_out_d[l].rearrange("(kc p) m -> p kc m", p=128) for l in range(DEPTH)]
    w_ff1_v = [w_ff1_d[l].rearrange("(kc p) m -> p kc m", p=128) for l in range(DEPTH)]
    w_ff2_v = [w_ff2_d[l].rearrange("(jj kc p) m -> jj p kc m", p=128, kc=KC) for l in range(DEPTH)]
    pool_w_v = [[pool_w_d[l, g].rearrange("(kk p) m -> p kk m", p=128) for g in range(4)] for l in range(DEPTH)]

    with ExitStack() as st:
        sb = lambda n, s, d: st.enter_context(nc.sbuf_tensor(n, s, d))
        xT = sb("xT", [128, KC, T], F32)
        Bm = sb("Bm", [128, 18432], BF16)
        Cm = sb("Cm", [128, 9216], F32)
        S = sb("S", [128, 6, TN], F32)
        wt = sb("wt", [128, NSLOT, KC, 256], BF16)
        ubuf = sb("ubuf", [128, 3, 32 + TN], BF16)
        hsave = sb("hsave", [128, 8, 32], BF16)
        usamp = sb("usamp", [128, 8, SBQ, 34], BF16)
        zf = sb("zf", [128, 2, TN], F32)
        zb = sb("zb", [128, 2, 16 + TN], BF16)
        zsave = sb("zsave", [128, 8, 16], BF16)
        zsamp = sb("zsamp", [128, 8, SBQ, 19], BF16)
        dbuf = sb("dbuf", [128, 2, 2, TN], BF16)
        prm = sb("prm_t", [128, NPRM], F32)
        idf = sb("idf", [128, 128], F32)
        idb = sb("idb", [128, 128], BF16)
        ones = sb("ones", [128, 128], BF16)
        ps = st.enter_context(nc.psum_tensor("ps", [128, 8, 512], F32))

        hT = Bm[:, 0:6144].rearrange("p (k t) -> p k t", k=KC)
        mix = Bm[:, 6144:12288].rearrange("p (k t) -> p k t", k=KC)
        sqb = Bm[:, 12288:18432].rearrange("p (k t) -> p k t", k=KC)
        utail = Bm[:, 12288:12288 + 1504].bitcast(F32).rearrange("p (c t) -> p c t", c=8)
        ztail = Bm[:, 12288 + 1504:12288 + 3008].bitcast(F32).rearrange("p (c t) -> p c t", c=8)
        h2T = Bm[:, :].rearrange("p (a k t) -> p a k t", a=3, k=KC)
        tstage = S[:, :, :].rearrange("p a t -> p (a t)")[:, 0:2048].rearrange("p (a c) -> p a c", a=2)
        xin = [Cm[:, s * 2048:(s + 1) * 2048] for s in range(4)]
        stgc = Cm[:, 0:4096].rearrange("p (g c) -> p g c", g=4)
        stgp = Cm[:, 4096:6144].rearrange("p (g c) -> p g c", g=2)
        vv = Cm[:, 0:3072].rearrange("p (c t) -> p c t", c=8)
        vbf = Cm[:, 3072:4608].bitcast(BF16).rearrange("p (c t) -> p c t", c=8)
        sqv = Cm[:, 4608:6144].bitcast(BF16).rearrange("p (c t) -> p c t", c=8)
        diag = Cm[:, 6144:8128].bitcast(BF16).rearrange("p (k j) -> p k j", k=31)
        sig = [Cm[:, 8128 + i * TN:8128 + (i + 1) * TN] for i in range(2)]
        t64 = Cm[:, 8896:8960]
        t16 = Cm[:, 8960:8976]
        fT = Cm[:, :].bitcast(BF16).rearrange("p (a k t) -> p a k t", a=3, k=KC)
        sqFin = Bm[:, :].rearrange("p (k t) -> p k t", k=KC)
        gbc = Cm[:, 0:2048]
        yout = [Cm[:, 2048 + s * 2048:2048 + (s + 1) * 2048] for s in range(3)]
        Sflat = S[:, 0:3, :].rearrange("p a t -> p (a t)")

        def pcol(c):
            return prm[:, c:c + 1]

        def emit(P, W):
            B = {}

            def bf(n):
                if n not in B:
                    B[n] = Buf(n)
                return B[n]

            mixbufs_ref = [None]
            pb = [bf("pb%d" % i) for i in range(8)]
            for b_ in pb:
                b_.excl = True
            xTb = [bf("xT%d" % t) for t in range(3)]
            state = {"ring": 0, "alt": 0}

            def ring3():
                b = state["ring"] % 3
                state["ring"] += 1
                return b

            def ring4():
                b = state["ring"] % 4
                state["ring"] += 1
                return b

            def alt_eng():
                state["alt"] += 1
                return "act" if state["alt"] % 2 else "dve"

            def copy_op(eng, out, in_, reads, writes):
                if eng == "act":
                    return P.op("act", lambda e: e.activation(out=out, in_=in_, func=AF.Copy), reads=reads, writes=writes)
                return P.op("dve", lambda e: e.tensor_copy(out=out, in_=in_), reads=reads, writes=writes)

            outs = []
            P.dma("sp", lambda e: e.dma_start(out=prm[:], in_=prm_d[:, :]), "c0", writes=[bf("prm")])
            P.dma("sp", lambda e: e.dma_start(out=idf[:], in_=ident_d[:, :]), "c1", writes=[bf("idf")])
            P.op("dve", lambda e: e.tensor_copy(out=idb[:], in_=idf[:]), reads=[bf("idf")], writes=[bf("idb")])
            P.op("dve", lambda e: e.memset(ones[:], 1.0), writes=[bf("ones")])
            for l in range(DEPTH if "nopt" not in _DBG else 0):
                outs.append(P.dma("sp", lambda e, l=l: e.dma_start(out=ocs_d[l, :, 0:26, :], in_=sconv_d[l, :, 4:30, :]), "pt"))
                outs.append(P.dma("sp", lambda e, l=l: e.dma_start(out=ops_d[l, :, 0:11, :], in_=spool_d[l, :, 4:15, :]), "pt"))

            xinb = [bf("xin%d" % i) for i in range(4)]
            bank = 0
            for i in range(T // 128):
                s = i % 4
                P.dma("sp", lambda e, i=i, s=s: e.dma_start(out=xin[s], in_=x_d[i * 128:(i + 1) * 128, :]),
                      "xi%d" % s, writes=[xinb[s]])
                for q in range(4):
                    bk = bank % 8
                    bank += 1
                    fns = [lambda e, s=s, q=q, j=j, bk=bk: e.transpose(
                        out=ps[:, bk, j * 128:(j + 1) * 128], in_=xin[s][:, (4 * q + j) * 128:(4 * q + j + 1) * 128], identity=idf[:])
                        for j in range(4)]
                    P.group(fns, reads=[xinb[s], bf("idf")], writes=[pb[bk]])
                    copy_op(alt_eng(), xT[:, 4 * q:4 * q + 4, i * 128:(i + 1) * 128],
                            ps[:, bk, :].rearrange("p (j t) -> p j t", t=128), [pb[bk]], [xTb[i // 3]])
            cbufs = list(xinb)

            def rms_stats(l_g_unused, src_cols, sq_view, sq_buf, bank_i, out_slot, out_buf, xbufs):
                for k in range(KC):
                    P.op("act", lambda e, k=k: e.activation(out=sq_view[:, k, :], in_=xT[:, k, src_cols[0]:src_cols[1]], func=AF.Square),
                         reads=xbufs, writes=[sq_buf])
                n = src_cols[1] - src_cols[0]
                fns = [lambda e, k=k: e.matmul(out=ps[:, bank_i, 0:n], lhsT=ones[:], rhs=sq_view[:, k, :], start=(k == 0), stop=(k == KC - 1))
                       for k in range(KC)]
                P.group(fns, reads=[sq_buf, bf("ones")], writes=[pb[bank_i]])
                rstd_finish(bank_i, n, out_slot, out_buf)

            def rstd_finish(bank_i, n, out_slot, out_buf):
                if out_slot is None:
                    out_slot, out_buf = ps[:, bank_i, 0:n], pb[bank_i]
                P.op("act", lambda e: e.activation(out=out_slot, in_=ps[:, bank_i, 0:n], func=AF.Sqrt, bias=EPS, scale=1.0 / D),
                     reads=[pb[bank_i]], writes=[out_buf])
                P.op("dve", lambda e: e.reciprocal(out=out_slot, in_=out_slot), reads=[out_buf], writes=[out_buf])

            def rms1_pre(l, t):
                c0 = t * TN
                rms_stats(None, (c0, c0 + TN), sqb, bf("sqb"), 7, None, None, [xTb[t]])
                for k in range(KC):
                    P.op("dve", lambda e, k=k: e.scalar_tensor_tensor(
                        out=hT[:, k, :], in0=xT[:, k, c0:c0 + TN], scalar=pcol(l * PL + O_G1 + k), in1=ps[:, 7, 0:TN],
                        op0=ALU.mult, op1=ALU.mult), reads=[xTb[t], pb[7], bf("prm")], writes=[bf("hT")])

            def mm_group(slot, wbuf, rhs_of_k, nk, bank_i, n, reads, half=0):
                fns = [lambda e, k=k: e.matmul(out=ps[:, bank_i, 0:n], lhsT=wt[:, slot, k, half * 128:(half + 1) * 128], rhs=rhs_of_k(k),
                                               start=(k == 0), stop=(k == nk - 1)) for k in range(nk)]
                return P.group(fns, reads=[wbuf] + reads, writes=[pb[bank_i]])

            def stt(out, in0, scalar, in1, op0, op1, reads, writes):
                return P.op("dve", lambda e: e.scalar_tensor_tensor(out=out, in0=in0, scalar=scalar, in1=in1, op0=op0, op1=op1),
                            reads=reads, writes=writes)

            def ts1(out, in0, s1, op0, reads, writes):
                return P.op("dve", lambda e: e.tensor_scalar(out=out, in0=in0, scalar1=s1, scalar2=None, op0=op0),
                            reads=reads, writes=writes)

            def ts2(out, in0, s1, s2, op0, op1, reads, writes):
                return P.op("dve", lambda e: e.tensor_scalar(out=out, in0=in0, scalar1=s1, scalar2=s2, op0=op0, op1=op1),
                            reads=reads, writes=writes)

            def act(out, in_, func, reads, writes, bias=None, scale=None):
                kw = {}
                if bias is not None:
                    kw["bias"] = bias
                if scale is not None:
                    kw["scale"] = scale
                return P.op("act", lambda e: e.activation(out=out, in_=in_, func=func, **kw), reads=reads, writes=writes)

            def mixer(l, t):
                L0 = l * PL
                c0 = t * TN
                npr = TN if t < 2 else TN - NSM
                prmb = bf("prm")
                hTb, mixb = bf("hT"), bf("mix")
                vb, vbfb, sqvb = bf("v"), bf("vbf"), bf("sqv")
                sigb = [bf("sig0"), bf("sig1")]
                ubb = [bf("ub0"), bf("ub1"), bf("ub2")]

                nxt = t + 1 if t < 2 else None
                dgb = [bf("diagA"), bf("diagB")]
                halves = [(0, 16), (16, 31)]

                def conv_build(c):
                    for hi, (k0, k1) in enumerate(halves):
                        nk = k1 - k0
                        P.op("dve", lambda e, k0=k0, k1=k1, nk=nk: e.tensor_tensor(
                            out=diag[:, k0:k1, :], in0=ps[:, 4, 0:128].unsqueeze(1).broadcast_to([128, nk, 128]),
                            in1=prm[:, L0 + O_CW + c * 31 + k0:L0 + O_CW + c * 31 + k1].unsqueeze(2).broadcast_to([128, nk, 128]),
                            op=ALU.mult), reads=[pb[4], prmb], writes=[dgb[hi]])

                def conv(c):
                    ui = c % 3
                    ub = ubuf[:, ui, :]
                    for hi, (k0, k1) in enumerate(halves):
                        fns = [lambda e, k=k: e.matmul(out=ps[:, 3, 0:npr], lhsT=diag[:, k, :], rhs=ub[:, 2 + k:2 + k + npr],
                                                       start=(k == 0), stop=(k == 30)) for k in range(k0, k1)]
                        P.group(fns, reads=[dgb[hi], ubb[ui]], writes=[pb[3]])
                    if t == 2:
                        fns = [lambda e, k=k: e.matmul(out=ps[:, 3, npr:TN], lhsT=diag[:, k, :], rhs=usamp[:, c, :, k:k + 4],
                                                       start=(k == 0), stop=(k == 30), skip_group_check=True) for k in range(31)]
                        P.group(fns, reads=[dgb[0], dgb[1], bf("usamp%d" % c)], writes=[pb[3]])
                    cb = pcol(L0 + O_CB + c)
                    act(vv[:, c, :], ps[:, 3, 0:TN], AF.Identity, [pb[3], prmb], [vb], bias=cb)
                    ts1(vbf[:, c, :], ps[:, 3, 0:TN], cb, ALU.add, [pb[3], prmb], [vbfb])
                    act(sqv[:, c, :], ps[:, 3, 0:TN], AF.Square, [pb[3], prmb], [sqvb], bias=cb)

                for c in range(8):
                    ui = c % 3
                    ub = ubuf[:, ui, :]
                    if c > 0:
                        conv_build(c - 1)
                    sa, wa = W.get(w_in_v[l][:, :, c * 256:(c + 1) * 256], KC)
                    ba = ring3()
                    mm_group(sa, wa, lambda k: hT[:, k, :], KC, ba, TN, [hTb], half=0)
                    bg = ring3()
                    mm_group(sa, wa, lambda k: hT[:, k, :], KC, bg, TN, [hTb], half=1)
                    si = c % 2
                    act(sig[si], ps[:, bg, 0:TN], AF.Sigmoid, [pb[bg], prmb], [sigb[si]], bias=pcol(L0 + O_BIN + 8 + c))
                    ba_col = pcol(L0 + O_BIN + c)
                    rdu = [pb[ba], sigb[si], prmb]
                    if t == 0:
                        P.op("dve", lambda e, ub=ub: e.memset(ub[:, 0:32], 0.0), writes=[ubb[ui]])
                    else:
                        P.op("dve", lambda e, ub=ub, c=c: e.tensor_copy(out=ub[:, 0:32], in_=hsave[:, c, :]),
                             reads=[bf("hsave%d" % c)], writes=[ubb[ui]])
                    if t == 0:
                        stt(t64, ps[:, ba, 0:HALO], ba_col, sig[si][:, 0:HALO], ALU.add, ALU.mult, rdu, [bf("t64")])
                        ts1(ub[:, 32:32 + HALO], t64, pcol(O_HM), ALU.mult, [bf("t64"), prmb], [ubb[ui]])
                        stt(ub[:, 32 + HALO:32 + TN], ps[:, ba, HALO:TN], ba_col, sig[si][:, HALO:TN], ALU.add, ALU.mult, rdu, [ubb[ui]])
                    else:
                        stt(ub[:, 32:32 + npr], ps[:, ba, 0:npr], ba_col, sig[si][:, 0:npr], ALU.add, ALU.mult, rdu, [ubb[ui]])
                    if t == 2:
                        stt(usamp[:, c, :, 30:34], ps[:, ba, npr:TN].rearrange("p (b j) -> p b j", j=4), ba_col,
                            sig[si][:, npr:TN].rearrange("p (b j) -> p b j", j=4), ALU.add, ALU.mult, rdu, [bf("usamp%d" % c)])
                        stt(utail[:, c, :], ps[:, ba, TN - 94:TN], ba_col, sig[si][:, TN - 94:TN], ALU.add, ALU.mult, rdu, [bf("sqb")])
                    else:
                        P.op("dve", lambda e, ub=ub, c=c: e.tensor_copy(out=hsave[:, c, :], in_=ub[:, TN:TN + 32]),
                             reads=[ubb[ui]], writes=[bf("hsave%d" % c)])
                    if nxt is not None:
                        n0 = nxt * TN
                        for k in (2 * c, 2 * c + 1):
                            P.op("act", lambda e, k=k, n0=n0: e.activation(out=sqb[:, k, :], in_=xT[:, k, n0:n0 + TN], func=AF.Square),
                                 reads=[xTb[nxt]], writes=[bf("sqb")])
                    if c > 0:
                        conv(c - 1)
                conv_build(7)
                conv(7)

                def pool_A(g):
                    bz = []
                    sz, wz = W.get(w_in_v[l][:, :, 2048 + g * 256:2048 + (g + 1) * 256], KC)
                    for j in range(2):
                        m = 2 * g + j
                        b_ = ring4()
                        bz.append(b_)
                        mm_group(sz, wz, lambda k: hT[:, k, :], KC, b_, TN, [hTb], half=j)
                        bzc = pcol(L0 + O_BIN + 16 + m)
                        zfB, zbB = bf("zf%d" % j), bf("zb%d" % j)
                        if t == 0:
                            P.op("dve", lambda e, j=j: e.memset(zb[:, j, 0:16], 0.0), writes=[zbB])
                            ts2(zb[:, j, 16:16 + HALO], ps[:, b_, 0:HALO], bzc, pcol(O_HM), ALU.add, ALU.mult, [pb[b_], prmb], [zbB])
                            ts1(zb[:, j, 16 + HALO:16 + TN], ps[:, b_, HALO:TN], bzc, ALU.add, [pb[b_], prmb], [zbB])
                        else:
                            P.op("dve", lambda e, j=j, m=m: e.tensor_copy(out=zb[:, j, 0:16], in_=zsave[:, m, :]),
                                 reads=[bf("zsave%d" % m)], writes=[zbB])
                            ts1(zb[:, j, 16:16 + npr], ps[:, b_, 0:npr], bzc, ALU.add, [pb[b_], prmb], [zbB])
                        if t == 2:
                            ts1(zsamp[:, m, :, 15:19], ps[:, b_, npr:TN].rearrange("p (b j) -> p b j", j=4), bzc, ALU.add,
                                [pb[b_], prmb], [bf("zsamp%d" % m)])
                        act(zf[:, j, :], ps[:, b_, 0:TN], AF.Identity, [pb[b_], prmb], [zfB], bias=bzc)
                        if t == 2:
                            ts1(ztail[:, m, :], ps[:, b_, TN - 94:TN], bzc, ALU.add, [pb[b_], prmb], [bf("sqb")])
                        else:
                            P.op("dve", lambda e, j=j, m=m: e.tensor_copy(out=zsave[:, m, :], in_=zb[:, j, TN:TN + 16]),
                                 reads=[zbB], writes=[bf("zsave%d" % m)])

                def pool_B(g):
                    w = 2 << g
                    db = g % 2
                    dB = bf("d%d" % db)
                    for j in range(2):
                        m = 2 * g + j
                        zfB, zbB = bf("zf%d" % j), bf("zb%d" % j)
                        bp = ring4()
                        fns = [lambda e, k=k, j=j, bp=bp, w=w: e.matmul(out=ps[:, bp, 0:npr], lhsT=idb[:], rhs=zb[:, j, 16 - k:16 - k + npr],
                                                                   start=(k == 0), stop=(k == w - 1)) for k in range(w)]
                        rd = [bf("idb"), zbB]
                        if t == 2:
                            fns += [lambda e, k=k, m=m, bp=bp, w=w: e.matmul(out=ps[:, bp, npr:TN], lhsT=idb[:], rhs=zsamp[:, m, :, 15 - k:19 - k],
                                                                        start=(k == 0), stop=(k == w - 1), skip_group_check=True) for k in range(w)]
                            rd.append(bf("zsamp%d" % m))
                        P.group(fns, reads=rd, writes=[pb[bp]])
                        rdd = [pb[bp], zfB]
                        if t == 0:
                            stt(dbuf[:, db, j, 0:HALO], ps[:, bp, 0:HALO], 1.0 / w, zf[:, j, 0:HALO], ALU.mult, ALU.subtract, rdd, [dB])
                            P.op("dve", lambda e, bp=bp, g=g: e.tensor_tensor(out=t16, in0=ps[:, bp, HALO:HALO + 16],
                                                                              in1=prm[:, O_IC + g * 16:O_IC + (g + 1) * 16], op=ALU.mult),
                                 reads=[pb[bp], prmb], writes=[bf("t16")])
                            P.op("dve", lambda e, j=j, db=db: e.tensor_tensor(out=dbuf[:, db, j, HALO:HALO + 16], in0=t16,
                                                                              in1=zf[:, j, HALO:HALO + 16], op=ALU.subtract),
                                 reads=[bf("t16"), zfB], writes=[dB])
                            stt(dbuf[:, db, j, HALO + 16:TN], ps[:, bp, HALO + 16:TN], 1.0 / w, zf[:, j, HALO + 16:TN],
                                ALU.mult, ALU.subtract, rdd, [dB])
                        else:
                            stt(dbuf[:, db, j, :], ps[:, bp, 0:TN], 1.0 / w, zf[:, j, :], ALU.mult, ALU.subtract, rdd, [dB])

                def pool_C(g):
                    db = g % 2
                    dB = bf("d%d" % db)
                    sp_, wp = W.get(pool_w_v[l][g][:, :, :], 2)
                    for e_ in range(2):
                        m = 2 * g + e_
                        bq = ring4()
                        mm_group(sp_, wp, lambda k, db=db: dbuf[:, db, k, :], 2, bq, TN, [dB], half=e_)
                        act(mix[:, 8 + m, :], ps[:, bq, 0:TN], AF.Copy, [pb[bq], prmb], [mixb], scale=pcol(L0 + O_PSC + m))

                fns = [lambda e, c=c: e.matmul(out=ps[:, 5, 0:TN], lhsT=ones[:], rhs=vbf[:, c, :], start=(c == 0), stop=(c == 7)) for c in range(8)]
                P.group(fns, reads=[vbfb, bf("ones")], writes=[pb[5]])
                fns = [lambda e, c=c: e.matmul(out=ps[:, 6, 0:TN], lhsT=ones[:], rhs=sqv[:, c, :], start=(c == 0), stop=(c == 7)) for c in range(8)]
                P.group(fns, reads=[sqvb, bf("ones")], writes=[pb[6]])
                if nxt is not None:
                    fns = [lambda e, k=k: e.matmul(out=ps[:, 7, 0:TN], lhsT=ones[:], rhs=sqb[:, k, :], start=(k == 0), stop=(k == KC - 1))
                           for k in range(KC)]
                    P.group(fns, reads=[bf("sqb"), bf("ones")], writes=[pb[7]])
                S1b = bf("S1")
                act(ps[:, 5, 0:TN], ps[:, 5, 0:TN], AF.Copy, [pb[5]], [pb[5]], scale=1.0 / 1024)
                act(S[:, 1, :], ps[:, 5, 0:TN], AF.Square, [pb[5]], [S1b])
                stt(ps[:, 6, 0:TN], ps[:, 6, 0:TN], 1.0 / 1024, S[:, 1, :], ALU.mult, ALU.subtract, [pb[6], S1b], [pb[6]])
                act(ps[:, 6, 0:TN], ps[:, 6, 0:TN], AF.Sqrt, [pb[6]], [pb[6]], bias=EPS)
                P.op("dve", lambda e: e.reciprocal(out=ps[:, 6, 0:TN], in_=ps[:, 6, 0:TN]), reads=[pb[6]], writes=[pb[6]])
                if nxt is not None:
                    rstd_finish(7, TN, None, None)
                def ln_apply(c):
                    sl = 2 + c % 2
                    Sb = bf("S%d" % sl)
                    P.op("dve", lambda e, c=c, sl=sl: e.tensor_tensor(out=S[:, sl, :], in0=vv[:, c, :], in1=ps[:, 5, 0:TN], op=ALU.subtract),
                         reads=[vb, pb[5]], writes=[Sb])
                    P.op("dve", lambda e, sl=sl: e.tensor_tensor(out=S[:, sl, :], in0=S[:, sl, :], in1=ps[:, 6, 0:TN], op=ALU.mult),
                         reads=[Sb, pb[6]], writes=[Sb])
                    act(mix[:, c, :], S[:, sl, :], AF.Silu, [Sb, prmb], [mixb], bias=pcol(L0 + O_LNB + c), scale=pcol(L0 + O_LNG + c))

                early2 = (t == 2 and "noffn" not in _DBG)
                sqCb = [bf("sqC%d" % i) for i in range(3)]
                if early2:
                    handoff([vbfb, sqvb], [sqCb[1]])
                    for k in range(KC):
                        P.op("act", lambda e, k=k: e.activation(out=fT[:, 1, k, :], in_=xT[:, k, TN:2 * TN], func=AF.Square),
                             reads=[xTb[1]], writes=[sqCb[1]])
                seq = ["A0", "B0", "A1", "C0", "B1", "A2", "C1", "B2", "A3", "C2", "B3", "C3"]
                for st_ in seq:
                    g = int(st_[1])
                    if st_[0] == "A":
                        pool_A(g)
                        if early2 and g == 1:
                            fns = [lambda e, k=k: e.matmul(out=ps[:, 7, 0:TN], lhsT=ones[:], rhs=fT[:, 1, k, :], start=(k == 0), stop=(k == KC - 1))
                                   for k in range(KC)]
                            P.group(fns, reads=[sqCb[1], bf("ones")], writes=[pb[7]])
                            rstd_finish(7, TN, None, None)
                    elif st_[0] == "B":
                        pool_B(g)
                        ln_apply(2 * g)
                        ln_apply(2 * g + 1)
                    else:
                        pool_C(g)
                if t == 2 and "notails" not in _DBG:
                    Sall6 = [bf("S%d" % i) for i in range(6)]
                    for a, (tl, tlb) in enumerate([(utail, bf("sqb")), (ztail, bf("sqb"))]):
                        for q in range(2):
                            bk = (2 * a + q) % 4
                            fns = [lambda e, tl=tl, q=q, i=i, bk=bk: e.transpose(out=ps[0:94, bk, i * 128:(i + 1) * 128], in_=tl[:, 4 * q + i, :], identity=idf[:])
                                   for i in range(4)]
                            P.group(fns, reads=[tlb, bf("idf")], writes=[pb[bk]])
                            copy_op(alt_eng(), tstage[0:94, a, q * 512:(q + 1) * 512], ps[0:94, bk, :], [pb[bk]], Sall6)
                    outs.append(P.dma("sp", lambda e: e.dma_start(out=ocp_d[l, :, :], in_=tstage[0:30, 0, :]), "to", reads=Sall6))
                    outs.append(P.dma("sp", lambda e: e.dma_start(out=opp_d[l, :, :], in_=tstage[15:30, 1, :]), "to", reads=Sall6))
                    for b in range(SBQ):
                        outs.append(P.dma("sp", lambda e, b=b: e.dma_start(out=ocs_d[l, b, 26:30, :], in_=tstage[30 + 4 * b:34 + 4 * b, 0, :]), "to", reads=Sall6))
                        outs.append(P.dma("sp", lambda e, b=b: e.dma_start(out=ops_d[l, b, 11:15, :], in_=tstage[30 + 4 * b:34 + 4 * b, 1, :]), "to", reads=Sall6))
                if early2:
                    handoff([vb], [sqCb[0]])
                    handoff([bf(n) for n in ("diagA", "diagB", "sig0", "sig1", "t64", "t16")], [sqCb[2]])
                    for k in range(KC):
                        P.op("act", lambda e, k=k: e.activation(out=fT[:, 0, k, :], in_=xT[:, k, 0:TN], func=AF.Square),
                             reads=[xTb[0]], writes=[sqCb[0]])
                for m in range(KC):
                    if m % 2 == 0:
                        so, wo = W.get(w_out_v[l][:, :, m * 128:(m + 2) * 128], KC)
                    bo = ring4()
                    mm_group(so, wo, lambda k: mix[:, k, :], KC, bo, TN, [mixb], half=m % 2)
                    P.op("dve", lambda e, m=m, bo=bo: e.tensor_tensor(out=xT[:, m, c0:c0 + TN], in0=xT[:, m, c0:c0 + TN], in1=ps[:, bo, 0:TN], op=ALU.add),
                         reads=[pb[bo], xTb[t]], writes=[xTb[t]])
                    if nxt is not None:
                        n0 = nxt * TN
                        stt(hT[:, m, :], xT[:, m, n0:n0 + TN], pcol(L0 + O_G1 + m), ps[:, 7, 0:TN], ALU.mult, ALU.mult,
                            [xTb[nxt], pb[7], prmb], [hTb])
                    elif early2:
                        P.op("act", lambda e, m=m: e.activation(out=fT[:, 2, m, :], in_=xT[:, m, c0:c0 + TN], func=AF.Square),
                             reads=[xTb[2]], writes=[sqCb[2]])
                        stt(h2T[:, TMAP[1], m, :], xT[:, m, TN:2 * TN], pcol(L0 + O_G2 + m), ps[:, 7, 0:TN], ALU.mult, ALU.mult,
                            [xTb[1], pb[7], prmb], [bf("sqb")])
                        if m == 4:
                            fns = [lambda e, k=k: e.matmul(out=ps[:, 5, 0:TN], lhsT=ones[:], rhs=fT[:, 0, k, :], start=(k == 0), stop=(k == KC - 1))
                                   for k in range(KC)]
                            P.group(fns, reads=[sqCb[0], bf("ones")], writes=[pb[5]])
                            rstd_finish(5, TN, None, None)
                        if 5 <= m <= 12:
                            for k in (2 * (m - 5), 2 * (m - 5) + 1):
                                stt(h2T[:, TMAP[0], k, :], xT[:, k, 0:TN], pcol(L0 + O_G2 + k), ps[:, 5, 0:TN], ALU.mult, ALU.mult,
                                    [xTb[0], pb[5], prmb], [hTb])
            def ffn(l):
                L0 = l * PL
                prmb = bf("prm")
                h2b, fCb = bf("h2T"), bf("fC")
                Sb = [bf("S%d" % i) for i in range(6)]
                early = "nomixer" not in _DBG and "t01" not in _DBG
                h2tb = [bf("hT"), bf("sqb"), bf("mix")] if early else None
                sqCb = [bf("sqC%d" % i) for i in range(3)]
                bks = [5, 7, 6]
                for t in range(3):
                    c0 = t * TN
                    bk = bks[t]
                    if not early:
                        for k in range(KC):
                            P.op("act", lambda e, k=k, t=t, c0=c0: e.activation(out=fT[:, t, k, :], in_=xT[:, k, c0:c0 + TN], func=AF.Square),
                                 reads=[xTb[t]], writes=[sqCb[t]])
                    if not early or t == 2:
                        fns = [lambda e, k=k, t=t, bk=bk: e.matmul(out=ps[:, bk, 0:TN], lhsT=ones[:], rhs=fT[:, t, k, :], start=(k == 0), stop=(k == KC - 1))
                               for k in range(KC)]
                        P.group(fns, reads=[sqCb[t], bf("ones")], writes=[pb[bk]])
                        rstd_finish(bk, TN, None, None)
                        for k in range(KC):
                            stt(h2T[:, TMAP[t], k, :], xT[:, k, c0:c0 + TN], pcol(L0 + O_G2 + k), ps[:, bk, 0:TN], ALU.mult, ALU.mult,
                                [xTb[t], pb[bk], prmb], [h2tb[t] if early else h2b])
                handoff(sqCb, [fCb])
                slot3 = 0
                sc_i = 0
                tr = [((HALO if l == DEPTH - 1 else HALO - 32) if t == 0 else 0, TN) for t in range(3)]
                for j in range(DFF // FB):
                    for m in range(KC):
                        if m % 2 == 0:
                            s1, w1 = W.get(w_ff1_v[l][:, :, j * FB + m * 128:j * FB + (m + 2) * 128], KC)
                        hf = m % 2
                        b0 = 3 * (slot3 % 2)
                        slot3 += 1
                        if j == 0 and m < 3 and h2tb is not None:
                            for t in range(3):
                                fns = [lambda e, k=k, t=t, s1=s1, b0=b0, hf=hf: e.matmul(out=ps[:, b0 + t, 0:tr[t][1] - tr[t][0]], lhsT=wt[:, s1, k, hf * 128:(hf + 1) * 128],
                                                                                rhs=h2T[:, TMAP[t], k, tr[t][0]:tr[t][1]],
                                                                                start=(k == 0), stop=(k == KC - 1)) for k in range(KC)]
                                P.group(fns, reads=[w1, h2tb[t]], writes=[pb[b0 + t]])
                        else:
                            fns = [lambda e, k=k, t=t, s1=s1, b0=b0, hf=hf: e.matmul(out=ps[:, b0 + t, 0:tr[t][1] - tr[t][0]], lhsT=wt[:, s1, k, hf * 128:(hf + 1) * 128],
                                                                            rhs=h2T[:, TMAP[t], k, tr[t][0]:tr[t][1]],
                                                                            start=(k == 0), stop=(k == KC - 1)) for k in range(KC) for t in range(3)]
                            P.group(fns, reads=[w1, h2b] + (h2tb or []), writes=[pb[b0], pb[b0 + 1], pb[b0 + 2]])
                        for t in range(3):
                            o0, o1 = tr[t]
                            si = 3 + sc_i % 3
                            sc_i += 1
                            act(S[:, si, 0:o1 - o0], ps[:, b0 + t, 0:o1 - o0], AF.Square, [pb[b0 + t]], [Sb[si]])
                            stt(fT[:, t, m, o0:o1], ps[:, b0 + t, 0:o1 - o0], 0.0, S[:, si, 0:o1 - o0], ALU.is_gt, ALU.mult, [pb[b0 + t], Sb[si]], [fCb])
                    for m in range(KC):
                        if m % 2 == 0:
                            s2, w2 = W.get(w_ff2_v[l][j][:, :, m * 128:(m + 2) * 128], KC)
                        hf = m % 2
                        b0 = 3 * (slot3 % 2)
                        slot3 += 1
                        fns = [lambda e, k=k, t=t, s2=s2, b0=b0, hf=hf: e.matmul(out=ps[:, b0 + t, 0:tr[t][1] - tr[t][0]], lhsT=wt[:, s2, k, hf * 128:(hf + 1) * 128],
                                                                        rhs=fT[:, t, k, tr[t][0]:tr[t][1]],
                                                                        start=(k == 0), stop=(k == KC - 1)) for k in range(KC) for t in range(3)]
                        P.group(fns, reads=[w2, fCb], writes=[pb[b0], pb[b0 + 1], pb[b0 + 2]])
                        for t in range(3):
                            o0, o1 = tr[t]
                            c0 = t * TN + o0
                            nn = o1 - o0
                            P.op("dve", lambda e, m=m, t=t, b0=b0, c0=c0, nn=nn: e.tensor_tensor(out=xT[:, m, c0:c0 + nn], in0=xT[:, m, c0:c0 + nn],
                                                                                             in1=ps[:, b0 + t, 0:nn], op=ALU.add),
                                 reads=[pb[b0 + t], xTb[t]], writes=[xTb[t]])
                        if l == DEPTH - 1 and j == DFF // FB - 1:
                            if m == 0:
                                handoff([h2b, bf("hT"), bf("mix"), bf("sqb")], [bf("sqFin")])
                            P.op("act", lambda e, m=m: e.activation(out=sqFin[:, m, :], in_=xT[:, m, :], func=AF.Square),
                                 reads=xTb, writes=[bf("sqFin")])
                return [h2b, fCb]

            for l in range(DEPTH if "nolayers" not in _DBG else 0):
                if l > 0:
                    handoff([bf("h2T"), bf("hT"), bf("mix"), bf("sqb")], [bf("hT"), bf("mix"), bf("sqb")])
                P.group([lambda e: e.transpose(out=ps[:, 4, 0:128], in_=idf[:], identity=idf[:])], reads=[bf("idf")], writes=[pb[4]])
                rms1_pre(l, 0)
                stb = bf("stage")
                handoff(cbufs, [stb])
                for gq in range(4):
                    P.dma("sp", lambda e, l=l, gq=gq: e.dma_start(out=stgc[0:120, gq, :],
                                                                  in_=sconv_d[l, 4 * gq:4 * gq + 4, :, :].rearrange("b j c -> (b j) c")),
                          "sg", writes=[stb])
                for gp in range(2):
                    P.dma("sp", lambda e, l=l, gp=gp: e.dma_start(out=stgp[0:120, gp, :],
                                                                  in_=spool_d[l, 8 * gp:8 * gp + 8, :, :].rearrange("b j c -> (b j) c")),
                          "sg", writes=[stb])
                hbanks = [0, 1, 2, 3, 5, 6]
                bank = 0
                for c in range(8):
                    bk = hbanks[bank % 6]
                    bank += 1
                    fns = [lambda e, c=c, gq=gq, bk=bk: e.transpose(out=ps[:, bk, gq * 120:(gq + 1) * 120], in_=stgc[0:120, gq, c * 128:(c + 1) * 128],
                                                                   identity=idf[0:120, 0:120]) for gq in range(4)]
                    P.group(fns, reads=[stb, bf("idf")], writes=[pb[bk]])
                    copy_op(alt_eng(), usamp[:, c, :, 0:30], ps[:, bk, 0:480].rearrange("p (b j) -> p b j", j=30), [pb[bk]], [bf("usamp%d" % c)])
                    bk = hbanks[bank % 6]
                    bank += 1
                    fns = [lambda e, c=c, gp=gp, bk=bk: e.transpose(out=ps[:, bk, gp * 120:(gp + 1) * 120], in_=stgp[0:120, gp, c * 128:(c + 1) * 128],
                                                                   identity=idf[0:120, 0:120]) for gp in range(2)]
                    P.group(fns, reads=[stb, bf("idf")], writes=[pb[bk]])
                    copy_op(alt_eng(), zsamp[:, c, :, 0:15], ps[:, bk, 0:240].rearrange("p (b j) -> p b j", j=15), [pb[bk]], [bf("zsamp%d" % c)])
                mixbufs = [bf(n) for n in ("v", "vbf", "sqv", "diagA", "diagB", "sig0", "sig1", "t64", "t16")]
                handoff([stb], mixbufs)
                mixbufs_ref[0] = mixbufs
                for t in range((2 if "t01" in _DBG else 3) if "nomixer" not in _DBG else 0):
                    mixer(l, t)
                if _DBG & {"nomixer", "t01"}:
                    handoff([bf("hT"), bf("mix"), bf("sqb")], [bf("h2T")])
                if "nomixer" in _DBG or "noffn" in _DBG or "t01" in _DBG:
                    handoff(mixbufs, [bf("sqC%d" % i) for i in range(3)])
                cbufs = ffn(l)[1:] if "noffn" not in _DBG else [bf("sqC%d" % i) for i in range(3)]

            gB, yob = bf("gbc"), [bf("yout%d" % i) for i in range(3)]
            sq2b = [bf("sqh0"), bf("sqh1")]
            handoff(cbufs, [gB] + yob)
            pre_sq = not (_DBG & {"nolayers", "noffn"})
            if "nolayers" not in _DBG and not pre_sq:
                handoff([bf("h2T"), bf("hT"), bf("mix"), bf("sqb")], sq2b)
            P.dma("sp", lambda e: e.dma_start(out=gbc, in_=gfb_d[:, :]), "c0", writes=[gB])
            rtb = bf("S5")
            bank = 0
            for i in range(9):
                n = 128 if i < 8 else 64
                col0 = HALO + 128 * i
                s2, s3 = i % 2, i % 3
                sbk = 6 + i % 2
                xb_ = sorted({col0 // TN, (col0 + n - 1) // TN})
                xbs = [xTb[j] for j in xb_]
                if pre_sq:
                    fns = [lambda e, k=k, n=n, sbk=sbk, i=i, col0=col0: e.matmul(out=ps[0:n, sbk, i:i + 1], lhsT=sqFin[:, k, col0:col0 + n], rhs=ones[:, 0:1],
                                                                                 start=(k == 0), stop=(k == KC - 1), skip_group_check=True) for k in range(KC)]
                    P.group(fns, reads=[bf("sqFin"), bf("ones")], writes=[pb[sbk]])
                else:
                    for k in range(KC):
                        P.op("act", lambda e, k=k, s2=s2, n=n, col0=col0: e.activation(out=sqb[:, k, s2 * 128:s2 * 128 + n], in_=xT[:, k, col0:col0 + n], func=AF.Square),
                             reads=xbs, writes=[sq2b[s2]])
                    fns = [lambda e, k=k, s2=s2, n=n, sbk=sbk, i=i: e.matmul(out=ps[0:n, sbk, i:i + 1], lhsT=sqb[:, k, s2 * 128:s2 * 128 + n], rhs=ones[:, 0:1],
                                                                           start=(k == 0), stop=(k == KC - 1), skip_group_check=True) for k in range(KC)]
                    P.group(fns, reads=[sq2b[s2], bf("ones")], writes=[pb[sbk]])
                P.op("act", lambda e, n=n, sbk=sbk, i=i: e.activation(out=S[0:n, 5, i:i + 1], in_=ps[0:n, sbk, i:i + 1], func=AF.Sqrt, bias=EPS, scale=1.0 / D),
                     reads=[pb[sbk]], writes=[rtb])
                P.op("dve", lambda e, n=n, i=i: e.reciprocal(out=S[0:n, 5, i:i + 1], in_=S[0:n, 5, i:i + 1]), reads=[rtb], writes=[rtb])
                for q in range(4):
                    bk = bank % 6
                    bank += 1
                    fns = [lambda e, q=q, j=j, bk=bk, n=n, col0=col0: e.transpose(out=ps[0:n, bk, j * 128:(j + 1) * 128], in_=xT[:, 4 * q + j, col0:col0 + n], identity=idf[:])
                           for j in range(4)]
                    P.group(fns, reads=xbs + [bf("idf")], writes=[pb[bk]])
                    stt(yout[s3][0:n, q * 512:(q + 1) * 512], ps[0:n, bk, :], S[0:n, 5, i:i + 1], gbc[0:n, q * 512:(q + 1) * 512], ALU.mult, ALU.mult,
                        [pb[bk], rtb, gB], [yob[s3]])
                outs.append(P.dma("sp", lambda e, i=i, n=n, s3=s3: e.dma_start(out=y_d[i * 128:i * 128 + n, :], in_=yout[s3][0:n, :]),
                                  "yo%d" % s3, reads=[yob[s3]]))
            last = {}
            for h in outs:
                if last.get(h[0], 0) < h[1]:
                    last[h[0]] = h[1]
            P.wait("sp", list(last.items()))

        Wd = WStream(Prog(), wt, plan=None)
        emit(Wd.P, Wd)
        P = Prog()
        W = WStream(P, wt, plan=Wd.req)
        emit(P, W)
        assert W.i == len(Wd.req)

        sems = {n: st.enter_context(nc.semaphore(n)) for n in P.sem_names()}
        block = st.enter_context(nc.Block())

        @block.tensor
        def _(e):
            P.replay("pe", e, sems)

        @block.scalar
        def _(e):
            P.replay("act", e, sems)

        @block.vector
        def _(e):
            P.replay("dve", e, sems)

        @block.gpsimd
        def _(e):
            P.replay("pool", e, sems)

        @block.sync
        def _(e):
            P.replay("sp", e, sems)
    return nc


def _pack_params(norm1_g, b_in, conv_w, conv_b, ln_g, ln_b, pool_scale, norm2_g, norm_f, half):
    prm = np.zeros((128, NPRM), np.float32)

    def cols(v):
        return np.ascontiguousarray(np.asarray(v, np.float32).reshape(-1, 128).T)

    for l in range(DEPTH):
        L0 = l * PL
        prm[:, L0 + O_G1:L0 + O_G1 + 16] = cols(norm1_g[l])
        prm[:, L0 + O_BIN:L0 + O_BIN + 24] = cols(b_in[l])
        cw = np.asarray(conv_w[l], np.float32)
        prm[:, L0 + O_CW:L0 + O_CW + 248] = cw.reshape(31, 8, 128).transpose(2, 1, 0).reshape(128, 248)
        prm[:, L0 + O_CB:L0 + O_CB + 8] = cols(conv_b[l])
        prm[:, L0 + O_LNG:L0 + O_LNG + 8] = cols(ln_g[l])
        prm[:, L0 + O_LNB:L0 + O_LNB + 8] = cols(ln_b[l])
        prm[:, L0 + O_PSC:L0 + O_PSC + 8] = cols(pool_scale[l])
        prm[:, L0 + O_G2:L0 + O_G2 + 16] = cols(norm2_g[l])
    prm[:, O_GF:O_GF + 16] = cols(norm_f)
    prm[:, O_HM] = float(half)
    pos = np.arange(16)
    for g in range(4):
        w = 2 << g
        cnt = np.minimum(w, pos + 1) if half == 0 else np.full(16, w)
        prm[:, O_IC + g * 16:O_IC + (g + 1) * 16] = (1.0 / cnt.astype(np.float32))[None, :]
    return prm


_NC_CACHE = {}


def kernel(x_prompt, x_sample, state_conv, state_pool, norm1_g, w_in, b_in, conv_w, conv_b,
           ln_g, ln_b, pool_w, pool_scale, w_out, norm2_g, w_ff1, w_ff2, norm_f):
    f = lambda a: np.ascontiguousarray(np.asarray(a, dtype=np.float32))
    x_prompt, x_sample, state_conv, state_pool = f(x_prompt), f(x_sample), f(state_conv), f(state_pool)
    w_in, pool_w, w_out, w_ff1, w_ff2 = f(w_in), f(pool_w), f(w_out), f(w_ff1), f(w_ff2)
    cidx = np.concatenate([np.concatenate([np.arange(c * 128, (c + 1) * 128), np.arange(1024 + c * 128, 1024 + (c + 1) * 128)])
                           for c in range(8)] + [np.arange(2048, DIN)])
    w_in = np.ascontiguousarray(w_in[:, :, cidx])
    if "nc" not in _NC_CACHE:
        _NC_CACHE["nc"] = build_program()
    nc = _NC_CACHE["nc"]
    ident = np.eye(128, dtype=np.float32)
    gfb = np.ascontiguousarray(np.broadcast_to(np.asarray(norm_f, np.float32)[None, :], (128, D)))
    prms = [_pack_params(norm1_g, b_in, conv_w, conv_b, ln_g, ln_b, pool_scale, norm2_g, norm_f, h) for h in range(2)]
    in_maps = []
    for i in range(NCORES):
        b, h = i // 2, i % 2
        xc = np.zeros((T, D), np.float32)
        if h == 1:
            xc[0:HALO] = x_prompt[b, NPR - HALO:NPR]
        xc[HALO:HALO + NPR] = x_prompt[b, h * NPR:(h + 1) * NPR]
        xc[HALO + NPR:] = x_sample[SBQ * i:SBQ * (i + 1)].reshape(NSM, D)
        in_maps.append({
            "x": xc,
            "sconv": np.ascontiguousarray(state_conv[:, SBQ * i:SBQ * (i + 1)]),
            "spool": np.ascontiguousarray(state_pool[:, SBQ * i:SBQ * (i + 1)]),
            "prm": prms[h], "ident": ident, "gfb": gfb,
            "w_in": w_in, "pool_w": pool_w, "w_out": w_out, "w_ff1": w_ff1, "w_ff2": w_ff2,
        })
    ncr = int(os.environ.get("KCORES", NCORES))
    res = run_bass_kernel_spmd(nc, in_maps[:ncr], core_ids=list(range(ncr)))
    R = list(res.results) + [res.results[0]] * (NCORES - ncr)
    B_, S_ = x_prompt.shape[0], x_prompt.shape[1]
    y_prompt = np.empty((B_, S_, D), np.float32)
    y_sample = np.empty(x_sample.shape, np.float32)
    ncp = np.empty((DEPTH, B_, 30, 1024), np.float32)
    npp = np.empty((DEPTH, B_, 15, 1024), np.float32)
    ncs = np.empty(state_conv.shape, np.float32)
    nps = np.empty(state_pool.shape, np.float32)
    for i in range(NCORES):
        b, h = i // 2, i % 2
        y = R[i]["y"]
        y_prompt[b, h * NPR:(h + 1) * NPR] = y[0:NPR]
        y_sample[SBQ * i:SBQ * (i + 1)] = y[NPR:].reshape(SBQ, 4, D)
        ncs[:, SBQ * i:SBQ * (i + 1)] = R[i]["ocs"]
        nps[:, SBQ * i:SBQ * (i + 1)] = R[i]["ops"]
        if h == 1:
            ncp[:, b] = R[i]["ocp"]
            npp[:, b] = R[i]["opp"]
    return (y_prompt, y_sample, ncp, npp, ncs, nps)
```

```python
import os
import numpy as np
from contextlib import ExitStack
import concourse.bass as bass
import concourse.mybir as mybir
from concourse.bass_utils import run_bass_kernel_spmd

F32 = mybir.dt.float32
BF16 = mybir.dt.bfloat16
AF = mybir.ActivationFunctionType
ALU = mybir.AluOpType

NCORES = 8
D = 2048
KC = 16
DIN = 3072
DFF = 8192
DEPTH = 2
T = 1152
TN = 384
HALO = 64
NPR = 1024
NSM = 64
SBQ = 16
NYR = NPR + NSM
EPS = 1e-6
NSLOT = 3
FB = 2048
TMAP = [0, 2, 1]

PL = 336
O_G1, O_BIN, O_CW, O_CB, O_LNG, O_LNB, O_PSC, O_G2 = 0, 16, 40, 288, 296, 304, 312, 320
O_GF = DEPTH * PL
O_HM = O_GF + 16
O_IC = O_HM + 1
NPRM = O_IC + 64

SAME_ENG_SYNC = True
_DBG = set(os.environ.get("KDBG", "").split(",")) - {""}


class Buf:
    __slots__ = ("name", "w", "r", "pr", "excl")

    def __init__(self, name, excl=False):
        self.name = name
        self.w = {}
        self.r = {}
        self.pr = {}
        self.excl = excl


def _split(reads, writes):
    ex = [b for b in reads if b.excl]
    if not ex:
        return reads, writes
    return [b for b in reads if not b.excl], list(writes) + ex


def _merge(dst, src):
    for k, v in src.items():
        if dst.get(k, 0) < v:
            dst[k] = v


def handoff(src_bufs, dst_bufs):
    u = {}
    for b in src_bufs:
        _merge(u, b.w)
        _merge(u, b.r)
        _merge(u, b.pr)
    for b in dst_bufs:
        b.w = dict(u)
        b.r = {}
        b.pr = dict(u)


class Prog:
    ENG = ("pe", "act", "dve", "pool", "sp")

    def __init__(self):
        self.ops = {e: [] for e in self.ENG}
        self.cnt = {e: 0 for e in self.ENG}
        self.dcnt = {}

    def _deps(self, reads, writes, deps):
        d = {}
        for b in reads:
            _merge(d, b.w)
        for b in writes:
            _merge(d, b.r)
            _merge(d, b.pr)
            _merge(d, b.w)
        for h in deps:
            if h is not None:
                _merge(d, {h[0]: h[1]})
        return d

    def _reg(self, h, reads, writes):
        k, v = h
        for b in reads:
            if b.r.get(k, 0) < v:
                b.r[k] = v
        for b in writes:
            if b.r:
                b.pr = b.r
                b.r = {}
                b.w = {k: v}
            else:
                if b.w.get(k, 0) < v:
                    b.w[k] = v

    def op(self, eng, fn, reads=(), writes=(), deps=()):
        reads, writes = _split(reads, writes)
        d = self._deps(reads, writes, deps)
        self.cnt[eng] += 1
        h = (eng, self.cnt[eng])
        self.ops[eng].append((fn, d, (eng, 1)))
        self._reg(h, reads, writes)
        return h

    def group(self, fns, reads=(), writes=(), deps=()):
        reads, writes = _split(reads, writes)
        d = self._deps(reads, writes, deps)
        self.cnt["pe"] += 1
        h = ("pe", self.cnt["pe"])
        n = len(fns)
        for i, fn in enumerate(fns):
            self.ops["pe"].append((fn, d if i == 0 else {}, ("pe", 1) if i == n - 1 else None))
        self._reg(h, reads, writes)
        return h

    def dma(self, queue, fn, sem, reads=(), writes=(), deps=()):
        reads, writes = _split(reads, writes)
        d = self._deps(reads, writes, deps)
        self.dcnt[sem] = self.dcnt.get(sem, 0) + 16
        h = (sem, self.dcnt[sem])
        self.ops[queue].append((fn, d, (sem, 16)))
        self._reg(h, reads, writes)
        return h

    def wait(self, eng, deps):
        d = {}
        for h in deps:
            _merge(d, {h[0]: h[1]})
        self.ops[eng].append((None, d, None))

    def sem_names(self):
        return list(self.ENG) + sorted(self.dcnt.keys())

    def replay(self, eng, e, sems):
        waited = {}
        for fn, d, inc in self.ops[eng]:
            for k, v in d.items():
                if k == eng and (eng == "pe" or not SAME_ENG_SYNC):
                    continue
                if waited.get(k, 0) < v:
                    e.wait_ge(sems[k], v)
                    waited[k] = v
            if fn is None:
                continue
            inst = fn(e)
            if inc is not None:
                inst.then_inc(sems[inc[0]], inc[1])


class WStream:
    def __init__(self, P, wt, plan=None):
        self.P = P
        self.wt = wt
        self.plan = plan
        self.req = []
        self.i = 0
        self.issued = 0
        self.bufs = [Buf("w%d" % s) for s in range(NSLOT)]

    def _issue(self):
        i = self.issued
        s = i % NSLOT
        src, kc = self.plan[i]
        wt = self.wt
        self.P.dma("pool", lambda e, s=s, src=src, kc=kc: e.dma_start(out=wt[:, s, 0:kc, :], in_=src),
                   "w%d" % s, writes=[self.bufs[s]])
        self.issued += 1

    def get(self, src, kc):
        i = self.i
        self.i += 1
        if self.plan is None:
            self.req.append((src, kc))
            return i % NSLOT, self.bufs[i % NSLOT]
        while self.issued < min(i + NSLOT, len(self.plan)):
            self._issue()
        return i % NSLOT, self.bufs[i % NSLOT]


def build_program():
    nc = bass.Bass("TRN2", target_bir_lowering=False)
    dt_in = lambda n, s: nc.dram_tensor(n, s, F32, kind="ExternalInput").ap()
    dt_out = lambda n, s: nc.dram_tensor(n, s, F32, kind="ExternalOutput").ap()
    x_d = dt_in("x", [T, D])
    sconv_d = dt_in("sconv", [DEPTH, SBQ, 30, 1024])
    spool_d = dt_in("spool", [DEPTH, SBQ, 15, 1024])
    prm_d = dt_in("prm", [128, NPRM])
    ident_d = dt_in("ident", [128, 128])
    gfb_d = dt_in("gfb", [128, D])
    w_in_d = dt_in("w_in", [DEPTH, D, DIN])
    pool_w_d = dt_in("pool_w", [DEPTH, 4, 256, 256])
    w_out_d = dt_in("w_out", [DEPTH, D, D])
    w_ff1_d = dt_in("w_ff1", [DEPTH, D, DFF])
    w_ff2_d = dt_in("w_ff2", [DEPTH, DFF, D])
    y_d = dt_out("y", [NYR, D])
    ocs_d = dt_out("ocs", [DEPTH, SBQ, 30, 1024])
    ops_d = dt_out("ops", [DEPTH, SBQ, 15, 1024])
    ocp_d = dt_out("ocp", [DEPTH, 30, 1024])
    opp_d = dt_out("opp", [DEPTH, 15, 1024])

    w_in_v = [w_in_d[l].rearrange("(kc p) m -> p kc m", p=128) for l in range(DEPTH)]
    w_out_v = [w_out_d[l].rearrange("(kc p) m -> p kc m", p=128) for l in range(DEPTH)]
    w_ff1_v = [w_ff1_d[l].rearrange("(kc p) m -> p kc m", p=128) for l in range(DEPTH)]
    w_ff2_v = [w_ff2_d[l].rearrange("(jj kc p) m -> jj p kc m", p=128, kc=KC) for l in range(DEPTH)]
    pool_w_v = [[pool_w_d[l, g].rearrange("(kk p) m -> p kk m", p=128) for g in range(4)] for l in range(DEPTH)]

    with ExitStack() as st:
        sb = lambda n, s, d: st.enter_context(nc.sbuf_tensor(n, s, d))
        xT = sb("xT", [128, KC, T], F32)
        Bm = sb("Bm", [128, 18432], BF16)
        Cm = sb("Cm", [128, 9216], F32)
        S = sb("S", [128, 6, TN], F32)
        wt = sb("wt", [128, NSLOT, KC, 256], BF16)
        ubuf = sb("ubuf", [128, 3, 32 + TN], BF16)
        hsave = sb("hsave", [128, 8, 32], BF16)
        usamp = sb("usamp", [128, 8, SBQ, 34], BF16)
        zf = sb("zf", [128, 2, TN], F32)
        zb = sb("zb", [128, 2, 16 + TN], BF16)
        zsave = sb("zsave", [128, 8, 16], BF16)
        zsamp = sb("zsamp", [128, 8, SBQ, 19], BF16)
        dbuf = sb("dbuf", [128, 2, 2, TN], BF16)
        prm = sb("prm_t", [128, NPRM], F32)
        idf = sb("idf", [128, 128], F32)
        idb = sb("idb", [128, 128], BF16)
        ones = sb("ones", [128, 128], BF16)
        ps = st.enter_context(nc.psum_tensor("ps", [128, 8, 512], F32))

        hT = Bm[:, 0:6144].rearrange("p (k t) -> p k t", k=KC)
        mix = Bm[:, 6144:12288].rearrange("p (k t) -> p k t", k=KC)
        sqb = Bm[:, 12288:18432].rearrange("p (k t) -> p k t", k=KC)
        utail = Bm[:, 12288:12288 + 1504].bitcast(F32).rearrange("p (c t) -> p c t", c=8)
        ztail = Bm[:, 12288 + 1504:12288 + 3008].bitcast(F32).rearrange("p (c t) -> p c t", c=8)
        h2T = Bm[:, :].rearrange("p (a k t) -> p a k t", a=3, k=KC)
        tstage = S[:, :, :].rearrange("p a t -> p (a t)")[:, 0:2048].rearrange("p (a c) -> p a c", a=2)
        xin = [Cm[:, s * 2048:(s + 1) * 2048] for s in range(4)]
        stgc = Cm[:, 0:4096].rearrange("p (g c) -> p g c", g=4)
        stgp = Cm[:, 4096:6144].rearrange("p (g c) -> p g c", g=2)
        vv = Cm[:, 0:3072].rearrange("p (c t) -> p c t", c=8)
        vbf = Cm[:, 3072:4608].bitcast(BF16).rearrange("p (c t) -> p c t", c=8)
        sqv = Cm[:, 4608:6144].bitcast(BF16).rearrange("p (c t) -> p c t", c=8)
        diag = Cm[:, 6144:8128].bitcast(BF16).rearrange("p (k j) -> p k j", k=31)
        sig = [Cm[:, 8128 + i * TN:8128 + (i + 1) * TN] for i in range(2)]
        t64 = Cm[:, 8896:8960]
        t16 = Cm[:, 8960:8976]
        fT = Cm[:, :].bitcast(BF16).rearrange("p (a k t) -> p a k t", a=3, k=KC)
        sqFin = Bm[:, :].rearrange("p (k t) -> p k t", k=KC)
        gbc = Cm[:, 0:2048]
        yout = [Cm[:, 2048 + s * 2048:2048 + (s + 1) * 2048] for s in range(3)]
        Sflat = S[:, 0:3, :].rearrange("p a t -> p (a t)")

        def pcol(c):
            return prm[:, c:c + 1]

        def emit(P, W):
            B = {}

            def bf(n):
                if n not in B:
                    B[n] = Buf(n)
                return B[n]

            mixbufs_ref = [None]
            pb = [bf("pb%d" % i) for i in range(8)]
            for b_ in pb:
                b_.excl = True
            xTb = [bf("xT%d" % t) for t in range(3)]
            state = {"ring": 0, "alt": 0}

            def ring3():
                b = state["ring"] % 3
                state["ring"] += 1
                return b

            def ring4():
                b = state["ring"] % 4
                state["ring"] += 1
                return b

            def alt_eng():
                state["alt"] += 1
                return "act" if state["alt"] % 2 else "dve"

            def copy_op(eng, out, in_, reads, writes):
                if eng == "act":
                    return P.op("act", lambda e: e.activation(out=out, in_=in_, func=AF.Copy), reads=reads, writes=writes)
                return P.op("dve", lambda e: e.tensor_copy(out=out, in_=in_), reads=reads, writes=writes)

            outs = []
            P.dma("sp", lambda e: e.dma_start(out=prm[:], in_=prm_d[:, :]), "c0", writes=[bf("prm")])
            P.dma("sp", lambda e: e.dma_start(out=idf[:], in_=ident_d[:, :]), "c1", writes=[bf("idf")])
            P.op("dve", lambda e: e.tensor_copy(out=idb[:], in_=idf[:]), reads=[bf("idf")], writes=[bf("idb")])
            P.op("dve", lambda e: e.memset(ones[:], 1.0), writes=[bf("ones")])
            for l in range(DEPTH if "nopt" not in _DBG else 0):
                outs.append(P.dma("sp", lambda e, l=l: e.dma_start(out=ocs_d[l, :, 0:26, :], in_=sconv_d[l, :, 4:30, :]), "pt"))
                outs.append(P.dma("sp", lambda e, l=l: e.dma_start(out=ops_d[l, :, 0:11, :], in_=spool_d[l, :, 4:15, :]), "pt"))

            xinb = [bf("xin%d" % i) for i in range(4)]
            bank = 0
            for i in range(T // 128):
                s = i % 4
                P.dma("sp", lambda e, i=i, s=s: e.dma_start(out=xin[s], in_=x_d[i * 128:(i + 1) * 128, :]),
                      "xi%d" % s, writes=[xinb[s]])
                for q in range(4):
                    bk = bank % 8
                    bank += 1
                    fns = [lambda e, s=s, q=q, j=j, bk=bk: e.transpose(
                        out=ps[:, bk, j * 128:(j + 1) * 128], in_=xin[s][:, (4 * q + j) * 128:(4 * q + j + 1) * 128], identity=idf[:])
                        for j in range(4)]
                    P.group(fns, reads=[xinb[s], bf("idf")], writes=[pb[bk]])
                    copy_op(alt_eng(), xT[:, 4 * q:4 * q + 4, i * 128:(i + 1) * 128],
                            ps[:, bk, :].rearrange("p (j t) -> p j t", t=128), [pb[bk]], [xTb[i // 3]])
            cbufs = list(xinb)

            def rms_stats(l_g_unused, src_cols, sq_view, sq_buf, bank_i, out_slot, out_buf, xbufs):
                for k in range(KC):
                    P.op("act", lambda e, k=k: e.activation(out=sq_view[:, k, :], in_=xT[:, k, src_cols[0]:src_cols[1]], func=AF.Square),
                         reads=xbufs, writes=[sq_buf])
                n = src_cols[1] - src_cols[0]
                fns = [lambda e, k=k: e.matmul(out=ps[:, bank_i, 0:n], lhsT=ones[:], rhs=sq_view[:, k, :], start=(k == 0), stop=(k == KC - 1))
                       for k in range(KC)]
                P.group(fns, reads=[sq_buf, bf("ones")], writes=[pb[bank_i]])
                rstd_finish(bank_i, n, out_slot, out_buf)

            def rstd_finish(bank_i, n, out_slot, out_buf):
                if out_slot is None:
                    out_slot, out_buf = ps[:, bank_i, 0:n], pb[bank_i]
                P.op("act", lambda e: e.activation(out=out_slot, in_=ps[:, bank_i, 0:n], func=AF.Sqrt, bias=EPS, scale=1.0 / D),
                     reads=[pb[bank_i]], writes=[out_buf])
                P.op("dve", lambda e: e.reciprocal(out=out_slot, in_=out_slot), reads=[out_buf], writes=[out_buf])

            def rms1_pre(l, t):
                c0 = t * TN
                rms_stats(None, (c0, c0 + TN), sqb, bf("sqb"), 7, None, None, [xTb[t]])
                for k in range(KC):
                    P.op("dve", lambda e, k=k: e.scalar_tensor_tensor(
                        out=hT[:, k, :], in0=xT[:, k, c0:c0 + TN], scalar=pcol(l * PL + O_G1 + k), in1=ps[:, 7, 0:TN],
                        op0=ALU.mult, op1=ALU.mult), reads=[xTb[t], pb[7], bf("prm")], writes=[bf("hT")])

            def mm_group(slot, wbuf, rhs_of_k, nk, bank_i, n, reads, half=0):
                fns = [lambda e, k=k: e.matmul(out=ps[:, bank_i, 0:n], lhsT=wt[:, slot, k, half * 128:(half + 1) * 128], rhs=rhs_of_k(k),
                                               start=(k == 0), stop=(k == nk - 1)) for k in range(nk)]
                return P.group(fns, reads=[wbuf] + reads, writes=[pb[bank_i]])

            def stt(out, in0, scalar, in1, op0, op1, reads, writes):
                return P.op("dve", lambda e: e.scalar_tensor_tensor(out=out, in0=in0, scalar=scalar, in1=in1, op0=op0, op1=op1),
                            reads=reads, writes=writes)

            def ts1(out, in0, s1, op0, reads, writes):
                return P.op("dve", lambda e: e.tensor_scalar(out=out, in0=in0, scalar1=s1, scalar2=None, op0=op0),
                            reads=reads, writes=writes)

            def ts2(out, in0, s1, s2, op0, op1, reads, writes):
                return P.op("dve", lambda e: e.tensor_scalar(out=out, in0=in0, scalar1=s1, scalar2=s2, op0=op0, op1=op1),
                            reads=reads, writes=writes)

            def act(out, in_, func, reads, writes, bias=None, scale=None):
                kw = {}
                if bias is not None:
                    kw["bias"] = bias
                if scale is not None:
                    kw["scale"] = scale
                return P.op("act", lambda e: e.activation(out=out, in_=in_, func=func, **kw), reads=reads, writes=writes)

            def mixer(l, t):
                L0 = l * PL
                c0 = t * TN
                npr = TN if t < 2 else TN - NSM
                prmb = bf("prm")
                hTb, mixb = bf("hT"), bf("mix")
                vb, vbfb, sqvb = bf("v"), bf("vbf"), bf("sqv")
                sigb = [bf("sig0"), bf("sig1")]
                ubb = [bf("ub0"), bf("ub1"), bf("ub2")]

                nxt = t + 1 if t < 2 else None
                dgb = [bf("diagA"), bf("diagB")]
                halves = [(0, 16), (16, 31)]

                def conv_build(c):
                    for hi, (k0, k1) in enumerate(halves):
                        nk = k1 - k0
                        P.op("dve", lambda e, k0=k0, k1=k1, nk=nk: e.tensor_tensor(
                            out=diag[:, k0:k1, :], in0=ps[:, 4, 0:128].unsqueeze(1).broadcast_to([128, nk, 128]),
                            in1=prm[:, L0 + O_CW + c * 31 + k0:L0 + O_CW + c * 31 + k1].unsqueeze(2).broadcast_to([128, nk, 128]),
                            op=ALU.mult), reads=[pb[4], prmb], writes=[dgb[hi]])

                def conv(c):
                    ui = c % 3
                    ub = ubuf[:, ui, :]
                    for hi, (k0, k1) in enumerate(halves):
                        fns = [lambda e, k=k: e.matmul(out=ps[:, 3, 0:npr], lhsT=diag[:, k, :], rhs=ub[:, 2 + k:2 + k + npr],
                                                       start=(k == 0), stop=(k == 30)) for k in range(k0, k1)]
                        P.group(fns, reads=[dgb[hi], ubb[ui]], writes=[pb[3]])
                    if t == 2:
                        fns = [lambda e, k=k: e.matmul(out=ps[:, 3, npr:TN], lhsT=diag[:, k, :], rhs=usamp[:, c, :, k:k + 4],
                                                       start=(k == 0), stop=(k == 30), skip_group_check=True) for k in range(31)]
                        P.group(fns, reads=[dgb[0], dgb[1], bf("usamp%d" % c)], writes=[pb[3]])
                    cb = pcol(L0 + O_CB + c)
                    act(vv[:, c, :], ps[:, 3, 0:TN], AF.Identity, [pb[3], prmb], [vb], bias=cb)
                    ts1(vbf[:, c, :], ps[:, 3, 0:TN], cb, ALU.add, [pb[3], prmb], [vbfb])
                    act(sqv[:, c, :], ps[:, 3, 0:TN], AF.Square, [pb[3], prmb], [sqvb], bias=cb)

                for c in range(8):
                    ui = c % 3
                    ub = ubuf[:, ui, :]
                    if c > 0:
                        conv_build(c - 1)
                    sa, wa = W.get(w_in_v[l][:, :, c * 256:(c + 1) * 256], KC)
                    ba = ring3()
                    mm_group(sa, wa, lambda k: hT[:, k, :], KC, ba, TN, [hTb], half=0)
                    bg = ring3()
                    mm_group(sa, wa, lambda k: hT[:, k, :], KC, bg, TN, [hTb], half=1)
                    si = c % 2
                    act(sig[si], ps[:, bg, 0:TN], AF.Sigmoid, [pb[bg], prmb], [sigb[si]], bias=pcol(L0 + O_BIN + 8 + c))
                    ba_col = pcol(L0 + O_BIN + c)
                    rdu = [pb[ba], sigb[si], prmb]
                    if t == 0:
                        P.op("dve", lambda e, ub=ub: e.memset(ub[:, 0:32], 0.0), writes=[ubb[ui]])
                    else:
                        P.op("dve", lambda e, ub=ub, c=c: e.tensor_copy(out=ub[:, 0:32], in_=hsave[:, c, :]),
                             reads=[bf("hsave%d" % c)], writes=[ubb[ui]])
                    if t == 0:
                        stt(t64, ps[:, ba, 0:HALO], ba_col, sig[si][:, 0:HALO], ALU.add, ALU.mult, rdu, [bf("t64")])
                        ts1(ub[:, 32:32 + HALO], t64, pcol(O_HM), ALU.mult, [bf("t64"), prmb], [ubb[ui]])
                        stt(ub[:, 32 + HALO:32 + TN], ps[:, ba, HALO:TN], ba_col, sig[si][:, HALO:TN], ALU.add, ALU.mult, rdu, [ubb[ui]])
                    else:
                        stt(ub[:, 32:32 + npr], ps[:, ba, 0:npr], ba_col, sig[si][:, 0:npr], ALU.add, ALU.mult, rdu, [ubb[ui]])
                    if t == 2:
                        stt(usamp[:, c, :, 30:34], ps[:, ba, npr:TN].rearrange("p (b j) -> p b j", j=4), ba_col,
                            sig[si][:, npr:TN].rearrange("p (b j) -> p b j", j=4), ALU.add, ALU.mult, rdu, [bf("usamp%d" % c)])
                        stt(utail[:, c, :], ps[:, ba, TN - 94:TN], ba_col, sig[si][:, TN - 94:TN], ALU.add, ALU.mult, rdu, [bf("sqb")])
                    else:
                        P.op("dve", lambda e, ub=ub, c=c: e.tensor_copy(out=hsave[:, c, :], in_=ub[:, TN:TN + 32]),
                             reads=[ubb[ui]], writes=[bf("hsave%d" % c)])
                    if nxt is not None:
                        n0 = nxt * TN
                        for k in (2 * c, 2 * c + 1):
                            P.op("act", lambda e, k=k, n0=n0: e.activation(out=sqb[:, k, :], in_=xT[:, k, n0:n0 + TN], func=AF.Square),
                                 reads=[xTb[nxt]], writes=[bf("sqb")])
                    if c > 0:
                        conv(c - 1)
                conv_build(7)
                conv(7)

                def pool_A(g):
                    bz = []
                    sz, wz = W.get(w_in_v[l][:, :, 2048 + g * 256:2048 + (g + 1) * 256], KC)
                    for j in range(2):
                        m = 2 * g + j
                        b_ = ring4()
                        bz.append(b_)
                        mm_group(sz, wz, lambda k: hT[:, k, :], KC, b_, TN, [hTb], half=j)
                        bzc = pcol(L0 + O_BIN + 16 + m)
                        zfB, zbB = bf("zf%d" % j), bf("zb%d" % j)
                        if t == 0:
                            P.op("dve", lambda e, j=j: e.memset(zb[:, j, 0:16], 0.0), writes=[zbB])
                            ts2(zb[:, j, 16:16 + HALO], ps[:, b_, 0:HALO], bzc, pcol(O_HM), ALU.add, ALU.mult, [pb[b_], prmb], [zbB])
                            ts1(zb[:, j, 16 + HALO:16 + TN], ps[:, b_, HALO:TN], bzc, ALU.add, [pb[b_], prmb], [zbB])
                        else:
                            P.op("dve", lambda e, j=j, m=m: e.tensor_copy(out=zb[:, j, 0:16], in_=zsave[:, m, :]),
                                 reads=[bf("zsave%d" % m)], writes=[zbB])
                            ts1(zb[:, j, 16:16 + npr], ps[:, b_, 0:npr], bzc, ALU.add, [pb[b_], prmb], [zbB])
                        if t == 2:
                            ts1(zsamp[:, m, :, 15:19], ps[:, b_, npr:TN].rearrange("p (b j) -> p b j", j=4), bzc, ALU.add,
                                [pb[b_], prmb], [bf("zsamp%d" % m)])
                        act(zf[:, j, :], ps[:, b_, 0:TN], AF.Identity, [pb[b_], prmb], [zfB], bias=bzc)
                        if t == 2:
                            ts1(ztail[:, m, :], ps[:, b_, TN - 94:TN], bzc, ALU.add, [pb[b_], prmb], [bf("sqb")])
                        else:
                            P.op("dve", lambda e, j=j, m=m: e.tensor_copy(out=zsave[:, m, :], in_=zb[:, j, TN:TN + 16]),
                                 reads=[zbB], writes=[bf("zsave%d" % m)])

                def pool_B(g):
                    w = 2 << g
                    db = g % 2
                    dB = bf("d%d" % db)
                    for j in range(2):
                        m = 2 * g + j
                        zfB, zbB = bf("zf%d" % j), bf("zb%d" % j)
                        bp = ring4()
                        fns = [lambda e, k=k, j=j, bp=bp, w=w: e.matmul(out=ps[:, bp, 0:npr], lhsT=idb[:], rhs=zb[:, j, 16 - k:16 - k + npr],
                                                                   start=(k == 0), stop=(k == w - 1)) for k in range(w)]
                        rd = [bf("idb"), zbB]
                        if t == 2:
                            fns += [lambda e, k=k, m=m, bp=bp, w=w: e.matmul(out=ps[:, bp, npr:TN], lhsT=idb[:], rhs=zsamp[:, m, :, 15 - k:19 - k],
                                                                        start=(k == 0), stop=(k == w - 1), skip_group_check=True) for k in range(w)]
                            rd.append(bf("zsamp%d" % m))
                        P.group(fns, reads=rd, writes=[pb[bp]])
                        rdd = [pb[bp], zfB]
                        if t == 0:
                            stt(dbuf[:, db, j, 0:HALO], ps[:, bp, 0:HALO], 1.0 / w, zf[:, j, 0:HALO], ALU.mult, ALU.subtract, rdd, [dB])
                            P.op("dve", lambda e, bp=bp, g=g: e.tensor_tensor(out=t16, in0=ps[:, bp, HALO:HALO + 16],
                                                                              in1=prm[:, O_IC + g * 16:O_IC + (g + 1) * 16], op=ALU.mult),
                                 reads=[pb[bp], prmb], writes=[bf("t16")])
                            P.op("dve", lambda e, j=j, db=db: e.tensor_tensor(out=dbuf[:, db, j, HALO:HALO + 16], in0=t16,
                                                                              in1=zf[:, j, HALO:HALO + 16], op=ALU.subtract),
                                 reads=[bf("t16"), zfB], writes=[dB])
                            stt(dbuf[:, db, j, HALO + 16:TN], ps[:, bp, HALO + 16:TN], 1.0 / w, zf[:, j, HALO + 16:TN],
                                ALU.mult, ALU.subtract, rdd, [dB])
                        else:
                            stt(dbuf[:, db, j, :], ps[:, bp, 0:TN], 1.0 / w, zf[:, j, :], ALU.mult, ALU.subtract, rdd, [dB])

                def pool_C(g):
                    db = g % 2
                    dB = bf("d%d" % db)
                    sp_, wp = W.get(pool_w_v[l][g][:, :, :], 2)
                    for e_ in range(2):
                        m = 2 * g + e_
                        bq = ring4()
                        mm_group(sp_, wp, lambda k, db=db: dbuf[:, db, k, :], 2, bq, TN, [dB], half=e_)
                        act(mix[:, 8 + m, :], ps[:, bq, 0:TN], AF.Copy, [pb[bq], prmb], [mixb], scale=pcol(L0 + O_PSC + m))

                fns = [lambda e, c=c: e.matmul(out=ps[:, 5, 0:TN], lhsT=ones[:], rhs=vbf[:, c, :], start=(c == 0), stop=(c == 7)) for c in range(8)]
                P.group(fns, reads=[vbfb, bf("ones")], writes=[pb[5]])
                fns = [lambda e, c=c: e.matmul(out=ps[:, 6, 0:TN], lhsT=ones[:], rhs=sqv[:, c, :], start=(c == 0), stop=(c == 7)) for c in range(8)]
                P.group(fns, reads=[sqvb, bf("ones")], writes=[pb[6]])
                if nxt is not None:
                    fns = [lambda e, k=k: e.matmul(out=ps[:, 7, 0:TN], lhsT=ones[:], rhs=sqb[:, k, :], start=(k == 0), stop=(k == KC - 1))
                           for k in range(KC)]
                    P.group(fns, reads=[bf("sqb"), bf("ones")], writes=[pb[7]])
                S1b = bf("S1")
                act(ps[:, 5, 0:TN], ps[:, 5, 0:TN], AF.Copy, [pb[5]], [pb[5]], scale=1.0 / 1024)
                act(S[:, 1, :], ps[:, 5, 0:TN], AF.Square, [pb[5]], [S1b])
                stt(ps[:, 6, 0:TN], ps[:, 6, 0:TN], 1.0 / 1024, S[:, 1, :], ALU.mult, ALU.subtract, [pb[6], S1b], [pb[6]])
                act(ps[:, 6, 0:TN], ps[:, 6, 0:TN], AF.Sqrt, [pb[6]], [pb[6]], bias=EPS)
                P.op("dve", lambda e: e.reciprocal(out=ps[:, 6, 0:TN], in_=ps[:, 6, 0:TN]), reads=[pb[6]], writes=[pb[6]])
                if nxt is not None:
                    rstd_finish(7, TN, None, None)
                def ln_apply(c):
                    sl = (0, 2, 3, 4)[c % 4]
                    Sb = bf("S%d" % sl)
                    P.op("dve", lambda e, c=c, sl=sl: e.tensor_tensor(out=S[:, sl, :], in0=vv[:, c, :], in1=ps[:, 5, 0:TN], op=ALU.subtract),
                         reads=[vb, pb[5]], writes=[Sb])
                    P.op("dve", lambda e, sl=sl: e.tensor_tensor(out=S[:, sl, :], in0=S[:, sl, :], in1=ps[:, 6, 0:TN], op=ALU.mult),
                         reads=[Sb, pb[6]], writes=[Sb])
                    act(mix[:, c, :], S[:, sl, :], AF.Silu, [Sb, prmb], [mixb], bias=pcol(L0 + O_LNB + c), scale=pcol(L0 + O_LNG + c))

                early2 = (t == 2 and "noffn" not in _DBG)
                sqCb = [bf("sqC%d" % i) for i in range(3)]
                if early2:
                    handoff([vbfb, sqvb], [sqCb[1]])
                    for k in range(KC):
                        P.op("act", lambda e, k=k: e.activation(out=fT[:, 1, k, :], in_=xT[:, k, TN:2 * TN], func=AF.Square),
                             reads=[xTb[1]], writes=[sqCb[1]])
                seq = ["A0", "B0", "A1", "C0", "B1", "A2", "C1", "B2", "A3", "C2", "B3", "C3"]
                for st_ in seq:
                    g = int(st_[1])
                    if st_[0] == "A":
                        pool_A(g)
                        if early2 and g == 1:
                            fns = [lambda e, k=k: e.matmul(out=ps[:, 7, 0:TN], lhsT=ones[:], rhs=fT[:, 1, k, :], start=(k == 0), stop=(k == KC - 1))
                                   for k in range(KC)]
                            P.group(fns, reads=[sqCb[1], bf("ones")], writes=[pb[7]])
                            rstd_finish(7, TN, None, None)
                    elif st_[0] == "B":
                        pool_B(g)
                        ln_apply(2 * g)
                        ln_apply(2 * g + 1)
                    else:
                        pool_C(g)
                if t == 2 and "notails" not in _DBG:
                    Sall6 = [bf("S%d" % i) for i in range(6)]
                    for a, (tl, tlb) in enumerate([(utail, bf("sqb")), (ztail, bf("sqb"))]):
                        for q in range(2):
                            bk = (2 * a + q) % 4
                            fns = [lambda e, tl=tl, q=q, i=i, bk=bk: e.transpose(out=ps[0:94, bk, i * 128:(i + 1) * 128], in_=tl[:, 4 * q + i, :], identity=idf[:])
                                   for i in range(4)]
                            P.group(fns, reads=[tlb, bf("idf")], writes=[pb[bk]])
                            copy_op(alt_eng(), tstage[0:94, a, q * 512:(q + 1) * 512], ps[0:94, bk, :], [pb[bk]], Sall6)
                    outs.append(P.dma("sp", lambda e: e.dma_start(out=ocp_d[l, :, :], in_=tstage[0:30, 0, :]), "to", reads=Sall6))
                    outs.append(P.dma("sp", lambda e: e.dma_start(out=opp_d[l, :, :], in_=tstage[15:30, 1, :]), "to", reads=Sall6))
                    for b in range(SBQ):
                        outs.append(P.dma("sp", lambda e, b=b: e.dma_start(out=ocs_d[l, b, 26:30, :], in_=tstage[30 + 4 * b:34 + 4 * b, 0, :]), "to", reads=Sall6))
                        outs.append(P.dma("sp", lambda e, b=b: e.dma_start(out=ops_d[l, b, 11:15, :], in_=tstage[30 + 4 * b:34 + 4 * b, 1, :]), "to", reads=Sall6))
                if early2:
                    handoff([vb], [sqCb[0]])
                    handoff([bf(n) for n in ("diagA", "diagB", "sig0", "sig1", "t64", "t16")], [sqCb[2]])
                    for k in range(KC):
                        P.op("act", lambda e, k=k: e.activation(out=fT[:, 0, k, :], in_=xT[:, k, 0:TN], func=AF.Square),
                             reads=[xTb[0]], writes=[sqCb[0]])
                for m in range(KC):
                    if m % 2 == 0:
                        so, wo = W.get(w_out_v[l][:, :, m * 128:(m + 2) * 128], KC)
                    bo = ring4()
                    mm_group(so, wo, lambda k: mix[:, k, :], KC, bo, TN, [mixb], half=m % 2)
                    P.op("dve", lambda e, m=m, bo=bo: e.tensor_tensor(out=xT[:, m, c0:c0 + TN], in0=xT[:, m, c0:c0 + TN], in1=ps[:, bo, 0:TN], op=ALU.add),
                         reads=[pb[bo], xTb[t]], writes=[xTb[t]])
                    if nxt is not None:
                        n0 = nxt * TN
                        stt(hT[:, m, :], xT[:, m, n0:n0 + TN], pcol(L0 + O_G1 + m), ps[:, 7, 0:TN], ALU.mult, ALU.mult,
                            [xTb[nxt], pb[7], prmb], [hTb])
                    elif early2:
                        P.op("act", lambda e, m=m: e.activation(out=fT[:, 2, m, :], in_=xT[:, m, c0:c0 + TN], func=AF.Square),
                             reads=[xTb[2]], writes=[sqCb[2]])
                        stt(h2T[:, TMAP[1], m, :], xT[:, m, TN:2 * TN], pcol(L0 + O_G2 + m), ps[:, 7, 0:TN], ALU.mult, ALU.mult,
                            [xTb[1], pb[7], prmb], [bf("sqb")])
                        if m == 4:
                            fns = [lambda e, k=k: e.matmul(out=ps[:, 5, 0:TN], lhsT=ones[:], rhs=fT[:, 0, k, :], start=(k == 0), stop=(k == KC - 1))
                                   for k in range(KC)]
                            P.group(fns, reads=[sqCb[0], bf("ones")], writes=[pb[5]])
                            rstd_finish(5, TN, None, None)
                        if 5 <= m <= 12:
                            for k in (2 * (m - 5), 2 * (m - 5) + 1):
                                stt(h2T[:, TMAP[0], k, :], xT[:, k, 0:TN], pcol(L0 + O_G2 + k), ps[:, 5, 0:TN], ALU.mult, ALU.mult,
                                    [xTb[0], pb[5], prmb], [hTb])
            def ffn(l):
                L0 = l * PL
                prmb = bf("prm")
                h2b, fCb, fLb = bf("h2T"), bf("fC"), bf("fL")
                Sb = [bf("S%d" % i) for i in range(6)]
                early = "nomixer" not in _DBG and "t01" not in _DBG
                h2tb = [bf("hT"), bf("sqb"), bf("mix")] if early else None
                sqCb = [bf("sqC%d" % i) for i in range(3)]
                bks = [5, 7, 6]
                for t in range(3):
                    c0 = t * TN
                    bk = bks[t]
                    if not early:
                        for k in range(KC):
                            P.op("act", lambda e, k=k, t=t, c0=c0: e.activation(out=fT[:, t, k, :], in_=xT[:, k, c0:c0 + TN], func=AF.Square),
                                 reads=[xTb[t]], writes=[sqCb[t]])
                    if not early or t == 2:
                        fns = [lambda e, k=k, t=t, bk=bk: e.matmul(out=ps[:, bk, 0:TN], lhsT=ones[:], rhs=fT[:, t, k, :], start=(k == 0), stop=(k == KC - 1))
                               for k in range(KC)]
                        P.group(fns, reads=[sqCb[t], bf("ones")], writes=[pb[bk]])
                        rstd_finish(bk, TN, None, None)
                        for k in range(KC):
                            stt(h2T[:, TMAP[t], k, :], xT[:, k, c0:c0 + TN], pcol(L0 + O_G2 + k), ps[:, bk, 0:TN], ALU.mult, ALU.mult,
                                [xTb[t], pb[bk], prmb], [h2tb[t] if early else h2b])
                handoff(sqCb, [fCb, fLb])
                slot3 = 0
                sc_i = 0
                tr = [((HALO if l == DEPTH - 1 else HALO - 32) if t == 0 else 0, TN) for t in range(3)]
                for j in range(DFF // FB):
                    for m in range(KC):
                        if m % 2 == 0:
                            s1, w1 = W.get(w_ff1_v[l][:, :, j * FB + m * 128:j * FB + (m + 2) * 128], KC)
                        hf = m % 2
                        b0 = 3 * (slot3 % 2)
                        slot3 += 1
                        if j == 0 and m < 3 and h2tb is not None:
                            for t in range(3):
                                fns = [lambda e, k=k, t=t, s1=s1, b0=b0, hf=hf: e.matmul(out=ps[:, b0 + t, 0:tr[t][1] - tr[t][0]], lhsT=wt[:, s1, k, hf * 128:(hf + 1) * 128],
                                                                                rhs=h2T[:, TMAP[t], k, tr[t][0]:tr[t][1]],
                                                                                start=(k == 0), stop=(k == KC - 1)) for k in range(KC)]
                                P.group(fns, reads=[w1, h2tb[t]], writes=[pb[b0 + t]])
                        else:
                            fns = [lambda e, k=k, t=t, s1=s1, b0=b0, hf=hf: e.matmul(out=ps[:, b0 + t, 0:tr[t][1] - tr[t][0]], lhsT=wt[:, s1, k, hf * 128:(hf + 1) * 128],
                                                                            rhs=h2T[:, TMAP[t], k, tr[t][0]:tr[t][1]],
                                                                            start=(k == 0), stop=(k == KC - 1)) for k in range(KC) for t in range(3)]
                            P.group(fns, reads=[w1, h2b] + (h2tb or []), writes=[pb[b0], pb[b0 + 1], pb[b0 + 2]])
                        for t in range(3):
                            o0, o1 = tr[t]
                            si = 3 + sc_i % 3
                            sc_i += 1
                            act(S[:, si, 0:o1 - o0], ps[:, b0 + t, 0:o1 - o0], AF.Square, [pb[b0 + t]], [Sb[si]])
                            stt(fT[:, t, m, o0:o1], ps[:, b0 + t, 0:o1 - o0], 0.0, S[:, si, 0:o1 - o0], ALU.is_gt, ALU.mult, [pb[b0 + t], Sb[si]],
                                [fCb if m < KC - 2 else fLb])
                    for m in range(KC):
                        if m % 2 == 0:
                            s2, w2 = W.get(w_ff2_v[l][j][:, :, m * 128:(m + 2) * 128], KC)
                        hf = m % 2
                        b0 = 3 * (slot3 % 2)
                        slot3 += 1
                        mk = lambda ks: [lambda e, k=k, t=t, s2=s2, b0=b0, hf=hf: e.matmul(out=ps[:, b0 + t, 0:tr[t][1] - tr[t][0]], lhsT=wt[:, s2, k, hf * 128:(hf + 1) * 128],
                                                                                     rhs=fT[:, t, k, tr[t][0]:tr[t][1]],
                                                                                     start=(k == 0), stop=(k == KC - 1)) for k in ks for t in range(3)]
                        if m == 0:
                            P.group(mk(range(KC - 2)), reads=[w2, fCb], writes=[pb[b0], pb[b0 + 1], pb[b0 + 2]])
                            P.group(mk(range(KC - 2, KC)), reads=[w2, fLb], writes=[pb[b0], pb[b0 + 1], pb[b0 + 2]])
                        else:
                            P.group(mk(range(KC)), reads=[w2, fCb, fLb], writes=[pb[b0], pb[b0 + 1], pb[b0 + 2]])
                        for t in range(3):
                            o0, o1 = tr[t]
                            c0 = t * TN + o0
                            nn = o1 - o0
                            P.op("dve", lambda e, m=m, t=t, b0=b0, c0=c0, nn=nn: e.tensor_tensor(out=xT[:, m, c0:c0 + nn], in0=xT[:, m, c0:c0 + nn],
                                                                                             in1=ps[:, b0 + t, 0:nn], op=ALU.add),
                                 reads=[pb[b0 + t], xTb[t]], writes=[xTb[t]])
                        if l == DEPTH - 1 and j == DFF // FB - 1:
                            if m == 0:
                                handoff([h2b, bf("hT"), bf("mix"), bf("sqb")], [bf("sqFin")])
                            P.op("act", lambda e, m=m: e.activation(out=sqFin[:, m, :], in_=xT[:, m, :], func=AF.Square),
                                 reads=xTb, writes=[bf("sqFin")])
                return [h2b, fCb, fLb]

            for l in range(DEPTH if "nolayers" not in _DBG else 0):
                if l > 0:
                    handoff([bf("h2T"), bf("hT"), bf("mix"), bf("sqb")], [bf("hT"), bf("mix"), bf("sqb")])
                P.group([lambda e: e.transpose(out=ps[:, 4, 0:128], in_=idf[:], identity=idf[:])], reads=[bf("idf")], writes=[pb[4]])
                rms1_pre(l, 0)
                stb = bf("stage")
                handoff(cbufs, [stb])
                for gq in range(4):
                    P.dma("sp", lambda e, l=l, gq=gq: e.dma_start(out=stgc[0:120, gq, :],
                                                                  in_=sconv_d[l, 4 * gq:4 * gq + 4, :, :].rearrange("b j c -> (b j) c")),
                          "sg", writes=[stb])
                for gp in range(2):
                    P.dma("sp", lambda e, l=l, gp=gp: e.dma_start(out=stgp[0:120, gp, :],
                                                                  in_=spool_d[l, 8 * gp:8 * gp + 8, :, :].rearrange("b j c -> (b j) c")),
                          "sg", writes=[stb])
                hbanks = [0, 1, 2, 3, 5, 6]
                bank = 0
                for c in range(8):
                    bk = hbanks[bank % 6]
                    bank += 1
                    fns = [lambda e, c=c, gq=gq, bk=bk: e.transpose(out=ps[:, bk, gq * 120:(gq + 1) * 120], in_=stgc[0:120, gq, c * 128:(c + 1) * 128],
                                                                   identity=idf[0:120, 0:120]) for gq in range(4)]
                    P.group(fns, reads=[stb, bf("idf")], writes=[pb[bk]])
                    copy_op(alt_eng(), usamp[:, c, :, 0:30], ps[:, bk, 0:480].rearrange("p (b j) -> p b j", j=30), [pb[bk]], [bf("usamp%d" % c)])
                    bk = hbanks[bank % 6]
                    bank += 1
                    fns = [lambda e, c=c, gp=gp, bk=bk: e.transpose(out=ps[:, bk, gp * 120:(gp + 1) * 120], in_=stgp[0:120, gp, c * 128:(c + 1) * 128],
                                                                   identity=idf[0:120, 0:120]) for gp in range(2)]
                    P.group(fns, reads=[stb, bf("idf")], writes=[pb[bk]])
                    copy_op(alt_eng(), zsamp[:, c, :, 0:15], ps[:, bk, 0:240].rearrange("p (b j) -> p b j", j=15), [pb[bk]], [bf("zsamp%d" % c)])
                mixbufs = [bf(n) for n in ("v", "vbf", "sqv", "diagA", "diagB", "sig0", "sig1", "t64", "t16")]
                handoff([stb], mixbufs)
                mixbufs_ref[0] = mixbufs
                for t in range((2 if "t01" in _DBG else 3) if "nomixer" not in _DBG else 0):
                    mixer(l, t)
                if _DBG & {"nomixer", "t01"}:
                    handoff([bf("hT"), bf("mix"), bf("sqb")], [bf("h2T")])
                if "nomixer" in _DBG or "noffn" in _DBG or "t01" in _DBG:
                    handoff(mixbufs, [bf("sqC%d" % i) for i in range(3)])
                cbufs = ffn(l)[1:] if "noffn" not in _DBG else [bf("sqC%d" % i) for i in range(3)]

            gB, yob = bf("gbc"), [bf("yout%d" % i) for i in range(3)]
            sq2b = [bf("sqh0"), bf("sqh1")]
            handoff(cbufs, [gB] + yob)
            pre_sq = not (_DBG & {"nolayers", "noffn"})
            if "nolayers" not in _DBG and not pre_sq:
                handoff([bf("h2T"), bf("hT"), bf("mix"), bf("sqb")], sq2b)
            P.dma("sp", lambda e: e.dma_start(out=gbc, in_=gfb_d[:, :]), "c0", writes=[gB])
            rtb = bf("S5")
            bank = 0
            for i in range(9):
                n = 128 if i < 8 else 64
                col0 = HALO + 128 * i
                s2, s3 = i % 2, i % 3
                sbk = 6 + i % 2
                xb_ = sorted({col0 // TN, (col0 + n - 1) // TN})
                xbs = [xTb[j] for j in xb_]
                if pre_sq:
                    fns = [lambda e, k=k, n=n, sbk=sbk, i=i, col0=col0: e.matmul(out=ps[0:n, sbk, i:i + 1], lhsT=sqFin[:, k, col0:col0 + n], rhs=ones[:, 0:1],
                                                                                 start=(k == 0), stop=(k == KC - 1), skip_group_check=True) for k in range(KC)]
                    P.group(fns, reads=[bf("sqFin"), bf("ones")], writes=[pb[sbk]])
                else:
                    for k in range(KC):
                        P.op("act", lambda e, k=k, s2=s2, n=n, col0=col0: e.activation(out=sqb[:, k, s2 * 128:s2 * 128 + n], in_=xT[:, k, col0:col0 + n], func=AF.Square),
                             reads=xbs, writes=[sq2b[s2]])
                    fns = [lambda e, k=k, s2=s2, n=n, sbk=sbk, i=i: e.matmul(out=ps[0:n, sbk, i:i + 1], lhsT=sqb[:, k, s2 * 128:s2 * 128 + n], rhs=ones[:, 0:1],
                                                                           start=(k == 0), stop=(k == KC - 1), skip_group_check=True) for k in range(KC)]
                    P.group(fns, reads=[sq2b[s2], bf("ones")], writes=[pb[sbk]])
                P.op("act", lambda e, n=n, sbk=sbk, i=i: e.activation(out=S[0:n, 5, i:i + 1], in_=ps[0:n, sbk, i:i + 1], func=AF.Sqrt, bias=EPS, scale=1.0 / D),
                     reads=[pb[sbk]], writes=[rtb])
                P.op("dve", lambda e, n=n, i=i: e.reciprocal(out=S[0:n, 5, i:i + 1], in_=S[0:n, 5, i:i + 1]), reads=[rtb], writes=[rtb])
                for q in range(4):
                    bk = bank % 6
                    bank += 1
                    fns = [lambda e, q=q, j=j, bk=bk, n=n, col0=col0: e.transpose(out=ps[0:n, bk, j * 128:(j + 1) * 128], in_=xT[:, 4 * q + j, col0:col0 + n], identity=idf[:])
                           for j in range(4)]
                    P.group(fns, reads=xbs + [bf("idf")], writes=[pb[bk]])
                    stt(yout[s3][0:n, q * 512:(q + 1) * 512], ps[0:n, bk, :], S[0:n, 5, i:i + 1], gbc[0:n, q * 512:(q + 1) * 512], ALU.mult, ALU.mult,
                        [pb[bk], rtb, gB], [yob[s3]])
                outs.append(P.dma("sp", lambda e, i=i, n=n, s3=s3: e.dma_start(out=y_d[i * 128:i * 128 + n, :], in_=yout[s3][0:n, :]),
                                  "yo%d" % s3, reads=[yob[s3]]))
            last = {}
            for h in outs:
                if last.get(h[0], 0) < h[1]:
                    last[h[0]] = h[1]
            P.wait("sp", list(last.items()))

        Wd = WStream(Prog(), wt, plan=None)
        emit(Wd.P, Wd)
        P = Prog()
        W = WStream(P, wt, plan=Wd.req)
        emit(P, W)
        assert W.i == len(Wd.req)

        sems = {n: st.enter_context(nc.semaphore(n)) for n in P.sem_names()}
        block = st.enter_context(nc.Block())

        @block.tensor
        def _(e):
            P.replay("pe", e, sems)

        @block.scalar
        def _(e):
            P.replay("act", e, sems)

        @block.vector
        def _(e):
            P.replay("dve", e, sems)

        @block.gpsimd
        def _(e):
            P.replay("pool", e, sems)

        @block.sync
        def _(e):
            P.replay("sp", e, sems)
    return nc


def _pack_params(norm1_g, b_in, conv_w, conv_b, ln_g, ln_b, pool_scale, norm2_g, norm_f, half):
    prm = np.zeros((128, NPRM), np.float32)

    def cols(v):
        return np.ascontiguousarray(np.asarray(v, np.float32).reshape(-1, 128).T)

    for l in range(DEPTH):
        L0 = l * PL
        prm[:, L0 + O_G1:L0 + O_G1 + 16] = cols(norm1_g[l])
        prm[:, L0 + O_BIN:L0 + O_BIN + 24] = cols(b_in[l])
        cw = np.asarray(conv_w[l], np.float32)
        prm[:, L0 + O_CW:L0 + O_CW + 248] = cw.reshape(31, 8, 128).transpose(2, 1, 0).reshape(128, 248)
        prm[:, L0 + O_CB:L0 + O_CB + 8] = cols(conv_b[l])
        prm[:, L0 + O_LNG:L0 + O_LNG + 8] = cols(ln_g[l])
        prm[:, L0 + O_LNB:L0 + O_LNB + 8] = cols(ln_b[l])
        prm[:, L0 + O_PSC:L0 + O_PSC + 8] = cols(pool_scale[l])
        prm[:, L0 + O_G2:L0 + O_G2 + 16] = cols(norm2_g[l])
    prm[:, O_GF:O_GF + 16] = cols(norm_f)
    prm[:, O_HM] = float(half)
    pos = np.arange(16)
    for g in range(4):
        w = 2 << g
        cnt = np.minimum(w, pos + 1) if half == 0 else np.full(16, w)
        prm[:, O_IC + g * 16:O_IC + (g + 1) * 16] = (1.0 / cnt.astype(np.float32))[None, :]
    return prm


_NC_CACHE = {}


def kernel(x_prompt, x_sample, state_conv, state_pool, norm1_g, w_in, b_in, conv_w, conv_b,
           ln_g, ln_b, pool_w, pool_scale, w_out, norm2_g, w_ff1, w_ff2, norm_f):
    f = lambda a: np.ascontiguousarray(np.asarray(a, dtype=np.float32))
    x_prompt, x_sample, state_conv, state_pool = f(x_prompt), f(x_sample), f(state_conv), f(state_pool)
    w_in, pool_w, w_out, w_ff1, w_ff2 = f(w_in), f(pool_w), f(w_out), f(w_ff1), f(w_ff2)
    cidx = np.concatenate([np.concatenate([np.arange(c * 128, (c + 1) * 128), np.arange(1024 + c * 128, 1024 + (c + 1) * 128)])
                           for c in range(8)] + [np.arange(2048, DIN)])
    w_in = np.ascontiguousarray(w_in[:, :, cidx])
    if "nc" not in _NC_CACHE:
        _NC_CACHE["nc"] = build_program()
    nc = _NC_CACHE["nc"]
    ident = np.eye(128, dtype=np.float32)
    gfb = np.ascontiguousarray(np.broadcast_to(np.asarray(norm_f, np.float32)[None, :], (128, D)))
    prms = [_pack_params(norm1_g, b_in, conv_w, conv_b, ln_g, ln_b, pool_scale, norm2_g, norm_f, h) for h in range(2)]
    in_maps = []
    for i in range(NCORES):
        b, h = i // 2, i % 2
        xc = np.zeros((T, D), np.float32)
        if h == 1:
            xc[0:HALO] = x_prompt[b, NPR - HALO:NPR]
        xc[HALO:HALO + NPR] = x_prompt[b, h * NPR:(h + 1) * NPR]
        xc[HALO + NPR:] = x_sample[SBQ * i:SBQ * (i + 1)].reshape(NSM, D)
        in_maps.append({
            "x": xc,
            "sconv": np.ascontiguousarray(state_conv[:, SBQ * i:SBQ * (i + 1)]),
            "spool": np.ascontiguousarray(state_pool[:, SBQ * i:SBQ * (i + 1)]),
            "prm": prms[h], "ident": ident, "gfb": gfb,
            "w_in": w_in, "pool_w": pool_w, "w_out": w_out, "w_ff1": w_ff1, "w_ff2": w_ff2,
        })
    ncr = int(os.environ.get("KCORES", NCORES))
    res = run_bass_kernel_spmd(nc, in_maps[:ncr], core_ids=list(range(ncr)))
    R = list(res.results) + [res.results[0]] * (NCORES - ncr)
    B_, S_ = x_prompt.shape[0], x_prompt.shape[1]
    y_prompt = np.empty((B_, S_, D), np.float32)
    y_sample = np.empty(x_sample.shape, np.float32)
    ncp = np.empty((DEPTH, B_, 30, 1024), np.float32)
    npp = np.empty((DEPTH, B_, 15, 1024), np.float32)
    ncs = np.empty(state_conv.shape, np.float32)
    nps = np.empty(state_pool.shape, np.float32)
    for i in range(NCORES):
        b, h = i // 2, i % 2
        y = R[i]["y"]
        y_prompt[b, h * NPR:(h + 1) * NPR] = y[0:NPR]
        y_sample[SBQ * i:SBQ * (i + 1)] = y[NPR:].reshape(SBQ, 4, D)
        ncs[:, SBQ * i:SBQ * (i + 1)] = R[i]["ocs"]
        nps[:, SBQ * i:SBQ * (i + 1)] = R[i]["ops"]
        if h == 1:
            ncp[:, b] = R[i]["ocp"]
            npp[:, b] = R[i]["opp"]
    return (y_prompt, y_sample, ncp, npp, ncs, nps)
```
